# Optimizing a Trainium2 kernel written in Bass

```python
import jax, jax.numpy as jnp
from jax import lax
import numpy as np

D_MODEL = 1024
BATCH = 16
SEQ = 2048
DEPTH = 2

GRID_W = 64
CTX_LEN = 256
ROPE_THETA = 10000.0
NORM_EPS = 1e-6
NEG = -1e30

BRANCH_WIDTH = D_MODEL // 2
N_BRANCH = 3

A_HEAD = 64
A_HEADS = BRANCH_WIDTH // A_HEAD
A_DECAY_LORA = 64
A_ICLR_LORA = 64
A_GN_EPS = 64e-5
A_SHIFT_COLS = 3 * BRANCH_WIDTH + 2 * A_DECAY_LORA + 2 * A_ICLR_LORA

B_HEADS = 8
B_NOPE = 64
B_ROPE = 32
B_QK = B_NOPE + B_ROPE
B_V = BRANCH_WIDTH // B_HEADS
B_Q_LORA = 256
B_KV_LORA = 128
B_PROJ_COLS = B_Q_LORA + B_KV_LORA + B_ROPE
Q_BLOCK = 128

C_HEAD = 64
C_HEADS = BRANCH_WIDTH // C_HEAD
C_KV_HEADS = 2
C_GROUP = C_HEADS // C_KV_HEADS
C_KV_WIDTH = C_KV_HEADS * C_HEAD
C_PROJ_COLS = BRANCH_WIDTH + 2 * C_KV_WIDTH
WINDOW = 128
W_BLOCK = 128

IN_CUTS = (
    A_SHIFT_COLS,
    A_SHIFT_COLS + BRANCH_WIDTH,
    A_SHIFT_COLS + BRANCH_WIDTH + B_PROJ_COLS,
    A_SHIFT_COLS + 2 * BRANCH_WIDTH + B_PROJ_COLS,
    A_SHIFT_COLS + 2 * BRANCH_WIDTH + B_PROJ_COLS + C_PROJ_COLS,
    A_SHIFT_COLS + 3 * BRANCH_WIDTH + B_PROJ_COLS + C_PROJ_COLS,
)
N_IN = IN_CUTS[-1] + N_BRANCH * D_MODEL

kernel_name = "hybrid_rwkv7_mla_swa_diffusion_block"


def rms_norm(x, gain, eps=NORM_EPS):
    xf = x.astype(jnp.float32)
    y = xf * lax.rsqrt(jnp.mean(xf * xf, axis=-1, keepdims=True) + eps)
    return (y * gain.astype(jnp.float32)).astype(x.dtype)


def axial_rope_tables(n_tokens, rot_dim):
    rows = n_tokens // GRID_W
    row = jnp.repeat(jnp.arange(rows), GRID_W).astype(jnp.float32)
    col = jnp.tile(jnp.arange(GRID_W), rows).astype(jnp.float32)
    axis_dim = rot_dim // 2
    inv = ROPE_THETA ** (-(2.0 * jnp.arange(axis_dim // 2, dtype=jnp.float32)) / axis_dim)
    ang = jnp.concatenate([row[:, None] * inv, col[:, None] * inv], axis=-1)
    return jnp.cos(ang), jnp.sin(ang)


def apply_axial_rope(x, cos, sin):
    a = x.shape[-1] // 2
    q = a // 2
    c = cos[:, None, :].astype(x.dtype)
    s = sin[:, None, :].astype(x.dtype)

    def rot(xa, ca, sa):
        x1, x2 = xa[..., :q], xa[..., q:]
        return jnp.concatenate([x1 * ca - x2 * sa, x1 * sa + x2 * ca], axis=-1)

    return jnp.concatenate([rot(x[..., :a], c[..., :q], s[..., :q]),
                            rot(x[..., a:], c[..., q:], s[..., q:])], axis=-1)


def token_shift(z, mu_prev, mu_next):
    prev = jnp.pad(z[:, :-1], ((0, 0), (1, 0), (0, 0)))
    nxt = jnp.pad(z[:, 1:], ((0, 0), (0, 1), (0, 0)))
    return z + mu_prev * (prev - z) + mu_next * (nxt - z)


def rwkv7_prepare(z, mu_prev, mu_next, w0, w_up, a0, a_up, k_k, k_a):
    B, T, _ = z.shape
    z = token_shift(z, mu_prev, mu_next).astype(jnp.float32)
    r, k, v, wd, ad = jnp.split(
        z, [BRANCH_WIDTH, 2 * BRANCH_WIDTH, 3 * BRANCH_WIDTH, 3 * BRANCH_WIDTH + 2 * A_DECAY_LORA], axis=-1)
    wd = wd.reshape(B, T, 2, A_DECAY_LORA)
    ad = ad.reshape(B, T, 2, A_ICLR_LORA)
    w_log = -jax.nn.softplus(-(w0 + jnp.einsum('btdr,drc->btdc', jnp.tanh(wd), w_up))) - 0.5
    decay = jnp.exp(-jnp.exp(w_log))
    a = jax.nn.sigmoid(a0 + jnp.einsum('btdr,drc->btdc', ad, a_up))
    kk = (k * k_k).reshape(B, T, A_HEADS, A_HEAD)
    kk = kk * lax.rsqrt(jnp.maximum(jnp.sum(kk * kk, axis=-1, keepdims=True), 1e-24))
    k_dir = k[:, :, None, :] * (1.0 + (a - 1.0) * k_a)
    heads = lambda t: t.reshape(B, T, 2, A_HEADS, A_HEAD)
    return (r.reshape(B, T, A_HEADS, A_HEAD), v.reshape(B, T, A_HEADS, A_HEAD), kk,
            heads(decay), heads(k_dir), heads(a))


def rwkv7_scan(r, decay, k, v, kk, a, s0, reverse):
    def step(S, inp):
        r_t, w_t, k_t, v_t, kk_t, a_t = inp
        sa = jnp.einsum('bhvk,bhk->bhv', S, kk_t)
        S = (S * w_t[:, :, None, :] - sa[..., None] * (kk_t * a_t)[:, :, None, :]
             + v_t[..., None] * k_t[:, :, None, :])
        return S, jnp.einsum('bhvk,bhk->bhv', S, r_t)

    xs = tuple(jnp.swapaxes(t, 0, 1) for t in (r, decay, k, v, kk, a))
    s_final, ys = lax.scan(step, s0, xs, reverse=reverse)
    return s_final, jnp.swapaxes(ys, 0, 1)


def rwkv7_readout(y, r, v, k_dir, r_k, gn_g, gn_b, dtype):
    B, T = y.shape[:2]
    mu = jnp.mean(y, axis=-1, keepdims=True)
    var = jnp.mean(jnp.square(y - mu), axis=-1, keepdims=True)
    yn = ((y - mu) * lax.rsqrt(var + A_GN_EPS)).reshape(B, T, BRANCH_WIDTH) * gn_g + gn_b
    bonus = jnp.sum(r[:, :, None] * k_dir * r_k, axis=-1, keepdims=True) * v[:, :, None]
    return (yn + jnp.sum(bonus, axis=2).reshape(B, T, BRANCH_WIDTH)).astype(dtype)


def rwkv7_branch(za, zac, mu_prev, mu_next, w0, w_up, a0, a_up, k_k, k_a, r_k, gn_g, gn_b, ctx_out):
    lat = rwkv7_prepare(za, mu_prev, mu_next, w0, w_up, a0, a_up, k_k, k_a)
    cx = rwkv7_prepare(zac, mu_prev, mu_next, w0, w_up, a0, a_up, k_k, k_a)
    B = za.shape[0]
    s0 = jnp.zeros((B, A_HEADS, A_HEAD, A_HEAD), jnp.float32)
    y_lat = jnp.zeros(lat[0].shape, jnp.float32)
    y_ctx = jnp.zeros(cx[0].shape, jnp.float32)
    for d, rev in ((0, False), (1, True)):
        sel = lambda p: (p[0], p[3][:, :, d], p[4][:, :, d], p[1], p[2], p[5][:, :, d])
        s_ctx, yc = rwkv7_scan(*sel(cx), s0, rev)
        _, yl = rwkv7_scan(*sel(lat), s_ctx, rev)
        y_lat = y_lat + yl
        y_ctx = y_ctx + yc
    out_lat = rwkv7_readout(y_lat, lat[0], lat[1], lat[4], r_k, gn_g, gn_b, za.dtype)
    out_ctx = rwkv7_readout(y_ctx, cx[0], cx[1], cx[4], r_k, gn_g, gn_b, za.dtype) if ctx_out else None
    return out_lat, out_ctx


def mla_project(zb, q_ln, kv_ln, w_uq, w_ukv, qn_g, kn_g, rope):
    B, T, _ = zb.shape
    cq, ckv, kr = jnp.split(zb, [B_Q_LORA, B_Q_LORA + B_KV_LORA], axis=-1)
    q = (rms_norm(cq, q_ln) @ w_uq).reshape(B, T, B_HEADS, B_QK)
    kv = (rms_norm(ckv, kv_ln) @ w_ukv).reshape(B, T, B_HEADS, B_NOPE + B_V)
    k_nope, v = kv[..., :B_NOPE], kv[..., B_NOPE:]
    k = jnp.concatenate([k_nope, jnp.broadcast_to(kr[:, :, None, :], (B, T, B_HEADS, B_ROPE))], axis=-1)
    q = rms_norm(q, qn_g)
    k = rms_norm(k, kn_g)
    if rope is not None:
        cos, sin = rope
        q = jnp.concatenate([q[..., :B_NOPE], apply_axial_rope(q[..., B_NOPE:], cos, sin)], axis=-1)
        k = jnp.concatenate([k[..., :B_NOPE], apply_axial_rope(k[..., B_NOPE:], cos, sin)], axis=-1)
    return q, k, v


def dense_block_attention(q, k, v):
    B, T, H, Dq = q.shape
    nb = T // Q_BLOCK
    qb = jnp.swapaxes(q.reshape(B, nb, Q_BLOCK, H, Dq), 0, 1)

    def one(qx):
        s = jnp.einsum('bqhd,bkhd->bhqk', qx, k, preferred_element_type=jnp.float32) * (Dq ** -0.5)
        p = jax.nn.softmax(s, axis=-1).astype(v.dtype)
        return jnp.einsum('bhqk,bkhd->bqhd', p, v)

    o = lax.map(one, qb)
    return jnp.swapaxes(o, 0, 1).reshape(B, T, H * v.shape[-1])


def gqa_project(zc, qn_g, kn_g, rope):
    B, T, _ = zc.shape
    q, k, v = jnp.split(zc, [BRANCH_WIDTH, BRANCH_WIDTH + C_KV_WIDTH], axis=-1)
    q = rms_norm(q.reshape(B, T, C_HEADS, C_HEAD), qn_g)
    k = rms_norm(k.reshape(B, T, C_KV_HEADS, C_HEAD), kn_g)
    v = v.reshape(B, T, C_KV_HEADS, C_HEAD)
    if rope is not None:
        q = apply_axial_rope(q, *rope)
        k = apply_axial_rope(k, *rope)
    return q, k, v


def sink_gqa(q, k, v, sink, valid):
    B, Q, H, D = q.shape
    qg = q.reshape(B, Q, C_KV_HEADS, C_GROUP, D)
    s = jnp.einsum('bqkgd,bskd->bkgqs', qg, k, preferred_element_type=jnp.float32) * (D ** -0.5)
    s = jnp.where(valid, s, NEG)
    sk = jnp.broadcast_to(sink.astype(jnp.float32).reshape(C_KV_HEADS, C_GROUP, 1, 1), s.shape[:-1] + (1,))
    p = jax.nn.softmax(jnp.concatenate([s, sk], axis=-1), axis=-1)[..., :-1].astype(v.dtype)
    return jnp.einsum('bkgqs,bskd->bqkgd', p, v).reshape(B, Q, H * D)


def window_attention(q, k, v, kc, vc, sink):
    B, T, H, D = q.shape
    L = kc.shape[1]
    nb = T // W_BLOCK

    def band(t):
        tp = jnp.pad(t, ((0, 0), (W_BLOCK, W_BLOCK), (0, 0), (0, 0)))
        views = [tp[:, j * W_BLOCK: j * W_BLOCK + T].reshape(B, nb, W_BLOCK, C_KV_HEADS, D) for j in range(3)]
        return jnp.swapaxes(jnp.concatenate(views, axis=2), 0, 1)

    kb, vb = band(k), band(v)
    qb = jnp.swapaxes(q.reshape(B, nb, W_BLOCK, H, D), 0, 1)
    qi = jnp.arange(W_BLOCK)[:, None]
    kj = jnp.arange(3 * W_BLOCK)[None, :]
    kpos = jnp.arange(nb)[:, None, None] * W_BLOCK + kj - W_BLOCK
    valid = (jnp.abs(kj - W_BLOCK - qi) <= WINDOW)[None] & (kpos >= 0) & (kpos < T)
    valid = jnp.concatenate([valid, jnp.ones((nb, W_BLOCK, L), bool)], axis=-1)

    def one(args):
        qx, kx, vx, m = args
        return sink_gqa(qx, jnp.concatenate([kx, kc], axis=1), jnp.concatenate([vx, vc], axis=1), sink, m)

    o = lax.map(one, (qb, kb, vb, valid))
    return jnp.swapaxes(o, 0, 1).reshape(B, T, H * D)


def merge_branches(ys, gs, zg, w_branch_out, w_out):
    gates = jax.nn.sigmoid(zg).reshape(zg.shape[:-1] + (N_BRANCH, D_MODEL))
    m = gates[..., 0, :] * ((ys[0] * jax.nn.silu(gs[0])) @ w_branch_out[0])
    for n in range(1, N_BRANCH):
        m = m + gates[..., n, :] * ((ys[n] * jax.nn.silu(gs[n])) @ w_branch_out[n])
    return m @ w_out


def trunk_layer(x, xc, c, c_ctx, ada_w, ada_b, norm_g, w_in,
                a_mu_prev, a_mu_next, a_w0, a_w_up, a_a0, a_a_up, a_k_k, a_k_a, a_r_k, a_gn_g, a_gn_b,
                b_q_ln, b_kv_ln, b_w_uq, b_w_ukv, b_qn_g, b_kn_g,
                c_qn_g, c_kn_g, c_sink, w_branch_out, w_out, rope_b, rope_c, ctx_out):
    mod = jax.nn.silu(c) @ ada_w + ada_b
    mod_c = jax.nn.silu(c_ctx) @ ada_w + ada_b
    shift, scale, gate = jnp.split(mod[:, None, :], 3, axis=-1)
    shift_c, scale_c, gate_c = jnp.split(mod_c, 3, axis=-1)
    h = rms_norm(x, norm_g) * (1.0 + scale) + shift
    hc = rms_norm(xc, norm_g) * (1.0 + scale_c) + shift_c

    za, ga, zb, gb, zc, gc, zg = jnp.split(h @ w_in, list(IN_CUTS), axis=-1)
    zac, gac, zbc, gbc, zcc, gcc, zgc = jnp.split(hc @ w_in, list(IN_CUTS), axis=-1)

    ya, yac = rwkv7_branch(za, zac, a_mu_prev, a_mu_next, a_w0, a_w_up, a_a0, a_a_up,
                           a_k_k, a_k_a, a_r_k, a_gn_g, a_gn_b, ctx_out)

    qb_, kb_, vb_ = mla_project(zb, b_q_ln, b_kv_ln, b_w_uq, b_w_ukv, b_qn_g, b_kn_g, rope_b)
    qbc, kbc, vbc = mla_project(zbc, b_q_ln, b_kv_ln, b_w_uq, b_w_ukv, b_qn_g, b_kn_g, None)
    yb = dense_block_attention(qb_, jnp.concatenate([kbc, kb_], axis=1), jnp.concatenate([vbc, vb_], axis=1))

    qc_, kc_, vc_ = gqa_project(zc, c_qn_g, c_kn_g, rope_c)
    qcc, kcc, vcc = gqa_project(zcc, c_qn_g, c_kn_g, None)
    yc = window_attention(qc_, kc_, vc_, kcc, vcc, c_sink)

    x_new = x + gate * merge_branches((ya, yb, yc), (ga, gb, gc), zg, w_branch_out, w_out)
    if ctx_out:
        ybc = dense_block_attention(qbc, kbc, vbc)
        L = xc.shape[1]
        ycc = sink_gqa(qcc, kcc, vcc, c_sink, jnp.ones((L, L), bool))
        xc_new = xc + gate_c * merge_branches((yac, ybc, ycc), (gac, gbc, gcc), zgc, w_branch_out, w_out)
    else:
        xc_new = xc
    return x_new, xc_new


def setup_inputs(seed: int = 0) -> dict:
    key = jax.random.key(seed)
    ks = iter(jax.random.split(key, 48))
    nrm = lambda shape, s: jax.random.normal(next(ks), shape, jnp.float32) * s
    uni = lambda shape, lo, hi: jax.random.uniform(next(ks), shape, jnp.float32, lo, hi)
    L = DEPTH
    return {
        "x": nrm((BATCH, SEQ, D_MODEL), 1.0),
        "c": nrm((BATCH, D_MODEL), 1.0),
        "ctx": nrm((BATCH, CTX_LEN, D_MODEL), 1.0),
        "c_ctx": nrm((D_MODEL,), 1.0),
        "ada_w": nrm((L, D_MODEL, 3 * D_MODEL), 0.5 * D_MODEL ** -0.5),
        "ada_b": nrm((L, 3 * D_MODEL), 0.02),
        "norm_g": 1.0 + nrm((L, D_MODEL), 0.02),
        "w_in": nrm((L, D_MODEL, N_IN), D_MODEL ** -0.5),
        "a_mu_prev": uni((L, A_SHIFT_COLS), 0.0, 0.5),
        "a_mu_next": uni((L, A_SHIFT_COLS), 0.0, 0.5),
        "a_w0": uni((L, 2, BRANCH_WIDTH), -6.0, 1.0),
        "a_w_up": nrm((L, 2, A_DECAY_LORA, BRANCH_WIDTH), 0.1),
        "a_a0": nrm((L, 2, BRANCH_WIDTH), 0.5),
        "a_a_up": nrm((L, 2, A_ICLR_LORA, BRANCH_WIDTH), 0.1),
        "a_k_k": 0.85 + nrm((L, BRANCH_WIDTH), 0.02),
        "a_k_a": 1.0 + nrm((L, BRANCH_WIDTH), 0.02),
        "a_r_k": nrm((L, A_HEADS, A_HEAD), 0.1),
        "a_gn_g": 1.0 + nrm((L, BRANCH_WIDTH), 0.02),
        "a_gn_b": nrm((L, BRANCH_WIDTH), 0.02),
        "b_q_ln": 1.0 + nrm((L, B_Q_LORA), 0.02),
        "b_kv_ln": 1.0 + nrm((L, B_KV_LORA), 0.02),
        "b_w_uq": nrm((L, B_Q_LORA, B_HEADS * B_QK), B_Q_LORA ** -0.5),
        "b_w_ukv": nrm((L, B_KV_LORA, B_HEADS * (B_NOPE + B_V)), B_KV_LORA ** -0.5),
        "b_qn_g": 1.0 + nrm((L, B_QK), 0.02),
        "b_kn_g": 1.0 + nrm((L, B_QK), 0.02),
        "c_qn_g": 1.0 + nrm((L, C_HEAD), 0.02),
        "c_kn_g": 1.0 + nrm((L, C_HEAD), 0.02),
        "c_sink": nrm((L, C_HEADS), 0.5),
        "w_branch_out": nrm((L, N_BRANCH, BRANCH_WIDTH, D_MODEL), BRANCH_WIDTH ** -0.5),
        "w_out": nrm((L, D_MODEL, D_MODEL), D_MODEL ** -0.5),
    }


def reference(x, c, ctx, c_ctx, ada_w, ada_b, norm_g, w_in,
              a_mu_prev, a_mu_next, a_w0, a_w_up, a_a0, a_a_up, a_k_k, a_k_a, a_r_k, a_gn_g, a_gn_b,
              b_q_ln, b_kv_ln, b_w_uq, b_w_ukv, b_qn_g, b_kn_g,
              c_qn_g, c_kn_g, c_sink, w_branch_out, w_out):
    n_tok = x.shape[1]
    rope_b = axial_rope_tables(n_tok, B_ROPE)
    rope_c = axial_rope_tables(n_tok, C_HEAD)
    xc = ctx
    for i in range(DEPTH):
        x, xc = trunk_layer(
            x, xc, c, c_ctx, ada_w[i], ada_b[i], norm_g[i], w_in[i],
            a_mu_prev[i], a_mu_next[i], a_w0[i], a_w_up[i], a_a0[i], a_a_up[i],
            a_k_k[i], a_k_a[i], a_r_k[i], a_gn_g[i], a_gn_b[i],
            b_q_ln[i], b_kv_ln[i], b_w_uq[i], b_w_ukv[i], b_qn_g[i], b_kn_g[i],
            c_qn_g[i], c_kn_g[i], c_sink[i], w_branch_out[i], w_out[i],
            rope_b, rope_c, i < DEPTH - 1)
    return x
```

```python
import math
import os
from contextlib import ExitStack

import numpy as np
import concourse.bass as bass
import concourse.mybir as mybir
from concourse.bass_utils import run_bass_kernel_spmd

F32 = mybir.dt.float32
ALU = mybir.AluOpType
AF = mybir.ActivationFunctionType
AX = mybir.AxisListType

D = 1024
NB = 2
LC = 256
TL = 2048
T = LC + TL
NT = T // 128
DEPTH = 2
NIN = 7584
EPS = 1e-6
GN_EPS = 64e-5
ZW = 4768
ZO_GA, ZO_ZB, ZO_ZC, ZO_ZG = 0, 512, 928, 1696
TCH = [(0, 512), (512, 512), (1024, 512), (1536, 512), (2048, 256)]


class Buf:
    __slots__ = ("t", "w", "r", "name", "psum")

    def __init__(self, t=None, name="", psum=False):
        self.t = t
        self.w = {}
        self.r = {}
        self.name = name
        self.psum = psum

    def __getitem__(self, k):
        return self.t[k]


class Sched:
    ENGS = ("pe", "act", "dve", "pool", "sp")
    QS = ("sp", "pool")

    def __init__(self, nc, stack, ndma=8):
        self.nc = nc
        self.prog = {e: [] for e in self.ENGS}
        self.sem = {e: stack.enter_context(nc.semaphore("s_" + e)) for e in self.ENGS}
        self.cnt = {e: 0 for e in self.ENGS}
        self.waited = {e: {} for e in self.ENGS}
        self.ndma = ndma
        self.dsem = {q: [stack.enter_context(nc.semaphore("d_%s%d" % (q, i))) for i in range(ndma)] for q in self.QS}
        self.dcnt = {q: [0] * ndma for q in self.QS}
        self.dnext = {q: 0 for q in self.QS}
        self.total = 0

    def _deps(self, e, reads, writes, pwrites):
        best = {}

        def add(d):
            for k, sv in d.items():
                if k not in best or best[k][1] < sv[1]:
                    best[k] = sv
        for b in reads:
            add(b.w)
            if b.psum:
                own = id(self.sem[e]) if e in self.sem else None
                add({k: sv for k, sv in b.r.items() if k != own})
        for b in writes:
            add(b.w)
            add(b.r)
        for b in pwrites:
            add(b.r)
        out = []
        wd = self.waited[e]
        for k, (s, v) in best.items():
            if e == "pe" and s is self.sem["pe"]:
                continue
            if wd.get(k, 0) >= v:
                continue
            wd[k] = v
            out.append((s, v))
        return out

    @staticmethod
    def _mark(reads, writes, pwrites, tok):
        k = id(tok[0])
        for b in reads:
            if k not in b.r or b.r[k][1] < tok[1]:
                b.r[k] = tok
        for b in writes:
            b.w = {k: tok}
            b.r = {}
        for b in pwrites:
            if k not in b.w or b.w[k][1] < tok[1]:
                b.w[k] = tok

    def op(self, e, fn, reads=(), writes=(), pwrites=()):
        deps = self._deps(e, reads, writes, pwrites)
        self.cnt[e] += 1
        tok = (self.sem[e], self.cnt[e])
        self.prog[e].append((deps, fn, (self.sem[e], 1)))
        self._mark(reads, writes, pwrites, tok)
        return tok

    def dma(self, q, out, in_, reads=(), writes=(), pwrites=(), **kw):
        i = self.dnext[q]
        self.dnext[q] = (i + 1) % self.ndma
        s = self.dsem[q][i]
        deps = self._deps(q, reads, writes, pwrites)
        prev = self.dcnt[q][i]
        if prev > 0 and self.waited[q].get(id(s), 0) < prev:
            deps.append((s, prev))
            self.waited[q][id(s)] = prev
        self.dcnt[q][i] = prev + 16
        tok = (s, prev + 16)
        self.prog[q].append((deps, (lambda eng: eng.dma_start(out=out, in_=in_, **kw)), (s, 16)))
        self._mark(reads, writes, pwrites, tok)
        return tok

    def barrier(self):
        alld = [(self.sem[x], self.cnt[x]) for x in self.ENGS if self.cnt[x] > 0]
        for q in self.QS:
            for i in range(self.ndma):
                if self.dcnt[q][i] > 0:
                    alld.append((self.dsem[q][i], self.dcnt[q][i]))
        for e in self.ENGS:
            deps = []
            wd = self.waited[e]
            for (s, v) in alld:
                if s is self.sem[e]:
                    continue
                if wd.get(id(s), 0) >= v:
                    continue
                wd[id(s)] = v
                deps.append((s, v))
            self.prog[e].append((deps, None, None))

    def emit(self):
        with self.nc.Block() as block:
            def mk(e):
                def body(eng):
                    for deps, fn, inc in self.prog[e]:
                        for (s, v) in deps:
                            eng.wait_ge(s, v)
                        if fn is not None:
                            fn(eng).then_inc(inc[0], inc[1])
                return body
            block.tensor(mk("pe"))
            block.scalar(mk("act"))
            block.vector(mk("dve"))
            block.gpsimd(mk("pool"))
            block.sync(mk("sp"))
        for e in self.ENGS:
            self.total += len(self.prog[e]) + sum(len(d) for d, _, _ in self.prog[e])
            self.prog[e] = []


def build_program(n_layers=DEPTH, dbg=(), stop_after=None):
    nc = bass.Bass("TRN2", target_bir_lowering=False)
    dbg = set(dbg)

    def din(name, shape):
        return nc.dram_tensor(name, list(shape), F32, kind="ExternalInput").ap()

    def dscr(name, shape):
        kind = "ExternalOutput" if name in dbg else "Internal"
        return nc.dram_tensor(name, list(shape), F32, kind=kind).ap()

    xin = din("xin", [NB, T, D])
    cT_d = din("cT", [128, 8, 3])
    consts_d = din("consts", [128, 1024])
    ropeB_d = din("ropeB", [T, 64])
    ropeC_d = din("ropeC", [T, 128])
    W = {}
    for nm, shp in [("ada_w", [DEPTH, D, 3 * D]), ("ada_bT", [DEPTH, 128, 24]), ("norm_gT", [DEPTH, 128, 8]),
                    ("w_in", [DEPTH, D, NIN]), ("mupT", [DEPTH, 128, 14]), ("munT", [DEPTH, 128, 14]),
                    ("w0T", [DEPTH, 128, 8]), ("a0T", [DEPTH, 128, 8]), ("kkT", [DEPTH, 128, 4]),
                    ("kaT", [DEPTH, 128, 4]), ("rkT", [DEPTH, 128, 4]), ("w_up", [DEPTH, 128, 512]),
                    ("a_up", [DEPTH, 128, 512]), ("gn_g", [DEPTH, 1, 512]), ("gn_b", [DEPTH, 1, 512]),
                    ("q_ln", [DEPTH, 1, 256]), ("kv_ln", [DEPTH, 1, 128]), ("w_uq", [DEPTH, 256, 768]),
                    ("w_ukv", [DEPTH, 128, 1024]), ("bqk_g", [DEPTH, 1, 192]), ("c_qn", [DEPTH, 1, 64]),
                    ("c_kn", [DEPTH, 1, 64]), ("c_sink", [DEPTH, 1, 8]), ("wbo", [DEPTH, 3, 512, D]),
                    ("w_out", [DEPTH, D, D])]:
        W[nm] = din(nm, shp)
    yout = nc.dram_tensor("yout", [NB, TL, D], F32, kind="ExternalOutput").ap()

    X1 = dscr("X1", [NB, T, D])
    MODG = dscr("MODG", [3, D])
    ZTM = dscr("ZTM", [NB, T, ZW])
    ZTA = dscr("ZTA", [NB, 14, 128, T])
    GT = dscr("GT", [NB, 2, 8, 64, T])
    SC = dscr("SC", [2, 128, 5, 8, T])
    VT = dscr("VT", [T, 1024])
    YD = dscr("YD", [2, T, 1024])
    BS = dscr("BS", [NB, T, 8])
    QKT = dscr("QKT", [NB, 16, 96, T])
    VB = dscr("VB", [NB, T, 512])
    QKTC = dscr("QKTC", [NB, 10, 64, T])
    VC = dscr("VC", [NB, T, 128])
    UT = dscr("UT", [NB, 2, 8, 64, T])

    bX = [Buf(None, "xin"), Buf(None, "X1"), Buf(None, "yout")]
    bMODG = Buf(None, "MODG")
    bZTM = [Buf(None, "ZTM%d" % b) for b in range(NB)]
    bZTA = [Buf(None, "ZTA%d" % b) for b in range(NB)]
    bGT = [Buf(None, "GT%d" % b) for b in range(NB)]
    bSC = Buf(None, "SC")
    bVT = Buf(None, "VT")
    bYD = Buf(None, "YD")
    bBS = [Buf(None, "BS%d" % b) for b in range(NB)]
    bQKT = [Buf(None, "QKT%d" % b) for b in range(NB)]
    bVB = [Buf(None, "VB%d" % b) for b in range(NB)]
    bQKTC = [Buf(None, "QKTC%d" % b) for b in range(NB)]
    bVC = [Buf(None, "VC%d" % b) for b in range(NB)]
    bUT = [Buf(None, "UT%d" % b) for b in range(NB)]
    bW = Buf(None, "weights")

    outer = ExitStack()
    S = Sched(nc, outer)
    uid = [0]

    def sbt(stack, shape, name=None):
        uid[0] += 1
        nm = "%s_%d" % (name or "t", uid[0])
        return Buf(stack.enter_context(nc.sbuf_tensor(nm, list(shape), F32)), nm)

    cst = sbt(outer, [128, 1024], "cst")
    S.dma("sp", cst[:], consts_d[:, :], reads=[bW], writes=[cst])
    ident = cst.t[:, 0:128]
    bones = cst.t[:, 128:256]
    Z2 = cst.t[:, 256:511]
    mask_lo = cst.t[:, 512:640]
    mask_hi = cst.t[:, 640:768]
    ind2 = cst.t[:, 768:770]
    ones64 = cst.t[:, 776:840]
    PS = [Buf(outer.enter_context(nc.psum_tensor("ps%d" % i, [128, 512], F32)), "ps%d" % i, psum=True) for i in range(8)]
    psi = [0]

    def nps():
        p = PS[psi[0] % 8]
        psi[0] += 1
        return p

    modT = sbt(outer, [128, 24, 3], "modT")
    Gt = sbt(outer, [128, 8, 3], "Gt")
    csil = sbt(outer, [128, 8, 3], "csil")
    S.dma("sp", csil[:], cT_d[:, :, :], reads=[bW], writes=[csil])
    S.op("act", lambda e: e.activation(csil[:], csil[:], AF.Silu), reads=[csil], writes=[csil])

    def phase_end(st):
        S.barrier()
        S.emit()
        st.close()

    def mm(ps, out_ap, lhsT, rhs, start, stop, reads):
        S.op("pe", lambda e: e.matmul(out_ap, lhsT, rhs, start=start, stop=stop), reads=reads, pwrites=[ps] if not start else (), writes=[ps] if start else ())

    def mmp(ps, out_ap, lhsT, rhs, start, stop, reads):
        S.op("pe", lambda e: e.matmul(out_ap, lhsT, rhs, start=start, stop=stop), reads=reads, pwrites=[ps])

    def bc_load(st, src_row_ap, n, parts=128, name="bc"):
        t = sbt(st, [parts, n], name)
        S.dma("sp", t[:], src_row_ap.partition_broadcast(parts), reads=[bW], writes=[t])
        return t

    evac_rr = [0]

    def evac(out_ap, in_ap, reads, writes=(), pwrites=(), func=None, bias=None, scale=None):
        if func is None and bias is None and scale is None:
            evac_rr[0] += 1
            if evac_rr[0] % 2 == 0:
                S.op("dve", lambda e: e.tensor_copy(out_ap, in_ap), reads=reads, writes=writes, pwrites=pwrites)
            else:
                S.op("act", lambda e: e.copy(out_ap, in_ap), reads=reads, writes=writes, pwrites=pwrites)
        else:
            kw = {}
            if bias is not None:
                kw["bias"] = bias
            if scale is not None:
                kw["scale"] = scale
            f = func if func is not None else AF.Identity
            S.op("act", lambda e: e.activation(out_ap, in_ap, f, **kw), reads=reads, writes=writes, pwrites=pwrites)

    def rstd_from_ss(st_tiles, ss, n, inv_n, eps):
        S.op("dve", lambda e: e.tensor_scalar(ss[:, 0:n], ss[:, 0:n], inv_n, eps, ALU.mult, ALU.add), reads=[ss], writes=[ss])
        S.op("act", lambda e: e.activation(ss[:, 0:n], ss[:, 0:n], AF.Sqrt), reads=[ss], writes=[ss])
        S.op("dve", lambda e: e.reciprocal(ss[:, 0:n], ss[:, 0:n]), reads=[ss], writes=[ss])

    def rope(st, x3, nh, R, cs, tmp1, tmp2):
        q = R // 4
        xb = x3_buf[0]
        t1 = tmp1.t[:, 0:nh * R].rearrange("p (h r) -> p h r", h=nh)
        cosb = cs.t[:, 0:R].unsqueeze(1).to_broadcast([128, nh, R])
        S.op("dve", lambda e: e.tensor_tensor(t1, x3, cosb, ALU.mult), reads=[xb, cs], writes=[tmp1])
        x5 = x3.rearrange("p h (a f i) -> p h a f i", a=2, f=2)
        t5 = t1.rearrange("p h (a f i) -> p h a f i", a=2, f=2)
        s4 = cs.t[:, R:2 * R].rearrange("p (a f i) -> p a f i", a=2, f=2)
        t2 = tmp2.t[:, 0:nh * R // 2].rearrange("p (h a i) -> p h a i", h=nh, a=2)
        for hf in (0, 1):
            xin_ = x5[:, :, :, 1 - hf, :]
            sb_ = s4[:, :, hf, :].unsqueeze(1).to_broadcast([128, nh, 2, q])
            S.op("dve", lambda e, xin_=xin_, sb_=sb_: e.tensor_tensor(t2, xin_, sb_, ALU.mult), reads=[xb, cs], writes=[tmp2])
            tt = t5[:, :, :, hf, :]
            S.op("dve", lambda e, tt=tt: e.tensor_tensor(tt, tt, t2, ALU.add), reads=[tmp2, tmp1], writes=[tmp1])
        S.op("dve", lambda e: e.tensor_copy(x3, t1), reads=[tmp1], writes=[xb])

    x3_buf = [None]

    Xcur, bXcur = xin, bX[0]
    for l in range(n_layers):
        last = (l == DEPTH - 1)
        Xnext, bXnext = (X1, bX[1])
        st = ExitStack()
        adab = sbt(st, [128, 24], "adab")
        ngt = sbt(st, [128, 8], "ngt")
        S.dma("sp", adab[:], W["ada_bT"][l, :, :], reads=[bW], writes=[adab])
        S.dma("sp", ngt[:], W["norm_gT"][l, :, :], reads=[bW], writes=[ngt])
        wa = [sbt(st, [128, 8, 128], "wa") for _ in range(3)]
        psM = nps()
        for ch in range(24):
            wt = wa[ch % 3]
            S.dma("sp", wt[:], W["ada_w"][l, :, ch * 128:(ch + 1) * 128].rearrange("(k p) n -> p k n", p=128), reads=[bW], writes=[wt])
            for k in range(8):
                mmp(psM, psM.t[:, ch * 3:ch * 3 + 3], wt.t[:, k, :], csil.t[:, k, :], k == 0, k == 7, [wt, csil]) if ch > 0 or k > 0 else \
                    mm(psM, psM.t[:, 0:3], wt.t[:, k, :], csil.t[:, k, :], True, False, [wt, csil])
        S.op("dve", lambda e: e.tensor_tensor(modT[:], psM.t[:, 0:72].rearrange("p (c r) -> p c r", r=3),
                                              adab.t[:, :].unsqueeze(2).to_broadcast([128, 24, 3]), ALU.add),
             reads=[psM, adab], writes=[modT])
        S.op("dve", lambda e: e.tensor_scalar(Gt[:], modT.t[:, 8:16, :], 1.0, None, ALU.add), reads=[modT], writes=[Gt])
        S.op("dve", lambda e: e.tensor_tensor(Gt[:], Gt[:], ngt.t[:, :].unsqueeze(2).to_broadcast([128, 8, 3]), ALU.mult),
             reads=[Gt, ngt], writes=[Gt])
        psG = [nps(), nps()]
        for ch in range(8):
            pg = psG[ch // 4]
            (mm if ch % 4 == 0 else mmp)(pg, pg.t[0:3, (ch % 4) * 128:(ch % 4 + 1) * 128], modT.t[:, 16 + ch, :], ident, True, True, [modT, cst])
        gsb = sbt(st, [3, 1024], "gsb")
        for hlf in range(2):
            S.op("dve", lambda e, hlf=hlf: e.tensor_copy(gsb.t[0:3, hlf * 512:(hlf + 1) * 512], psG[hlf].t[0:3, :]), reads=[psG[hlf]], pwrites=[gsb])
        S.dma("pool", MODG[:, :], gsb[:], reads=[gsb], writes=[bMODG])
        phase_end(st)
        if stop_after == "P0":
            break

        for b in range(NB):
            st = ExitStack()
            hT = sbt(st, [128, 8, T], "hT")
            xts = [sbt(st, [128, D], "xt") for _ in range(2)]
            sqs = [sbt(st, [128, D], "sq") for _ in range(2)]
            sss = [sbt(st, [128, 1], "ss") for _ in range(2)]
            for i in range(NT):
                r = b if i >= 2 else 2
                xt, sq, ss = xts[i % 2], sqs[i % 2], sss[i % 2]
                S.dma("sp", xt[:], Xcur[b, i * 128:(i + 1) * 128, :], reads=[bXcur], writes=[xt])
                S.op("pool", lambda e, ss=ss: e.memset(ss[:], 0.0), writes=[ss])
                S.op("act", lambda e, xt=xt, sq=sq, ss=ss: e.activation(sq[:], xt[:], AF.Square, accum_out=ss[:]), reads=[xt, ss], writes=[sq, ss])
                NLV = int(os.environ.get("KDBG_NLV", 9))
                if NLV < 2:
                    continue
                rstd_from_ss(None, ss, 1, 1.0 / D, EPS)
                S.op("dve", lambda e, xt=xt, sq=sq, ss=ss: e.tensor_scalar(sq[:], xt[:], ss.t[:, 0:1], None, ALU.mult), reads=[xt, ss], writes=[sq])
                if NLV < 3:
                    continue
                for half in range(2):
                    pt = nps()
                    for c4 in range(4):
                        ch = half * 4 + c4
                        S.op("pe", lambda e, pt=pt, c4=c4, ch=ch, sq=sq: e.transpose(pt.t[:, c4 * 128:(c4 + 1) * 128], sq.t[:, ch * 128:(ch + 1) * 128], ident),
                             reads=[sq, cst], writes=[pt] if c4 == 0 else (), pwrites=[pt] if c4 > 0 else ())
                    if NLV < 4:
                        continue
                    for c4 in range(4):
                        ch = half * 4 + c4
                        o_ap = hT.t[:, ch, i * 128:(i + 1) * 128]
                        i_ap = pt.t[:, c4 * 128:(c4 + 1) * 128]
                        g_ap = Gt.t[:, ch, r:r + 1]
                        s_ap = modT.t[:, ch, r:r + 1]
                        EV = os.environ.get("KDBG_EV", "")
                        if (c4 % 2 == 0 and EV != "act") or EV == "dve":
                            S.op("dve", lambda e, o_ap=o_ap, i_ap=i_ap, g_ap=g_ap, s_ap=s_ap: e.tensor_scalar(o_ap, i_ap, g_ap, s_ap, ALU.mult, ALU.add),
                                 reads=[pt, Gt, modT], pwrites=[hT])
                        else:
                            S.op("act", lambda e, o_ap=o_ap, i_ap=i_ap, g_ap=g_ap, s_ap=s_ap: e.activation(o_ap, i_ap, AF.Identity, bias=s_ap, scale=g_ap),
                                 reads=[pt, Gt, modT], pwrites=[hT])
            if "hT" in dbg and b == 0 and l == 0:
                hTd = nc.dram_tensor("hTd", [128, 8, T], F32, kind="ExternalOutput").ap()
                for k in range(8):
                    S.dma("pool", hTd[:, k, :], hT.t[:, k, :], reads=[hT])
            if stop_after == "P1a":
                phase_end(st)
                break
            wbufs = [sbt(st, [128, 8, 512], "wb") for _ in range(2)]
            zos = [sbt(st, [128, 512], "zo") for _ in range(3)]
            groups = [(1792, 512, ZO_GA, AF.Silu), (2304, 416, ZO_ZB, None), (3232, 512, ZO_ZC, None), (3744, 256, ZO_ZC + 512, None)]
            for g in range(6):
                groups.append((4512 + g * 512, 512, ZO_ZG + g * 512, AF.Sigmoid))
            zi = 0
            for gi, (c0, n, zoff, fn) in enumerate(groups):
                wb = wbufs[gi % 2]
                S.dma("sp", wb.t[:, :, 0:n], W["w_in"][l, :, c0:c0 + n].rearrange("(k p) n -> p k n", p=128), reads=[bW], writes=[wb])
                for i in range(NT):
                    ps = nps()
                    for k in range(8):
                        mm(ps, ps.t[:, 0:n], hT.t[:, k, i * 128:(i + 1) * 128], wb.t[:, k, 0:n], k == 0, k == 7, [hT, wb])
                    zo = zos[zi % 3]
                    zi += 1
                    evac(zo.t[:, 0:n], ps.t[:, 0:n], reads=[ps], writes=[zo], func=fn)
                    S.dma("pool", ZTM[b, i * 128:(i + 1) * 128, zoff:zoff + n], zo.t[:, 0:n], reads=[zo], pwrites=[bZTM[b]])
            zfs = [sbt(st, [128, T], "zf") for _ in range(2)]
            fm = [(c * 128, 128, ("A", c), None) for c in range(14)]
            fm += [(2720 + h * 64, 64, ("G", 0, h), AF.Silu) for h in range(8)]
            fm += [(4000 + h * 64, 64, ("G", 1, h), AF.Silu) for h in range(8)]
            for fi, (c0, m, dst, fn) in enumerate(fm):
                wb = wbufs[fi % 2]
                zf = zfs[fi % 2]
                S.dma("sp", wb.t[:, :, 0:m], W["w_in"][l, :, c0:c0 + m].rearrange("(k p) n -> p k n", p=128), reads=[bW], writes=[wb])
                for ci, (t0, tn) in enumerate(TCH):
                    ps = nps()
                    for k in range(8):
                        mm(ps, ps.t[0:m, 0:tn], wb.t[:, k, 0:m], hT.t[:, k, t0:t0 + tn], k == 0, k == 7, [hT, wb])
                    evac(zf.t[0:m, t0:t0 + tn], ps.t[0:m, 0:tn], reads=[ps], writes=[zf] if ci == 0 else (), pwrites=[zf] if ci > 0 else (), func=fn)
                if dst[0] == "A":
                    S.dma("pool", ZTA[b, dst[1], :, :], zf.t[:, :], reads=[zf], pwrites=[bZTA[b]])
                else:
                    S.dma("pool", GT[b, dst[1], dst[2], :, :], zf.t[0:64, :], reads=[zf], pwrites=[bGT[b]])
            phase_end(st)
            if stop_after == "P1":
                break

            st = ExitStack()
            mup = sbt(st, [128, 14], "mup")
            mun = sbt(st, [128, 14], "mun")
            c0t = sbt(st, [128, 14], "c0t")
            w0t = sbt(st, [128, 8], "w0t")
            a0t = sbt(st, [128, 8], "a0t")
            kkp = sbt(st, [128, 4], "kkp")
            kap = sbt(st, [128, 4], "kap")
            rkp = sbt(st, [128, 4], "rkp")
            wup = sbt(st, [128, 512], "wup")
            aup = sbt(st, [128, 512], "aup")
            for tl_, nm in [(mup, "mupT"), (mun, "munT"), (w0t, "w0T"), (a0t, "a0T"), (kkp, "kkT"), (kap, "kaT"), (rkp, "rkT"), (wup, "w_up"), (aup, "a_up")]:
                S.dma("sp", tl_[:], W[nm][l, :, :], reads=[bW], writes=[tl_])
            S.op("dve", lambda e: e.tensor_tensor(c0t[:], mup[:], mun[:], ALU.add), reads=[mup, mun], writes=[c0t])
            S.op("dve", lambda e: e.tensor_scalar(c0t[:], c0t[:], -1.0, 1.0, ALU.mult, ALU.add), reads=[c0t], writes=[c0t])
            NBT = 15
            bts = [sbt(st, [128, T], "bt") for _ in range(NBT)]
            zraw = [bts[0], bts[1]]
            zri = [0]

            def shift_load(c, dst):
                zr = zraw[zri[0] % 2]
                zri[0] += 1
                S.dma("sp", zr[:], ZTA[b, c, :, :], reads=[bZTA[b]], writes=[zr])
                S.op("act", lambda e: e.activation(dst[:], zr[:], AF.Identity, scale=c0t.t[:, c:c + 1]), reads=[zr, c0t], writes=[dst])
                for (o0, o1, i0, i1, mt) in [(1, 256, 0, 255, mup), (257, T, 256, T - 1, mup), (0, 255, 1, 256, mun), (256, T - 1, 257, T, mun)]:
                    S.op("dve", lambda e, o0=o0, o1=o1, i0=i0, i1=i1, mt=mt: e.scalar_tensor_tensor(dst.t[:, o0:o1], zr.t[:, i0:i1], mt.t[:, c:c + 1], dst.t[:, o0:o1], ALU.mult, ALU.add),
                         reads=[zr, mt, dst], writes=[dst])

            twd, ads = bts[2], bts[3]
            shift_load(12, twd)
            S.op("act", lambda e: e.activation(twd[:], twd[:], AF.Tanh), reads=[twd], writes=[twd])
            shift_load(13, ads)
            rs_, ks_, vs_, kk, tq, a_d, dec, ka_d, kd0, kd1, uu = bts[4:15]
            bsS = sbt(st, [128, NT, 8], "bsS")
            vtm = [sbt(st, [128, 4, 128], "vtm") for _ in range(2)]
            for q in range(4):
                shift_load(q, rs_)
                shift_load(4 + q, ks_)
                shift_load(8 + q, vs_)
                S.op("dve", lambda e, q=q: e.tensor_scalar(kk[:], ks_[:], kkp.t[:, q:q + 1], None, ALU.mult), reads=[ks_, kkp], writes=[kk])
                S.op("pool", lambda e: e.tensor_tensor(tq[:], kk[:], kk[:], ALU.mult), reads=[kk], writes=[tq])
                for ci, (t0, tn) in enumerate(TCH):
                    ps = nps()
                    mm(ps, ps.t[:, 0:tn], bones, tq.t[:, t0:t0 + tn], True, True, [tq, cst])
                    S.op("dve", lambda e, ps=ps, t0=t0, tn=tn: e.tensor_scalar_max(a_d.t[:, t0:t0 + tn], ps.t[:, 0:tn], 1e-24), reads=[ps],
                         writes=[a_d] if ci == 0 else (), pwrites=[a_d] if ci > 0 else ())
                S.op("act", lambda e: e.activation(a_d[:], a_d[:], AF.Sqrt), reads=[a_d], writes=[a_d])
                S.op("dve", lambda e: e.reciprocal(a_d[:], a_d[:]), reads=[a_d], writes=[a_d])
                S.op("dve", lambda e: e.tensor_tensor(kk[:], kk[:], a_d[:], ALU.mult), reads=[kk, a_d], writes=[kk])
                S.dma("pool", SC[0, :, 3, b * 4 + q, :], kk[:], reads=[kk], pwrites=[bSC])
                S.dma("pool", SC[1, :, 3, b * 4 + q, :], kk[:], reads=[kk], pwrites=[bSC])
                S.dma("pool", SC[0, :, 4, b * 4 + q, :], rs_[:], reads=[rs_], pwrites=[bSC])
                S.dma("pool", SC[1, :, 4, b * 4 + q, :], rs_[:], reads=[rs_], pwrites=[bSC])
                for d in range(2):
                    kd = kd0 if d == 0 else kd1
                    for ci, (t0, tn) in enumerate(TCH):
                        ps = nps()
                        mm(ps, ps.t[:, 0:tn], wup.t[d * 64:(d + 1) * 64, q * 128:(q + 1) * 128], twd.t[d * 64:(d + 1) * 64, t0:t0 + tn], True, True, [wup, twd])
                        evac(dec.t[:, t0:t0 + tn], ps.t[:, 0:tn], reads=[ps, w0t], writes=[dec] if ci == 0 else (), pwrites=[dec] if ci > 0 else (),
                             func=AF.Sigmoid, bias=w0t.t[:, d * 4 + q:d * 4 + q + 1])
                    S.op("act", lambda e: e.activation(dec[:], dec[:], AF.Exp, scale=-math.exp(-0.5)), reads=[dec], writes=[dec])
                    S.dma("pool", SC[d, :, 0, b * 4 + q, :], dec[:], reads=[dec], pwrites=[bSC])
                    for ci, (t0, tn) in enumerate(TCH):
                        ps = nps()
                        mm(ps, ps.t[:, 0:tn], aup.t[d * 64:(d + 1) * 64, q * 128:(q + 1) * 128], ads.t[d * 64:(d + 1) * 64, t0:t0 + tn], True, True, [aup, ads])
                        evac(a_d.t[:, t0:t0 + tn], ps.t[:, 0:tn], reads=[ps, a0t], writes=[a_d] if ci == 0 else (), pwrites=[a_d] if ci > 0 else (),
                             func=AF.Sigmoid, bias=a0t.t[:, d * 4 + q:d * 4 + q + 1])
                    S.op("pool", lambda e: e.tensor_tensor(ka_d[:], kk[:], a_d[:], ALU.mult), reads=[kk, a_d], writes=[ka_d])
                    S.dma("pool", SC[d, :, 1, b * 4 + q, :], ka_d[:], reads=[ka_d], pwrites=[bSC])
                    S.op("dve", lambda e, kd=kd, q=q: e.tensor_scalar(kd[:], a_d[:], kap.t[:, q:q + 1], kap.t[:, q:q + 1], ALU.mult, ALU.subtract), reads=[a_d, kap], writes=[kd])
                    S.op("dve", lambda e, kd=kd: e.scalar_tensor_tensor(kd[:], kd[:], 1.0, ks_[:], ALU.add, ALU.mult), reads=[kd, ks_], writes=[kd])
                    S.dma("pool", SC[d, :, 2, b * 4 + q, :], kd[:], reads=[kd], pwrites=[bSC])
                S.op("pool", lambda e: e.tensor_tensor(uu[:], kd0[:], kd1[:], ALU.add), reads=[kd0, kd1], writes=[uu])
                S.op("dve", lambda e, q=q: e.scalar_tensor_tensor(uu[:], uu[:], rkp.t[:, q:q + 1], rs_[:], ALU.mult, ALU.mult), reads=[uu, rkp, rs_], writes=[uu])
                psb = nps()
                for i in range(NT):
                    mm(psb, psb.t[:, i * 2:i * 2 + 2], uu.t[:, i * 128:(i + 1) * 128], ind2, True, True, [uu, cst]) if i == 0 else \
                        mmp(psb, psb.t[:, i * 2:i * 2 + 2], uu.t[:, i * 128:(i + 1) * 128], ind2, True, True, [uu, cst])
                S.op("dve", lambda e, q=q, psb=psb: e.tensor_copy(bsS.t[:, :, 2 * q:2 * q + 2], psb.t[:, 0:2 * NT].rearrange("p (i c) -> p i c", c=2)),
                     reads=[psb], writes=[bsS] if q == 0 else (), pwrites=[bsS] if q > 0 else ())
                for g0 in range(0, NT, 4):
                    ng = min(4, NT - g0)
                    pt = nps()
                    for j in range(ng):
                        i = g0 + j
                        S.op("pe", lambda e, pt=pt, j=j, i=i: e.transpose(pt.t[:, j * 128:(j + 1) * 128], vs_.t[:, i * 128:(i + 1) * 128], ident),
                             reads=[vs_, cst], writes=[pt] if j == 0 else (), pwrites=[pt] if j > 0 else ())
                    vt_ = vtm[(g0 // 4) % 2]
                    evac(vt_.t[:, 0:ng, :], pt.t[:, 0:ng * 128].rearrange("p (j f) -> p j f", f=128), reads=[pt], writes=[vt_])
                    for j in range(ng):
                        i = g0 + j
                        dst = VT[i * 128:(i + 1) * 128, :].rearrange("p (c bb qq v) -> p c bb qq v", c=2, bb=2, qq=4)[:, :, b, q, :]
                        S.dma("pool", dst, vt_.t[:, j, :].rearrange("p (c v) -> p c v", c=2), reads=[vt_], pwrites=[bVT])
            S.dma("pool", BS[b, :, :].rearrange("(i p) h -> p i h", p=128), bsS[:], reads=[bsS], writes=[bBS[b]])
            phase_end(st)
            if stop_after == "P2":
                break

            st = ExitStack()
            qln = bc_load(st, W["q_ln"][l, :, :], 256, name="qln")
            kvln = bc_load(st, W["kv_ln"][l, :, :], 128, name="kvln")
            bqkg = bc_load(st, W["bqk_g"][l, :, :], 192, name="bqkg")
            cqn = bc_load(st, W["c_qn"][l, :, :], 64, name="cqn")
            ckn = bc_load(st, W["c_kn"][l, :, :], 64, name="ckn")
            wuq = sbt(st, [128, 2, 768], "wuq")
            wukv = sbt(st, [128, 1024], "wukv")
            S.dma("sp", wuq[:], W["w_uq"][l, :, :].rearrange("(k p) n -> p k n", p=128), reads=[bW], writes=[wuq])
            S.dma("sp", wukv[:], W["w_ukv"][l, :, :], reads=[bW], writes=[wukv])
            NBUF = 2
            zbs = [sbt(st, [128, 416], "zb") for _ in range(NBUF)]
            zcs = [sbt(st, [128, 768], "zc") for _ in range(NBUF)]
            rbs = [sbt(st, [128, 64], "rb") for _ in range(NBUF)]
            rcs = [sbt(st, [128, 128], "rc") for _ in range(NBUF)]
            ss2 = [sbt(st, [128, 2], "ss2") for _ in range(NBUF)]
            junk = sbt(st, [128, 1536], "junk")
            cn = [sbt(st, [128, 384], "cn") for _ in range(NBUF)]
            cT3 = [sbt(st, [128, 3, 128], "cT3") for _ in range(NBUF)]
            qk = [sbt(st, [128, 16, 96], "qk") for _ in range(NBUF)]
            kv = [sbt(st, [128, 8, 128], "kv") for _ in range(NBUF)]
            ssq = [sbt(st, [128, 16], "ssq") for _ in range(NBUF)]
            rt1 = sbt(st, [128, 640], "rt1")
            rt2 = sbt(st, [128, 320], "rt2")
            qkT = [sbt(st, [96, 16, 128], "qkT") for _ in range(NBUF)]
            qkTc = [sbt(st, [64, 10, 128], "qkTc") for _ in range(NBUF)]
            for i in range(NT):
                u = i % NBUF
                zb, zc, rb, rc = zbs[u], zcs[u], rbs[u], rcs[u]
                S.dma("sp", zb[:], ZTM[b, i * 128:(i + 1) * 128, ZO_ZB:ZO_ZB + 416], reads=[bZTM[b]], writes=[zb])
                S.dma("sp", zc[:], ZTM[b, i * 128:(i + 1) * 128, ZO_ZC:ZO_ZC + 768], reads=[bZTM[b]], writes=[zc])
                S.dma("sp", rb[:], ropeB_d[i * 128:(i + 1) * 128, :], reads=[bW], writes=[rb])
                S.dma("sp", rc[:], ropeC_d[i * 128:(i + 1) * 128, :], reads=[bW], writes=[rc])
                s2 = ss2[u]
                S.op("pool", lambda e, s2=s2: e.memset(s2[:], 0.0), writes=[s2])
                S.op("act", lambda e, zb=zb, s2=s2: e.activation(junk.t[:, 0:256], zb.t[:, 0:256], AF.Square, accum_out=s2.t[:, 0:1]), reads=[zb, s2], writes=[junk, s2])
                S.op("act", lambda e, zb=zb, s2=s2: e.activation(junk.t[:, 256:384], zb.t[:, 256:384], AF.Square, accum_out=s2.t[:, 1:2]), reads=[zb, s2], writes=[junk, s2])
                S.op("dve", lambda e, s2=s2: e.tensor_scalar(s2.t[:, 0:1], s2.t[:, 0:1], 1.0 / 256, EPS, ALU.mult, ALU.add), reads=[s2], writes=[s2])
                S.op("dve", lambda e, s2=s2: e.tensor_scalar(s2.t[:, 1:2], s2.t[:, 1:2], 1.0 / 128, EPS, ALU.mult, ALU.add), reads=[s2], writes=[s2])
                S.op("act", lambda e, s2=s2: e.activation(s2[:], s2[:], AF.Sqrt), reads=[s2], writes=[s2])
                S.op("dve", lambda e, s2=s2: e.reciprocal(s2[:], s2[:]), reads=[s2], writes=[s2])
                c_ = cn[u]
                S.op("dve", lambda e, c_=c_, zb=zb, s2=s2: e.scalar_tensor_tensor(c_.t[:, 0:256], zb.t[:, 0:256], s2.t[:, 0:1], qln[:], ALU.mult, ALU.mult), reads=[zb, s2, qln], writes=[c_])
                S.op("dve", lambda e, c_=c_, zb=zb, s2=s2: e.scalar_tensor_tensor(c_.t[:, 256:384], zb.t[:, 256:384], s2.t[:, 1:2], kvln[:], ALU.mult, ALU.mult), reads=[zb, s2, kvln], pwrites=[c_])
                pt = nps()
                for j in range(3):
                    S.op("pe", lambda e, pt=pt, j=j, c_=c_: e.transpose(pt.t[:, j * 128:(j + 1) * 128], c_.t[:, j * 128:(j + 1) * 128], ident),
                         reads=[c_, cst], writes=[pt] if j == 0 else (), pwrites=[pt] if j > 0 else ())
                c3 = cT3[u]
                evac(c3[:], pt.t[:, 0:384].rearrange("p (j f) -> p j f", f=128), reads=[pt], writes=[c3])
                qk_ = qk[u]
                kv_ = kv[u]
                for nh in range(2):
                    ps = nps()
                    for k in range(2):
                        mm(ps, ps.t[:, 0:384], c3.t[:, k, :], wuq.t[:, k, nh * 384:(nh + 1) * 384], k == 0, k == 1, [c3, wuq])
                    evac(qk_.t[:, nh * 4:(nh + 1) * 4, :], ps.t[:, 0:384].rearrange("p (h r) -> p h r", r=96), reads=[ps], writes=[qk_] if nh == 0 else (), pwrites=[qk_] if nh > 0 else ())
                for nh in range(2):
                    ps = nps()
                    mm(ps, ps.t[:, :], c3.t[:, 2, :], wukv.t[:, nh * 512:(nh + 1) * 512], True, True, [c3, wukv])
                    evac(kv_.t[:, nh * 4:(nh + 1) * 4, :], ps.t[:, :].rearrange("p (h r) -> p h r", r=128), reads=[ps], writes=[kv_] if nh == 0 else (), pwrites=[kv_] if nh > 0 else ())
                S.op("dve", lambda e, qk_=qk_, kv_=kv_: e.tensor_copy(qk_.t[:, 8:16, 0:64], kv_.t[:, :, 0:64]), reads=[kv_], pwrites=[qk_])
                S.op("dve", lambda e, qk_=qk_, zb=zb: e.tensor_copy(qk_.t[:, 8:16, 64:96], zb.t[:, 384:416].unsqueeze(1).to_broadcast([128, 8, 32])), reads=[zb], pwrites=[qk_])
                S.dma("pool", VB[b, i * 128:(i + 1) * 128, :].rearrange("p (h v) -> p h v", v=64), kv_.t[:, :, 64:128], reads=[kv_], pwrites=[bVB[b]])
                sq_ = ssq[u]
                S.op("pool", lambda e, qk_=qk_: e.tensor_tensor(junk.t[:, 0:1536], qk_.t[:, :, :].rearrange("p h r -> p (h r)"), qk_.t[:, :, :].rearrange("p h r -> p (h r)"), ALU.mult), reads=[qk_], writes=[junk])
                S.op("dve", lambda e, sq_=sq_: e.tensor_reduce(sq_[:], junk.t[:, 0:1536].rearrange("p (h r) -> p h r", r=96), AX.X, ALU.add), reads=[junk], writes=[sq_])
                rstd_from_ss(None, sq_, 16, 1.0 / 96, EPS)
                S.op("dve", lambda e, qk_=qk_, sq_=sq_: e.tensor_tensor(qk_[:], qk_[:], sq_.t[:, 0:16].unsqueeze(2).to_broadcast([128, 16, 96]), ALU.mult), reads=[qk_, sq_], writes=[qk_])
                S.op("dve", lambda e, qk_=qk_: e.tensor_tensor(qk_.t[:, :, :].rearrange("p (a h) r -> p a h r", a=2), qk_.t[:, :, :].rearrange("p (a h) r -> p a h r", a=2),
                                                              bqkg.t[:, :].rearrange("p (a r) -> p a r", a=2).unsqueeze(2).to_broadcast([128, 2, 8, 96]), ALU.mult), reads=[qk_, bqkg], writes=[qk_])
                x3_buf[0] = qk_
                rope(st, qk_.t[:, :, 64:96], 16, 32, rb, rt1, rt2)
                qT_ = qkT[u]
                for g0 in range(0, 16, 4):
                    pt = nps()
                    for j in range(4):
                        S.op("pe", lambda e, pt=pt, j=j, g0=g0, qk_=qk_: e.transpose(pt.t[0:96, j * 128:(j + 1) * 128], qk_.t[:, g0 + j, :], ident),
                             reads=[qk_, cst], writes=[pt] if j == 0 else (), pwrites=[pt] if j > 0 else ())
                    evac(qT_.t[0:96, g0:g0 + 4, :], pt.t[0:96, :].rearrange("p (j f) -> p j f", f=128), reads=[pt], writes=[qT_] if g0 == 0 else (), pwrites=[qT_] if g0 > 0 else ())
                S.dma("pool", QKT[b, :, :, i * 128:(i + 1) * 128].rearrange("h p t -> p h t"), qT_[:], reads=[qT_], pwrites=[bQKT[b]])
                S.dma("pool", VC[b, i * 128:(i + 1) * 128, :], zc.t[:, 640:768], reads=[zc], pwrites=[bVC[b]])
                S.op("pool", lambda e, zc=zc: e.tensor_tensor(junk.t[:, 0:640], zc.t[:, 0:640], zc.t[:, 0:640], ALU.mult), reads=[zc], writes=[junk])
                S.op("dve", lambda e, sq_=sq_: e.tensor_reduce(sq_.t[:, 0:10], junk.t[:, 0:640].rearrange("p (h r) -> p h r", r=64), AX.X, ALU.add), reads=[junk], writes=[sq_])
                rstd_from_ss(None, sq_, 10, 1.0 / 64, EPS)
                z3 = zc.t[:, 0:640].rearrange("p (h r) -> p h r", r=64)
                S.op("dve", lambda e, z3=z3, sq_=sq_, zc=zc: e.tensor_tensor(z3, z3, sq_.t[:, 0:10].unsqueeze(2).to_broadcast([128, 10, 64]), ALU.mult), reads=[zc, sq_], writes=[zc])
                S.op("dve", lambda e, z3=z3, zc=zc: e.tensor_tensor(z3[:, 0:8, :], z3[:, 0:8, :], cqn.t[:, :].unsqueeze(1).to_broadcast([128, 8, 64]), ALU.mult), reads=[zc, cqn], writes=[zc])
                S.op("dve", lambda e, z3=z3, zc=zc: e.tensor_tensor(z3[:, 8:10, :], z3[:, 8:10, :], ckn.t[:, :].unsqueeze(1).to_broadcast([128, 2, 64]), ALU.mult), reads=[zc, ckn], writes=[zc])
                x3_buf[0] = zc
                rope(st, z3, 10, 64, rc, rt1, rt2)
                qTc_ = qkTc[u]
                for g0 in range(0, 10, 4):
                    ng = min(4, 10 - g0)
                    pt = nps()
                    for j in range(ng):
                        S.op("pe", lambda e, pt=pt, j=j, g0=g0, z3=z3: e.transpose(pt.t[0:64, j * 128:(j + 1) * 128], z3[:, g0 + j, :], ident),
                             reads=[zc, cst], writes=[pt] if j == 0 else (), pwrites=[pt] if j > 0 else ())
                    evac(qTc_.t[0:64, g0:g0 + ng, :], pt.t[0:64, 0:ng * 128].rearrange("p (j f) -> p j f", f=128), reads=[pt], writes=[qTc_] if g0 == 0 else (), pwrites=[qTc_] if g0 > 0 else ())
                S.dma("pool", QKTC[b, :, :, i * 128:(i + 1) * 128].rearrange("h p t -> p h t"), qTc_[:], reads=[qTc_], pwrites=[bQKTC[b]])
            phase_end(st)
        if stop_after in ("P1a", "P1", "P2", "P3"):
            break

        st = ExitStack()
        Sd = [sbt(st, [128, 8, 64], "S%d" % d) for d in range(2)]
        for d in range(2):
            S.op("dve", lambda e, d=d: e.memset(Sd[d][:], 0.0), writes=[Sd[d]])
        tA = [[sbt(st, [128, 8, 64], "tA") for _ in range(2)] for d in range(2)]
        tB = [sbt(st, [128, 8, 64], "tB") for d in range(2)]
        tC = [sbt(st, [128, 8, 64], "tC") for d in range(2)]
        t4 = [[sbt(st, [128, 8, 64], "t4") for _ in range(2)] for d in range(2)]
        SCb = [[sbt(st, [128, 5, 8, 64], "SCb") for _ in range(2)] for d in range(2)]
        Vb = [[sbt(st, [64, 1024], "Vb") for _ in range(2)] for d in range(2)]
        ysb = [sbt(st, [128, 512], "ysb") for d in range(2)]
        psV = [[PS[0], PS[1]], [PS[2], PS[3]]]
        psSA = [PS[4], PS[5]]
        psY = [PS[6], PS[7]]
        NBLK = T // 64
        n_scan_blocks = int(os.environ.get("KDBG_SCAN_BLOCKS", NBLK))

        def tok0(d, B):
            if d == 0:
                return 64 * B
            if B < 4:
                return LC - 64 * (B + 1)
            return T - 64 * (B - 3)

        def v3(buf):
            return buf.t[:, :, :]

        for B in range(n_scan_blocks):
            u = B % 2
            for d in range(2):
                t0 = tok0(d, B)
                for a in range(5):
                    S.dma("sp", SCb[d][u].t[:, a, :, :], SC[d, :, a, :, t0:t0 + 64], reads=[bSC], writes=[SCb[d][u]] if a == 0 else (), pwrites=[SCb[d][u]] if a else ())
                S.dma("sp", Vb[d][u][:], VT[t0:t0 + 64, :], reads=[bVT], writes=[Vb[d][u]])
            for s in range(64):
                tls = [s, 63 - s]
                sc = [SCb[d][u] for d in range(2)]

                def bcs(d, a):
                    return sc[d].t[:, a, :, tls[d]].unsqueeze(2).to_broadcast([128, 8, 64])
                pv = [psV[d][s % 2] for d in range(2)]
                ta = [tA[d][s % 2] for d in range(2)]
                tt4 = [t4[d][s % 2] for d in range(2)]
                for d in range(2):
                    for c2 in range(2):
                        S.op("pe", lambda e, d=d, c2=c2, p=pv[d], tl=tls[d], vb_=Vb[d][u]: e.matmul(p.t[c2 * 64:(c2 + 1) * 64, :], ident[0:64, tl:tl + 1].to_broadcast([64, 64]),
                                                                                   vb_.t[0:64, c2 * 512:(c2 + 1) * 512], start=True, stop=True),
                             reads=[Vb[d][u], cst], writes=[pv[d]] if c2 == 0 else (), pwrites=[pv[d]] if c2 == 1 else ())
                for d in range(2):
                    S.op("dve", lambda e, d=d, o=ta[d], kb=bcs(d, 3): e.tensor_tensor(v3(o), v3(Sd[d]), kb, ALU.mult), reads=[Sd[d], sc[d]], writes=[ta[d]])
                for d in range(2):
                    S.op("pe", lambda e, d=d, i_=ta[d]: e.matmul(psSA[d].t[:, :], bones, i_.t[:, :, :].rearrange("p a v -> p (a v)"), start=True, stop=True),
                         reads=[ta[d], cst], writes=[psSA[d]])
                for d in range(2):
                    S.op("dve", lambda e, d=d, wb_=bcs(d, 0): e.tensor_tensor(v3(Sd[d]), v3(Sd[d]), wb_, ALU.mult), reads=[Sd[d], sc[d]], writes=[Sd[d]])
                for d in range(2):
                    S.op("dve", lambda e, d=d, kb=bcs(d, 1): e.tensor_tensor(v3(tB[d]), psSA[d].t[:, :].rearrange("p (a v) -> p a v", v=64), kb, ALU.mult), reads=[psSA[d], sc[d]], writes=[tB[d]])
                    S.op("dve", lambda e, d=d: e.tensor_tensor(v3(Sd[d]), v3(Sd[d]), v3(tB[d]), ALU.subtract), reads=[Sd[d], tB[d]], writes=[Sd[d]])
                for d in range(2):
                    S.op("dve", lambda e, d=d, kb=bcs(d, 2), p=pv[d]: e.tensor_tensor(v3(tC[d]), p.t[:, :].rearrange("p (a v) -> p a v", v=64), kb, ALU.mult), reads=[pv[d], sc[d]], writes=[tC[d]])
                    S.op("dve", lambda e, d=d: e.tensor_tensor(v3(Sd[d]), v3(Sd[d]), v3(tC[d]), ALU.add), reads=[Sd[d], tC[d]], writes=[Sd[d]])
                for d in range(2):
                    S.op("dve", lambda e, d=d, o=tt4[d], rb_=bcs(d, 4): e.tensor_tensor(v3(o), v3(Sd[d]), rb_, ALU.mult), reads=[Sd[d], sc[d]], writes=[tt4[d]])
                for d in range(2):
                    S.op("pe", lambda e, d=d, i_=tt4[d], tl=tls[d], s=s: e.matmul(psY[d].t[:, :], Z2[:, 127 - tl:255 - tl], i_.t[:, :, :].rearrange("p a v -> p (a v)"), start=(s == 0), stop=(s == 63)),
                         reads=[tt4[d], cst], writes=[psY[d]] if s == 0 else (), pwrites=[psY[d]] if s > 0 else ())
            for d in range(2):
                t0 = tok0(d, B)
                evac(ysb[d][:], psY[d].t[:, :], reads=[psY[d]], writes=[ysb[d]])
                for c2 in range(2):
                    S.dma("pool", YD[d, t0:t0 + 64, c2 * 512:(c2 + 1) * 512], ysb[d].t[c2 * 64:(c2 + 1) * 64, :], reads=[ysb[d]], pwrites=[bYD])
        phase_end(st)
        if stop_after == "P4":
            break

        st = ExitStack()
        KTs = [sbt(st, [96, T], "KT") for _ in range(2)]
        QTs = [sbt(st, [96, T], "QT") for _ in range(2)]
        GTs = [sbt(st, [64, T], "GTh") for _ in range(2)]
        Vhs = [sbt(st, [128, NT, 64], "Vh") for _ in range(2)]
        Pbs = [sbt(st, [128, 512], "Pb") for _ in range(3)]
        rdn = [sbt(st, [64, 512], "rdn") for _ in range(2)]
        uob = [sbt(st, [64, T], "uob") for _ in range(2)]
        pi = 0
        scale_b = 96 ** -0.5
        for b in range(NB):
            for h in range(8):
                u = (b * 8 + h) % 2
                KT, QT, GTh, Vh, uo = KTs[u], QTs[u], GTs[u], Vhs[u], uob[u]
                S.dma("sp", KT[:], QKT[b, 8 + h, :, :], reads=[bQKT[b]], writes=[KT])
                S.dma("sp", QT[:], QKT[b, h, :, :], reads=[bQKT[b]], writes=[QT])
                S.dma("sp", GTh[:], GT[b, 0, h, :, :], reads=[bGT[b]], writes=[GTh])
                S.dma("sp", Vh[:], VB[b, :, h * 64:(h + 1) * 64].rearrange("(i p) v -> p i v", p=128), reads=[bVB[b]], writes=[Vh])
                qchunks = [(LC + j * 512, 512, list(range(NT))) for j in range(4)]
                if not last:
                    qchunks.append((0, 256, [0, 1]))
                for ci, (q0, qn, kts) in enumerate(qchunks):
                    psO, psD = PS[(ci % 2) * 2], PS[(ci % 2) * 2 + 1]
                    for ki, kt in enumerate(kts):
                        psS = PS[4 + pi % 4]
                        mm(psS, psS.t[:, 0:qn], KT.t[0:96, kt * 128:(kt + 1) * 128], QT.t[0:96, q0:q0 + qn], True, True, [KT, QT])
                        Pb = Pbs[pi % 3]
                        pi += 1
                        S.op("act", lambda e, Pb=Pb, psS=psS, qn=qn: e.activation(Pb.t[:, 0:qn], psS.t[:, 0:qn], AF.Exp, scale=scale_b), reads=[psS], writes=[Pb])
                        mm(psO, psO.t[0:64, 0:qn], Vh.t[:, kt, :], Pb.t[:, 0:qn], ki == 0, ki == len(kts) - 1, [Vh, Pb])
                        mm(psD, psD.t[0:64, 0:qn], ones64, Pb.t[:, 0:qn], ki == 0, ki == len(kts) - 1, [cst, Pb])
                    rd = rdn[ci % 2]
                    S.op("dve", lambda e, rd=rd, psD=psD, qn=qn: e.reciprocal(rd.t[:, 0:qn], psD.t[0:64, 0:qn]), reads=[psD], writes=[rd])
                    S.op("dve", lambda e, rd=rd, psO=psO, qn=qn: e.tensor_tensor(rd.t[:, 0:qn], psO.t[0:64, 0:qn], rd.t[:, 0:qn], ALU.mult), reads=[psO, rd], writes=[rd])
                    S.op("pool", lambda e, rd=rd, uo=uo, GTh=GTh, q0=q0, qn=qn: e.tensor_tensor(uo.t[:, q0:q0 + qn], rd.t[:, 0:qn], GTh.t[:, q0:q0 + qn], ALU.mult), reads=[rd, GTh],
                         writes=[uo] if ci == 0 else (), pwrites=[uo] if ci > 0 else ())
                if last:
                    S.dma("pool", UT[b, 0, h, :, LC:T], uo.t[:, LC:T], reads=[uo], pwrites=[bUT[b]])
                else:
                    S.dma("pool", UT[b, 0, h, :, :], uo[:], reads=[uo], pwrites=[bUT[b]])
        phase_end(st)
        if stop_after == "P5":
            break

        st = ExitStack()
        esk = bc_load(st, W["c_sink"][l, :, :], 8, parts=64, name="esk")
        S.op("act", lambda e: e.activation(esk[:], esk[:], AF.Exp), reads=[esk], writes=[esk])
        KTg = [sbt(st, [64, T], "KTg") for _ in range(2)]
        Q4 = [[sbt(st, [64, T], "Q4") for _ in range(4)] for _ in range(2)]
        G4 = [[sbt(st, [64, T], "G4") for _ in range(4)] for _ in range(2)]
        Vg = [sbt(st, [128, NT, 64], "Vg") for _ in range(2)]
        Pcs = [sbt(st, [128, 512], "Pc") for _ in range(3)]
        rdc = [sbt(st, [64, 512], "rdc") for _ in range(2)]
        uoc = [sbt(st, [64, 4, 128], "uoc") for _ in range(2)]
        scale_c = 64 ** -0.5
        pi = 0
        bi = 0
        for b in range(NB):
            for g in range(2):
                u = (b * 2 + g) % 2
                S.dma("sp", KTg[u][:], QKTC[b, 8 + g, :, :], reads=[bQKTC[b]], writes=[KTg[u]])
                S.dma("sp", Vg[u][:], VC[b, :, g * 64:(g + 1) * 64].rearrange("(i p) v -> p i v", p=128), reads=[bVC[b]], writes=[Vg[u]])
                for hh in range(4):
                    S.dma("sp", Q4[u][hh][:], QKTC[b, 4 * g + hh, :, :], reads=[bQKTC[b]], writes=[Q4[u][hh]])
                    S.dma("sp", G4[u][hh][:], GT[b, 1, 4 * g + hh, :, :], reads=[bGT[b]], writes=[G4[u][hh]])
                blocks = list(range(2, NT))
                if not last:
                    blocks = [0, 1] + blocks
                for n in blocks:
                    if n < 2:
                        kts = [(0, None), (1, None)]
                    else:
                        kts = [(0, None), (1, None)]
                        if n - 1 >= 2:
                            kts.append((n - 1, mask_lo))
                        kts.append((n, None))
                        if n + 1 < NT:
                            kts.append((n + 1, mask_hi))
                    psO, psD = PS[(bi % 2) * 2], PS[(bi % 2) * 2 + 1]
                    for ki, (kt, msk) in enumerate(kts):
                        psS = PS[4 + pi % 4]
                        for hh in range(4):
                            S.op("pe", lambda e, psS=psS, hh=hh, kt=kt, n=n, u=u: e.matmul(psS.t[:, hh * 128:(hh + 1) * 128], KTg[u].t[0:64, kt * 128:(kt + 1) * 128],
                                                                                       Q4[u][hh].t[0:64, n * 128:(n + 1) * 128], start=True, stop=True),
                                 reads=[KTg[u], Q4[u][hh]], writes=[psS] if hh == 0 else (), pwrites=[psS] if hh > 0 else ())
                        Pc = Pcs[pi % 3]
                        pi += 1
                        S.op("act", lambda e, Pc=Pc, psS=psS: e.activation(Pc[:], psS.t[:, :], AF.Exp, scale=scale_c), reads=[psS], writes=[Pc])
                        if msk is not None:
                            S.op("dve", lambda e, Pc=Pc, msk=msk: e.tensor_tensor(Pc.t[:, :].rearrange("p (h q) -> p h q", h=4), Pc.t[:, :].rearrange("p (h q) -> p h q", h=4),
                                                                                   msk.unsqueeze(1).to_broadcast([128, 4, 128]), ALU.mult), reads=[Pc, cst], writes=[Pc])
                        mm(psO, psO.t[0:64, :], Vg[u].t[:, kt, :], Pc.t[:, :], ki == 0, ki == len(kts) - 1, [Vg[u], Pc])
                        mm(psD, psD.t[0:64, :], ones64, Pc.t[:, :], ki == 0, ki == len(kts) - 1, [cst, Pc])
                    rd = rdc[bi % 2]
                    uo = uoc[bi % 2]
                    bi += 1
                    S.op("dve", lambda e, rd=rd, psD=psD, g=g: e.tensor_tensor(rd.t[:, :].rearrange("p (h q) -> p h q", h=4), psD.t[0:64, :].rearrange("p (h q) -> p h q", h=4),
                                                                                esk.t[:, 4 * g:4 * g + 4].unsqueeze(2).to_broadcast([64, 4, 128]), ALU.add), reads=[psD, esk], writes=[rd])
                    S.op("dve", lambda e, rd=rd: e.reciprocal(rd[:], rd[:]), reads=[rd], writes=[rd])
                    S.op("dve", lambda e, rd=rd, psO=psO: e.tensor_tensor(rd[:], psO.t[0:64, :], rd[:], ALU.mult), reads=[psO, rd], writes=[rd])
                    for hh in range(4):
                        S.op("pool", lambda e, rd=rd, uo=uo, hh=hh, n=n, u=u: e.tensor_tensor(uo.t[:, hh, :], rd.t[:, hh * 128:(hh + 1) * 128], G4[u][hh].t[:, n * 128:(n + 1) * 128], ALU.mult),
                             reads=[rd, G4[u][hh]], writes=[uo] if hh == 0 else (), pwrites=[uo] if hh > 0 else ())
                    S.dma("pool", UT[b, 1, 4 * g:4 * g + 4, :, n * 128:(n + 1) * 128].rearrange("h p t -> p h t"), uo[:], reads=[uo], pwrites=[bUT[b]])
        phase_end(st)
        if stop_after == "P6":
            break

        st = ExitStack()
        gng = bc_load(st, W["gn_g"][l, :, :], 512, name="gng")
        gnb = bc_load(st, W["gn_b"][l, :, :], 512, name="gnb")
        wbo0 = sbt(st, [128, 4, D], "wbo0")
        wbo1 = sbt(st, [128, 4, D], "wbo1")
        wbo2 = sbt(st, [128, 4, D], "wbo2")
        wo = sbt(st, [128, 8, D], "wo")
        S.dma("sp", wbo0[:], W["wbo"][l, 0, :, :].rearrange("(k p) n -> p k n", p=128), reads=[bW], writes=[wbo0])
        S.dma("sp", wbo1[:], W["wbo"][l, 1, :, :].rearrange("(k p) n -> p k n", p=128), reads=[bW], writes=[wbo1])
        S.dma("sp", wbo2[:], W["wbo"][l, 2, :, :].rearrange("(k p) n -> p k n", p=128), reads=[bW], writes=[wbo2])
        S.dma("sp", wo[:], W["w_out"][l, :, :].rearrange("(k p) n -> p k n", p=128), reads=[bW], writes=[wo])
        gateb = [sbt(st, [128, D], "gateb") for _ in range(3)]
        for r in range(3):
            S.dma("sp", gateb[r][:], MODG[r:r + 1, :].partition_broadcast(128), reads=[bMODG], writes=[gateb[r]])
        y0s = [sbt(st, [128, 512], "y0") for _ in range(2)]
        y1s = [sbt(st, [128, 512], "y1") for _ in range(2)]
        vts = [sbt(st, [128, 512], "vt") for _ in range(2)]
        sgas = [sbt(st, [128, 512], "sga") for _ in range(2)]
        bss = [sbt(st, [128, 8], "bs") for _ in range(2)]
        sgs = [sbt(st, [128, 3 * D], "sg") for _ in range(2)]
        xts = [sbt(st, [128, D], "xt") for _ in range(2)]
        utb = [sbt(st, [128, 4, 128], "utb") for _ in range(2)]
        utc = [sbt(st, [128, 4, 128], "utc") for _ in range(2)]
        st8 = sbt(st, [128, 8], "st8")
        yc = sbt(st, [128, 512], "yc")
        ysq = sbt(st, [128, 512], "ysq")
        uAT = sbt(st, [128, 4, 128], "uAT")
        mt = sbt(st, [128, D], "mt")
        tmpm = sbt(st, [128, 512], "tmpm")
        mTt = sbt(st, [128, 8, 128], "mTt")
        xo = [sbt(st, [128, D], "xo") for _ in range(2)]
        it = 0
        for b in range(NB):
            for i in (range(2, NT) if last else range(NT)):
                u = it % 2
                it += 1
                r = b if i >= 2 else 2
                y0, y1, vt, sga, bs_, sg, xt = y0s[u], y1s[u], vts[u], sgas[u], bss[u], sgs[u], xts[u]
                rows = slice(i * 128, (i + 1) * 128)

                def perm_src(a, c):
                    return a.rearrange("p (c bb qq v) -> p c bb qq v", c=2, bb=2, qq=4)[:, c, b, :, :]

                def perm_dst(tile, c):
                    return tile.t[:, :].rearrange("p (qq c v) -> p c qq v", qq=4, c=2)[:, c, :, :]
                for c in range(2):
                    S.dma("sp", perm_dst(y0, c), perm_src(YD[0, rows, :], c), reads=[bYD], writes=[y0] if c == 0 else (), pwrites=[y0] if c else ())
                    S.dma("sp", perm_dst(y1, c), perm_src(YD[1, rows, :], c), reads=[bYD], writes=[y1] if c == 0 else (), pwrites=[y1] if c else ())
                    S.dma("sp", perm_dst(vt, c), perm_src(VT[rows, :], c), reads=[bVT], writes=[vt] if c == 0 else (), pwrites=[vt] if c else ())
                S.dma("sp", sga[:], ZTM[b, rows, ZO_GA:ZO_GA + 512], reads=[bZTM[b]], writes=[sga])
                S.dma("sp", bs_[:], BS[b, rows, :], reads=[bBS[b]], writes=[bs_])
                S.dma("sp", sg[:], ZTM[b, rows, ZO_ZG:ZO_ZG + 3 * D], reads=[bZTM[b]], writes=[sg])
                S.dma("sp", xt[:], Xcur[b, rows, :], reads=[bXcur], writes=[xt])
                S.dma("sp", utb[u][:], UT[b, 0, :, :, :].rearrange("h p t -> (h p) t").rearrange("(k q) t -> q k t", q=128)[:, :, rows], reads=[bUT[b]], writes=[utb[u]])
                S.dma("sp", utc[u][:], UT[b, 1, :, :, :].rearrange("h p t -> (h p) t").rearrange("(k q) t -> q k t", q=128)[:, :, rows], reads=[bUT[b]], writes=[utc[u]])
                S.op("pool", lambda e, y0=y0, y1=y1: e.tensor_tensor(y0[:], y0[:], y1[:], ALU.add), reads=[y0, y1], writes=[y0])
                y3 = y0.t[:, :].rearrange("p (h v) -> p h v", v=64)
                yc3 = yc.t[:, :].rearrange("p (h v) -> p h v", v=64)
                S.op("dve", lambda e, y3=y3: e.tensor_reduce(st8[:], y3, AX.X, ALU.add), reads=[y0], writes=[st8])
                S.op("dve", lambda e: e.tensor_scalar(st8[:], st8[:], 1.0 / 64, None, ALU.mult), reads=[st8], writes=[st8])
                S.op("dve", lambda e, y3=y3, yc3=yc3: e.tensor_tensor(yc3, y3, st8.t[:, :].unsqueeze(2).to_broadcast([128, 8, 64]), ALU.subtract), reads=[y0, st8], writes=[yc])
                S.op("pool", lambda e: e.tensor_tensor(ysq[:], yc[:], yc[:], ALU.mult), reads=[yc], writes=[ysq])
                S.op("dve", lambda e: e.tensor_reduce(st8[:], ysq.t[:, :].rearrange("p (h v) -> p h v", v=64), AX.X, ALU.add), reads=[ysq], writes=[st8])
                rstd_from_ss(None, st8, 8, 1.0 / 64, GN_EPS)
                S.op("dve", lambda e, yc3=yc3: e.tensor_tensor(yc3, yc3, st8.t[:, :].unsqueeze(2).to_broadcast([128, 8, 64]), ALU.mult), reads=[yc, st8], writes=[yc])
                S.op("dve", lambda e: e.tensor_tensor(yc[:], yc[:], gng[:], ALU.mult), reads=[yc, gng], writes=[yc])
                S.op("pool", lambda e: e.tensor_tensor(yc[:], yc[:], gnb[:], ALU.add), reads=[yc, gnb], writes=[yc])
                S.op("dve", lambda e, vt=vt, bs_=bs_: e.tensor_tensor(vt.t[:, :].rearrange("p (h v) -> p h v", v=64), vt.t[:, :].rearrange("p (h v) -> p h v", v=64),
                                                                      bs_.t[:, :].unsqueeze(2).to_broadcast([128, 8, 64]), ALU.mult), reads=[vt, bs_], writes=[vt])
                S.op("pool", lambda e, vt=vt: e.tensor_tensor(yc[:], yc[:], vt[:], ALU.add), reads=[yc, vt], writes=[yc])
                S.op("dve", lambda e, sga=sga: e.tensor_tensor(yc[:], yc[:], sga[:], ALU.mult), reads=[yc, sga], writes=[yc])
                if "uA" in dbg and l == 0 and b == 0 and i == 2:
                    uAd = nc.dram_tensor("uAd", [128, 512], F32, kind="ExternalOutput").ap()
                    S.dma("pool", uAd[:, :], yc[:], reads=[yc])
                pt = nps()
                for j in range(4):
                    S.op("pe", lambda e, pt=pt, j=j: e.transpose(pt.t[:, j * 128:(j + 1) * 128], yc.t[:, j * 128:(j + 1) * 128], ident),
                         reads=[yc, cst], writes=[pt] if j == 0 else (), pwrites=[pt] if j > 0 else ())
                evac(uAT[:], pt.t[:, :].rearrange("p (j f) -> p j f", f=128), reads=[pt], writes=[uAT])
                for cg in range(2):
                    cs_ = slice(cg * 512, (cg + 1) * 512)
                    pA, pB, pC = nps(), nps(), nps()
                    for k in range(4):
                        mm(pA, pA.t[:, :], uAT.t[:, k, :], wbo0.t[:, k, cs_], k == 0, k == 3, [uAT, wbo0])
                    for k in range(4):
                        mm(pB, pB.t[:, :], utb[u].t[:, k, :], wbo1.t[:, k, cs_], k == 0, k == 3, [utb[u], wbo1])
                    for k in range(4):
                        mm(pC, pC.t[:, :], utc[u].t[:, k, :], wbo2.t[:, k, cs_], k == 0, k == 3, [utc[u], wbo2])
                    S.op("dve", lambda e, pA=pA, cs_=cs_, sg=sg, cg=cg: e.tensor_tensor(mt.t[:, cs_], pA.t[:, :], sg.t[:, cg * 512:(cg + 1) * 512], ALU.mult), reads=[pA, sg],
                         writes=[mt] if cg == 0 else (), pwrites=[mt] if cg > 0 else ())
                    S.op("dve", lambda e, pB=pB, sg=sg, cg=cg: e.tensor_tensor(tmpm[:], pB.t[:, :], sg.t[:, D + cg * 512:D + (cg + 1) * 512], ALU.mult), reads=[pB, sg], writes=[tmpm])
                    S.op("pool", lambda e, cs_=cs_: e.tensor_tensor(mt.t[:, cs_], mt.t[:, cs_], tmpm[:], ALU.add), reads=[mt, tmpm], writes=[mt])
                    S.op("dve", lambda e, pC=pC, sg=sg, cg=cg: e.tensor_tensor(tmpm[:], pC.t[:, :], sg.t[:, 2 * D + cg * 512:2 * D + (cg + 1) * 512], ALU.mult), reads=[pC, sg], writes=[tmpm])
                    S.op("pool", lambda e, cs_=cs_: e.tensor_tensor(mt.t[:, cs_], mt.t[:, cs_], tmpm[:], ALU.add), reads=[mt, tmpm], writes=[mt])
                for half in range(2):
                    pt = nps()
                    for j in range(4):
                        k = half * 4 + j
                        S.op("pe", lambda e, pt=pt, j=j, k=k: e.transpose(pt.t[:, j * 128:(j + 1) * 128], mt.t[:, k * 128:(k + 1) * 128], ident),
                             reads=[mt, cst], writes=[pt] if j == 0 else (), pwrites=[pt] if j > 0 else ())
                    evac(mTt.t[:, half * 4:(half + 1) * 4, :], pt.t[:, :].rearrange("p (j f) -> p j f", f=128), reads=[pt], writes=[mTt] if half == 0 else (), pwrites=[mTt] if half > 0 else ())
                xo_ = xo[u]
                for cg in range(2):
                    cs_ = slice(cg * 512, (cg + 1) * 512)
                    pO = nps()
                    for k in range(8):
                        mm(pO, pO.t[:, :], mTt.t[:, k, :], wo.t[:, k, cs_], k == 0, k == 7, [mTt, wo])
                    S.op("dve", lambda e, pO=pO, cs_=cs_, xo_=xo_, r=r: e.tensor_tensor(xo_.t[:, cs_], pO.t[:, :], gateb[r].t[:, cs_], ALU.mult), reads=[pO, gateb[r]],
                         writes=[xo_] if cg == 0 else (), pwrites=[xo_] if cg > 0 else ())
                    S.op("pool", lambda e, cs_=cs_, xo_=xo_, xt=xt: e.tensor_tensor(xo_.t[:, cs_], xo_.t[:, cs_], xt.t[:, cs_], ALU.add), reads=[xo_, xt], writes=[xo_])
                if last:
                    S.dma("pool", yout[b, (i - 2) * 128:(i - 1) * 128, :], xo_[:], reads=[xo_], pwrites=[bX[2]])
                else:
                    S.dma("pool", Xnext[b, rows, :], xo_[:], reads=[xo_], pwrites=[bXnext])
        phase_end(st)
        Xcur, bXcur = Xnext, bXnext

    S.barrier()
    S.emit()
    outer.close()
    return nc, S.total


def _rope_full(rot_dim):
    grid_w = 64
    t = np.arange(TL)
    row = (t // grid_w).astype(np.float32)
    col = (t % grid_w).astype(np.float32)
    axis_dim = rot_dim // 2
    inv = (np.float32(10000.0) ** (-(2.0 * np.arange(axis_dim // 2, dtype=np.float32)) / np.float32(axis_dim))).astype(np.float32)
    ang = np.concatenate([row[:, None] * inv, col[:, None] * inv], axis=-1).astype(np.float32)
    cos, sin = np.cos(ang).astype(np.float32), np.sin(ang).astype(np.float32)
    q = rot_dim // 4
    cr, cc, sr, sc = cos[:, :q], cos[:, q:], sin[:, :q], sin[:, q:]
    cosF = np.concatenate([cr, cr, cc, cc], axis=-1)
    sinF = np.concatenate([-sr, sr, -sc, sc], axis=-1)
    out = np.zeros((T, 2 * rot_dim), np.float32)
    out[:LC, :rot_dim] = 1.0
    out[LC:, :rot_dim] = cosF
    out[LC:, rot_dim:] = sinF
    return out


def _consts():
    c = np.zeros((128, 1024), np.float32)
    c[:, 0:128] = np.eye(128, dtype=np.float32)
    c[0:64, 128:192] = 1.0
    c[64:128, 192:256] = 1.0
    c[0:64, 256 + 127] = 1.0
    c[64:128, 256 + 191] = 1.0
    kj = np.arange(128)[:, None]
    qi = np.arange(128)[None, :]
    c[:, 512:640] = (kj >= qi)
    c[:, 640:768] = (kj <= qi)
    c[0:64, 768] = 1.0
    c[64:128, 769] = 1.0
    c[:, 776:840] = 1.0
    return c


def _fm(a, nch):
    lead = a.shape[:-1]
    return np.ascontiguousarray(np.swapaxes(a.reshape(lead + (nch, 128)), -1, -2))


def prep_shared(inp):
    f = lambda k: np.asarray(inp[k], np.float32)
    L = DEPTH
    sh = {
        "consts": _consts(), "ropeB": _rope_full(32), "ropeC": _rope_full(64),
        "ada_w": f("ada_w"), "ada_bT": _fm(f("ada_b"), 24), "norm_gT": _fm(f("norm_g"), 8),
        "w_in": f("w_in"), "mupT": _fm(f("a_mu_prev"), 14), "munT": _fm(f("a_mu_next"), 14),
        "w0T": np.ascontiguousarray(np.transpose(f("a_w0").reshape(L, 2, 4, 128), (0, 3, 1, 2)).reshape(L, 128, 8)),
        "a0T": np.ascontiguousarray(np.transpose(f("a_a0").reshape(L, 2, 4, 128), (0, 3, 1, 2)).reshape(L, 128, 8)),
        "kkT": _fm(f("a_k_k"), 4), "kaT": _fm(f("a_k_a"), 4), "rkT": _fm(f("a_r_k").reshape(L, 512), 4),
        "w_up": np.ascontiguousarray(f("a_w_up").reshape(L, 128, 512)), "a_up": np.ascontiguousarray(f("a_a_up").reshape(L, 128, 512)),
        "gn_g": f("a_gn_g").reshape(L, 1, 512), "gn_b": f("a_gn_b").reshape(L, 1, 512),
        "q_ln": f("b_q_ln").reshape(L, 1, 256), "kv_ln": f("b_kv_ln").reshape(L, 1, 128),
        "w_uq": f("b_w_uq"), "w_ukv": f("b_w_ukv"),
        "bqk_g": np.ascontiguousarray(np.concatenate([f("b_qn_g"), f("b_kn_g")], axis=-1).reshape(L, 1, 192)),
        "c_qn": f("c_qn_g").reshape(L, 1, 64), "c_kn": f("c_kn_g").reshape(L, 1, 64), "c_sink": f("c_sink").reshape(L, 1, 8),
        "wbo": f("w_branch_out"), "w_out": f("w_out"),
    }
    return sh


def prep_core(inp, core, sh):
    x = np.asarray(inp["x"], np.float32)
    ctx = np.asarray(inp["ctx"], np.float32)
    c = np.asarray(inp["c"], np.float32)
    c_ctx = np.asarray(inp["c_ctx"], np.float32)
    b0 = core * NB
    xin = np.concatenate([ctx[b0:b0 + NB], x[b0:b0 + NB]], axis=1)
    rows = np.stack([c[b0], c[b0 + 1], c_ctx], axis=0)
    cT = np.ascontiguousarray(np.transpose(rows.reshape(3, 8, 128), (2, 1, 0)))
    m = dict(sh)
    m["xin"] = np.ascontiguousarray(xin)
    m["cT"] = cT
    return m


_CACHE = {}


def kernel(**inputs):
    n_cores = 8
    if "nc" not in _CACHE:
        _CACHE["nc"] = build_program()[0]
    nc = _CACHE["nc"]
    sh = prep_shared(inputs)
    in_maps = [prep_core(inputs, c, sh) for c in range(n_cores)]
    res = run_bass_kernel_spmd(nc, in_maps, core_ids=list(range(n_cores)))
    out = np.concatenate([np.asarray(r["yout"], np.float32) for r in res.results], axis=0)
    return out
```

```python
import math
import os
from contextlib import ExitStack

import numpy as np
import concourse.bass as bass
import concourse.mybir as mybir
from concourse.bass_utils import run_bass_kernel_spmd

F32 = mybir.dt.float32
ALU = mybir.AluOpType
AF = mybir.ActivationFunctionType
AX = mybir.AxisListType

D = 1024
NB = 2
LC = 256
TL = 2048
T = LC + TL
NT = T // 128
DEPTH = 2
NIN = 7584
EPS = 1e-6
GN_EPS = 64e-5
ZW = 4768
ZO_GA, ZO_ZB, ZO_ZC, ZO_ZG = 0, 512, 928, 1696
TCH = [(0, 512), (512, 512), (1024, 512), (1536, 512), (2048, 256)]


class Buf:
    __slots__ = ("t", "w", "r", "name", "psum")

    def __init__(self, t=None, name="", psum=False):
        self.t = t
        self.w = {}
        self.r = {}
        self.name = name
        self.psum = psum

    def __getitem__(self, k):
        return self.t[k]


class Sched:
    ENGS = ("pe", "act", "dve", "pool", "sp")
    QS = ("sp", "pool")

    def __init__(self, nc, stack, ndma=8):
        self.nc = nc
        self.prog = {e: [] for e in self.ENGS}
        self.sem = {e: stack.enter_context(nc.semaphore("s_" + e)) for e in self.ENGS}
        self.cnt = {e: 0 for e in self.ENGS}
        self.waited = {e: {} for e in self.ENGS}
        self.ndma = ndma
        self.dsem = {q: [stack.enter_context(nc.semaphore("d_%s%d" % (q, i))) for i in range(ndma)] for q in self.QS}
        self.dcnt = {q: [0] * ndma for q in self.QS}
        self.dnext = {q: 0 for q in self.QS}
        self.total = 0

    def _deps(self, e, reads, writes, pwrites):
        best = {}

        def add(d):
            for k, sv in d.items():
                if k not in best or best[k][1] < sv[1]:
                    best[k] = sv
        for b in reads:
            add(b.w)
            if b.psum:
                own = id(self.sem[e]) if e in self.sem else None
                add({k: sv for k, sv in b.r.items() if k != own})
        for b in writes:
            add(b.w)
            add(b.r)
        for b in pwrites:
            add(b.r)
        out = []
        wd = self.waited[e]
        for k, (s, v) in best.items():
            if e == "pe" and s is self.sem["pe"]:
                continue
            if wd.get(k, 0) >= v:
                continue
            wd[k] = v
            out.append((s, v))
        return out

    @staticmethod
    def _mark(reads, writes, pwrites, tok):
        k = id(tok[0])
        for b in reads:
            if k not in b.r or b.r[k][1] < tok[1]:
                b.r[k] = tok
        for b in writes:
            b.w = {k: tok}
            b.r = {}
        for b in pwrites:
            if k not in b.w or b.w[k][1] < tok[1]:
                b.w[k] = tok

    def op(self, e, fn, reads=(), writes=(), pwrites=()):
        deps = self._deps(e, reads, writes, pwrites)
        self.cnt[e] += 1
        tok = (self.sem[e], self.cnt[e])
        self.prog[e].append((deps, fn, (self.sem[e], 1)))
        self._mark(reads, writes, pwrites, tok)
        return tok

    def dma(self, q, out, in_, reads=(), writes=(), pwrites=(), **kw):
        i = self.dnext[q]
        self.dnext[q] = (i + 1) % self.ndma
        s = self.dsem[q][i]
        deps = self._deps(q, reads, writes, pwrites)
        prev = self.dcnt[q][i]
        if prev > 0 and self.waited[q].get(id(s), 0) < prev:
            deps.append((s, prev))
            self.waited[q][id(s)] = prev
        self.dcnt[q][i] = prev + 16
        tok = (s, prev + 16)
        self.prog[q].append((deps, (lambda eng: eng.dma_start(out=out, in_=in_, **kw)), (s, 16)))
        self._mark(reads, writes, pwrites, tok)
        return tok

    def barrier(self):
        alld = [(self.sem[x], self.cnt[x]) for x in self.ENGS if self.cnt[x] > 0]
        for q in self.QS:
            for i in range(self.ndma):
                if self.dcnt[q][i] > 0:
                    alld.append((self.dsem[q][i], self.dcnt[q][i]))
        for e in self.ENGS:
            deps = []
            wd = self.waited[e]
            for (s, v) in alld:
                if s is self.sem[e]:
                    continue
                if wd.get(id(s), 0) >= v:
                    continue
                wd[id(s)] = v
                deps.append((s, v))
            self.prog[e].append((deps, None, None))

    def emit(self):
        with self.nc.Block() as block:
            def mk(e):
                def body(eng):
                    for deps, fn, inc in self.prog[e]:
                        for (s, v) in deps:
                            eng.wait_ge(s, v)
                        if fn is not None:
                            fn(eng).then_inc(inc[0], inc[1])
                return body
            block.tensor(mk("pe"))
            block.scalar(mk("act"))
            block.vector(mk("dve"))
            block.gpsimd(mk("pool"))
            block.sync(mk("sp"))
        for e in self.ENGS:
            self.total += len(self.prog[e]) + sum(len(d) for d, _, _ in self.prog[e])
            self.prog[e] = []


def build_program(n_layers=DEPTH, dbg=(), stop_after=None):
    nc = bass.Bass("TRN2", target_bir_lowering=False)
    dbg = set(dbg)

    def din(name, shape):
        return nc.dram_tensor(name, list(shape), F32, kind="ExternalInput").ap()

    def dscr(name, shape):
        kind = "ExternalOutput" if name in dbg else "Internal"
        return nc.dram_tensor(name, list(shape), F32, kind=kind).ap()

    xin = din("xin", [NB, T, D])
    cT_d = din("cT", [128, 8, 3])
    consts_d = din("consts", [128, 1024])
    ropeB_d = din("ropeB", [T, 64])
    ropeC_d = din("ropeC", [T, 128])
    W = {}
    for nm, shp in [("ada_w", [DEPTH, D, 3 * D]), ("ada_bT", [DEPTH, 128, 24]), ("norm_gT", [DEPTH, 128, 8]),
                    ("w_in", [DEPTH, D, NIN]), ("mupT", [DEPTH, 128, 14]), ("munT", [DEPTH, 128, 14]),
                    ("w0T", [DEPTH, 128, 8]), ("a0T", [DEPTH, 128, 8]), ("kkT", [DEPTH, 128, 4]),
                    ("kaT", [DEPTH, 128, 4]), ("rkT", [DEPTH, 128, 4]), ("w_up", [DEPTH, 128, 512]),
                    ("a_up", [DEPTH, 128, 512]), ("gn_g", [DEPTH, 1, 512]), ("gn_b", [DEPTH, 1, 512]),
                    ("q_ln", [DEPTH, 1, 256]), ("kv_ln", [DEPTH, 1, 128]), ("w_uq", [DEPTH, 256, 768]),
                    ("w_ukv", [DEPTH, 128, 1024]), ("bqk_g", [DEPTH, 1, 192]), ("c_qn", [DEPTH, 1, 64]),
                    ("c_kn", [DEPTH, 1, 64]), ("c_sink", [DEPTH, 1, 8]), ("wbo", [DEPTH, 3, 512, D]),
                    ("w_out", [DEPTH, D, D])]:
        W[nm] = din(nm, shp)
    yout = nc.dram_tensor("yout", [NB, TL, D], F32, kind="ExternalOutput").ap()

    X1 = dscr("X1", [NB, T, D])
    MODG = dscr("MODG", [3, D])
    ZTM = dscr("ZTM", [NB, T, ZW])
    ZTA = dscr("ZTA", [NB, 14, 128, T])
    GT = dscr("GT", [NB, 2, 8, 64, T])
    SC = dscr("SC", [2, 128, 5, 8, T])
    VT = dscr("VT", [T, 1024])
    YD = dscr("YD", [2, T, 1024])
    BS = dscr("BS", [NB, T, 8])
    QKT = dscr("QKT", [NB, 16, 96, T])
    VB = dscr("VB", [NB, T, 512])
    QKTC = dscr("QKTC", [NB, 10, 64, T])
    VC = dscr("VC", [NB, T, 128])
    UT = dscr("UT", [NB, 2, 8, 64, T])

    bX = [Buf(None, "xin"), Buf(None, "X1"), Buf(None, "yout")]
    bMODG = Buf(None, "MODG")
    bZTM = [Buf(None, "ZTM%d" % b) for b in range(NB)]
    bZTA = [Buf(None, "ZTA%d" % b) for b in range(NB)]
    bGT = [Buf(None, "GT%d" % b) for b in range(NB)]
    bSC = Buf(None, "SC")
    bVT = Buf(None, "VT")
    bYD = Buf(None, "YD")
    bBS = [Buf(None, "BS%d" % b) for b in range(NB)]
    bQKT = [Buf(None, "QKT%d" % b) for b in range(NB)]
    bVB = [Buf(None, "VB%d" % b) for b in range(NB)]
    bQKTC = [Buf(None, "QKTC%d" % b) for b in range(NB)]
    bVC = [Buf(None, "VC%d" % b) for b in range(NB)]
    bUT = [Buf(None, "UT%d" % b) for b in range(NB)]
    bW = Buf(None, "weights")

    outer = ExitStack()
    S = Sched(nc, outer)
    uid = [0]

    def sbt(stack, shape, name=None):
        uid[0] += 1
        nm = "%s_%d" % (name or "t", uid[0])
        return Buf(stack.enter_context(nc.sbuf_tensor(nm, list(shape), F32)), nm)

    cst = sbt(outer, [128, 1024], "cst")
    S.dma("sp", cst[:], consts_d[:, :], reads=[bW], writes=[cst])
    ident = cst.t[:, 0:128]
    bones = cst.t[:, 128:256]
    Z2 = cst.t[:, 256:511]
    mask_lo = cst.t[:, 512:640]
    mask_hi = cst.t[:, 640:768]
    ind2 = cst.t[:, 768:770]
    ones64 = cst.t[:, 776:840]
    PS = [Buf(outer.enter_context(nc.psum_tensor("ps%d" % i, [128, 512], F32)), "ps%d" % i, psum=True) for i in range(8)]
    psi = [0]

    def nps():
        p = PS[psi[0] % 8]
        psi[0] += 1
        return p

    modT = sbt(outer, [128, 24, 3], "modT")
    Gt = sbt(outer, [128, 8, 3], "Gt")
    csil = sbt(outer, [128, 8, 3], "csil")
    S.dma("sp", csil[:], cT_d[:, :, :], reads=[bW], writes=[csil])
    S.op("act", lambda e: e.activation(csil[:], csil[:], AF.Silu), reads=[csil], writes=[csil])

    def phase_end(st):
        S.barrier()
        S.emit()
        st.close()

    def mm(ps, out_ap, lhsT, rhs, start, stop, reads):
        S.op("pe", lambda e: e.matmul(out_ap, lhsT, rhs, start=start, stop=stop), reads=reads, pwrites=[ps] if not start else (), writes=[ps] if start else ())

    def mmp(ps, out_ap, lhsT, rhs, start, stop, reads):
        S.op("pe", lambda e: e.matmul(out_ap, lhsT, rhs, start=start, stop=stop), reads=reads, pwrites=[ps])

    def bc_load(st, src_row_ap, n, parts=128, name="bc"):
        t = sbt(st, [parts, n], name)
        S.dma("sp", t[:], src_row_ap.partition_broadcast(parts), reads=[bW], writes=[t])
        return t

    evac_rr = [0]

    def evac(out_ap, in_ap, reads, writes=(), pwrites=(), func=None, bias=None, scale=None):
        if func is None and bias is None and scale is None:
            evac_rr[0] += 1
            if evac_rr[0] % 2 == 0:
                S.op("dve", lambda e: e.tensor_copy(out_ap, in_ap), reads=reads, writes=writes, pwrites=pwrites)
            else:
                S.op("act", lambda e: e.copy(out_ap, in_ap), reads=reads, writes=writes, pwrites=pwrites)
        else:
            kw = {}
            if bias is not None:
                kw["bias"] = bias
            if scale is not None:
                kw["scale"] = scale
            f = func if func is not None else AF.Identity
            S.op("act", lambda e: e.activation(out_ap, in_ap, f, **kw), reads=reads, writes=writes, pwrites=pwrites)

    def rstd_from_ss(st_tiles, ss, n, inv_n, eps):
        S.op("dve", lambda e: e.tensor_scalar(ss[:, 0:n], ss[:, 0:n], inv_n, eps, ALU.mult, ALU.add), reads=[ss], writes=[ss])
        S.op("act", lambda e: e.activation(ss[:, 0:n], ss[:, 0:n], AF.Sqrt), reads=[ss], writes=[ss])
        S.op("dve", lambda e: e.reciprocal(ss[:, 0:n], ss[:, 0:n]), reads=[ss], writes=[ss])

    def rope(st, x3, nh, R, cs, tmp1, tmp2):
        q = R // 4
        xb = x3_buf[0]
        t1 = tmp1.t[:, 0:nh * R].rearrange("p (h r) -> p h r", h=nh)
        cosb = cs.t[:, 0:R].unsqueeze(1).to_broadcast([128, nh, R])
        S.op("dve", lambda e: e.tensor_tensor(t1, x3, cosb, ALU.mult), reads=[xb, cs], writes=[tmp1])
        x5 = x3.rearrange("p h (a f i) -> p h a f i", a=2, f=2)
        t5 = t1.rearrange("p h (a f i) -> p h a f i", a=2, f=2)
        s4 = cs.t[:, R:2 * R].rearrange("p (a f i) -> p a f i", a=2, f=2)
        t2 = tmp2.t[:, 0:nh * R // 2].rearrange("p (h a i) -> p h a i", h=nh, a=2)
        for hf in (0, 1):
            xin_ = x5[:, :, :, 1 - hf, :]
            sb_ = s4[:, :, hf, :].unsqueeze(1).to_broadcast([128, nh, 2, q])
            S.op("dve", lambda e, xin_=xin_, sb_=sb_: e.tensor_tensor(t2, xin_, sb_, ALU.mult), reads=[xb, cs], writes=[tmp2])
            tt = t5[:, :, :, hf, :]
            S.op("dve", lambda e, tt=tt: e.tensor_tensor(tt, tt, t2, ALU.add), reads=[tmp2, tmp1], writes=[tmp1])
        S.op("dve", lambda e: e.tensor_copy(x3, t1), reads=[tmp1], writes=[xb])

    x3_buf = [None]

    Xcur, bXcur = xin, bX[0]
    for l in range(n_layers):
        last = (l == DEPTH - 1)
        Xnext, bXnext = (X1, bX[1])
        st = ExitStack()
        adab = sbt(st, [128, 24], "adab")
        ngt = sbt(st, [128, 8], "ngt")
        S.dma("sp", adab[:], W["ada_bT"][l, :, :], reads=[bW], writes=[adab])
        S.dma("sp", ngt[:], W["norm_gT"][l, :, :], reads=[bW], writes=[ngt])
        wa = [sbt(st, [128, 8, 128], "wa") for _ in range(3)]
        psM = nps()
        for ch in range(24):
            wt = wa[ch % 3]
            S.dma("sp", wt[:], W["ada_w"][l, :, ch * 128:(ch + 1) * 128].rearrange("(k p) n -> p k n", p=128), reads=[bW], writes=[wt])
            for k in range(8):
                mmp(psM, psM.t[:, ch * 3:ch * 3 + 3], wt.t[:, k, :], csil.t[:, k, :], k == 0, k == 7, [wt, csil]) if ch > 0 or k > 0 else \
                    mm(psM, psM.t[:, 0:3], wt.t[:, k, :], csil.t[:, k, :], True, False, [wt, csil])
        S.op("dve", lambda e: e.tensor_tensor(modT[:], psM.t[:, 0:72].rearrange("p (c r) -> p c r", r=3),
                                              adab.t[:, :].unsqueeze(2).to_broadcast([128, 24, 3]), ALU.add),
             reads=[psM, adab], writes=[modT])
        S.op("dve", lambda e: e.tensor_scalar(Gt[:], modT.t[:, 8:16, :], 1.0, None, ALU.add), reads=[modT], writes=[Gt])
        S.op("dve", lambda e: e.tensor_tensor(Gt[:], Gt[:], ngt.t[:, :].unsqueeze(2).to_broadcast([128, 8, 3]), ALU.mult),
             reads=[Gt, ngt], writes=[Gt])
        psG = [nps(), nps()]
        for ch in range(8):
            pg = psG[ch // 4]
            (mm if ch % 4 == 0 else mmp)(pg, pg.t[0:3, (ch % 4) * 128:(ch % 4 + 1) * 128], modT.t[:, 16 + ch, :], ident, True, True, [modT, cst])
        gsb = sbt(st, [3, 1024], "gsb")
        for hlf in range(2):
            S.op("dve", lambda e, hlf=hlf: e.tensor_copy(gsb.t[0:3, hlf * 512:(hlf + 1) * 512], psG[hlf].t[0:3, :]), reads=[psG[hlf]], pwrites=[gsb])
        S.dma("pool", MODG[:, :], gsb[:], reads=[gsb], writes=[bMODG])
        phase_end(st)
        if stop_after == "P0":
            break

        for b in range(NB):
            st = ExitStack()
            hT = sbt(st, [128, 8, T], "hT")
            xts = [sbt(st, [128, D], "xt") for _ in range(2)]
            sqs = [sbt(st, [128, D], "sq") for _ in range(2)]
            sss = [sbt(st, [128, 1], "ss") for _ in range(2)]
            for i in range(NT):
                r = b if i >= 2 else 2
                xt, sq, ss = xts[i % 2], sqs[i % 2], sss[i % 2]
                S.dma("sp", xt[:], Xcur[b, i * 128:(i + 1) * 128, :], reads=[bXcur], writes=[xt])
                S.op("pool", lambda e, ss=ss: e.memset(ss[:], 0.0), writes=[ss])
                S.op("act", lambda e, xt=xt, sq=sq, ss=ss: e.activation(sq[:], xt[:], AF.Square, accum_out=ss[:]), reads=[xt, ss], writes=[sq, ss])
                NLV = int(os.environ.get("KDBG_NLV", 9))
                if NLV < 2:
                    continue
                rstd_from_ss(None, ss, 1, 1.0 / D, EPS)
                S.op("dve", lambda e, xt=xt, sq=sq, ss=ss: e.tensor_scalar(sq[:], xt[:], ss.t[:, 0:1], None, ALU.mult), reads=[xt, ss], writes=[sq])
                if NLV < 3:
                    continue
                for half in range(2):
                    pt = nps()
                    for c4 in range(4):
                        ch = half * 4 + c4
                        S.op("pe", lambda e, pt=pt, c4=c4, ch=ch, sq=sq: e.transpose(pt.t[:, c4 * 128:(c4 + 1) * 128], sq.t[:, ch * 128:(ch + 1) * 128], ident),
                             reads=[sq, cst], writes=[pt] if c4 == 0 else (), pwrites=[pt] if c4 > 0 else ())
                    if NLV < 4:
                        continue
                    for c4 in range(4):
                        ch = half * 4 + c4
                        o_ap = hT.t[:, ch, i * 128:(i + 1) * 128]
                        i_ap = pt.t[:, c4 * 128:(c4 + 1) * 128]
                        g_ap = Gt.t[:, ch, r:r + 1]
                        s_ap = modT.t[:, ch, r:r + 1]
                        EV = os.environ.get("KDBG_EV", "")
                        if (c4 % 2 == 0 and EV != "act") or EV == "dve":
                            S.op("dve", lambda e, o_ap=o_ap, i_ap=i_ap, g_ap=g_ap, s_ap=s_ap: e.tensor_scalar(o_ap, i_ap, g_ap, s_ap, ALU.mult, ALU.add),
                                 reads=[pt, Gt, modT], pwrites=[hT])
                        else:
                            S.op("act", lambda e, o_ap=o_ap, i_ap=i_ap, g_ap=g_ap, s_ap=s_ap: e.activation(o_ap, i_ap, AF.Identity, bias=s_ap, scale=g_ap),
                                 reads=[pt, Gt, modT], pwrites=[hT])
            if "hT" in dbg and b == 0 and l == 0:
                hTd = nc.dram_tensor("hTd", [128, 8, T], F32, kind="ExternalOutput").ap()
                for k in range(8):
                    S.dma("pool", hTd[:, k, :], hT.t[:, k, :], reads=[hT])
            if stop_after == "P1a":
                phase_end(st)
                break
            wbufs = [sbt(st, [128, 8, 512], "wb") for _ in range(2)]
            zos = [sbt(st, [128, 512], "zo") for _ in range(3)]
            groups = [(1792, 512, ZO_GA, AF.Silu), (2304, 416, ZO_ZB, None), (3232, 512, ZO_ZC, None), (3744, 256, ZO_ZC + 512, None)]
            for g in range(6):
                groups.append((4512 + g * 512, 512, ZO_ZG + g * 512, AF.Sigmoid))
            zi = 0
            for gi, (c0, n, zoff, fn) in enumerate(groups):
                wb = wbufs[gi % 2]
                S.dma("sp", wb.t[:, :, 0:n], W["w_in"][l, :, c0:c0 + n].rearrange("(k p) n -> p k n", p=128), reads=[bW], writes=[wb])
                for i in range(NT):
                    ps = nps()
                    for k in range(8):
                        mm(ps, ps.t[:, 0:n], hT.t[:, k, i * 128:(i + 1) * 128], wb.t[:, k, 0:n], k == 0, k == 7, [hT, wb])
                    zo = zos[zi % 3]
                    zi += 1
                    evac(zo.t[:, 0:n], ps.t[:, 0:n], reads=[ps], writes=[zo], func=fn)
                    S.dma("pool", ZTM[b, i * 128:(i + 1) * 128, zoff:zoff + n], zo.t[:, 0:n], reads=[zo], pwrites=[bZTM[b]])
            zfs = [sbt(st, [128, T], "zf") for _ in range(2)]
            fm = [(c * 128, 128, ("A", c), None) for c in range(14)]
            fm += [(2720 + h * 64, 64, ("G", 0, h), AF.Silu) for h in range(8)]
            fm += [(4000 + h * 64, 64, ("G", 1, h), AF.Silu) for h in range(8)]
            for fi, (c0, m, dst, fn) in enumerate(fm):
                wb = wbufs[fi % 2]
                zf = zfs[fi % 2]
                S.dma("sp", wb.t[:, :, 0:m], W["w_in"][l, :, c0:c0 + m].rearrange("(k p) n -> p k n", p=128), reads=[bW], writes=[wb])
                for ci, (t0, tn) in enumerate(TCH):
                    ps = nps()
                    for k in range(8):
                        mm(ps, ps.t[0:m, 0:tn], wb.t[:, k, 0:m], hT.t[:, k, t0:t0 + tn], k == 0, k == 7, [hT, wb])
                    evac(zf.t[0:m, t0:t0 + tn], ps.t[0:m, 0:tn], reads=[ps], writes=[zf] if ci == 0 else (), pwrites=[zf] if ci > 0 else (), func=fn)
                if dst[0] == "A":
                    S.dma("pool", ZTA[b, dst[1], :, :], zf.t[:, :], reads=[zf], pwrites=[bZTA[b]])
                else:
                    S.dma("pool", GT[b, dst[1], dst[2], :, :], zf.t[0:64, :], reads=[zf], pwrites=[bGT[b]])
            phase_end(st)
            if stop_after == "P1":
                break

            st = ExitStack()
            mup = sbt(st, [128, 14], "mup")
            mun = sbt(st, [128, 14], "mun")
            c0t = sbt(st, [128, 14], "c0t")
            w0t = sbt(st, [128, 8], "w0t")
            a0t = sbt(st, [128, 8], "a0t")
            kkp = sbt(st, [128, 4], "kkp")
            kap = sbt(st, [128, 4], "kap")
            rkp = sbt(st, [128, 4], "rkp")
            wup = sbt(st, [128, 512], "wup")
            aup = sbt(st, [128, 512], "aup")
            for tl_, nm in [(mup, "mupT"), (mun, "munT"), (w0t, "w0T"), (a0t, "a0T"), (kkp, "kkT"), (kap, "kaT"), (rkp, "rkT"), (wup, "w_up"), (aup, "a_up")]:
                S.dma("sp", tl_[:], W[nm][l, :, :], reads=[bW], writes=[tl_])
            S.op("dve", lambda e: e.tensor_tensor(c0t[:], mup[:], mun[:], ALU.add), reads=[mup, mun], writes=[c0t])
            S.op("dve", lambda e: e.tensor_scalar(c0t[:], c0t[:], -1.0, 1.0, ALU.mult, ALU.add), reads=[c0t], writes=[c0t])
            NBT = 15
            bts = [sbt(st, [128, T], "bt") for _ in range(NBT)]
            zraw = [bts[0], bts[1]]
            zri = [0]

            def shift_load(c, dst):
                zr = zraw[zri[0] % 2]
                zri[0] += 1
                S.dma("sp", zr[:], ZTA[b, c, :, :], reads=[bZTA[b]], writes=[zr])
                S.op("act", lambda e: e.activation(dst[:], zr[:], AF.Identity, scale=c0t.t[:, c:c + 1]), reads=[zr, c0t], writes=[dst])
                for (o0, o1, i0, i1, mt) in [(1, 256, 0, 255, mup), (257, T, 256, T - 1, mup), (0, 255, 1, 256, mun), (256, T - 1, 257, T, mun)]:
                    S.op("dve", lambda e, o0=o0, o1=o1, i0=i0, i1=i1, mt=mt: e.scalar_tensor_tensor(dst.t[:, o0:o1], zr.t[:, i0:i1], mt.t[:, c:c + 1], dst.t[:, o0:o1], ALU.mult, ALU.add),
                         reads=[zr, mt, dst], writes=[dst])

            twd, ads = bts[2], bts[3]
            shift_load(12, twd)
            S.op("act", lambda e: e.activation(twd[:], twd[:], AF.Tanh), reads=[twd], writes=[twd])
            shift_load(13, ads)
            rs_, ks_, vs_, kk, tq, a_d, dec, ka_d, kd0, kd1, uu = bts[4:15]
            bsS = sbt(st, [128, NT, 8], "bsS")
            vtm = [sbt(st, [128, 4, 128], "vtm") for _ in range(2)]
            for q in range(4):
                shift_load(q, rs_)
                shift_load(4 + q, ks_)
                shift_load(8 + q, vs_)
                S.op("dve", lambda e, q=q: e.tensor_scalar(kk[:], ks_[:], kkp.t[:, q:q + 1], None, ALU.mult), reads=[ks_, kkp], writes=[kk])
                S.op("pool", lambda e: e.tensor_tensor(tq[:], kk[:], kk[:], ALU.mult), reads=[kk], writes=[tq])
                for ci, (t0, tn) in enumerate(TCH):
                    ps = nps()
                    mm(ps, ps.t[:, 0:tn], bones, tq.t[:, t0:t0 + tn], True, True, [tq, cst])
                    S.op("dve", lambda e, ps=ps, t0=t0, tn=tn: e.tensor_scalar_max(a_d.t[:, t0:t0 + tn], ps.t[:, 0:tn], 1e-24), reads=[ps],
                         writes=[a_d] if ci == 0 else (), pwrites=[a_d] if ci > 0 else ())
                S.op("act", lambda e: e.activation(a_d[:], a_d[:], AF.Sqrt), reads=[a_d], writes=[a_d])
                S.op("dve", lambda e: e.reciprocal(a_d[:], a_d[:]), reads=[a_d], writes=[a_d])
                S.op("dve", lambda e: e.tensor_tensor(kk[:], kk[:], a_d[:], ALU.mult), reads=[kk, a_d], writes=[kk])
                S.dma("pool", SC[0, :, 3, b * 4 + q, :], kk[:], reads=[kk], pwrites=[bSC])
                S.dma("pool", SC[1, :, 3, b * 4 + q, :], kk[:], reads=[kk], pwrites=[bSC])
                S.dma("pool", SC[0, :, 4, b * 4 + q, :], rs_[:], reads=[rs_], pwrites=[bSC])
                S.dma("pool", SC[1, :, 4, b * 4 + q, :], rs_[:], reads=[rs_], pwrites=[bSC])
                for d in range(2):
                    kd = kd0 if d == 0 else kd1
                    for ci, (t0, tn) in enumerate(TCH):
                        ps = nps()
                        mm(ps, ps.t[:, 0:tn], wup.t[d * 64:(d + 1) * 64, q * 128:(q + 1) * 128], twd.t[d * 64:(d + 1) * 64, t0:t0 + tn], True, True, [wup, twd])
                        evac(dec.t[:, t0:t0 + tn], ps.t[:, 0:tn], reads=[ps, w0t], writes=[dec] if ci == 0 else (), pwrites=[dec] if ci > 0 else (),
                             func=AF.Sigmoid, bias=w0t.t[:, d * 4 + q:d * 4 + q + 1])
                    S.op("act", lambda e: e.activation(dec[:], dec[:], AF.Exp, scale=-math.exp(-0.5)), reads=[dec], writes=[dec])
                    S.dma("pool", SC[d, :, 0, b * 4 + q, :], dec[:], reads=[dec], pwrites=[bSC])
                    for ci, (t0, tn) in enumerate(TCH):
                        ps = nps()
                        mm(ps, ps.t[:, 0:tn], aup.t[d * 64:(d + 1) * 64, q * 128:(q + 1) * 128], ads.t[d * 64:(d + 1) * 64, t0:t0 + tn], True, True, [aup, ads])
                        evac(a_d.t[:, t0:t0 + tn], ps.t[:, 0:tn], reads=[ps, a0t], writes=[a_d] if ci == 0 else (), pwrites=[a_d] if ci > 0 else (),
                             func=AF.Sigmoid, bias=a0t.t[:, d * 4 + q:d * 4 + q + 1])
                    S.op("pool", lambda e: e.tensor_tensor(ka_d[:], kk[:], a_d[:], ALU.mult), reads=[kk, a_d], writes=[ka_d])
                    S.dma("pool", SC[d, :, 1, b * 4 + q, :], ka_d[:], reads=[ka_d], pwrites=[bSC])
                    S.op("dve", lambda e, kd=kd, q=q: e.tensor_scalar(kd[:], a_d[:], kap.t[:, q:q + 1], kap.t[:, q:q + 1], ALU.mult, ALU.subtract), reads=[a_d, kap], writes=[kd])
                    S.op("dve", lambda e, kd=kd: e.scalar_tensor_tensor(kd[:], kd[:], 1.0, ks_[:], ALU.add, ALU.mult), reads=[kd, ks_], writes=[kd])
                    S.dma("pool", SC[d, :, 2, b * 4 + q, :], kd[:], reads=[kd], pwrites=[bSC])
                S.op("pool", lambda e: e.tensor_tensor(uu[:], kd0[:], kd1[:], ALU.add), reads=[kd0, kd1], writes=[uu])
                S.op("dve", lambda e, q=q: e.scalar_tensor_tensor(uu[:], uu[:], rkp.t[:, q:q + 1], rs_[:], ALU.mult, ALU.mult), reads=[uu, rkp, rs_], writes=[uu])
                psb = nps()
                for i in range(NT):
                    mm(psb, psb.t[:, i * 2:i * 2 + 2], uu.t[:, i * 128:(i + 1) * 128], ind2, True, True, [uu, cst]) if i == 0 else \
                        mmp(psb, psb.t[:, i * 2:i * 2 + 2], uu.t[:, i * 128:(i + 1) * 128], ind2, True, True, [uu, cst])
                S.op("dve", lambda e, q=q, psb=psb: e.tensor_copy(bsS.t[:, :, 2 * q:2 * q + 2], psb.t[:, 0:2 * NT].rearrange("p (i c) -> p i c", c=2)),
                     reads=[psb], writes=[bsS] if q == 0 else (), pwrites=[bsS] if q > 0 else ())
                for g0 in range(0, NT, 4):
                    ng = min(4, NT - g0)
                    pt = nps()
                    for j in range(ng):
                        i = g0 + j
                        S.op("pe", lambda e, pt=pt, j=j, i=i: e.transpose(pt.t[:, j * 128:(j + 1) * 128], vs_.t[:, i * 128:(i + 1) * 128], ident),
                             reads=[vs_, cst], writes=[pt] if j == 0 else (), pwrites=[pt] if j > 0 else ())
                    vt_ = vtm[(g0 // 4) % 2]
                    evac(vt_.t[:, 0:ng, :], pt.t[:, 0:ng * 128].rearrange("p (j f) -> p j f", f=128), reads=[pt], writes=[vt_])
                    for j in range(ng):
                        i = g0 + j
                        dst = VT[i * 128:(i + 1) * 128, :].rearrange("p (c bb qq v) -> p c bb qq v", c=2, bb=2, qq=4)[:, :, b, q, :]
                        S.dma("pool", dst, vt_.t[:, j, :].rearrange("p (c v) -> p c v", c=2), reads=[vt_], pwrites=[bVT])
            S.dma("pool", BS[b, :, :].rearrange("(i p) h -> p i h", p=128), bsS[:], reads=[bsS], writes=[bBS[b]])
            phase_end(st)
            if stop_after == "P2":
                break

            st = ExitStack()
            qln = bc_load(st, W["q_ln"][l, :, :], 256, name="qln")
            kvln = bc_load(st, W["kv_ln"][l, :, :], 128, name="kvln")
            bqkg = bc_load(st, W["bqk_g"][l, :, :], 192, name="bqkg")
            cqn = bc_load(st, W["c_qn"][l, :, :], 64, name="cqn")
            ckn = bc_load(st, W["c_kn"][l, :, :], 64, name="ckn")
            wuq = sbt(st, [128, 2, 768], "wuq")
            wukv = sbt(st, [128, 1024], "wukv")
            S.dma("sp", wuq[:], W["w_uq"][l, :, :].rearrange("(k p) n -> p k n", p=128), reads=[bW], writes=[wuq])
            S.dma("sp", wukv[:], W["w_ukv"][l, :, :], reads=[bW], writes=[wukv])
            NBUF = 2
            zbs = [sbt(st, [128, 416], "zb") for _ in range(NBUF)]
            zcs = [sbt(st, [128, 768], "zc") for _ in range(NBUF)]
            rbs = [sbt(st, [128, 64], "rb") for _ in range(NBUF)]
            rcs = [sbt(st, [128, 128], "rc") for _ in range(NBUF)]
            ss2 = [sbt(st, [128, 2], "ss2") for _ in range(NBUF)]
            junk = sbt(st, [128, 1536], "junk")
            cn = [sbt(st, [128, 384], "cn") for _ in range(NBUF)]
            cT3 = [sbt(st, [128, 3, 128], "cT3") for _ in range(NBUF)]
            qk = [sbt(st, [128, 16, 96], "qk") for _ in range(NBUF)]
            kv = [sbt(st, [128, 8, 128], "kv") for _ in range(NBUF)]
            ssq = [sbt(st, [128, 16], "ssq") for _ in range(NBUF)]
            rt1 = sbt(st, [128, 640], "rt1")
            rt2 = sbt(st, [128, 320], "rt2")
            qkT = [sbt(st, [96, 16, 128], "qkT") for _ in range(NBUF)]
            qkTc = [sbt(st, [64, 10, 128], "qkTc") for _ in range(NBUF)]
            for i in range(NT):
                u = i % NBUF
                zb, zc, rb, rc = zbs[u], zcs[u], rbs[u], rcs[u]
                S.dma("sp", zb[:], ZTM[b, i * 128:(i + 1) * 128, ZO_ZB:ZO_ZB + 416], reads=[bZTM[b]], writes=[zb])
                S.dma("sp", zc[:], ZTM[b, i * 128:(i + 1) * 128, ZO_ZC:ZO_ZC + 768], reads=[bZTM[b]], writes=[zc])
                S.dma("sp", rb[:], ropeB_d[i * 128:(i + 1) * 128, :], reads=[bW], writes=[rb])
                S.dma("sp", rc[:], ropeC_d[i * 128:(i + 1) * 128, :], reads=[bW], writes=[rc])
                s2 = ss2[u]
                S.op("pool", lambda e, s2=s2: e.memset(s2[:], 0.0), writes=[s2])
                S.op("act", lambda e, zb=zb, s2=s2: e.activation(junk.t[:, 0:256], zb.t[:, 0:256], AF.Square, accum_out=s2.t[:, 0:1]), reads=[zb, s2], writes=[junk, s2])
                S.op("act", lambda e, zb=zb, s2=s2: e.activation(junk.t[:, 256:384], zb.t[:, 256:384], AF.Square, accum_out=s2.t[:, 1:2]), reads=[zb, s2], writes=[junk, s2])
                S.op("dve", lambda e, s2=s2: e.tensor_scalar(s2.t[:, 0:1], s2.t[:, 0:1], 1.0 / 256, EPS, ALU.mult, ALU.add), reads=[s2], writes=[s2])
                S.op("dve", lambda e, s2=s2: e.tensor_scalar(s2.t[:, 1:2], s2.t[:, 1:2], 1.0 / 128, EPS, ALU.mult, ALU.add), reads=[s2], writes=[s2])
                S.op("act", lambda e, s2=s2: e.activation(s2[:], s2[:], AF.Sqrt), reads=[s2], writes=[s2])
                S.op("dve", lambda e, s2=s2: e.reciprocal(s2[:], s2[:]), reads=[s2], writes=[s2])
                c_ = cn[u]
                S.op("dve", lambda e, c_=c_, zb=zb, s2=s2: e.scalar_tensor_tensor(c_.t[:, 0:256], zb.t[:, 0:256], s2.t[:, 0:1], qln[:], ALU.mult, ALU.mult), reads=[zb, s2, qln], writes=[c_])
                S.op("dve", lambda e, c_=c_, zb=zb, s2=s2: e.scalar_tensor_tensor(c_.t[:, 256:384], zb.t[:, 256:384], s2.t[:, 1:2], kvln[:], ALU.mult, ALU.mult), reads=[zb, s2, kvln], pwrites=[c_])
                pt = nps()
                for j in range(3):
                    S.op("pe", lambda e, pt=pt, j=j, c_=c_: e.transpose(pt.t[:, j * 128:(j + 1) * 128], c_.t[:, j * 128:(j + 1) * 128], ident),
                         reads=[c_, cst], writes=[pt] if j == 0 else (), pwrites=[pt] if j > 0 else ())
                c3 = cT3[u]
                evac(c3[:], pt.t[:, 0:384].rearrange("p (j f) -> p j f", f=128), reads=[pt], writes=[c3])
                qk_ = qk[u]
                kv_ = kv[u]
                for nh in range(2):
                    ps = nps()
                    for k in range(2):
                        mm(ps, ps.t[:, 0:384], c3.t[:, k, :], wuq.t[:, k, nh * 384:(nh + 1) * 384], k == 0, k == 1, [c3, wuq])
                    evac(qk_.t[:, nh * 4:(nh + 1) * 4, :], ps.t[:, 0:384].rearrange("p (h r) -> p h r", r=96), reads=[ps], writes=[qk_] if nh == 0 else (), pwrites=[qk_] if nh > 0 else ())
                for nh in range(2):
                    ps = nps()
                    mm(ps, ps.t[:, :], c3.t[:, 2, :], wukv.t[:, nh * 512:(nh + 1) * 512], True, True, [c3, wukv])
                    evac(kv_.t[:, nh * 4:(nh + 1) * 4, :], ps.t[:, :].rearrange("p (h r) -> p h r", r=128), reads=[ps], writes=[kv_] if nh == 0 else (), pwrites=[kv_] if nh > 0 else ())
                S.op("dve", lambda e, qk_=qk_, kv_=kv_: e.tensor_copy(qk_.t[:, 8:16, 0:64], kv_.t[:, :, 0:64]), reads=[kv_], pwrites=[qk_])
                S.op("dve", lambda e, qk_=qk_, zb=zb: e.tensor_copy(qk_.t[:, 8:16, 64:96], zb.t[:, 384:416].unsqueeze(1).to_broadcast([128, 8, 32])), reads=[zb], pwrites=[qk_])
                S.dma("pool", VB[b, i * 128:(i + 1) * 128, :].rearrange("p (h v) -> p h v", v=64), kv_.t[:, :, 64:128], reads=[kv_], pwrites=[bVB[b]])
                sq_ = ssq[u]
                S.op("pool", lambda e, qk_=qk_: e.tensor_tensor(junk.t[:, 0:1536], qk_.t[:, :, :].rearrange("p h r -> p (h r)"), qk_.t[:, :, :].rearrange("p h r -> p (h r)"), ALU.mult), reads=[qk_], writes=[junk])
                S.op("dve", lambda e, sq_=sq_: e.tensor_reduce(sq_[:], junk.t[:, 0:1536].rearrange("p (h r) -> p h r", r=96), AX.X, ALU.add), reads=[junk], writes=[sq_])
                rstd_from_ss(None, sq_, 16, 1.0 / 96, EPS)
                S.op("dve", lambda e, qk_=qk_, sq_=sq_: e.tensor_tensor(qk_[:], qk_[:], sq_.t[:, 0:16].unsqueeze(2).to_broadcast([128, 16, 96]), ALU.mult), reads=[qk_, sq_], writes=[qk_])
                S.op("dve", lambda e, qk_=qk_: e.tensor_tensor(qk_.t[:, :, :].rearrange("p (a h) r -> p a h r", a=2), qk_.t[:, :, :].rearrange("p (a h) r -> p a h r", a=2),
                                                              bqkg.t[:, :].rearrange("p (a r) -> p a r", a=2).unsqueeze(2).to_broadcast([128, 2, 8, 96]), ALU.mult), reads=[qk_, bqkg], writes=[qk_])
                x3_buf[0] = qk_
                rope(st, qk_.t[:, :, 64:96], 16, 32, rb, rt1, rt2)
                qT_ = qkT[u]
                for g0 in range(0, 16, 4):
                    pt = nps()
                    for j in range(4):
                        S.op("pe", lambda e, pt=pt, j=j, g0=g0, qk_=qk_: e.transpose(pt.t[0:96, j * 128:(j + 1) * 128], qk_.t[:, g0 + j, :], ident),
                             reads=[qk_, cst], writes=[pt] if j == 0 else (), pwrites=[pt] if j > 0 else ())
                    evac(qT_.t[0:96, g0:g0 + 4, :], pt.t[0:96, :].rearrange("p (j f) -> p j f", f=128), reads=[pt], writes=[qT_] if g0 == 0 else (), pwrites=[qT_] if g0 > 0 else ())
                S.dma("pool", QKT[b, :, :, i * 128:(i + 1) * 128].rearrange("h p t -> p h t"), qT_[:], reads=[qT_], pwrites=[bQKT[b]])
                S.dma("pool", VC[b, i * 128:(i + 1) * 128, :], zc.t[:, 640:768], reads=[zc], pwrites=[bVC[b]])
                S.op("pool", lambda e, zc=zc: e.tensor_tensor(junk.t[:, 0:640], zc.t[:, 0:640], zc.t[:, 0:640], ALU.mult), reads=[zc], writes=[junk])
                S.op("dve", lambda e, sq_=sq_: e.tensor_reduce(sq_.t[:, 0:10], junk.t[:, 0:640].rearrange("p (h r) -> p h r", r=64), AX.X, ALU.add), reads=[junk], writes=[sq_])
                rstd_from_ss(None, sq_, 10, 1.0 / 64, EPS)
                z3 = zc.t[:, 0:640].rearrange("p (h r) -> p h r", r=64)
                S.op("dve", lambda e, z3=z3, sq_=sq_, zc=zc: e.tensor_tensor(z3, z3, sq_.t[:, 0:10].unsqueeze(2).to_broadcast([128, 10, 64]), ALU.mult), reads=[zc, sq_], writes=[zc])
                S.op("dve", lambda e, z3=z3, zc=zc: e.tensor_tensor(z3[:, 0:8, :], z3[:, 0:8, :], cqn.t[:, :].unsqueeze(1).to_broadcast([128, 8, 64]), ALU.mult), reads=[zc, cqn], writes=[zc])
                S.op("dve", lambda e, z3=z3, zc=zc: e.tensor_tensor(z3[:, 8:10, :], z3[:, 8:10, :], ckn.t[:, :].unsqueeze(1).to_broadcast([128, 2, 64]), ALU.mult), reads=[zc, ckn], writes=[zc])
                x3_buf[0] = zc
                rope(st, z3, 10, 64, rc, rt1, rt2)
                qTc_ = qkTc[u]
                for g0 in range(0, 10, 4):
                    ng = min(4, 10 - g0)
                    pt = nps()
                    for j in range(ng):
                        S.op("pe", lambda e, pt=pt, j=j, g0=g0, z3=z3: e.transpose(pt.t[0:64, j * 128:(j + 1) * 128], z3[:, g0 + j, :], ident),
                             reads=[zc, cst], writes=[pt] if j == 0 else (), pwrites=[pt] if j > 0 else ())
                    evac(qTc_.t[0:64, g0:g0 + ng, :], pt.t[0:64, 0:ng * 128].rearrange("p (j f) -> p j f", f=128), reads=[pt], writes=[qTc_] if g0 == 0 else (), pwrites=[qTc_] if g0 > 0 else ())
                S.dma("pool", QKTC[b, :, :, i * 128:(i + 1) * 128].rearrange("h p t -> p h t"), qTc_[:], reads=[qTc_], pwrites=[bQKTC[b]])
            phase_end(st)
        if stop_after in ("P1a", "P1", "P2", "P3"):
            break

        st = ExitStack()
        Sb = [[sbt(st, [128, 8, 64], "S%d_%d" % (d, k)) for k in range(2)] for d in range(2)]
        for d in range(2):
            S.op("dve", lambda e, d=d: e.memset(Sb[d][0][:], 0.0), writes=[Sb[d][0]])
        Sw = [sbt(st, [128, 8, 64], "Sw") for d in range(2)]
        tA = [[sbt(st, [128, 8, 64], "tA") for _ in range(2)] for d in range(2)]
        tB = [sbt(st, [128, 8, 64], "tB") for d in range(2)]
        tC = [[sbt(st, [128, 8, 64], "tC") for _ in range(2)] for d in range(2)]
        t4 = [[sbt(st, [128, 8, 64], "t4") for _ in range(2)] for d in range(2)]
        SCb = [[sbt(st, [128, 5, 8, 64], "SCb") for _ in range(2)] for d in range(2)]
        Vb = [[sbt(st, [64, 1024], "Vb") for _ in range(2)] for d in range(2)]
        ysb = [sbt(st, [128, 512], "ysb") for d in range(2)]
        psV = [[PS[0], PS[1]], [PS[2], PS[3]]]
        psSA = [PS[4], PS[5]]
        psY = [PS[6], PS[7]]
        NBLK = T // 64
        n_scan_blocks = int(os.environ.get("KDBG_SCAN_BLOCKS", NBLK))

        def tok0(d, B):
            if d == 0:
                return 64 * B
            if B < 4:
                return LC - 64 * (B + 1)
            return T - 64 * (B - 3)

        def v3(buf):
            return buf.t[:, :, :]

        def f2(buf):
            return buf.t[:, :, :].rearrange("p a v -> p (a v)")

        gstep = 0
        for B in range(n_scan_blocks):
            u = B % 2
            for d in range(2):
                t0 = tok0(d, B)
                for a in range(5):
                    S.dma("sp", SCb[d][u].t[:, a, :, :], SC[d, :, a, :, t0:t0 + 64], reads=[bSC], writes=[SCb[d][u]] if a == 0 else (), pwrites=[SCb[d][u]] if a else ())
                S.dma("sp", Vb[d][u][:], VT[t0:t0 + 64, :], reads=[bVT], writes=[Vb[d][u]])
            sc = [SCb[d][u] for d in range(2)]
            pend = None

            def emit_t4(pd):
                s_, tls_, Sn_, par_ = pd
                for d in range(2):
                    rb_ = sc[d].t[:, 4, :, tls_[d]].unsqueeze(2).to_broadcast([128, 8, 64])
                    S.op("pool", lambda e, d=d, o=t4[d][par_], sn=Sn_[d], rb_=rb_: e.tensor_tensor(v3(o), v3(sn), rb_, ALU.mult), reads=[Sn_[d], sc[d]], writes=[t4[d][par_]])

            def emit_y(pd):
                s_, tls_, Sn_, par_ = pd
                for d in range(2):
                    S.op("pe", lambda e, d=d, i_=t4[d][par_], tl=tls_[d], s_=s_: e.matmul(psY[d].t[:, :], Z2[:, 127 - tl:255 - tl], f2(i_), start=(s_ == 0), stop=(s_ == 63)),
                         reads=[t4[d][par_], cst], writes=[psY[d]] if s_ == 0 else (), pwrites=[psY[d]] if s_ > 0 else ())

            for s in range(64):
                tls = [s, 63 - s]
                par = gstep % 2
                Sc = [Sb[d][par] for d in range(2)]
                Sn = [Sb[d][1 - par] for d in range(2)]
                gstep += 1

                def bcs(d, a):
                    return sc[d].t[:, a, :, tls[d]].unsqueeze(2).to_broadcast([128, 8, 64])
                pv = [psV[d][s % 2] for d in range(2)]
                ta = [tA[d][s % 2] for d in range(2)]
                tc = [tC[d][s % 2] for d in range(2)]
                for d in range(2):
                    for c2 in range(2):
                        S.op("pe", lambda e, d=d, c2=c2, p=pv[d], tl=tls[d], vb_=Vb[d][u]: e.matmul(p.t[c2 * 64:(c2 + 1) * 64, :], ident[0:64, tl:tl + 1].to_broadcast([64, 64]),
                                                                                              vb_.t[0:64, c2 * 512:(c2 + 1) * 512], start=True, stop=True),
                             reads=[Vb[d][u], cst], writes=[pv[d]] if c2 == 0 else (), pwrites=[pv[d]] if c2 == 1 else ())
                for d in range(2):
                    S.op("dve", lambda e, d=d, o=ta[d], sc_=Sc[d], kb=bcs(d, 3): e.tensor_tensor(v3(o), v3(sc_), kb, ALU.mult), reads=[Sc[d], sc[d]], writes=[ta[d]])
                for d in range(2):
                    S.op("pe", lambda e, d=d, i_=ta[d]: e.matmul(psSA[d].t[:, :], bones, f2(i_), start=True, stop=True), reads=[ta[d], cst], writes=[psSA[d]])
                for d in range(2):
                    S.op("dve", lambda e, d=d, o=tc[d], kb=bcs(d, 2), p=pv[d]: e.tensor_tensor(v3(o), p.t[:, :].rearrange("p (a v) -> p a v", v=64), kb, ALU.mult), reads=[pv[d], sc[d]], writes=[tc[d]])
                S.op("pool", lambda e, sc_=Sc[0], wb_=bcs(0, 0): e.tensor_tensor(v3(Sw[0]), v3(sc_), wb_, ALU.mult), reads=[Sc[0], sc[0]], writes=[Sw[0]])
                S.op("pool", lambda e, t_=tc[0]: e.tensor_tensor(v3(Sw[0]), v3(Sw[0]), v3(t_), ALU.add), reads=[Sw[0], tc[0]], writes=[Sw[0]])
                S.op("pool", lambda e, sc_=Sc[1], wb_=bcs(1, 0): e.tensor_tensor(v3(Sw[1]), v3(sc_), wb_, ALU.mult), reads=[Sc[1], sc[1]], writes=[Sw[1]])
                if pend is not None:
                    emit_t4(pend)
                    emit_y(pend)
                S.op("dve", lambda e, kb=bcs(0, 1): e.tensor_tensor(v3(tB[0]), psSA[0].t[:, :].rearrange("p (a v) -> p a v", v=64), kb, ALU.mult), reads=[psSA[0], sc[0]], writes=[tB[0]])
                S.op("dve", lambda e, sn=Sn[0]: e.tensor_tensor(v3(sn), v3(Sw[0]), v3(tB[0]), ALU.subtract), reads=[Sw[0], tB[0]], writes=[Sn[0]])
                S.op("dve", lambda e, t_=tc[1]: e.tensor_tensor(v3(Sw[1]), v3(Sw[1]), v3(t_), ALU.add), reads=[Sw[1], tc[1]], writes=[Sw[1]])
                S.op("dve", lambda e, kb=bcs(1, 1): e.tensor_tensor(v3(tB[1]), psSA[1].t[:, :].rearrange("p (a v) -> p a v", v=64), kb, ALU.mult), reads=[psSA[1], sc[1]], writes=[tB[1]])
                S.op("dve", lambda e, sn=Sn[1]: e.tensor_tensor(v3(sn), v3(Sw[1]), v3(tB[1]), ALU.subtract), reads=[Sw[1], tB[1]], writes=[Sn[1]])
                pend = (s, tls, Sn, s % 2)
            emit_t4(pend)
            emit_y(pend)
            for d in range(2):
                t0 = tok0(d, B)
                evac(ysb[d][:], psY[d].t[:, :], reads=[psY[d]], writes=[ysb[d]])
                for c2 in range(2):
                    S.dma("pool", YD[d, t0:t0 + 64, c2 * 512:(c2 + 1) * 512], ysb[d].t[c2 * 64:(c2 + 1) * 64, :], reads=[ysb[d]], pwrites=[bYD])
        phase_end(st)
        if stop_after == "P4":
            break

        st = ExitStack()
        KTs = [sbt(st, [96, T], "KT") for _ in range(2)]
        QTs = [sbt(st, [96, T], "QT") for _ in range(2)]
        GTs = [sbt(st, [64, T], "GTh") for _ in range(2)]
        Vhs = [sbt(st, [128, NT, 64], "Vh") for _ in range(2)]
        Pbs = [sbt(st, [128, 512], "Pb") for _ in range(3)]
        rdn = [sbt(st, [64, 512], "rdn") for _ in range(2)]
        uob = [sbt(st, [64, T], "uob") for _ in range(2)]
        pi = 0
        scale_b = 96 ** -0.5
        for b in range(NB):
            for h in range(8):
                u = (b * 8 + h) % 2
                KT, QT, GTh, Vh, uo = KTs[u], QTs[u], GTs[u], Vhs[u], uob[u]
                S.dma("sp", KT[:], QKT[b, 8 + h, :, :], reads=[bQKT[b]], writes=[KT])
                S.dma("sp", QT[:], QKT[b, h, :, :], reads=[bQKT[b]], writes=[QT])
                S.dma("sp", GTh[:], GT[b, 0, h, :, :], reads=[bGT[b]], writes=[GTh])
                S.dma("sp", Vh[:], VB[b, :, h * 64:(h + 1) * 64].rearrange("(i p) v -> p i v", p=128), reads=[bVB[b]], writes=[Vh])
                qchunks = [(LC + j * 512, 512, list(range(NT))) for j in range(4)]
                if not last:
                    qchunks.append((0, 256, [0, 1]))
                for ci, (q0, qn, kts) in enumerate(qchunks):
                    psO, psD = PS[(ci % 2) * 2], PS[(ci % 2) * 2 + 1]
                    for ki, kt in enumerate(kts):
                        psS = PS[4 + pi % 4]
                        mm(psS, psS.t[:, 0:qn], KT.t[0:96, kt * 128:(kt + 1) * 128], QT.t[0:96, q0:q0 + qn], True, True, [KT, QT])
                        Pb = Pbs[pi % 3]
                        pi += 1
                        S.op("act", lambda e, Pb=Pb, psS=psS, qn=qn: e.activation(Pb.t[:, 0:qn], psS.t[:, 0:qn], AF.Exp, scale=scale_b), reads=[psS], writes=[Pb])
                        mm(psO, psO.t[0:64, 0:qn], Vh.t[:, kt, :], Pb.t[:, 0:qn], ki == 0, ki == len(kts) - 1, [Vh, Pb])
                        mm(psD, psD.t[0:64, 0:qn], ones64, Pb.t[:, 0:qn], ki == 0, ki == len(kts) - 1, [cst, Pb])
                    rd = rdn[ci % 2]
                    S.op("dve", lambda e, rd=rd, psD=psD, qn=qn: e.reciprocal(rd.t[:, 0:qn], psD.t[0:64, 0:qn]), reads=[psD], writes=[rd])
                    S.op("dve", lambda e, rd=rd, psO=psO, qn=qn: e.tensor_tensor(rd.t[:, 0:qn], psO.t[0:64, 0:qn], rd.t[:, 0:qn], ALU.mult), reads=[psO, rd], writes=[rd])
                    S.op("pool", lambda e, rd=rd, uo=uo, GTh=GTh, q0=q0, qn=qn: e.tensor_tensor(uo.t[:, q0:q0 + qn], rd.t[:, 0:qn], GTh.t[:, q0:q0 + qn], ALU.mult), reads=[rd, GTh],
                         writes=[uo] if ci == 0 else (), pwrites=[uo] if ci > 0 else ())
                if last:
                    S.dma("pool", UT[b, 0, h, :, LC:T], uo.t[:, LC:T], reads=[uo], pwrites=[bUT[b]])
                else:
                    S.dma("pool", UT[b, 0, h, :, :], uo[:], reads=[uo], pwrites=[bUT[b]])
        phase_end(st)
        if stop_after == "P5":
            break

        st = ExitStack()
        esk = bc_load(st, W["c_sink"][l, :, :], 8, parts=64, name="esk")
        S.op("act", lambda e: e.activation(esk[:], esk[:], AF.Exp), reads=[esk], writes=[esk])
        KTg = [sbt(st, [64, T], "KTg") for _ in range(2)]
        Q4 = [[sbt(st, [64, T], "Q4") for _ in range(4)] for _ in range(2)]
        G4 = [[sbt(st, [64, T], "G4") for _ in range(4)] for _ in range(2)]
        Vg = [sbt(st, [128, NT, 64], "Vg") for _ in range(2)]
        Pcs = [sbt(st, [128, 512], "Pc") for _ in range(3)]
        rdc = [sbt(st, [64, 512], "rdc") for _ in range(2)]
        uoc = [sbt(st, [64, 4, 128], "uoc") for _ in range(2)]
        scale_c = 64 ** -0.5
        pi = 0
        bi = 0
        for b in range(NB):
            for g in range(2):
                u = (b * 2 + g) % 2
                S.dma("sp", KTg[u][:], QKTC[b, 8 + g, :, :], reads=[bQKTC[b]], writes=[KTg[u]])
                S.dma("sp", Vg[u][:], VC[b, :, g * 64:(g + 1) * 64].rearrange("(i p) v -> p i v", p=128), reads=[bVC[b]], writes=[Vg[u]])
                for hh in range(4):
                    S.dma("sp", Q4[u][hh][:], QKTC[b, 4 * g + hh, :, :], reads=[bQKTC[b]], writes=[Q4[u][hh]])
                    S.dma("sp", G4[u][hh][:], GT[b, 1, 4 * g + hh, :, :], reads=[bGT[b]], writes=[G4[u][hh]])
                blocks = list(range(2, NT))
                if not last:
                    blocks = [0, 1] + blocks
                for n in blocks:
                    if n < 2:
                        kts = [(0, None), (1, None)]
                    else:
                        kts = [(0, None), (1, None)]
                        if n - 1 >= 2:
                            kts.append((n - 1, mask_lo))
                        kts.append((n, None))
                        if n + 1 < NT:
                            kts.append((n + 1, mask_hi))
                    psO, psD = PS[(bi % 2) * 2], PS[(bi % 2) * 2 + 1]
                    for ki, (kt, msk) in enumerate(kts):
                        psS = PS[4 + pi % 4]
                        for hh in range(4):
                            S.op("pe", lambda e, psS=psS, hh=hh, kt=kt, n=n, u=u: e.matmul(psS.t[:, hh * 128:(hh + 1) * 128], KTg[u].t[0:64, kt * 128:(kt + 1) * 128],
                                                                                       Q4[u][hh].t[0:64, n * 128:(n + 1) * 128], start=True, stop=True),
                                 reads=[KTg[u], Q4[u][hh]], writes=[psS] if hh == 0 else (), pwrites=[psS] if hh > 0 else ())
                        Pc = Pcs[pi % 3]
                        pi += 1
                        S.op("act", lambda e, Pc=Pc, psS=psS: e.activation(Pc[:], psS.t[:, :], AF.Exp, scale=scale_c), reads=[psS], writes=[Pc])
                        if msk is not None:
                            S.op("dve", lambda e, Pc=Pc, msk=msk: e.tensor_tensor(Pc.t[:, :].rearrange("p (h q) -> p h q", h=4), Pc.t[:, :].rearrange("p (h q) -> p h q", h=4),
                                                                                   msk.unsqueeze(1).to_broadcast([128, 4, 128]), ALU.mult), reads=[Pc, cst], writes=[Pc])
                        mm(psO, psO.t[0:64, :], Vg[u].t[:, kt, :], Pc.t[:, :], ki == 0, ki == len(kts) - 1, [Vg[u], Pc])
                        mm(psD, psD.t[0:64, :], ones64, Pc.t[:, :], ki == 0, ki == len(kts) - 1, [cst, Pc])
                    rd = rdc[bi % 2]
                    uo = uoc[bi % 2]
                    bi += 1
                    S.op("dve", lambda e, rd=rd, psD=psD, g=g: e.tensor_tensor(rd.t[:, :].rearrange("p (h q) -> p h q", h=4), psD.t[0:64, :].rearrange("p (h q) -> p h q", h=4),
                                                                                esk.t[:, 4 * g:4 * g + 4].unsqueeze(2).to_broadcast([64, 4, 128]), ALU.add), reads=[psD, esk], writes=[rd])
                    S.op("dve", lambda e, rd=rd: e.reciprocal(rd[:], rd[:]), reads=[rd], writes=[rd])
                    S.op("dve", lambda e, rd=rd, psO=psO: e.tensor_tensor(rd[:], psO.t[0:64, :], rd[:], ALU.mult), reads=[psO, rd], writes=[rd])
                    for hh in range(4):
                        S.op("pool", lambda e, rd=rd, uo=uo, hh=hh, n=n, u=u: e.tensor_tensor(uo.t[:, hh, :], rd.t[:, hh * 128:(hh + 1) * 128], G4[u][hh].t[:, n * 128:(n + 1) * 128], ALU.mult),
                             reads=[rd, G4[u][hh]], writes=[uo] if hh == 0 else (), pwrites=[uo] if hh > 0 else ())
                    S.dma("pool", UT[b, 1, 4 * g:4 * g + 4, :, n * 128:(n + 1) * 128].rearrange("h p t -> p h t"), uo[:], reads=[uo], pwrites=[bUT[b]])
        phase_end(st)
        if stop_after == "P6":
            break

        st = ExitStack()
        gng = bc_load(st, W["gn_g"][l, :, :], 512, name="gng")
        gnb = bc_load(st, W["gn_b"][l, :, :], 512, name="gnb")
        wbo0 = sbt(st, [128, 4, D], "wbo0")
        wbo1 = sbt(st, [128, 4, D], "wbo1")
        wbo2 = sbt(st, [128, 4, D], "wbo2")
        wo = sbt(st, [128, 8, D], "wo")
        S.dma("sp", wbo0[:], W["wbo"][l, 0, :, :].rearrange("(k p) n -> p k n", p=128), reads=[bW], writes=[wbo0])
        S.dma("sp", wbo1[:], W["wbo"][l, 1, :, :].rearrange("(k p) n -> p k n", p=128), reads=[bW], writes=[wbo1])
        S.dma("sp", wbo2[:], W["wbo"][l, 2, :, :].rearrange("(k p) n -> p k n", p=128), reads=[bW], writes=[wbo2])
        S.dma("sp", wo[:], W["w_out"][l, :, :].rearrange("(k p) n -> p k n", p=128), reads=[bW], writes=[wo])
        gateb = [sbt(st, [128, D], "gateb") for _ in range(3)]
        for r in range(3):
            S.dma("sp", gateb[r][:], MODG[r:r + 1, :].partition_broadcast(128), reads=[bMODG], writes=[gateb[r]])
        y0s = [sbt(st, [128, 512], "y0") for _ in range(2)]
        y1s = [sbt(st, [128, 512], "y1") for _ in range(2)]
        vts = [sbt(st, [128, 512], "vt") for _ in range(2)]
        sgas = [sbt(st, [128, 512], "sga") for _ in range(2)]
        bss = [sbt(st, [128, 8], "bs") for _ in range(2)]
        sgs = [sbt(st, [128, 3 * D], "sg") for _ in range(2)]
        xts = [sbt(st, [128, D], "xt") for _ in range(2)]
        utb = [sbt(st, [128, 4, 128], "utb") for _ in range(2)]
        utc = [sbt(st, [128, 4, 128], "utc") for _ in range(2)]
        st8 = sbt(st, [128, 8], "st8")
        yc = sbt(st, [128, 512], "yc")
        ysq = sbt(st, [128, 512], "ysq")
        uAT = sbt(st, [128, 4, 128], "uAT")
        mt = sbt(st, [128, D], "mt")
        tmpm = sbt(st, [128, 512], "tmpm")
        mTt = sbt(st, [128, 8, 128], "mTt")
        xo = [sbt(st, [128, D], "xo") for _ in range(2)]
        it = 0
        for b in range(NB):
            for i in (range(2, NT) if last else range(NT)):
                u = it % 2
                it += 1
                r = b if i >= 2 else 2
                y0, y1, vt, sga, bs_, sg, xt = y0s[u], y1s[u], vts[u], sgas[u], bss[u], sgs[u], xts[u]
                rows = slice(i * 128, (i + 1) * 128)

                def perm_src(a, c):
                    return a.rearrange("p (c bb qq v) -> p c bb qq v", c=2, bb=2, qq=4)[:, c, b, :, :]

                def perm_dst(tile, c):
                    return tile.t[:, :].rearrange("p (qq c v) -> p c qq v", qq=4, c=2)[:, c, :, :]
                for c in range(2):
                    S.dma("sp", perm_dst(y0, c), perm_src(YD[0, rows, :], c), reads=[bYD], writes=[y0] if c == 0 else (), pwrites=[y0] if c else ())
                    S.dma("sp", perm_dst(y1, c), perm_src(YD[1, rows, :], c), reads=[bYD], writes=[y1] if c == 0 else (), pwrites=[y1] if c else ())
                    S.dma("sp", perm_dst(vt, c), perm_src(VT[rows, :], c), reads=[bVT], writes=[vt] if c == 0 else (), pwrites=[vt] if c else ())
                S.dma("sp", sga[:], ZTM[b, rows, ZO_GA:ZO_GA + 512], reads=[bZTM[b]], writes=[sga])
                S.dma("sp", bs_[:], BS[b, rows, :], reads=[bBS[b]], writes=[bs_])
                S.dma("sp", sg[:], ZTM[b, rows, ZO_ZG:ZO_ZG + 3 * D], reads=[bZTM[b]], writes=[sg])
                S.dma("sp", xt[:], Xcur[b, rows, :], reads=[bXcur], writes=[xt])
                S.dma("sp", utb[u][:], UT[b, 0, :, :, :].rearrange("h p t -> (h p) t").rearrange("(k q) t -> q k t", q=128)[:, :, rows], reads=[bUT[b]], writes=[utb[u]])
                S.dma("sp", utc[u][:], UT[b, 1, :, :, :].rearrange("h p t -> (h p) t").rearrange("(k q) t -> q k t", q=128)[:, :, rows], reads=[bUT[b]], writes=[utc[u]])
                S.op("pool", lambda e, y0=y0, y1=y1: e.tensor_tensor(y0[:], y0[:], y1[:], ALU.add), reads=[y0, y1], writes=[y0])
                y3 = y0.t[:, :].rearrange("p (h v) -> p h v", v=64)
                yc3 = yc.t[:, :].rearrange("p (h v) -> p h v", v=64)
                S.op("dve", lambda e, y3=y3: e.tensor_reduce(st8[:], y3, AX.X, ALU.add), reads=[y0], writes=[st8])
                S.op("dve", lambda e: e.tensor_scalar(st8[:], st8[:], 1.0 / 64, None, ALU.mult), reads=[st8], writes=[st8])
                S.op("dve", lambda e, y3=y3, yc3=yc3: e.tensor_tensor(yc3, y3, st8.t[:, :].unsqueeze(2).to_broadcast([128, 8, 64]), ALU.subtract), reads=[y0, st8], writes=[yc])
                S.op("pool", lambda e: e.tensor_tensor(ysq[:], yc[:], yc[:], ALU.mult), reads=[yc], writes=[ysq])
                S.op("dve", lambda e: e.tensor_reduce(st8[:], ysq.t[:, :].rearrange("p (h v) -> p h v", v=64), AX.X, ALU.add), reads=[ysq], writes=[st8])
                rstd_from_ss(None, st8, 8, 1.0 / 64, GN_EPS)
                S.op("dve", lambda e, yc3=yc3: e.tensor_tensor(yc3, yc3, st8.t[:, :].unsqueeze(2).to_broadcast([128, 8, 64]), ALU.mult), reads=[yc, st8], writes=[yc])
                S.op("dve", lambda e: e.tensor_tensor(yc[:], yc[:], gng[:], ALU.mult), reads=[yc, gng], writes=[yc])
                S.op("pool", lambda e: e.tensor_tensor(yc[:], yc[:], gnb[:], ALU.add), reads=[yc, gnb], writes=[yc])
                S.op("dve", lambda e, vt=vt, bs_=bs_: e.tensor_tensor(vt.t[:, :].rearrange("p (h v) -> p h v", v=64), vt.t[:, :].rearrange("p (h v) -> p h v", v=64),
                                                                      bs_.t[:, :].unsqueeze(2).to_broadcast([128, 8, 64]), ALU.mult), reads=[vt, bs_], writes=[vt])
                S.op("pool", lambda e, vt=vt: e.tensor_tensor(yc[:], yc[:], vt[:], ALU.add), reads=[yc, vt], writes=[yc])
                S.op("dve", lambda e, sga=sga: e.tensor_tensor(yc[:], yc[:], sga[:], ALU.mult), reads=[yc, sga], writes=[yc])
                if "uA" in dbg and l == 0 and b == 0 and i == 2:
                    uAd = nc.dram_tensor("uAd", [128, 512], F32, kind="ExternalOutput").ap()
                    S.dma("pool", uAd[:, :], yc[:], reads=[yc])
                pt = nps()
                for j in range(4):
                    S.op("pe", lambda e, pt=pt, j=j: e.transpose(pt.t[:, j * 128:(j + 1) * 128], yc.t[:, j * 128:(j + 1) * 128], ident),
                         reads=[yc, cst], writes=[pt] if j == 0 else (), pwrites=[pt] if j > 0 else ())
                evac(uAT[:], pt.t[:, :].rearrange("p (j f) -> p j f", f=128), reads=[pt], writes=[uAT])
                for cg in range(2):
                    cs_ = slice(cg * 512, (cg + 1) * 512)
                    pA, pB, pC = nps(), nps(), nps()
                    for k in range(4):
                        mm(pA, pA.t[:, :], uAT.t[:, k, :], wbo0.t[:, k, cs_], k == 0, k == 3, [uAT, wbo0])
                    for k in range(4):
                        mm(pB, pB.t[:, :], utb[u].t[:, k, :], wbo1.t[:, k, cs_], k == 0, k == 3, [utb[u], wbo1])
                    for k in range(4):
                        mm(pC, pC.t[:, :], utc[u].t[:, k, :], wbo2.t[:, k, cs_], k == 0, k == 3, [utc[u], wbo2])
                    S.op("dve", lambda e, pA=pA, cs_=cs_, sg=sg, cg=cg: e.tensor_tensor(mt.t[:, cs_], pA.t[:, :], sg.t[:, cg * 512:(cg + 1) * 512], ALU.mult), reads=[pA, sg],
                         writes=[mt] if cg == 0 else (), pwrites=[mt] if cg > 0 else ())
                    S.op("dve", lambda e, pB=pB, sg=sg, cg=cg: e.tensor_tensor(tmpm[:], pB.t[:, :], sg.t[:, D + cg * 512:D + (cg + 1) * 512], ALU.mult), reads=[pB, sg], writes=[tmpm])
                    S.op("pool", lambda e, cs_=cs_: e.tensor_tensor(mt.t[:, cs_], mt.t[:, cs_], tmpm[:], ALU.add), reads=[mt, tmpm], writes=[mt])
                    S.op("dve", lambda e, pC=pC, sg=sg, cg=cg: e.tensor_tensor(tmpm[:], pC.t[:, :], sg.t[:, 2 * D + cg * 512:2 * D + (cg + 1) * 512], ALU.mult), reads=[pC, sg], writes=[tmpm])
                    S.op("pool", lambda e, cs_=cs_: e.tensor_tensor(mt.t[:, cs_], mt.t[:, cs_], tmpm[:], ALU.add), reads=[mt, tmpm], writes=[mt])
                for half in range(2):
                    pt = nps()
                    for j in range(4):
                        k = half * 4 + j
                        S.op("pe", lambda e, pt=pt, j=j, k=k: e.transpose(pt.t[:, j * 128:(j + 1) * 128], mt.t[:, k * 128:(k + 1) * 128], ident),
                             reads=[mt, cst], writes=[pt] if j == 0 else (), pwrites=[pt] if j > 0 else ())
                    evac(mTt.t[:, half * 4:(half + 1) * 4, :], pt.t[:, :].rearrange("p (j f) -> p j f", f=128), reads=[pt], writes=[mTt] if half == 0 else (), pwrites=[mTt] if half > 0 else ())
                xo_ = xo[u]
                for cg in range(2):
                    cs_ = slice(cg * 512, (cg + 1) * 512)
                    pO = nps()
                    for k in range(8):
                        mm(pO, pO.t[:, :], mTt.t[:, k, :], wo.t[:, k, cs_], k == 0, k == 7, [mTt, wo])
                    S.op("dve", lambda e, pO=pO, cs_=cs_, xo_=xo_, r=r: e.tensor_tensor(xo_.t[:, cs_], pO.t[:, :], gateb[r].t[:, cs_], ALU.mult), reads=[pO, gateb[r]],
                         writes=[xo_] if cg == 0 else (), pwrites=[xo_] if cg > 0 else ())
                    S.op("pool", lambda e, cs_=cs_, xo_=xo_, xt=xt: e.tensor_tensor(xo_.t[:, cs_], xo_.t[:, cs_], xt.t[:, cs_], ALU.add), reads=[xo_, xt], writes=[xo_])
                if last:
                    S.dma("pool", yout[b, (i - 2) * 128:(i - 1) * 128, :], xo_[:], reads=[xo_], pwrites=[bX[2]])
                else:
                    S.dma("pool", Xnext[b, rows, :], xo_[:], reads=[xo_], pwrites=[bXnext])
        phase_end(st)
        Xcur, bXcur = Xnext, bXnext

    S.barrier()
    S.emit()
    outer.close()
    return nc, S.total


def _rope_full(rot_dim):
    grid_w = 64
    t = np.arange(TL)
    row = (t // grid_w).astype(np.float32)
    col = (t % grid_w).astype(np.float32)
    axis_dim = rot_dim // 2
    inv = (np.float32(10000.0) ** (-(2.0 * np.arange(axis_dim // 2, dtype=np.float32)) / np.float32(axis_dim))).astype(np.float32)
    ang = np.concatenate([row[:, None] * inv, col[:, None] * inv], axis=-1).astype(np.float32)
    cos, sin = np.cos(ang).astype(np.float32), np.sin(ang).astype(np.float32)
    q = rot_dim // 4
    cr, cc, sr, sc = cos[:, :q], cos[:, q:], sin[:, :q], sin[:, q:]
    cosF = np.concatenate([cr, cr, cc, cc], axis=-1)
    sinF = np.concatenate([-sr, sr, -sc, sc], axis=-1)
    out = np.zeros((T, 2 * rot_dim), np.float32)
    out[:LC, :rot_dim] = 1.0
    out[LC:, :rot_dim] = cosF
    out[LC:, rot_dim:] = sinF
    return out


def _consts():
    c = np.zeros((128, 1024), np.float32)
    c[:, 0:128] = np.eye(128, dtype=np.float32)
    c[0:64, 128:192] = 1.0
    c[64:128, 192:256] = 1.0
    c[0:64, 256 + 127] = 1.0
    c[64:128, 256 + 191] = 1.0
    kj = np.arange(128)[:, None]
    qi = np.arange(128)[None, :]
    c[:, 512:640] = (kj >= qi)
    c[:, 640:768] = (kj <= qi)
    c[0:64, 768] = 1.0
    c[64:128, 769] = 1.0
    c[:, 776:840] = 1.0
    return c


def _fm(a, nch):
    lead = a.shape[:-1]
    return np.ascontiguousarray(np.swapaxes(a.reshape(lead + (nch, 128)), -1, -2))


def prep_shared(inp):
    f = lambda k: np.asarray(inp[k], np.float32)
    L = DEPTH
    sh = {
        "consts": _consts(), "ropeB": _rope_full(32), "ropeC": _rope_full(64),
        "ada_w": f("ada_w"), "ada_bT": _fm(f("ada_b"), 24), "norm_gT": _fm(f("norm_g"), 8),
        "w_in": f("w_in"), "mupT": _fm(f("a_mu_prev"), 14), "munT": _fm(f("a_mu_next"), 14),
        "w0T": np.ascontiguousarray(np.transpose(f("a_w0").reshape(L, 2, 4, 128), (0, 3, 1, 2)).reshape(L, 128, 8)),
        "a0T": np.ascontiguousarray(np.transpose(f("a_a0").reshape(L, 2, 4, 128), (0, 3, 1, 2)).reshape(L, 128, 8)),
        "kkT": _fm(f("a_k_k"), 4), "kaT": _fm(f("a_k_a"), 4), "rkT": _fm(f("a_r_k").reshape(L, 512), 4),
        "w_up": np.ascontiguousarray(f("a_w_up").reshape(L, 128, 512)), "a_up": np.ascontiguousarray(f("a_a_up").reshape(L, 128, 512)),
        "gn_g": f("a_gn_g").reshape(L, 1, 512), "gn_b": f("a_gn_b").reshape(L, 1, 512),
        "q_ln": f("b_q_ln").reshape(L, 1, 256), "kv_ln": f("b_kv_ln").reshape(L, 1, 128),
        "w_uq": f("b_w_uq"), "w_ukv": f("b_w_ukv"),
        "bqk_g": np.ascontiguousarray(np.concatenate([f("b_qn_g"), f("b_kn_g")], axis=-1).reshape(L, 1, 192)),
        "c_qn": f("c_qn_g").reshape(L, 1, 64), "c_kn": f("c_kn_g").reshape(L, 1, 64), "c_sink": f("c_sink").reshape(L, 1, 8),
        "wbo": f("w_branch_out"), "w_out": f("w_out"),
    }
    return sh


def prep_core(inp, core, sh):
    x = np.asarray(inp["x"], np.float32)
    ctx = np.asarray(inp["ctx"], np.float32)
    c = np.asarray(inp["c"], np.float32)
    c_ctx = np.asarray(inp["c_ctx"], np.float32)
    b0 = core * NB
    xin = np.concatenate([ctx[b0:b0 + NB], x[b0:b0 + NB]], axis=1)
    rows = np.stack([c[b0], c[b0 + 1], c_ctx], axis=0)
    cT = np.ascontiguousarray(np.transpose(rows.reshape(3, 8, 128), (2, 1, 0)))
    m = dict(sh)
    m["xin"] = np.ascontiguousarray(xin)
    m["cT"] = cT
    return m


_CACHE = {}


def kernel(**inputs):
    n_cores = 8
    if "nc" not in _CACHE:
        _CACHE["nc"] = build_program()[0]
    nc = _CACHE["nc"]
    sh = prep_shared(inputs)
    in_maps = [prep_core(inputs, c, sh) for c in range(n_cores)]
    res = run_bass_kernel_spmd(nc, in_maps, core_ids=list(range(n_cores)))
    out = np.concatenate([np.asarray(r["yout"], np.float32) for r in res.results], axis=0)
    return out
```

```python
import math
import os
from contextlib import ExitStack

import numpy as np
import concourse.bass as bass
import concourse.mybir as mybir
from concourse.bass_utils import run_bass_kernel_spmd

F32 = mybir.dt.float32
ALU = mybir.AluOpType
AF = mybir.ActivationFunctionType
AX = mybir.AxisListType

D = 1024
NB = 2
LC = 256
TL = 2048
T = LC + TL
NT = T // 128
DEPTH = 2
NIN = 7584
EPS = 1e-6
GN_EPS = 64e-5
ZW = 4768
ZO_GA, ZO_ZB, ZO_ZC, ZO_ZG = 0, 512, 928, 1696
TCH = [(0, 512), (512, 512), (1024, 512), (1536, 512), (2048, 256)]


class Buf:
    __slots__ = ("t", "w", "r", "name", "psum")

    def __init__(self, t=None, name="", psum=False):
        self.t = t
        self.w = {}
        self.r = {}
        self.name = name
        self.psum = psum

    def __getitem__(self, k):
        return self.t[k]


class Sched:
    ENGS = ("pe", "act", "dve", "pool", "sp")
    QS = ("sp", "pool")

    def __init__(self, nc, stack, ndma=8):
        self.nc = nc
        self.prog = {e: [] for e in self.ENGS}
        self.sem = {e: stack.enter_context(nc.semaphore("s_" + e)) for e in self.ENGS}
        self.cnt = {e: 0 for e in self.ENGS}
        self.waited = {e: {} for e in self.ENGS}
        self.ndma = ndma
        self.dsem = {q: [stack.enter_context(nc.semaphore("d_%s%d" % (q, i))) for i in range(ndma)] for q in self.QS}
        self.dcnt = {q: [0] * ndma for q in self.QS}
        self.dnext = {q: 0 for q in self.QS}
        self.total = 0

    def _deps(self, e, reads, writes, pwrites):
        best = {}

        def add(d):
            for k, sv in d.items():
                if k not in best or best[k][1] < sv[1]:
                    best[k] = sv
        for b in reads:
            add(b.w)
            if b.psum:
                own = id(self.sem[e]) if e in self.sem else None
                add({k: sv for k, sv in b.r.items() if k != own})
        for b in writes:
            add(b.w)
            add(b.r)
        for b in pwrites:
            add(b.r)
        out = []
        wd = self.waited[e]
        for k, (s, v) in best.items():
            if e == "pe" and s is self.sem["pe"]:
                continue
            if wd.get(k, 0) >= v:
                continue
            wd[k] = v
            out.append((s, v))
        return out

    @staticmethod
    def _mark(reads, writes, pwrites, tok):
        k = id(tok[0])
        for b in reads:
            if k not in b.r or b.r[k][1] < tok[1]:
                b.r[k] = tok
        for b in writes:
            b.w = {k: tok}
            b.r = {}
        for b in pwrites:
            if k not in b.w or b.w[k][1] < tok[1]:
                b.w[k] = tok

    def op(self, e, fn, reads=(), writes=(), pwrites=()):
        deps = self._deps(e, reads, writes, pwrites)
        self.cnt[e] += 1
        tok = (self.sem[e], self.cnt[e])
        self.prog[e].append((deps, fn, (self.sem[e], 1)))
        self._mark(reads, writes, pwrites, tok)
        return tok

    def dma(self, q, out, in_, reads=(), writes=(), pwrites=(), **kw):
        i = self.dnext[q]
        self.dnext[q] = (i + 1) % self.ndma
        s = self.dsem[q][i]
        deps = self._deps(q, reads, writes, pwrites)
        prev = self.dcnt[q][i]
        if prev > 0 and self.waited[q].get(id(s), 0) < prev:
            deps.append((s, prev))
            self.waited[q][id(s)] = prev
        self.dcnt[q][i] = prev + 16
        tok = (s, prev + 16)
        self.prog[q].append((deps, (lambda eng: eng.dma_start(out=out, in_=in_, **kw)), (s, 16)))
        self._mark(reads, writes, pwrites, tok)
        return tok

    def barrier(self):
        alld = [(self.sem[x], self.cnt[x]) for x in self.ENGS if self.cnt[x] > 0]
        for q in self.QS:
            for i in range(self.ndma):
                if self.dcnt[q][i] > 0:
                    alld.append((self.dsem[q][i], self.dcnt[q][i]))
        for e in self.ENGS:
            deps = []
            wd = self.waited[e]
            for (s, v) in alld:
                if s is self.sem[e]:
                    continue
                if wd.get(id(s), 0) >= v:
                    continue
                wd[id(s)] = v
                deps.append((s, v))
            self.prog[e].append((deps, None, None))

    def emit(self):
        with self.nc.Block() as block:
            def mk(e):
                def body(eng):
                    for deps, fn, inc in self.prog[e]:
                        for (s, v) in deps:
                            eng.wait_ge(s, v)
                        if fn is not None:
                            fn(eng).then_inc(inc[0], inc[1])
                return body
            block.tensor(mk("pe"))
            block.scalar(mk("act"))
            block.vector(mk("dve"))
            block.gpsimd(mk("pool"))
            block.sync(mk("sp"))
        for e in self.ENGS:
            self.total += len(self.prog[e]) + sum(len(d) for d, _, _ in self.prog[e])
            self.prog[e] = []


def build_program(n_layers=DEPTH, dbg=(), stop_after=None):
    nc = bass.Bass("TRN2", target_bir_lowering=False)
    dbg = set(dbg)

    def din(name, shape):
        return nc.dram_tensor(name, list(shape), F32, kind="ExternalInput").ap()

    def dscr(name, shape):
        kind = "ExternalOutput" if name in dbg else "Internal"
        return nc.dram_tensor(name, list(shape), F32, kind=kind).ap()

    xin = din("xin", [NB, T, D])
    cT_d = din("cT", [128, 8, 3])
    consts_d = din("consts", [128, 1024])
    ropeB_d = din("ropeB", [T, 64])
    ropeC_d = din("ropeC", [T, 128])
    W = {}
    for nm, shp in [("ada_w", [DEPTH, D, 3 * D]), ("ada_bT", [DEPTH, 128, 24]), ("norm_gT", [DEPTH, 128, 8]),
                    ("w_in", [DEPTH, D, NIN]), ("mupT", [DEPTH, 128, 14]), ("munT", [DEPTH, 128, 14]),
                    ("w0T", [DEPTH, 128, 8]), ("a0T", [DEPTH, 128, 8]), ("kkT", [DEPTH, 128, 4]),
                    ("kaT", [DEPTH, 128, 4]), ("rkT", [DEPTH, 128, 4]), ("w_up", [DEPTH, 128, 512]),
                    ("a_up", [DEPTH, 128, 512]), ("gn_g", [DEPTH, 1, 512]), ("gn_b", [DEPTH, 1, 512]),
                    ("q_ln", [DEPTH, 1, 256]), ("kv_ln", [DEPTH, 1, 128]), ("w_uq", [DEPTH, 256, 768]),
                    ("w_ukv", [DEPTH, 128, 1024]), ("bqk_g", [DEPTH, 1, 192]), ("c_qn", [DEPTH, 1, 64]),
                    ("c_kn", [DEPTH, 1, 64]), ("c_sink", [DEPTH, 1, 8]), ("wbo", [DEPTH, 3, 512, D]),
                    ("w_out", [DEPTH, D, D])]:
        W[nm] = din(nm, shp)
    yout = nc.dram_tensor("yout", [NB, TL, D], F32, kind="ExternalOutput").ap()

    X1 = dscr("X1", [NB, T, D])
    MODG = dscr("MODG", [3, D])
    ZTM = dscr("ZTM", [NB, T, ZW])
    ZTA = dscr("ZTA", [NB, 14, 128, T])
    GT = dscr("GT", [NB, 2, 8, 64, T])
    SC = dscr("SC", [2, 128, 5, 8, T])
    VT = dscr("VT", [T, 1024])
    YD = dscr("YD", [2, T, 1024])
    BS = dscr("BS", [NB, T, 8])
    QKT = dscr("QKT", [NB, 16, 96, T])
    VB = dscr("VB", [NB, T, 512])
    QKTC = dscr("QKTC", [NB, 10, 64, T])
    VC = dscr("VC", [NB, T, 128])
    UT = dscr("UT", [NB, 2, 8, 64, T])

    bX = [Buf(None, "xin"), Buf(None, "X1"), Buf(None, "yout")]
    bMODG = Buf(None, "MODG")
    bZTM = [Buf(None, "ZTM%d" % b) for b in range(NB)]
    bZTA = [Buf(None, "ZTA%d" % b) for b in range(NB)]
    bGT = [Buf(None, "GT%d" % b) for b in range(NB)]
    bSC = Buf(None, "SC")
    bVT = Buf(None, "VT")
    bYD = Buf(None, "YD")
    bBS = [Buf(None, "BS%d" % b) for b in range(NB)]
    bQKT = [Buf(None, "QKT%d" % b) for b in range(NB)]
    bVB = [Buf(None, "VB%d" % b) for b in range(NB)]
    bQKTC = [Buf(None, "QKTC%d" % b) for b in range(NB)]
    bVC = [Buf(None, "VC%d" % b) for b in range(NB)]
    bUT = [Buf(None, "UT%d" % b) for b in range(NB)]
    bW = Buf(None, "weights")

    outer = ExitStack()
    S = Sched(nc, outer)
    uid = [0]

    def sbt(stack, shape, name=None):
        uid[0] += 1
        nm = "%s_%d" % (name or "t", uid[0])
        return Buf(stack.enter_context(nc.sbuf_tensor(nm, list(shape), F32)), nm)

    cst = sbt(outer, [128, 1024], "cst")
    S.dma("sp", cst[:], consts_d[:, :], reads=[bW], writes=[cst])
    ident = cst.t[:, 0:128]
    bones = cst.t[:, 128:256]
    Z2 = cst.t[:, 256:511]
    mask_lo = cst.t[:, 512:640]
    mask_hi = cst.t[:, 640:768]
    ind2 = cst.t[:, 768:770]
    ones64 = cst.t[:, 776:840]
    PS = [Buf(outer.enter_context(nc.psum_tensor("ps%d" % i, [128, 512], F32)), "ps%d" % i, psum=True) for i in range(8)]
    psi = [0]

    def nps():
        p = PS[psi[0] % 8]
        psi[0] += 1
        return p

    modT = sbt(outer, [128, 24, 3], "modT")
    Gt = sbt(outer, [128, 8, 3], "Gt")
    csil = sbt(outer, [128, 8, 3], "csil")
    S.dma("sp", csil[:], cT_d[:, :, :], reads=[bW], writes=[csil])
    S.op("act", lambda e: e.activation(csil[:], csil[:], AF.Silu), reads=[csil], writes=[csil])

    def phase_end(st):
        S.barrier()
        S.emit()
        st.close()

    def mm(ps, out_ap, lhsT, rhs, start, stop, reads):
        S.op("pe", lambda e: e.matmul(out_ap, lhsT, rhs, start=start, stop=stop), reads=reads, pwrites=[ps] if not start else (), writes=[ps] if start else ())

    def mmp(ps, out_ap, lhsT, rhs, start, stop, reads):
        S.op("pe", lambda e: e.matmul(out_ap, lhsT, rhs, start=start, stop=stop), reads=reads, pwrites=[ps])

    def bc_load(st, src_row_ap, n, parts=128, name="bc"):
        t = sbt(st, [parts, n], name)
        S.dma("sp", t[:], src_row_ap.partition_broadcast(parts), reads=[bW], writes=[t])
        return t

    evac_rr = [0]

    def evac(out_ap, in_ap, reads, writes=(), pwrites=(), func=None, bias=None, scale=None):
        if func is None and bias is None and scale is None:
            evac_rr[0] += 1
            if evac_rr[0] % 2 == 0:
                S.op("dve", lambda e: e.tensor_copy(out_ap, in_ap), reads=reads, writes=writes, pwrites=pwrites)
            else:
                S.op("act", lambda e: e.copy(out_ap, in_ap), reads=reads, writes=writes, pwrites=pwrites)
        else:
            kw = {}
            if bias is not None:
                kw["bias"] = bias
            if scale is not None:
                kw["scale"] = scale
            f = func if func is not None else AF.Identity
            S.op("act", lambda e: e.activation(out_ap, in_ap, f, **kw), reads=reads, writes=writes, pwrites=pwrites)

    def rstd_from_ss(st_tiles, ss, n, inv_n, eps):
        S.op("dve", lambda e: e.tensor_scalar(ss[:, 0:n], ss[:, 0:n], inv_n, eps, ALU.mult, ALU.add), reads=[ss], writes=[ss])
        S.op("act", lambda e: e.activation(ss[:, 0:n], ss[:, 0:n], AF.Sqrt), reads=[ss], writes=[ss])
        S.op("dve", lambda e: e.reciprocal(ss[:, 0:n], ss[:, 0:n]), reads=[ss], writes=[ss])

    def rope(st, x3, nh, R, cs, tmp1, tmp2):
        q = R // 4
        xb = x3_buf[0]
        t1 = tmp1.t[:, 0:nh * R].rearrange("p (h r) -> p h r", h=nh)
        cosb = cs.t[:, 0:R].unsqueeze(1).to_broadcast([128, nh, R])
        S.op("dve", lambda e: e.tensor_tensor(t1, x3, cosb, ALU.mult), reads=[xb, cs], writes=[tmp1])
        x5 = x3.rearrange("p h (a f i) -> p h a f i", a=2, f=2)
        t5 = t1.rearrange("p h (a f i) -> p h a f i", a=2, f=2)
        s4 = cs.t[:, R:2 * R].rearrange("p (a f i) -> p a f i", a=2, f=2)
        t2 = tmp2.t[:, 0:nh * R // 2].rearrange("p (h a i) -> p h a i", h=nh, a=2)
        for hf in (0, 1):
            xin_ = x5[:, :, :, 1 - hf, :]
            sb_ = s4[:, :, hf, :].unsqueeze(1).to_broadcast([128, nh, 2, q])
            S.op("dve", lambda e, xin_=xin_, sb_=sb_: e.tensor_tensor(t2, xin_, sb_, ALU.mult), reads=[xb, cs], writes=[tmp2])
            tt = t5[:, :, :, hf, :]
            S.op("dve", lambda e, tt=tt: e.tensor_tensor(tt, tt, t2, ALU.add), reads=[tmp2, tmp1], writes=[tmp1])
        S.op("dve", lambda e: e.tensor_copy(x3, t1), reads=[tmp1], writes=[xb])

    x3_buf = [None]

    Xcur, bXcur = xin, bX[0]
    for l in range(n_layers):
        last = (l == DEPTH - 1)
        Xnext, bXnext = (X1, bX[1])
        st = ExitStack()
        adab = sbt(st, [128, 24], "adab")
        ngt = sbt(st, [128, 8], "ngt")
        S.dma("sp", adab[:], W["ada_bT"][l, :, :], reads=[bW], writes=[adab])
        S.dma("sp", ngt[:], W["norm_gT"][l, :, :], reads=[bW], writes=[ngt])
        wa = [sbt(st, [128, 8, 128], "wa") for _ in range(3)]
        psM = nps()
        for ch in range(24):
            wt = wa[ch % 3]
            S.dma("sp", wt[:], W["ada_w"][l, :, ch * 128:(ch + 1) * 128].rearrange("(k p) n -> p k n", p=128), reads=[bW], writes=[wt])
            for k in range(8):
                mmp(psM, psM.t[:, ch * 3:ch * 3 + 3], wt.t[:, k, :], csil.t[:, k, :], k == 0, k == 7, [wt, csil]) if ch > 0 or k > 0 else \
                    mm(psM, psM.t[:, 0:3], wt.t[:, k, :], csil.t[:, k, :], True, False, [wt, csil])
        S.op("dve", lambda e: e.tensor_tensor(modT[:], psM.t[:, 0:72].rearrange("p (c r) -> p c r", r=3),
                                              adab.t[:, :].unsqueeze(2).to_broadcast([128, 24, 3]), ALU.add),
             reads=[psM, adab], writes=[modT])
        S.op("dve", lambda e: e.tensor_scalar(Gt[:], modT.t[:, 8:16, :], 1.0, None, ALU.add), reads=[modT], writes=[Gt])
        S.op("dve", lambda e: e.tensor_tensor(Gt[:], Gt[:], ngt.t[:, :].unsqueeze(2).to_broadcast([128, 8, 3]), ALU.mult),
             reads=[Gt, ngt], writes=[Gt])
        psG = [nps(), nps()]
        for ch in range(8):
            pg = psG[ch // 4]
            (mm if ch % 4 == 0 else mmp)(pg, pg.t[0:3, (ch % 4) * 128:(ch % 4 + 1) * 128], modT.t[:, 16 + ch, :], ident, True, True, [modT, cst])
        gsb = sbt(st, [3, 1024], "gsb")
        for hlf in range(2):
            S.op("dve", lambda e, hlf=hlf: e.tensor_copy(gsb.t[0:3, hlf * 512:(hlf + 1) * 512], psG[hlf].t[0:3, :]), reads=[psG[hlf]], pwrites=[gsb])
        S.dma("pool", MODG[:, :], gsb[:], reads=[gsb], writes=[bMODG])
        phase_end(st)
        if stop_after == "P0":
            break

        for b in range(NB):
            st = ExitStack()
            hT = sbt(st, [128, 8, T], "hT")
            xts = [sbt(st, [128, D], "xt") for _ in range(2)]
            sqs = [sbt(st, [128, D], "sq") for _ in range(2)]
            sss = [sbt(st, [128, 1], "ss") for _ in range(2)]
            for i in range(NT):
                r = b if i >= 2 else 2
                xt, sq, ss = xts[i % 2], sqs[i % 2], sss[i % 2]
                S.dma("sp", xt[:], Xcur[b, i * 128:(i + 1) * 128, :], reads=[bXcur], writes=[xt])
                S.op("pool", lambda e, ss=ss: e.memset(ss[:], 0.0), writes=[ss])
                S.op("act", lambda e, xt=xt, sq=sq, ss=ss: e.activation(sq[:], xt[:], AF.Square, accum_out=ss[:]), reads=[xt, ss], writes=[sq, ss])
                NLV = int(os.environ.get("KDBG_NLV", 9))
                if NLV < 2:
                    continue
                rstd_from_ss(None, ss, 1, 1.0 / D, EPS)
                S.op("dve", lambda e, xt=xt, sq=sq, ss=ss: e.tensor_scalar(sq[:], xt[:], ss.t[:, 0:1], None, ALU.mult), reads=[xt, ss], writes=[sq])
                if NLV < 3:
                    continue
                for half in range(2):
                    pt = nps()
                    for c4 in range(4):
                        ch = half * 4 + c4
                        S.op("pe", lambda e, pt=pt, c4=c4, ch=ch, sq=sq: e.transpose(pt.t[:, c4 * 128:(c4 + 1) * 128], sq.t[:, ch * 128:(ch + 1) * 128], ident),
                             reads=[sq, cst], writes=[pt] if c4 == 0 else (), pwrites=[pt] if c4 > 0 else ())
                    if NLV < 4:
                        continue
                    for c4 in range(4):
                        ch = half * 4 + c4
                        o_ap = hT.t[:, ch, i * 128:(i + 1) * 128]
                        i_ap = pt.t[:, c4 * 128:(c4 + 1) * 128]
                        g_ap = Gt.t[:, ch, r:r + 1]
                        s_ap = modT.t[:, ch, r:r + 1]
                        EV = os.environ.get("KDBG_EV", "")
                        if (c4 % 2 == 0 and EV != "act") or EV == "dve":
                            S.op("dve", lambda e, o_ap=o_ap, i_ap=i_ap, g_ap=g_ap, s_ap=s_ap: e.tensor_scalar(o_ap, i_ap, g_ap, s_ap, ALU.mult, ALU.add),
                                 reads=[pt, Gt, modT], pwrites=[hT])
                        else:
                            S.op("act", lambda e, o_ap=o_ap, i_ap=i_ap, g_ap=g_ap, s_ap=s_ap: e.activation(o_ap, i_ap, AF.Identity, bias=s_ap, scale=g_ap),
                                 reads=[pt, Gt, modT], pwrites=[hT])
            if "hT" in dbg and b == 0 and l == 0:
                hTd = nc.dram_tensor("hTd", [128, 8, T], F32, kind="ExternalOutput").ap()
                for k in range(8):
                    S.dma("pool", hTd[:, k, :], hT.t[:, k, :], reads=[hT])
            if stop_after == "P1a":
                phase_end(st)
                break
            wbufs = [sbt(st, [128, 8, 512], "wb") for _ in range(2)]
            zos = [sbt(st, [128, 512], "zo") for _ in range(3)]
            groups = [(1792, 512, ZO_GA, AF.Silu), (2304, 416, ZO_ZB, None), (3232, 512, ZO_ZC, None), (3744, 256, ZO_ZC + 512, None)]
            for g in range(6):
                groups.append((4512 + g * 512, 512, ZO_ZG + g * 512, AF.Sigmoid))
            zi = 0
            for gi, (c0, n, zoff, fn) in enumerate(groups):
                wb = wbufs[gi % 2]
                S.dma("sp", wb.t[:, :, 0:n], W["w_in"][l, :, c0:c0 + n].rearrange("(k p) n -> p k n", p=128), reads=[bW], writes=[wb])
                for i in range(NT):
                    ps = nps()
                    for k in range(8):
                        mm(ps, ps.t[:, 0:n], hT.t[:, k, i * 128:(i + 1) * 128], wb.t[:, k, 0:n], k == 0, k == 7, [hT, wb])
                    zo = zos[zi % 3]
                    zi += 1
                    evac(zo.t[:, 0:n], ps.t[:, 0:n], reads=[ps], writes=[zo], func=fn)
                    S.dma("pool", ZTM[b, i * 128:(i + 1) * 128, zoff:zoff + n], zo.t[:, 0:n], reads=[zo], pwrites=[bZTM[b]])
            zfs = [sbt(st, [128, T], "zf") for _ in range(2)]
            fm = [(c * 128, 128, ("A", c), None) for c in range(14)]
            fm += [(2720 + h * 64, 64, ("G", 0, h), AF.Silu) for h in range(8)]
            fm += [(4000 + h * 64, 64, ("G", 1, h), AF.Silu) for h in range(8)]
            for fi, (c0, m, dst, fn) in enumerate(fm):
                wb = wbufs[fi % 2]
                zf = zfs[fi % 2]
                S.dma("sp", wb.t[:, :, 0:m], W["w_in"][l, :, c0:c0 + m].rearrange("(k p) n -> p k n", p=128), reads=[bW], writes=[wb])
                for ci, (t0, tn) in enumerate(TCH):
                    ps = nps()
                    for k in range(8):
                        mm(ps, ps.t[0:m, 0:tn], wb.t[:, k, 0:m], hT.t[:, k, t0:t0 + tn], k == 0, k == 7, [hT, wb])
                    evac(zf.t[0:m, t0:t0 + tn], ps.t[0:m, 0:tn], reads=[ps], writes=[zf] if ci == 0 else (), pwrites=[zf] if ci > 0 else (), func=fn)
                if dst[0] == "A":
                    S.dma("pool", ZTA[b, dst[1], :, :], zf.t[:, :], reads=[zf], pwrites=[bZTA[b]])
                else:
                    S.dma("pool", GT[b, dst[1], dst[2], :, :], zf.t[0:64, :], reads=[zf], pwrites=[bGT[b]])
            phase_end(st)
            if stop_after == "P1":
                break

            st = ExitStack()
            mup = sbt(st, [128, 14], "mup")
            mun = sbt(st, [128, 14], "mun")
            c0t = sbt(st, [128, 14], "c0t")
            w0t = sbt(st, [128, 8], "w0t")
            a0t = sbt(st, [128, 8], "a0t")
            kkp = sbt(st, [128, 4], "kkp")
            kap = sbt(st, [128, 4], "kap")
            rkp = sbt(st, [128, 4], "rkp")
            wup = sbt(st, [128, 512], "wup")
            aup = sbt(st, [128, 512], "aup")
            for tl_, nm in [(mup, "mupT"), (mun, "munT"), (w0t, "w0T"), (a0t, "a0T"), (kkp, "kkT"), (kap, "kaT"), (rkp, "rkT"), (wup, "w_up"), (aup, "a_up")]:
                S.dma("sp", tl_[:], W[nm][l, :, :], reads=[bW], writes=[tl_])
            S.op("dve", lambda e: e.tensor_tensor(c0t[:], mup[:], mun[:], ALU.add), reads=[mup, mun], writes=[c0t])
            S.op("dve", lambda e: e.tensor_scalar(c0t[:], c0t[:], -1.0, 1.0, ALU.mult, ALU.add), reads=[c0t], writes=[c0t])
            NBT = 15
            bts = [sbt(st, [128, T], "bt") for _ in range(NBT)]
            zraw = [bts[0], bts[1]]
            zri = [0]

            def shift_load(c, dst):
                zr = zraw[zri[0] % 2]
                zri[0] += 1
                S.dma("sp", zr[:], ZTA[b, c, :, :], reads=[bZTA[b]], writes=[zr])
                S.op("act", lambda e: e.activation(dst[:], zr[:], AF.Identity, scale=c0t.t[:, c:c + 1]), reads=[zr, c0t], writes=[dst])
                for (o0, o1, i0, i1, mt) in [(1, 256, 0, 255, mup), (257, T, 256, T - 1, mup), (0, 255, 1, 256, mun), (256, T - 1, 257, T, mun)]:
                    S.op("dve", lambda e, o0=o0, o1=o1, i0=i0, i1=i1, mt=mt: e.scalar_tensor_tensor(dst.t[:, o0:o1], zr.t[:, i0:i1], mt.t[:, c:c + 1], dst.t[:, o0:o1], ALU.mult, ALU.add),
                         reads=[zr, mt, dst], writes=[dst])

            twd, ads = bts[2], bts[3]
            shift_load(12, twd)
            S.op("act", lambda e: e.activation(twd[:], twd[:], AF.Tanh), reads=[twd], writes=[twd])
            shift_load(13, ads)
            rs_, ks_, vs_, kk, tq, a_d, dec, ka_d, kd0, kd1, uu = bts[4:15]
            bsS = sbt(st, [128, NT, 8], "bsS")
            vtm = [sbt(st, [128, 4, 128], "vtm") for _ in range(2)]
            for q in range(4):
                shift_load(q, rs_)
                shift_load(4 + q, ks_)
                shift_load(8 + q, vs_)
                S.op("dve", lambda e, q=q: e.tensor_scalar(kk[:], ks_[:], kkp.t[:, q:q + 1], None, ALU.mult), reads=[ks_, kkp], writes=[kk])
                S.op("pool", lambda e: e.tensor_tensor(tq[:], kk[:], kk[:], ALU.mult), reads=[kk], writes=[tq])
                for ci, (t0, tn) in enumerate(TCH):
                    ps = nps()
                    mm(ps, ps.t[:, 0:tn], bones, tq.t[:, t0:t0 + tn], True, True, [tq, cst])
                    S.op("dve", lambda e, ps=ps, t0=t0, tn=tn: e.tensor_scalar_max(a_d.t[:, t0:t0 + tn], ps.t[:, 0:tn], 1e-24), reads=[ps],
                         writes=[a_d] if ci == 0 else (), pwrites=[a_d] if ci > 0 else ())
                S.op("act", lambda e: e.activation(a_d[:], a_d[:], AF.Sqrt), reads=[a_d], writes=[a_d])
                S.op("dve", lambda e: e.reciprocal(a_d[:], a_d[:]), reads=[a_d], writes=[a_d])
                S.op("dve", lambda e: e.tensor_tensor(kk[:], kk[:], a_d[:], ALU.mult), reads=[kk, a_d], writes=[kk])
                S.dma("pool", SC[0, :, 3, b * 4 + q, :], kk[:], reads=[kk], pwrites=[bSC])
                S.dma("pool", SC[1, :, 3, b * 4 + q, :], kk[:], reads=[kk], pwrites=[bSC])
                S.dma("pool", SC[0, :, 4, b * 4 + q, :], rs_[:], reads=[rs_], pwrites=[bSC])
                S.dma("pool", SC[1, :, 4, b * 4 + q, :], rs_[:], reads=[rs_], pwrites=[bSC])
                for d in range(2):
                    kd = kd0 if d == 0 else kd1
                    for ci, (t0, tn) in enumerate(TCH):
                        ps = nps()
                        mm(ps, ps.t[:, 0:tn], wup.t[d * 64:(d + 1) * 64, q * 128:(q + 1) * 128], twd.t[d * 64:(d + 1) * 64, t0:t0 + tn], True, True, [wup, twd])
                        evac(dec.t[:, t0:t0 + tn], ps.t[:, 0:tn], reads=[ps, w0t], writes=[dec] if ci == 0 else (), pwrites=[dec] if ci > 0 else (),
                             func=AF.Sigmoid, bias=w0t.t[:, d * 4 + q:d * 4 + q + 1])
                    S.op("act", lambda e: e.activation(dec[:], dec[:], AF.Exp, scale=-math.exp(-0.5)), reads=[dec], writes=[dec])
                    S.dma("pool", SC[d, :, 0, b * 4 + q, :], dec[:], reads=[dec], pwrites=[bSC])
                    for ci, (t0, tn) in enumerate(TCH):
                        ps = nps()
                        mm(ps, ps.t[:, 0:tn], aup.t[d * 64:(d + 1) * 64, q * 128:(q + 1) * 128], ads.t[d * 64:(d + 1) * 64, t0:t0 + tn], True, True, [aup, ads])
                        evac(a_d.t[:, t0:t0 + tn], ps.t[:, 0:tn], reads=[ps, a0t], writes=[a_d] if ci == 0 else (), pwrites=[a_d] if ci > 0 else (),
                             func=AF.Sigmoid, bias=a0t.t[:, d * 4 + q:d * 4 + q + 1])
                    S.op("pool", lambda e: e.tensor_tensor(ka_d[:], kk[:], a_d[:], ALU.mult), reads=[kk, a_d], writes=[ka_d])
                    S.dma("pool", SC[d, :, 1, b * 4 + q, :], ka_d[:], reads=[ka_d], pwrites=[bSC])
                    S.op("dve", lambda e, kd=kd, q=q: e.tensor_scalar(kd[:], a_d[:], kap.t[:, q:q + 1], kap.t[:, q:q + 1], ALU.mult, ALU.subtract), reads=[a_d, kap], writes=[kd])
                    S.op("dve", lambda e, kd=kd: e.scalar_tensor_tensor(kd[:], kd[:], 1.0, ks_[:], ALU.add, ALU.mult), reads=[kd, ks_], writes=[kd])
                    S.dma("pool", SC[d, :, 2, b * 4 + q, :], kd[:], reads=[kd], pwrites=[bSC])
                S.op("pool", lambda e: e.tensor_tensor(uu[:], kd0[:], kd1[:], ALU.add), reads=[kd0, kd1], writes=[uu])
                S.op("dve", lambda e, q=q: e.scalar_tensor_tensor(uu[:], uu[:], rkp.t[:, q:q + 1], rs_[:], ALU.mult, ALU.mult), reads=[uu, rkp, rs_], writes=[uu])
                psb = nps()
                for i in range(NT):
                    mm(psb, psb.t[:, i * 2:i * 2 + 2], uu.t[:, i * 128:(i + 1) * 128], ind2, True, True, [uu, cst]) if i == 0 else \
                        mmp(psb, psb.t[:, i * 2:i * 2 + 2], uu.t[:, i * 128:(i + 1) * 128], ind2, True, True, [uu, cst])
                S.op("dve", lambda e, q=q, psb=psb: e.tensor_copy(bsS.t[:, :, 2 * q:2 * q + 2], psb.t[:, 0:2 * NT].rearrange("p (i c) -> p i c", c=2)),
                     reads=[psb], writes=[bsS] if q == 0 else (), pwrites=[bsS] if q > 0 else ())
                for g0 in range(0, NT, 4):
                    ng = min(4, NT - g0)
                    pt = nps()
                    for j in range(ng):
                        i = g0 + j
                        S.op("pe", lambda e, pt=pt, j=j, i=i: e.transpose(pt.t[:, j * 128:(j + 1) * 128], vs_.t[:, i * 128:(i + 1) * 128], ident),
                             reads=[vs_, cst], writes=[pt] if j == 0 else (), pwrites=[pt] if j > 0 else ())
                    vt_ = vtm[(g0 // 4) % 2]
                    evac(vt_.t[:, 0:ng, :], pt.t[:, 0:ng * 128].rearrange("p (j f) -> p j f", f=128), reads=[pt], writes=[vt_])
                    for j in range(ng):
                        i = g0 + j
                        dst = VT[i * 128:(i + 1) * 128, :].rearrange("p (c bb qq v) -> p c bb qq v", c=2, bb=2, qq=4)[:, :, b, q, :]
                        S.dma("pool", dst, vt_.t[:, j, :].rearrange("p (c v) -> p c v", c=2), reads=[vt_], pwrites=[bVT])
            S.dma("pool", BS[b, :, :].rearrange("(i p) h -> p i h", p=128), bsS[:], reads=[bsS], writes=[bBS[b]])
            phase_end(st)
            if stop_after == "P2":
                break

            st = ExitStack()
            qln = bc_load(st, W["q_ln"][l, :, :], 256, name="qln")
            kvln = bc_load(st, W["kv_ln"][l, :, :], 128, name="kvln")
            bqkg = bc_load(st, W["bqk_g"][l, :, :], 192, name="bqkg")
            cqn = bc_load(st, W["c_qn"][l, :, :], 64, name="cqn")
            ckn = bc_load(st, W["c_kn"][l, :, :], 64, name="ckn")
            wuq = sbt(st, [128, 2, 768], "wuq")
            wukv = sbt(st, [128, 1024], "wukv")
            S.dma("sp", wuq[:], W["w_uq"][l, :, :].rearrange("(k p) n -> p k n", p=128), reads=[bW], writes=[wuq])
            S.dma("sp", wukv[:], W["w_ukv"][l, :, :], reads=[bW], writes=[wukv])
            NBUF = 2
            zbs = [sbt(st, [128, 416], "zb") for _ in range(NBUF)]
            zcs = [sbt(st, [128, 768], "zc") for _ in range(NBUF)]
            rbs = [sbt(st, [128, 64], "rb") for _ in range(NBUF)]
            rcs = [sbt(st, [128, 128], "rc") for _ in range(NBUF)]
            ss2 = [sbt(st, [128, 2], "ss2") for _ in range(NBUF)]
            junk = sbt(st, [128, 1536], "junk")
            cn = [sbt(st, [128, 384], "cn") for _ in range(NBUF)]
            cT3 = [sbt(st, [128, 3, 128], "cT3") for _ in range(NBUF)]
            qk = [sbt(st, [128, 16, 96], "qk") for _ in range(NBUF)]
            kv = [sbt(st, [128, 8, 128], "kv") for _ in range(NBUF)]
            ssq = [sbt(st, [128, 16], "ssq") for _ in range(NBUF)]
            rt1 = sbt(st, [128, 640], "rt1")
            rt2 = sbt(st, [128, 320], "rt2")
            qkT = [sbt(st, [96, 16, 128], "qkT") for _ in range(NBUF)]
            qkTc = [sbt(st, [64, 10, 128], "qkTc") for _ in range(NBUF)]
            for i in range(NT):
                u = i % NBUF
                zb, zc, rb, rc = zbs[u], zcs[u], rbs[u], rcs[u]
                S.dma("sp", zb[:], ZTM[b, i * 128:(i + 1) * 128, ZO_ZB:ZO_ZB + 416], reads=[bZTM[b]], writes=[zb])
                S.dma("sp", zc[:], ZTM[b, i * 128:(i + 1) * 128, ZO_ZC:ZO_ZC + 768], reads=[bZTM[b]], writes=[zc])
                S.dma("sp", rb[:], ropeB_d[i * 128:(i + 1) * 128, :], reads=[bW], writes=[rb])
                S.dma("sp", rc[:], ropeC_d[i * 128:(i + 1) * 128, :], reads=[bW], writes=[rc])
                s2 = ss2[u]
                S.op("pool", lambda e, s2=s2: e.memset(s2[:], 0.0), writes=[s2])
                S.op("act", lambda e, zb=zb, s2=s2: e.activation(junk.t[:, 0:256], zb.t[:, 0:256], AF.Square, accum_out=s2.t[:, 0:1]), reads=[zb, s2], writes=[junk, s2])
                S.op("act", lambda e, zb=zb, s2=s2: e.activation(junk.t[:, 256:384], zb.t[:, 256:384], AF.Square, accum_out=s2.t[:, 1:2]), reads=[zb, s2], writes=[junk, s2])
                S.op("dve", lambda e, s2=s2: e.tensor_scalar(s2.t[:, 0:1], s2.t[:, 0:1], 1.0 / 256, EPS, ALU.mult, ALU.add), reads=[s2], writes=[s2])
                S.op("dve", lambda e, s2=s2: e.tensor_scalar(s2.t[:, 1:2], s2.t[:, 1:2], 1.0 / 128, EPS, ALU.mult, ALU.add), reads=[s2], writes=[s2])
                S.op("act", lambda e, s2=s2: e.activation(s2[:], s2[:], AF.Sqrt), reads=[s2], writes=[s2])
                S.op("dve", lambda e, s2=s2: e.reciprocal(s2[:], s2[:]), reads=[s2], writes=[s2])
                c_ = cn[u]
                S.op("dve", lambda e, c_=c_, zb=zb, s2=s2: e.scalar_tensor_tensor(c_.t[:, 0:256], zb.t[:, 0:256], s2.t[:, 0:1], qln[:], ALU.mult, ALU.mult), reads=[zb, s2, qln], writes=[c_])
                S.op("dve", lambda e, c_=c_, zb=zb, s2=s2: e.scalar_tensor_tensor(c_.t[:, 256:384], zb.t[:, 256:384], s2.t[:, 1:2], kvln[:], ALU.mult, ALU.mult), reads=[zb, s2, kvln], pwrites=[c_])
                pt = nps()
                for j in range(3):
                    S.op("pe", lambda e, pt=pt, j=j, c_=c_: e.transpose(pt.t[:, j * 128:(j + 1) * 128], c_.t[:, j * 128:(j + 1) * 128], ident),
                         reads=[c_, cst], writes=[pt] if j == 0 else (), pwrites=[pt] if j > 0 else ())
                c3 = cT3[u]
                evac(c3[:], pt.t[:, 0:384].rearrange("p (j f) -> p j f", f=128), reads=[pt], writes=[c3])
                qk_ = qk[u]
                kv_ = kv[u]
                for nh in range(2):
                    ps = nps()
                    for k in range(2):
                        mm(ps, ps.t[:, 0:384], c3.t[:, k, :], wuq.t[:, k, nh * 384:(nh + 1) * 384], k == 0, k == 1, [c3, wuq])
                    evac(qk_.t[:, nh * 4:(nh + 1) * 4, :], ps.t[:, 0:384].rearrange("p (h r) -> p h r", r=96), reads=[ps], writes=[qk_] if nh == 0 else (), pwrites=[qk_] if nh > 0 else ())
                for nh in range(2):
                    ps = nps()
                    mm(ps, ps.t[:, :], c3.t[:, 2, :], wukv.t[:, nh * 512:(nh + 1) * 512], True, True, [c3, wukv])
                    evac(kv_.t[:, nh * 4:(nh + 1) * 4, :], ps.t[:, :].rearrange("p (h r) -> p h r", r=128), reads=[ps], writes=[kv_] if nh == 0 else (), pwrites=[kv_] if nh > 0 else ())
                S.op("dve", lambda e, qk_=qk_, kv_=kv_: e.tensor_copy(qk_.t[:, 8:16, 0:64], kv_.t[:, :, 0:64]), reads=[kv_], pwrites=[qk_])
                S.op("dve", lambda e, qk_=qk_, zb=zb: e.tensor_copy(qk_.t[:, 8:16, 64:96], zb.t[:, 384:416].unsqueeze(1).to_broadcast([128, 8, 32])), reads=[zb], pwrites=[qk_])
                S.dma("pool", VB[b, i * 128:(i + 1) * 128, :].rearrange("p (h v) -> p h v", v=64), kv_.t[:, :, 64:128], reads=[kv_], pwrites=[bVB[b]])
                sq_ = ssq[u]
                S.op("pool", lambda e, qk_=qk_: e.tensor_tensor(junk.t[:, 0:1536], qk_.t[:, :, :].rearrange("p h r -> p (h r)"), qk_.t[:, :, :].rearrange("p h r -> p (h r)"), ALU.mult), reads=[qk_], writes=[junk])
                S.op("dve", lambda e, sq_=sq_: e.tensor_reduce(sq_[:], junk.t[:, 0:1536].rearrange("p (h r) -> p h r", r=96), AX.X, ALU.add), reads=[junk], writes=[sq_])
                rstd_from_ss(None, sq_, 16, 1.0 / 96, EPS)
                S.op("dve", lambda e, qk_=qk_, sq_=sq_: e.tensor_tensor(qk_[:], qk_[:], sq_.t[:, 0:16].unsqueeze(2).to_broadcast([128, 16, 96]), ALU.mult), reads=[qk_, sq_], writes=[qk_])
                S.op("dve", lambda e, qk_=qk_: e.tensor_tensor(qk_.t[:, :, :].rearrange("p (a h) r -> p a h r", a=2), qk_.t[:, :, :].rearrange("p (a h) r -> p a h r", a=2),
                                                              bqkg.t[:, :].rearrange("p (a r) -> p a r", a=2).unsqueeze(2).to_broadcast([128, 2, 8, 96]), ALU.mult), reads=[qk_, bqkg], writes=[qk_])
                x3_buf[0] = qk_
                rope(st, qk_.t[:, :, 64:96], 16, 32, rb, rt1, rt2)
                qT_ = qkT[u]
                for g0 in range(0, 16, 4):
                    pt = nps()
                    for j in range(4):
                        S.op("pe", lambda e, pt=pt, j=j, g0=g0, qk_=qk_: e.transpose(pt.t[0:96, j * 128:(j + 1) * 128], qk_.t[:, g0 + j, :], ident),
                             reads=[qk_, cst], writes=[pt] if j == 0 else (), pwrites=[pt] if j > 0 else ())
                    evac(qT_.t[0:96, g0:g0 + 4, :], pt.t[0:96, :].rearrange("p (j f) -> p j f", f=128), reads=[pt], writes=[qT_] if g0 == 0 else (), pwrites=[qT_] if g0 > 0 else ())
                S.dma("pool", QKT[b, :, :, i * 128:(i + 1) * 128].rearrange("h p t -> p h t"), qT_[:], reads=[qT_], pwrites=[bQKT[b]])
                S.dma("pool", VC[b, i * 128:(i + 1) * 128, :], zc.t[:, 640:768], reads=[zc], pwrites=[bVC[b]])
                S.op("pool", lambda e, zc=zc: e.tensor_tensor(junk.t[:, 0:640], zc.t[:, 0:640], zc.t[:, 0:640], ALU.mult), reads=[zc], writes=[junk])
                S.op("dve", lambda e, sq_=sq_: e.tensor_reduce(sq_.t[:, 0:10], junk.t[:, 0:640].rearrange("p (h r) -> p h r", r=64), AX.X, ALU.add), reads=[junk], writes=[sq_])
                rstd_from_ss(None, sq_, 10, 1.0 / 64, EPS)
                z3 = zc.t[:, 0:640].rearrange("p (h r) -> p h r", r=64)
                S.op("dve", lambda e, z3=z3, sq_=sq_, zc=zc: e.tensor_tensor(z3, z3, sq_.t[:, 0:10].unsqueeze(2).to_broadcast([128, 10, 64]), ALU.mult), reads=[zc, sq_], writes=[zc])
                S.op("dve", lambda e, z3=z3, zc=zc: e.tensor_tensor(z3[:, 0:8, :], z3[:, 0:8, :], cqn.t[:, :].unsqueeze(1).to_broadcast([128, 8, 64]), ALU.mult), reads=[zc, cqn], writes=[zc])
                S.op("dve", lambda e, z3=z3, zc=zc: e.tensor_tensor(z3[:, 8:10, :], z3[:, 8:10, :], ckn.t[:, :].unsqueeze(1).to_broadcast([128, 2, 64]), ALU.mult), reads=[zc, ckn], writes=[zc])
                x3_buf[0] = zc
                rope(st, z3, 10, 64, rc, rt1, rt2)
                qTc_ = qkTc[u]
                for g0 in range(0, 10, 4):
                    ng = min(4, 10 - g0)
                    pt = nps()
                    for j in range(ng):
                        S.op("pe", lambda e, pt=pt, j=j, g0=g0, z3=z3: e.transpose(pt.t[0:64, j * 128:(j + 1) * 128], z3[:, g0 + j, :], ident),
                             reads=[zc, cst], writes=[pt] if j == 0 else (), pwrites=[pt] if j > 0 else ())
                    evac(qTc_.t[0:64, g0:g0 + ng, :], pt.t[0:64, 0:ng * 128].rearrange("p (j f) -> p j f", f=128), reads=[pt], writes=[qTc_] if g0 == 0 else (), pwrites=[qTc_] if g0 > 0 else ())
                S.dma("pool", QKTC[b, :, :, i * 128:(i + 1) * 128].rearrange("h p t -> p h t"), qTc_[:], reads=[qTc_], pwrites=[bQKTC[b]])
            phase_end(st)
        if stop_after in ("P1a", "P1", "P2", "P3"):
            break

        st = ExitStack()
        Sb = [[sbt(st, [128, 8, 64], "S%d_%d" % (d, k)) for k in range(2)] for d in range(2)]
        for d in range(2):
            S.op("dve", lambda e, d=d: e.memset(Sb[d][0][:], 0.0), writes=[Sb[d][0]])
        Sw = [sbt(st, [128, 8, 64], "Sw") for d in range(2)]
        tA = [[sbt(st, [128, 8, 64], "tA") for _ in range(2)] for d in range(2)]
        tB = [sbt(st, [128, 8, 64], "tB") for d in range(2)]
        tC = [[sbt(st, [128, 8, 64], "tC") for _ in range(2)] for d in range(2)]
        t4 = [[sbt(st, [128, 8, 64], "t4") for _ in range(2)] for d in range(2)]
        SCb = [[sbt(st, [128, 5, 8, 64], "SCb") for _ in range(2)] for d in range(2)]
        Vb = [[sbt(st, [64, 1024], "Vb") for _ in range(2)] for d in range(2)]
        ysb = [sbt(st, [128, 512], "ysb") for d in range(2)]
        psV = [[PS[0], PS[1]], [PS[2], PS[3]]]
        psSA = [PS[4], PS[5]]
        psY = [PS[6], PS[7]]
        NBLK = T // 64
        n_scan_blocks = int(os.environ.get("KDBG_SCAN_BLOCKS", NBLK))

        def tok0(d, B):
            if d == 0:
                return 64 * B
            if B < 4:
                return LC - 64 * (B + 1)
            return T - 64 * (B - 3)

        def v3(buf):
            return buf.t[:, :, :]

        def f2(buf):
            return buf.t[:, :, :].rearrange("p a v -> p (a v)")

        gstep = 0
        for B in range(n_scan_blocks):
            u = B % 2
            for d in range(2):
                t0 = tok0(d, B)
                for a in range(5):
                    S.dma("sp", SCb[d][u].t[:, a, :, :], SC[d, :, a, :, t0:t0 + 64], reads=[bSC], writes=[SCb[d][u]] if a == 0 else (), pwrites=[SCb[d][u]] if a else ())
                S.dma("sp", Vb[d][u][:], VT[t0:t0 + 64, :], reads=[bVT], writes=[Vb[d][u]])
            sc = [SCb[d][u] for d in range(2)]
            pend = None

            def emit_t4(pd):
                s_, tls_, Sn_, par_ = pd
                for d in range(2):
                    for j in range(8):
                        S.op("act", lambda e, d=d, j=j, o=t4[d][par_], sn=Sn_[d], r_=sc[d].t[:, 4, j, tls_[d]:tls_[d] + 1]: e.activation(o.t[:, j, :], sn.t[:, j, :], AF.Identity, scale=r_),
                             reads=[Sn_[d], sc[d]], writes=[t4[d][par_]] if j == 0 else (), pwrites=[t4[d][par_]] if j else ())

            def emit_y(pd):
                s_, tls_, Sn_, par_ = pd
                for d in range(2):
                    S.op("pe", lambda e, d=d, i_=t4[d][par_], tl=tls_[d], s_=s_: e.matmul(psY[d].t[:, :], Z2[:, 127 - tl:255 - tl], f2(i_), start=(s_ == 0), stop=(s_ == 63)),
                         reads=[t4[d][par_], cst], writes=[psY[d]] if s_ == 0 else (), pwrites=[psY[d]] if s_ > 0 else ())

            for s in range(64):
                tls = [s, 63 - s]
                par = gstep % 2
                Sc = [Sb[d][par] for d in range(2)]
                Sn = [Sb[d][1 - par] for d in range(2)]
                gstep += 1

                def bcs(d, a):
                    return sc[d].t[:, a, :, tls[d]].unsqueeze(2).to_broadcast([128, 8, 64])
                pv = [psV[d][s % 2] for d in range(2)]
                ta = [tA[d][s % 2] for d in range(2)]
                tc = [tC[d][s % 2] for d in range(2)]
                for d in range(2):
                    for c2 in range(2):
                        S.op("pe", lambda e, d=d, c2=c2, p=pv[d], tl=tls[d], vb_=Vb[d][u]: e.matmul(p.t[c2 * 64:(c2 + 1) * 64, :], ident[0:64, tl:tl + 1].to_broadcast([64, 64]),
                                                                                              vb_.t[0:64, c2 * 512:(c2 + 1) * 512], start=True, stop=True),
                             reads=[Vb[d][u], cst], writes=[pv[d]] if c2 == 0 else (), pwrites=[pv[d]] if c2 == 1 else ())
                for d in range(2):
                    S.op("dve", lambda e, d=d, o=ta[d], sc_=Sc[d], kb=bcs(d, 3): e.tensor_tensor(v3(o), v3(sc_), kb, ALU.mult), reads=[Sc[d], sc[d]], writes=[ta[d]])
                for d in range(2):
                    S.op("pe", lambda e, d=d, i_=ta[d]: e.matmul(psSA[d].t[:, :], bones, f2(i_), start=True, stop=True), reads=[ta[d], cst], writes=[psSA[d]])
                for d in range(2):
                    S.op("dve", lambda e, d=d, o=tc[d], kb=bcs(d, 2), p=pv[d]: e.tensor_tensor(v3(o), p.t[:, :].rearrange("p (a v) -> p a v", v=64), kb, ALU.mult), reads=[pv[d], sc[d]], writes=[tc[d]])
                for d in range(2):
                    S.op("pool", lambda e, d=d, sc_=Sc[d], wb_=bcs(d, 0): e.tensor_tensor(v3(Sw[d]), v3(sc_), wb_, ALU.mult), reads=[Sc[d], sc[d]], writes=[Sw[d]])
                    S.op("pool", lambda e, d=d, t_=tc[d]: e.tensor_tensor(v3(Sw[d]), v3(Sw[d]), v3(t_), ALU.add), reads=[Sw[d], tc[d]], writes=[Sw[d]])
                if pend is not None:
                    emit_t4(pend)
                    emit_y(pend)
                S.op("dve", lambda e, kb=bcs(0, 1): e.tensor_tensor(v3(tB[0]), psSA[0].t[:, :].rearrange("p (a v) -> p a v", v=64), kb, ALU.mult), reads=[psSA[0], sc[0]], writes=[tB[0]])
                S.op("dve", lambda e, sn=Sn[0]: e.tensor_tensor(v3(sn), v3(Sw[0]), v3(tB[0]), ALU.subtract), reads=[Sw[0], tB[0]], writes=[Sn[0]])
                S.op("dve", lambda e, kb=bcs(1, 1): e.tensor_tensor(v3(tB[1]), psSA[1].t[:, :].rearrange("p (a v) -> p a v", v=64), kb, ALU.mult), reads=[psSA[1], sc[1]], writes=[tB[1]])
                S.op("dve", lambda e, sn=Sn[1]: e.tensor_tensor(v3(sn), v3(Sw[1]), v3(tB[1]), ALU.subtract), reads=[Sw[1], tB[1]], writes=[Sn[1]])
                pend = (s, tls, Sn, s % 2)
            emit_t4(pend)
            emit_y(pend)
            for d in range(2):
                t0 = tok0(d, B)
                evac(ysb[d][:], psY[d].t[:, :], reads=[psY[d]], writes=[ysb[d]])
                for c2 in range(2):
                    S.dma("pool", YD[d, t0:t0 + 64, c2 * 512:(c2 + 1) * 512], ysb[d].t[c2 * 64:(c2 + 1) * 64, :], reads=[ysb[d]], pwrites=[bYD])
        phase_end(st)
        if stop_after == "P4":
            break

        st = ExitStack()
        KTs = [sbt(st, [96, T], "KT") for _ in range(2)]
        QTs = [sbt(st, [96, T], "QT") for _ in range(2)]
        GTs = [sbt(st, [64, T], "GTh") for _ in range(2)]
        Vhs = [sbt(st, [128, NT, 64], "Vh") for _ in range(2)]
        Pbs = [sbt(st, [128, 512], "Pb") for _ in range(3)]
        rdn = [sbt(st, [64, 512], "rdn") for _ in range(2)]
        uob = [sbt(st, [64, T], "uob") for _ in range(2)]
        pi = 0
        scale_b = 96 ** -0.5
        for b in range(NB):
            for h in range(8):
                u = (b * 8 + h) % 2
                KT, QT, GTh, Vh, uo = KTs[u], QTs[u], GTs[u], Vhs[u], uob[u]
                S.dma("sp", KT[:], QKT[b, 8 + h, :, :], reads=[bQKT[b]], writes=[KT])
                S.dma("sp", QT[:], QKT[b, h, :, :], reads=[bQKT[b]], writes=[QT])
                S.dma("sp", GTh[:], GT[b, 0, h, :, :], reads=[bGT[b]], writes=[GTh])
                S.dma("sp", Vh[:], VB[b, :, h * 64:(h + 1) * 64].rearrange("(i p) v -> p i v", p=128), reads=[bVB[b]], writes=[Vh])
                qchunks = [(LC + j * 512, 512, list(range(NT))) for j in range(4)]
                if not last:
                    qchunks.append((0, 256, [0, 1]))
                for ci, (q0, qn, kts) in enumerate(qchunks):
                    psO, psD = PS[(ci % 2) * 2], PS[(ci % 2) * 2 + 1]
                    for ki, kt in enumerate(kts):
                        psS = PS[4 + pi % 4]
                        mm(psS, psS.t[:, 0:qn], KT.t[0:96, kt * 128:(kt + 1) * 128], QT.t[0:96, q0:q0 + qn], True, True, [KT, QT])
                        Pb = Pbs[pi % 3]
                        pi += 1
                        S.op("act", lambda e, Pb=Pb, psS=psS, qn=qn: e.activation(Pb.t[:, 0:qn], psS.t[:, 0:qn], AF.Exp, scale=scale_b), reads=[psS], writes=[Pb])
                        mm(psO, psO.t[0:64, 0:qn], Vh.t[:, kt, :], Pb.t[:, 0:qn], ki == 0, ki == len(kts) - 1, [Vh, Pb])
                        mm(psD, psD.t[0:64, 0:qn], ones64, Pb.t[:, 0:qn], ki == 0, ki == len(kts) - 1, [cst, Pb])
                    rd = rdn[ci % 2]
                    S.op("dve", lambda e, rd=rd, psD=psD, qn=qn: e.reciprocal(rd.t[:, 0:qn], psD.t[0:64, 0:qn]), reads=[psD], writes=[rd])
                    S.op("dve", lambda e, rd=rd, psO=psO, qn=qn: e.tensor_tensor(rd.t[:, 0:qn], psO.t[0:64, 0:qn], rd.t[:, 0:qn], ALU.mult), reads=[psO, rd], writes=[rd])
                    S.op("pool", lambda e, rd=rd, uo=uo, GTh=GTh, q0=q0, qn=qn: e.tensor_tensor(uo.t[:, q0:q0 + qn], rd.t[:, 0:qn], GTh.t[:, q0:q0 + qn], ALU.mult), reads=[rd, GTh],
                         writes=[uo] if ci == 0 else (), pwrites=[uo] if ci > 0 else ())
                if last:
                    S.dma("pool", UT[b, 0, h, :, LC:T], uo.t[:, LC:T], reads=[uo], pwrites=[bUT[b]])
                else:
                    S.dma("pool", UT[b, 0, h, :, :], uo[:], reads=[uo], pwrites=[bUT[b]])
        phase_end(st)
        if stop_after == "P5":
            break

        st = ExitStack()
        esk = bc_load(st, W["c_sink"][l, :, :], 8, parts=64, name="esk")
        S.op("act", lambda e: e.activation(esk[:], esk[:], AF.Exp), reads=[esk], writes=[esk])
        KTg = [sbt(st, [64, T], "KTg") for _ in range(2)]
        Q4 = [[sbt(st, [64, T], "Q4") for _ in range(4)] for _ in range(2)]
        G4 = [[sbt(st, [64, T], "G4") for _ in range(4)] for _ in range(2)]
        Vg = [sbt(st, [128, NT, 64], "Vg") for _ in range(2)]
        Pcs = [sbt(st, [128, 512], "Pc") for _ in range(3)]
        rdc = [sbt(st, [64, 512], "rdc") for _ in range(2)]
        uoc = [sbt(st, [64, 4, 128], "uoc") for _ in range(2)]
        scale_c = 64 ** -0.5
        pi = 0
        bi = 0
        for b in range(NB):
            for g in range(2):
                u = (b * 2 + g) % 2
                S.dma("sp", KTg[u][:], QKTC[b, 8 + g, :, :], reads=[bQKTC[b]], writes=[KTg[u]])
                S.dma("sp", Vg[u][:], VC[b, :, g * 64:(g + 1) * 64].rearrange("(i p) v -> p i v", p=128), reads=[bVC[b]], writes=[Vg[u]])
                for hh in range(4):
                    S.dma("sp", Q4[u][hh][:], QKTC[b, 4 * g + hh, :, :], reads=[bQKTC[b]], writes=[Q4[u][hh]])
                    S.dma("sp", G4[u][hh][:], GT[b, 1, 4 * g + hh, :, :], reads=[bGT[b]], writes=[G4[u][hh]])
                blocks = list(range(2, NT))
                if not last:
                    blocks = [0, 1] + blocks
                for n in blocks:
                    if n < 2:
                        kts = [(0, None), (1, None)]
                    else:
                        kts = [(0, None), (1, None)]
                        if n - 1 >= 2:
                            kts.append((n - 1, mask_lo))
                        kts.append((n, None))
                        if n + 1 < NT:
                            kts.append((n + 1, mask_hi))
                    psO, psD = PS[(bi % 2) * 2], PS[(bi % 2) * 2 + 1]
                    for ki, (kt, msk) in enumerate(kts):
                        psS = PS[4 + pi % 4]
                        for hh in range(4):
                            S.op("pe", lambda e, psS=psS, hh=hh, kt=kt, n=n, u=u: e.matmul(psS.t[:, hh * 128:(hh + 1) * 128], KTg[u].t[0:64, kt * 128:(kt + 1) * 128],
                                                                                       Q4[u][hh].t[0:64, n * 128:(n + 1) * 128], start=True, stop=True),
                                 reads=[KTg[u], Q4[u][hh]], writes=[psS] if hh == 0 else (), pwrites=[psS] if hh > 0 else ())
                        Pc = Pcs[pi % 3]
                        pi += 1
                        S.op("act", lambda e, Pc=Pc, psS=psS: e.activation(Pc[:], psS.t[:, :], AF.Exp, scale=scale_c), reads=[psS], writes=[Pc])
                        if msk is not None:
                            S.op("dve", lambda e, Pc=Pc, msk=msk: e.tensor_tensor(Pc.t[:, :].rearrange("p (h q) -> p h q", h=4), Pc.t[:, :].rearrange("p (h q) -> p h q", h=4),
                                                                                   msk.unsqueeze(1).to_broadcast([128, 4, 128]), ALU.mult), reads=[Pc, cst], writes=[Pc])
                        mm(psO, psO.t[0:64, :], Vg[u].t[:, kt, :], Pc.t[:, :], ki == 0, ki == len(kts) - 1, [Vg[u], Pc])
                        mm(psD, psD.t[0:64, :], ones64, Pc.t[:, :], ki == 0, ki == len(kts) - 1, [cst, Pc])
                    rd = rdc[bi % 2]
                    uo = uoc[bi % 2]
                    bi += 1
                    S.op("dve", lambda e, rd=rd, psD=psD, g=g: e.tensor_tensor(rd.t[:, :].rearrange("p (h q) -> p h q", h=4), psD.t[0:64, :].rearrange("p (h q) -> p h q", h=4),
                                                                                esk.t[:, 4 * g:4 * g + 4].unsqueeze(2).to_broadcast([64, 4, 128]), ALU.add), reads=[psD, esk], writes=[rd])
                    S.op("dve", lambda e, rd=rd: e.reciprocal(rd[:], rd[:]), reads=[rd], writes=[rd])
                    S.op("dve", lambda e, rd=rd, psO=psO: e.tensor_tensor(rd[:], psO.t[0:64, :], rd[:], ALU.mult), reads=[psO, rd], writes=[rd])
                    for hh in range(4):
                        S.op("pool", lambda e, rd=rd, uo=uo, hh=hh, n=n, u=u: e.tensor_tensor(uo.t[:, hh, :], rd.t[:, hh * 128:(hh + 1) * 128], G4[u][hh].t[:, n * 128:(n + 1) * 128], ALU.mult),
                             reads=[rd, G4[u][hh]], writes=[uo] if hh == 0 else (), pwrites=[uo] if hh > 0 else ())
                    S.dma("pool", UT[b, 1, 4 * g:4 * g + 4, :, n * 128:(n + 1) * 128].rearrange("h p t -> p h t"), uo[:], reads=[uo], pwrites=[bUT[b]])
        phase_end(st)
        if stop_after == "P6":
            break

        st = ExitStack()
        gng = bc_load(st, W["gn_g"][l, :, :], 512, name="gng")
        gnb = bc_load(st, W["gn_b"][l, :, :], 512, name="gnb")
        wbo0 = sbt(st, [128, 4, D], "wbo0")
        wbo1 = sbt(st, [128, 4, D], "wbo1")
        wbo2 = sbt(st, [128, 4, D], "wbo2")
        wo = sbt(st, [128, 8, D], "wo")
        S.dma("sp", wbo0[:], W["wbo"][l, 0, :, :].rearrange("(k p) n -> p k n", p=128), reads=[bW], writes=[wbo0])
        S.dma("sp", wbo1[:], W["wbo"][l, 1, :, :].rearrange("(k p) n -> p k n", p=128), reads=[bW], writes=[wbo1])
        S.dma("sp", wbo2[:], W["wbo"][l, 2, :, :].rearrange("(k p) n -> p k n", p=128), reads=[bW], writes=[wbo2])
        S.dma("sp", wo[:], W["w_out"][l, :, :].rearrange("(k p) n -> p k n", p=128), reads=[bW], writes=[wo])
        gateb = [sbt(st, [128, D], "gateb") for _ in range(3)]
        for r in range(3):
            S.dma("sp", gateb[r][:], MODG[r:r + 1, :].partition_broadcast(128), reads=[bMODG], writes=[gateb[r]])
        y0s = [sbt(st, [128, 512], "y0") for _ in range(2)]
        y1s = [sbt(st, [128, 512], "y1") for _ in range(2)]
        vts = [sbt(st, [128, 512], "vt") for _ in range(2)]
        sgas = [sbt(st, [128, 512], "sga") for _ in range(2)]
        bss = [sbt(st, [128, 8], "bs") for _ in range(2)]
        sgs = [sbt(st, [128, 3 * D], "sg") for _ in range(2)]
        xts = [sbt(st, [128, D], "xt") for _ in range(2)]
        utb = [sbt(st, [128, 4, 128], "utb") for _ in range(2)]
        utc = [sbt(st, [128, 4, 128], "utc") for _ in range(2)]
        st8 = sbt(st, [128, 8], "st8")
        yc = sbt(st, [128, 512], "yc")
        ysq = sbt(st, [128, 512], "ysq")
        uAT = sbt(st, [128, 4, 128], "uAT")
        mt = sbt(st, [128, D], "mt")
        tmpm = sbt(st, [128, 512], "tmpm")
        mTt = sbt(st, [128, 8, 128], "mTt")
        xo = [sbt(st, [128, D], "xo") for _ in range(2)]
        it = 0
        for b in range(NB):
            for i in (range(2, NT) if last else range(NT)):
                u = it % 2
                it += 1
                r = b if i >= 2 else 2
                y0, y1, vt, sga, bs_, sg, xt = y0s[u], y1s[u], vts[u], sgas[u], bss[u], sgs[u], xts[u]
                rows = slice(i * 128, (i + 1) * 128)

                def perm_src(a, c):
                    return a.rearrange("p (c bb qq v) -> p c bb qq v", c=2, bb=2, qq=4)[:, c, b, :, :]

                def perm_dst(tile, c):
                    return tile.t[:, :].rearrange("p (qq c v) -> p c qq v", qq=4, c=2)[:, c, :, :]
                for c in range(2):
                    S.dma("sp", perm_dst(y0, c), perm_src(YD[0, rows, :], c), reads=[bYD], writes=[y0] if c == 0 else (), pwrites=[y0] if c else ())
                    S.dma("sp", perm_dst(y1, c), perm_src(YD[1, rows, :], c), reads=[bYD], writes=[y1] if c == 0 else (), pwrites=[y1] if c else ())
                    S.dma("sp", perm_dst(vt, c), perm_src(VT[rows, :], c), reads=[bVT], writes=[vt] if c == 0 else (), pwrites=[vt] if c else ())
                S.dma("sp", sga[:], ZTM[b, rows, ZO_GA:ZO_GA + 512], reads=[bZTM[b]], writes=[sga])
                S.dma("sp", bs_[:], BS[b, rows, :], reads=[bBS[b]], writes=[bs_])
                S.dma("sp", sg[:], ZTM[b, rows, ZO_ZG:ZO_ZG + 3 * D], reads=[bZTM[b]], writes=[sg])
                S.dma("sp", xt[:], Xcur[b, rows, :], reads=[bXcur], writes=[xt])
                S.dma("sp", utb[u][:], UT[b, 0, :, :, :].rearrange("h p t -> (h p) t").rearrange("(k q) t -> q k t", q=128)[:, :, rows], reads=[bUT[b]], writes=[utb[u]])
                S.dma("sp", utc[u][:], UT[b, 1, :, :, :].rearrange("h p t -> (h p) t").rearrange("(k q) t -> q k t", q=128)[:, :, rows], reads=[bUT[b]], writes=[utc[u]])
                S.op("pool", lambda e, y0=y0, y1=y1: e.tensor_tensor(y0[:], y0[:], y1[:], ALU.add), reads=[y0, y1], writes=[y0])
                y3 = y0.t[:, :].rearrange("p (h v) -> p h v", v=64)
                yc3 = yc.t[:, :].rearrange("p (h v) -> p h v", v=64)
                S.op("dve", lambda e, y3=y3: e.tensor_reduce(st8[:], y3, AX.X, ALU.add), reads=[y0], writes=[st8])
                S.op("dve", lambda e: e.tensor_scalar(st8[:], st8[:], 1.0 / 64, None, ALU.mult), reads=[st8], writes=[st8])
                S.op("dve", lambda e, y3=y3, yc3=yc3: e.tensor_tensor(yc3, y3, st8.t[:, :].unsqueeze(2).to_broadcast([128, 8, 64]), ALU.subtract), reads=[y0, st8], writes=[yc])
                S.op("pool", lambda e: e.tensor_tensor(ysq[:], yc[:], yc[:], ALU.mult), reads=[yc], writes=[ysq])
                S.op("dve", lambda e: e.tensor_reduce(st8[:], ysq.t[:, :].rearrange("p (h v) -> p h v", v=64), AX.X, ALU.add), reads=[ysq], writes=[st8])
                rstd_from_ss(None, st8, 8, 1.0 / 64, GN_EPS)
                S.op("dve", lambda e, yc3=yc3: e.tensor_tensor(yc3, yc3, st8.t[:, :].unsqueeze(2).to_broadcast([128, 8, 64]), ALU.mult), reads=[yc, st8], writes=[yc])
                S.op("dve", lambda e: e.tensor_tensor(yc[:], yc[:], gng[:], ALU.mult), reads=[yc, gng], writes=[yc])
                S.op("pool", lambda e: e.tensor_tensor(yc[:], yc[:], gnb[:], ALU.add), reads=[yc, gnb], writes=[yc])
                S.op("dve", lambda e, vt=vt, bs_=bs_: e.tensor_tensor(vt.t[:, :].rearrange("p (h v) -> p h v", v=64), vt.t[:, :].rearrange("p (h v) -> p h v", v=64),
                                                                      bs_.t[:, :].unsqueeze(2).to_broadcast([128, 8, 64]), ALU.mult), reads=[vt, bs_], writes=[vt])
                S.op("pool", lambda e, vt=vt: e.tensor_tensor(yc[:], yc[:], vt[:], ALU.add), reads=[yc, vt], writes=[yc])
                S.op("dve", lambda e, sga=sga: e.tensor_tensor(yc[:], yc[:], sga[:], ALU.mult), reads=[yc, sga], writes=[yc])
                if "uA" in dbg and l == 0 and b == 0 and i == 2:
                    uAd = nc.dram_tensor("uAd", [128, 512], F32, kind="ExternalOutput").ap()
                    S.dma("pool", uAd[:, :], yc[:], reads=[yc])
                pt = nps()
                for j in range(4):
                    S.op("pe", lambda e, pt=pt, j=j: e.transpose(pt.t[:, j * 128:(j + 1) * 128], yc.t[:, j * 128:(j + 1) * 128], ident),
                         reads=[yc, cst], writes=[pt] if j == 0 else (), pwrites=[pt] if j > 0 else ())
                evac(uAT[:], pt.t[:, :].rearrange("p (j f) -> p j f", f=128), reads=[pt], writes=[uAT])
                for cg in range(2):
                    cs_ = slice(cg * 512, (cg + 1) * 512)
                    pA, pB, pC = nps(), nps(), nps()
                    for k in range(4):
                        mm(pA, pA.t[:, :], uAT.t[:, k, :], wbo0.t[:, k, cs_], k == 0, k == 3, [uAT, wbo0])
                    for k in range(4):
                        mm(pB, pB.t[:, :], utb[u].t[:, k, :], wbo1.t[:, k, cs_], k == 0, k == 3, [utb[u], wbo1])
                    for k in range(4):
                        mm(pC, pC.t[:, :], utc[u].t[:, k, :], wbo2.t[:, k, cs_], k == 0, k == 3, [utc[u], wbo2])
                    S.op("dve", lambda e, pA=pA, cs_=cs_, sg=sg, cg=cg: e.tensor_tensor(mt.t[:, cs_], pA.t[:, :], sg.t[:, cg * 512:(cg + 1) * 512], ALU.mult), reads=[pA, sg],
                         writes=[mt] if cg == 0 else (), pwrites=[mt] if cg > 0 else ())
                    S.op("dve", lambda e, pB=pB, sg=sg, cg=cg: e.tensor_tensor(tmpm[:], pB.t[:, :], sg.t[:, D + cg * 512:D + (cg + 1) * 512], ALU.mult), reads=[pB, sg], writes=[tmpm])
                    S.op("pool", lambda e, cs_=cs_: e.tensor_tensor(mt.t[:, cs_], mt.t[:, cs_], tmpm[:], ALU.add), reads=[mt, tmpm], writes=[mt])
                    S.op("dve", lambda e, pC=pC, sg=sg, cg=cg: e.tensor_tensor(tmpm[:], pC.t[:, :], sg.t[:, 2 * D + cg * 512:2 * D + (cg + 1) * 512], ALU.mult), reads=[pC, sg], writes=[tmpm])
                    S.op("pool", lambda e, cs_=cs_: e.tensor_tensor(mt.t[:, cs_], mt.t[:, cs_], tmpm[:], ALU.add), reads=[mt, tmpm], writes=[mt])
                for half in range(2):
                    pt = nps()
                    for j in range(4):
                        k = half * 4 + j
                        S.op("pe", lambda e, pt=pt, j=j, k=k: e.transpose(pt.t[:, j * 128:(j + 1) * 128], mt.t[:, k * 128:(k + 1) * 128], ident),
                             reads=[mt, cst], writes=[pt] if j == 0 else (), pwrites=[pt] if j > 0 else ())
                    evac(mTt.t[:, half * 4:(half + 1) * 4, :], pt.t[:, :].rearrange("p (j f) -> p j f", f=128), reads=[pt], writes=[mTt] if half == 0 else (), pwrites=[mTt] if half > 0 else ())
                xo_ = xo[u]
                for cg in range(2):
                    cs_ = slice(cg * 512, (cg + 1) * 512)
                    pO = nps()
                    for k in range(8):
                        mm(pO, pO.t[:, :], mTt.t[:, k, :], wo.t[:, k, cs_], k == 0, k == 7, [mTt, wo])
                    S.op("dve", lambda e, pO=pO, cs_=cs_, xo_=xo_, r=r: e.tensor_tensor(xo_.t[:, cs_], pO.t[:, :], gateb[r].t[:, cs_], ALU.mult), reads=[pO, gateb[r]],
                         writes=[xo_] if cg == 0 else (), pwrites=[xo_] if cg > 0 else ())
                    S.op("pool", lambda e, cs_=cs_, xo_=xo_, xt=xt: e.tensor_tensor(xo_.t[:, cs_], xo_.t[:, cs_], xt.t[:, cs_], ALU.add), reads=[xo_, xt], writes=[xo_])
                if last:
                    S.dma("pool", yout[b, (i - 2) * 128:(i - 1) * 128, :], xo_[:], reads=[xo_], pwrites=[bX[2]])
                else:
                    S.dma("pool", Xnext[b, rows, :], xo_[:], reads=[xo_], pwrites=[bXnext])
        phase_end(st)
        Xcur, bXcur = Xnext, bXnext

    S.barrier()
    S.emit()
    outer.close()
    return nc, S.total


def _rope_full(rot_dim):
    grid_w = 64
    t = np.arange(TL)
    row = (t // grid_w).astype(np.float32)
    col = (t % grid_w).astype(np.float32)
    axis_dim = rot_dim // 2
    inv = (np.float32(10000.0) ** (-(2.0 * np.arange(axis_dim // 2, dtype=np.float32)) / np.float32(axis_dim))).astype(np.float32)
    ang = np.concatenate([row[:, None] * inv, col[:, None] * inv], axis=-1).astype(np.float32)
    cos, sin = np.cos(ang).astype(np.float32), np.sin(ang).astype(np.float32)
    q = rot_dim // 4
    cr, cc, sr, sc = cos[:, :q], cos[:, q:], sin[:, :q], sin[:, q:]
    cosF = np.concatenate([cr, cr, cc, cc], axis=-1)
    sinF = np.concatenate([-sr, sr, -sc, sc], axis=-1)
    out = np.zeros((T, 2 * rot_dim), np.float32)
    out[:LC, :rot_dim] = 1.0
    out[LC:, :rot_dim] = cosF
    out[LC:, rot_dim:] = sinF
    return out


def _consts():
    c = np.zeros((128, 1024), np.float32)
    c[:, 0:128] = np.eye(128, dtype=np.float32)
    c[0:64, 128:192] = 1.0
    c[64:128, 192:256] = 1.0
    c[0:64, 256 + 127] = 1.0
    c[64:128, 256 + 191] = 1.0
    kj = np.arange(128)[:, None]
    qi = np.arange(128)[None, :]
    c[:, 512:640] = (kj >= qi)
    c[:, 640:768] = (kj <= qi)
    c[0:64, 768] = 1.0
    c[64:128, 769] = 1.0
    c[:, 776:840] = 1.0
    return c


def _fm(a, nch):
    lead = a.shape[:-1]
    return np.ascontiguousarray(np.swapaxes(a.reshape(lead + (nch, 128)), -1, -2))


def prep_shared(inp):
    f = lambda k: np.asarray(inp[k], np.float32)
    L = DEPTH
    sh = {
        "consts": _consts(), "ropeB": _rope_full(32), "ropeC": _rope_full(64),
        "ada_w": f("ada_w"), "ada_bT": _fm(f("ada_b"), 24), "norm_gT": _fm(f("norm_g"), 8),
        "w_in": f("w_in"), "mupT": _fm(f("a_mu_prev"), 14), "munT": _fm(f("a_mu_next"), 14),
        "w0T": np.ascontiguousarray(np.transpose(f("a_w0").reshape(L, 2, 4, 128), (0, 3, 1, 2)).reshape(L, 128, 8)),
        "a0T": np.ascontiguousarray(np.transpose(f("a_a0").reshape(L, 2, 4, 128), (0, 3, 1, 2)).reshape(L, 128, 8)),
        "kkT": _fm(f("a_k_k"), 4), "kaT": _fm(f("a_k_a"), 4), "rkT": _fm(f("a_r_k").reshape(L, 512), 4),
        "w_up": np.ascontiguousarray(f("a_w_up").reshape(L, 128, 512)), "a_up": np.ascontiguousarray(f("a_a_up").reshape(L, 128, 512)),
        "gn_g": f("a_gn_g").reshape(L, 1, 512), "gn_b": f("a_gn_b").reshape(L, 1, 512),
        "q_ln": f("b_q_ln").reshape(L, 1, 256), "kv_ln": f("b_kv_ln").reshape(L, 1, 128),
        "w_uq": f("b_w_uq"), "w_ukv": f("b_w_ukv"),
        "bqk_g": np.ascontiguousarray(np.concatenate([f("b_qn_g"), f("b_kn_g")], axis=-1).reshape(L, 1, 192)),
        "c_qn": f("c_qn_g").reshape(L, 1, 64), "c_kn": f("c_kn_g").reshape(L, 1, 64), "c_sink": f("c_sink").reshape(L, 1, 8),
        "wbo": f("w_branch_out"), "w_out": f("w_out"),
    }
    return sh


def prep_core(inp, core, sh):
    x = np.asarray(inp["x"], np.float32)
    ctx = np.asarray(inp["ctx"], np.float32)
    c = np.asarray(inp["c"], np.float32)
    c_ctx = np.asarray(inp["c_ctx"], np.float32)
    b0 = core * NB
    xin = np.concatenate([ctx[b0:b0 + NB], x[b0:b0 + NB]], axis=1)
    rows = np.stack([c[b0], c[b0 + 1], c_ctx], axis=0)
    cT = np.ascontiguousarray(np.transpose(rows.reshape(3, 8, 128), (2, 1, 0)))
    m = dict(sh)
    m["xin"] = np.ascontiguousarray(xin)
    m["cT"] = cT
    return m


_CACHE = {}


def kernel(**inputs):
    n_cores = 8
    if "nc" not in _CACHE:
        _CACHE["nc"] = build_program()[0]
    nc = _CACHE["nc"]
    sh = prep_shared(inputs)
    in_maps = [prep_core(inputs, c, sh) for c in range(n_cores)]
    res = run_bass_kernel_spmd(nc, in_maps, core_ids=list(range(n_cores)))
    out = np.concatenate([np.asarray(r["yout"], np.float32) for r in res.results], axis=0)
    return out
```

```python
import math
import os
from contextlib import ExitStack

import numpy as np
import concourse.bass as bass
import concourse.mybir as mybir
from concourse.bass_utils import run_bass_kernel_spmd

F32 = mybir.dt.float32
ALU = mybir.AluOpType
AF = mybir.ActivationFunctionType
AX = mybir.AxisListType

D = 1024
NB = 2
LC = 256
TL = 2048
T = LC + TL
NT = T // 128
DEPTH = 2
NIN = 7584
EPS = 1e-6
GN_EPS = 64e-5
ZW = 4768
ZO_GA, ZO_ZB, ZO_ZC, ZO_ZG = 0, 512, 928, 1696
TCH = [(0, 512), (512, 512), (1024, 512), (1536, 512), (2048, 256)]


class Buf:
    __slots__ = ("t", "w", "r", "name", "psum")

    def __init__(self, t=None, name="", psum=False):
        self.t = t
        self.w = {}
        self.r = {}
        self.name = name
        self.psum = psum

    def __getitem__(self, k):
        return self.t[k]


class Sched:
    ENGS = ("pe", "act", "dve", "pool", "sp")
    QS = ("sp", "pool")

    def __init__(self, nc, stack, ndma=8):
        self.nc = nc
        self.prog = {e: [] for e in self.ENGS}
        self.sem = {e: stack.enter_context(nc.semaphore("s_" + e)) for e in self.ENGS}
        self.cnt = {e: 0 for e in self.ENGS}
        self.waited = {e: {} for e in self.ENGS}
        self.ndma = ndma
        self.dsem = {q: [stack.enter_context(nc.semaphore("d_%s%d" % (q, i))) for i in range(ndma)] for q in self.QS}
        self.dcnt = {q: [0] * ndma for q in self.QS}
        self.dnext = {q: 0 for q in self.QS}
        self.total = 0

    def _deps(self, e, reads, writes, pwrites):
        best = {}

        def add(d):
            for k, sv in d.items():
                if k not in best or best[k][1] < sv[1]:
                    best[k] = sv
        for b in reads:
            add(b.w)
            if b.psum:
                own = id(self.sem[e]) if e in self.sem else None
                add({k: sv for k, sv in b.r.items() if k != own})
        for b in writes:
            add(b.w)
            add(b.r)
        for b in pwrites:
            add(b.r)
        out = []
        wd = self.waited[e]
        for k, (s, v) in best.items():
            if e == "pe" and s is self.sem["pe"]:
                continue
            if wd.get(k, 0) >= v:
                continue
            wd[k] = v
            out.append((s, v))
        return out

    @staticmethod
    def _mark(reads, writes, pwrites, tok):
        k = id(tok[0])
        for b in reads:
            if k not in b.r or b.r[k][1] < tok[1]:
                b.r[k] = tok
        for b in writes:
            b.w = {k: tok}
            b.r = {}
        for b in pwrites:
            if k not in b.w or b.w[k][1] < tok[1]:
                b.w[k] = tok

    def op(self, e, fn, reads=(), writes=(), pwrites=()):
        deps = self._deps(e, reads, writes, pwrites)
        self.cnt[e] += 1
        tok = (self.sem[e], self.cnt[e])
        self.prog[e].append((deps, fn, (self.sem[e], 1)))
        self._mark(reads, writes, pwrites, tok)
        return tok

    def dma(self, q, out, in_, reads=(), writes=(), pwrites=(), **kw):
        i = self.dnext[q]
        self.dnext[q] = (i + 1) % self.ndma
        s = self.dsem[q][i]
        deps = self._deps(q, reads, writes, pwrites)
        prev = self.dcnt[q][i]
        if prev > 0 and self.waited[q].get(id(s), 0) < prev:
            deps.append((s, prev))
            self.waited[q][id(s)] = prev
        self.dcnt[q][i] = prev + 16
        tok = (s, prev + 16)
        self.prog[q].append((deps, (lambda eng: eng.dma_start(out=out, in_=in_, **kw)), (s, 16)))
        self._mark(reads, writes, pwrites, tok)
        return tok

    def barrier(self):
        alld = [(self.sem[x], self.cnt[x]) for x in self.ENGS if self.cnt[x] > 0]
        for q in self.QS:
            for i in range(self.ndma):
                if self.dcnt[q][i] > 0:
                    alld.append((self.dsem[q][i], self.dcnt[q][i]))
        for e in self.ENGS:
            deps = []
            wd = self.waited[e]
            for (s, v) in alld:
                if s is self.sem[e]:
                    continue
                if wd.get(id(s), 0) >= v:
                    continue
                wd[id(s)] = v
                deps.append((s, v))
            self.prog[e].append((deps, None, None))

    def emit(self):
        with self.nc.Block() as block:
            def mk(e):
                def body(eng):
                    for deps, fn, inc in self.prog[e]:
                        for (s, v) in deps:
                            eng.wait_ge(s, v)
                        if fn is not None:
                            fn(eng).then_inc(inc[0], inc[1])
                return body
            block.tensor(mk("pe"))
            block.scalar(mk("act"))
            block.vector(mk("dve"))
            block.gpsimd(mk("pool"))
            block.sync(mk("sp"))
        for e in self.ENGS:
            self.total += len(self.prog[e]) + sum(len(d) for d, _, _ in self.prog[e])
            self.prog[e] = []


def build_program(n_layers=DEPTH, dbg=(), stop_after=None):
    nc = bass.Bass("TRN2", target_bir_lowering=False)
    dbg = set(dbg)

    def din(name, shape):
        return nc.dram_tensor(name, list(shape), F32, kind="ExternalInput").ap()

    def dscr(name, shape):
        kind = "ExternalOutput" if name in dbg else "Internal"
        return nc.dram_tensor(name, list(shape), F32, kind=kind).ap()

    xin = din("xin", [NB, T, D])
    cT_d = din("cT", [128, 8, 3])
    consts_d = din("consts", [128, 1024])
    ropeB_d = din("ropeB", [T, 64])
    ropeC_d = din("ropeC", [T, 128])
    W = {}
    for nm, shp in [("ada_w", [DEPTH, D, 3 * D]), ("ada_bT", [DEPTH, 128, 24]), ("norm_gT", [DEPTH, 128, 8]),
                    ("w_in", [DEPTH, D, NIN]), ("mupT", [DEPTH, 128, 14]), ("munT", [DEPTH, 128, 14]),
                    ("w0T", [DEPTH, 128, 8]), ("a0T", [DEPTH, 128, 8]), ("kkT", [DEPTH, 128, 4]),
                    ("kaT", [DEPTH, 128, 4]), ("rkT", [DEPTH, 128, 4]), ("w_up", [DEPTH, 128, 512]),
                    ("a_up", [DEPTH, 128, 512]), ("gn_g", [DEPTH, 1, 512]), ("gn_b", [DEPTH, 1, 512]),
                    ("q_ln", [DEPTH, 1, 256]), ("kv_ln", [DEPTH, 1, 128]), ("w_uq", [DEPTH, 256, 768]),
                    ("w_ukv", [DEPTH, 128, 1024]), ("bqk_g", [DEPTH, 1, 192]), ("c_qn", [DEPTH, 1, 64]),
                    ("c_kn", [DEPTH, 1, 64]), ("c_sink", [DEPTH, 1, 8]), ("wbo", [DEPTH, 3, 512, D]),
                    ("w_out", [DEPTH, D, D])]:
        W[nm] = din(nm, shp)
    yout = nc.dram_tensor("yout", [NB, TL, D], F32, kind="ExternalOutput").ap()

    X1 = dscr("X1", [NB, T, D])
    MODG = dscr("MODG", [3, D])
    ZTM = dscr("ZTM", [NB, T, ZW])
    ZTA = dscr("ZTA", [NB, 14, 128, T])
    GT = dscr("GT", [NB, 2, 8, 64, T])
    SC = dscr("SC", [2, 128, 5, 8, T])
    VT = dscr("VT", [T, 1024])
    YD = dscr("YD", [2, T, 1024])
    BS = dscr("BS", [NB, T, 8])
    QKT = dscr("QKT", [NB, 16, 96, T])
    VB = dscr("VB", [NB, T, 512])
    QKTC = dscr("QKTC", [NB, 10, 64, T])
    VC = dscr("VC", [NB, T, 128])
    UT = dscr("UT", [NB, 2, 8, 64, T])

    bX = [Buf(None, "xin"), Buf(None, "X1"), Buf(None, "yout")]
    bMODG = Buf(None, "MODG")
    bZTM = [Buf(None, "ZTM%d" % b) for b in range(NB)]
    bZTA = [Buf(None, "ZTA%d" % b) for b in range(NB)]
    bGT = [Buf(None, "GT%d" % b) for b in range(NB)]
    bSC = Buf(None, "SC")
    bVT = Buf(None, "VT")
    bYD = Buf(None, "YD")
    bBS = [Buf(None, "BS%d" % b) for b in range(NB)]
    bQKT = [Buf(None, "QKT%d" % b) for b in range(NB)]
    bVB = [Buf(None, "VB%d" % b) for b in range(NB)]
    bQKTC = [Buf(None, "QKTC%d" % b) for b in range(NB)]
    bVC = [Buf(None, "VC%d" % b) for b in range(NB)]
    bUT = [Buf(None, "UT%d" % b) for b in range(NB)]
    bW = Buf(None, "weights")

    outer = ExitStack()
    S = Sched(nc, outer)
    uid = [0]

    def sbt(stack, shape, name=None):
        uid[0] += 1
        nm = "%s_%d" % (name or "t", uid[0])
        return Buf(stack.enter_context(nc.sbuf_tensor(nm, list(shape), F32)), nm)

    cst = sbt(outer, [128, 1024], "cst")
    S.dma("sp", cst[:], consts_d[:, :], reads=[bW], writes=[cst])
    ident = cst.t[:, 0:128]
    bones = cst.t[:, 128:256]
    Z2 = cst.t[:, 256:511]
    mask_lo = cst.t[:, 512:640]
    mask_hi = cst.t[:, 640:768]
    ind2 = cst.t[:, 768:770]
    ones64 = cst.t[:, 776:840]
    Z3 = cst.t[:, 848:943]
    PS = [Buf(outer.enter_context(nc.psum_tensor("ps%d" % i, [128, 512], F32)), "ps%d" % i, psum=True) for i in range(8)]
    psi = [0]

    def nps():
        p = PS[psi[0] % 8]
        psi[0] += 1
        return p

    modT = sbt(outer, [128, 24, 3], "modT")
    Gt = sbt(outer, [128, 8, 3], "Gt")
    csil = sbt(outer, [128, 8, 3], "csil")
    S.dma("sp", csil[:], cT_d[:, :, :], reads=[bW], writes=[csil])
    S.op("act", lambda e: e.activation(csil[:], csil[:], AF.Silu), reads=[csil], writes=[csil])

    def phase_end(st):
        S.barrier()
        S.emit()
        st.close()

    def mm(ps, out_ap, lhsT, rhs, start, stop, reads):
        S.op("pe", lambda e: e.matmul(out_ap, lhsT, rhs, start=start, stop=stop), reads=reads, pwrites=[ps] if not start else (), writes=[ps] if start else ())

    def mmp(ps, out_ap, lhsT, rhs, start, stop, reads):
        S.op("pe", lambda e: e.matmul(out_ap, lhsT, rhs, start=start, stop=stop), reads=reads, pwrites=[ps])

    def bc_load(st, src_row_ap, n, parts=128, name="bc"):
        t = sbt(st, [parts, n], name)
        S.dma("sp", t[:], src_row_ap.partition_broadcast(parts), reads=[bW], writes=[t])
        return t

    evac_rr = [0]

    def evac(out_ap, in_ap, reads, writes=(), pwrites=(), func=None, bias=None, scale=None):
        if func is None and bias is None and scale is None:
            evac_rr[0] += 1
            if evac_rr[0] % 2 == 0:
                S.op("dve", lambda e: e.tensor_copy(out_ap, in_ap), reads=reads, writes=writes, pwrites=pwrites)
            else:
                S.op("act", lambda e: e.copy(out_ap, in_ap), reads=reads, writes=writes, pwrites=pwrites)
        else:
            kw = {}
            if bias is not None:
                kw["bias"] = bias
            if scale is not None:
                kw["scale"] = scale
            f = func if func is not None else AF.Identity
            S.op("act", lambda e: e.activation(out_ap, in_ap, f, **kw), reads=reads, writes=writes, pwrites=pwrites)

    def rstd_from_ss(st_tiles, ss, n, inv_n, eps):
        S.op("dve", lambda e: e.tensor_scalar(ss[:, 0:n], ss[:, 0:n], inv_n, eps, ALU.mult, ALU.add), reads=[ss], writes=[ss])
        S.op("act", lambda e: e.activation(ss[:, 0:n], ss[:, 0:n], AF.Sqrt), reads=[ss], writes=[ss])
        S.op("dve", lambda e: e.reciprocal(ss[:, 0:n], ss[:, 0:n]), reads=[ss], writes=[ss])

    def rope(st, x3, nh, R, cs, tmp1, tmp2):
        q = R // 4
        xb = x3_buf[0]
        t1 = tmp1.t[:, 0:nh * R].rearrange("p (h r) -> p h r", h=nh)
        cosb = cs.t[:, 0:R].unsqueeze(1).to_broadcast([128, nh, R])
        S.op("dve", lambda e: e.tensor_tensor(t1, x3, cosb, ALU.mult), reads=[xb, cs], writes=[tmp1])
        x5 = x3.rearrange("p h (a f i) -> p h a f i", a=2, f=2)
        t5 = t1.rearrange("p h (a f i) -> p h a f i", a=2, f=2)
        s4 = cs.t[:, R:2 * R].rearrange("p (a f i) -> p a f i", a=2, f=2)
        t2 = tmp2.t[:, 0:nh * R // 2].rearrange("p (h a i) -> p h a i", h=nh, a=2)
        for hf in (0, 1):
            xin_ = x5[:, :, :, 1 - hf, :]
            sb_ = s4[:, :, hf, :].unsqueeze(1).to_broadcast([128, nh, 2, q])
            S.op("dve", lambda e, xin_=xin_, sb_=sb_: e.tensor_tensor(t2, xin_, sb_, ALU.mult), reads=[xb, cs], writes=[tmp2])
            tt = t5[:, :, :, hf, :]
            S.op("dve", lambda e, tt=tt: e.tensor_tensor(tt, tt, t2, ALU.add), reads=[tmp2, tmp1], writes=[tmp1])
        S.op("dve", lambda e: e.tensor_copy(x3, t1), reads=[tmp1], writes=[xb])

    x3_buf = [None]

    Xcur, bXcur = xin, bX[0]
    for l in range(n_layers):
        last = (l == DEPTH - 1)
        Xnext, bXnext = (X1, bX[1])
        st = ExitStack()
        adab = sbt(st, [128, 24], "adab")
        ngt = sbt(st, [128, 8], "ngt")
        S.dma("sp", adab[:], W["ada_bT"][l, :, :], reads=[bW], writes=[adab])
        S.dma("sp", ngt[:], W["norm_gT"][l, :, :], reads=[bW], writes=[ngt])
        wa = [sbt(st, [128, 8, 128], "wa") for _ in range(3)]
        psM = nps()
        for ch in range(24):
            wt = wa[ch % 3]
            S.dma("sp", wt[:], W["ada_w"][l, :, ch * 128:(ch + 1) * 128].rearrange("(k p) n -> p k n", p=128), reads=[bW], writes=[wt])
            for k in range(8):
                mmp(psM, psM.t[:, ch * 3:ch * 3 + 3], wt.t[:, k, :], csil.t[:, k, :], k == 0, k == 7, [wt, csil]) if ch > 0 or k > 0 else \
                    mm(psM, psM.t[:, 0:3], wt.t[:, k, :], csil.t[:, k, :], True, False, [wt, csil])
        S.op("dve", lambda e: e.tensor_tensor(modT[:], psM.t[:, 0:72].rearrange("p (c r) -> p c r", r=3),
                                              adab.t[:, :].unsqueeze(2).to_broadcast([128, 24, 3]), ALU.add),
             reads=[psM, adab], writes=[modT])
        S.op("dve", lambda e: e.tensor_scalar(Gt[:], modT.t[:, 8:16, :], 1.0, None, ALU.add), reads=[modT], writes=[Gt])
        S.op("dve", lambda e: e.tensor_tensor(Gt[:], Gt[:], ngt.t[:, :].unsqueeze(2).to_broadcast([128, 8, 3]), ALU.mult),
             reads=[Gt, ngt], writes=[Gt])
        psG = [nps(), nps()]
        for ch in range(8):
            pg = psG[ch // 4]
            (mm if ch % 4 == 0 else mmp)(pg, pg.t[0:3, (ch % 4) * 128:(ch % 4 + 1) * 128], modT.t[:, 16 + ch, :], ident, True, True, [modT, cst])
        gsb = sbt(st, [3, 1024], "gsb")
        for hlf in range(2):
            S.op("dve", lambda e, hlf=hlf: e.tensor_copy(gsb.t[0:3, hlf * 512:(hlf + 1) * 512], psG[hlf].t[0:3, :]), reads=[psG[hlf]], pwrites=[gsb])
        S.dma("pool", MODG[:, :], gsb[:], reads=[gsb], writes=[bMODG])
        phase_end(st)
        if stop_after == "P0":
            break

        for b in range(NB):
            st = ExitStack()
            hT = sbt(st, [128, 8, T], "hT")
            xts = [sbt(st, [128, D], "xt") for _ in range(2)]
            sqs = [sbt(st, [128, D], "sq") for _ in range(2)]
            sss = [sbt(st, [128, 1], "ss") for _ in range(2)]
            for i in range(NT):
                r = b if i >= 2 else 2
                xt, sq, ss = xts[i % 2], sqs[i % 2], sss[i % 2]
                S.dma("sp", xt[:], Xcur[b, i * 128:(i + 1) * 128, :], reads=[bXcur], writes=[xt])
                S.op("pool", lambda e, ss=ss: e.memset(ss[:], 0.0), writes=[ss])
                S.op("act", lambda e, xt=xt, sq=sq, ss=ss: e.activation(sq[:], xt[:], AF.Square, accum_out=ss[:]), reads=[xt, ss], writes=[sq, ss])
                NLV = int(os.environ.get("KDBG_NLV", 9))
                if NLV < 2:
                    continue
                rstd_from_ss(None, ss, 1, 1.0 / D, EPS)
                S.op("dve", lambda e, xt=xt, sq=sq, ss=ss: e.tensor_scalar(sq[:], xt[:], ss.t[:, 0:1], None, ALU.mult), reads=[xt, ss], writes=[sq])
                if NLV < 3:
                    continue
                for half in range(2):
                    pt = nps()
                    for c4 in range(4):
                        ch = half * 4 + c4
                        S.op("pe", lambda e, pt=pt, c4=c4, ch=ch, sq=sq: e.transpose(pt.t[:, c4 * 128:(c4 + 1) * 128], sq.t[:, ch * 128:(ch + 1) * 128], ident),
                             reads=[sq, cst], writes=[pt] if c4 == 0 else (), pwrites=[pt] if c4 > 0 else ())
                    if NLV < 4:
                        continue
                    for c4 in range(4):
                        ch = half * 4 + c4
                        o_ap = hT.t[:, ch, i * 128:(i + 1) * 128]
                        i_ap = pt.t[:, c4 * 128:(c4 + 1) * 128]
                        g_ap = Gt.t[:, ch, r:r + 1]
                        s_ap = modT.t[:, ch, r:r + 1]
                        EV = os.environ.get("KDBG_EV", "")
                        if (c4 % 2 == 0 and EV != "act") or EV == "dve":
                            S.op("dve", lambda e, o_ap=o_ap, i_ap=i_ap, g_ap=g_ap, s_ap=s_ap: e.tensor_scalar(o_ap, i_ap, g_ap, s_ap, ALU.mult, ALU.add),
                                 reads=[pt, Gt, modT], pwrites=[hT])
                        else:
                            S.op("act", lambda e, o_ap=o_ap, i_ap=i_ap, g_ap=g_ap, s_ap=s_ap: e.activation(o_ap, i_ap, AF.Identity, bias=s_ap, scale=g_ap),
                                 reads=[pt, Gt, modT], pwrites=[hT])
            if "hT" in dbg and b == 0 and l == 0:
                hTd = nc.dram_tensor("hTd", [128, 8, T], F32, kind="ExternalOutput").ap()
                for k in range(8):
                    S.dma("pool", hTd[:, k, :], hT.t[:, k, :], reads=[hT])
            if stop_after == "P1a":
                phase_end(st)
                break
            wbufs = [sbt(st, [128, 8, 512], "wb") for _ in range(2)]
            zos = [sbt(st, [128, 512], "zo") for _ in range(3)]
            groups = [(1792, 512, ZO_GA, AF.Silu), (2304, 416, ZO_ZB, None), (3232, 512, ZO_ZC, None), (3744, 256, ZO_ZC + 512, None)]
            for g in range(6):
                groups.append((4512 + g * 512, 512, ZO_ZG + g * 512, AF.Sigmoid))
            zi = 0
            for gi, (c0, n, zoff, fn) in enumerate(groups):
                wb = wbufs[gi % 2]
                S.dma("sp", wb.t[:, :, 0:n], W["w_in"][l, :, c0:c0 + n].rearrange("(k p) n -> p k n", p=128), reads=[bW], writes=[wb])
                for i in range(NT):
                    ps = nps()
                    for k in range(8):
                        mm(ps, ps.t[:, 0:n], hT.t[:, k, i * 128:(i + 1) * 128], wb.t[:, k, 0:n], k == 0, k == 7, [hT, wb])
                    zo = zos[zi % 3]
                    zi += 1
                    evac(zo.t[:, 0:n], ps.t[:, 0:n], reads=[ps], writes=[zo], func=fn)
                    S.dma("pool", ZTM[b, i * 128:(i + 1) * 128, zoff:zoff + n], zo.t[:, 0:n], reads=[zo], pwrites=[bZTM[b]])
            zfs = [sbt(st, [128, T], "zf") for _ in range(2)]
            fm = [(c * 128, 128, ("A", c), None) for c in range(14)]
            fm += [(2720 + h * 64, 64, ("G", 0, h), AF.Silu) for h in range(8)]
            fm += [(4000 + h * 64, 64, ("G", 1, h), AF.Silu) for h in range(8)]
            for fi, (c0, m, dst, fn) in enumerate(fm):
                wb = wbufs[fi % 2]
                zf = zfs[fi % 2]
                S.dma("sp", wb.t[:, :, 0:m], W["w_in"][l, :, c0:c0 + m].rearrange("(k p) n -> p k n", p=128), reads=[bW], writes=[wb])
                for ci, (t0, tn) in enumerate(TCH):
                    ps = nps()
                    for k in range(8):
                        mm(ps, ps.t[0:m, 0:tn], wb.t[:, k, 0:m], hT.t[:, k, t0:t0 + tn], k == 0, k == 7, [hT, wb])
                    evac(zf.t[0:m, t0:t0 + tn], ps.t[0:m, 0:tn], reads=[ps], writes=[zf] if ci == 0 else (), pwrites=[zf] if ci > 0 else (), func=fn)
                if dst[0] == "A":
                    S.dma("pool", ZTA[b, dst[1], :, :], zf.t[:, :], reads=[zf], pwrites=[bZTA[b]])
                else:
                    S.dma("pool", GT[b, dst[1], dst[2], :, :], zf.t[0:64, :], reads=[zf], pwrites=[bGT[b]])
            phase_end(st)
            if stop_after == "P1":
                break

            st = ExitStack()
            mup = sbt(st, [128, 14], "mup")
            mun = sbt(st, [128, 14], "mun")
            c0t = sbt(st, [128, 14], "c0t")
            w0t = sbt(st, [128, 8], "w0t")
            a0t = sbt(st, [128, 8], "a0t")
            kkp = sbt(st, [128, 4], "kkp")
            kap = sbt(st, [128, 4], "kap")
            rkp = sbt(st, [128, 4], "rkp")
            wup = sbt(st, [128, 512], "wup")
            aup = sbt(st, [128, 512], "aup")
            for tl_, nm in [(mup, "mupT"), (mun, "munT"), (w0t, "w0T"), (a0t, "a0T"), (kkp, "kkT"), (kap, "kaT"), (rkp, "rkT"), (wup, "w_up"), (aup, "a_up")]:
                S.dma("sp", tl_[:], W[nm][l, :, :], reads=[bW], writes=[tl_])
            S.op("dve", lambda e: e.tensor_tensor(c0t[:], mup[:], mun[:], ALU.add), reads=[mup, mun], writes=[c0t])
            S.op("dve", lambda e: e.tensor_scalar(c0t[:], c0t[:], -1.0, 1.0, ALU.mult, ALU.add), reads=[c0t], writes=[c0t])
            NBT = 15
            bts = [sbt(st, [128, T], "bt") for _ in range(NBT)]
            zraw = [bts[0], bts[1]]
            zri = [0]

            def shift_load(c, dst):
                zr = zraw[zri[0] % 2]
                zri[0] += 1
                S.dma("sp", zr[:], ZTA[b, c, :, :], reads=[bZTA[b]], writes=[zr])
                S.op("act", lambda e: e.activation(dst[:], zr[:], AF.Identity, scale=c0t.t[:, c:c + 1]), reads=[zr, c0t], writes=[dst])
                for (o0, o1, i0, i1, mt) in [(1, 256, 0, 255, mup), (257, T, 256, T - 1, mup), (0, 255, 1, 256, mun), (256, T - 1, 257, T, mun)]:
                    S.op("dve", lambda e, o0=o0, o1=o1, i0=i0, i1=i1, mt=mt: e.scalar_tensor_tensor(dst.t[:, o0:o1], zr.t[:, i0:i1], mt.t[:, c:c + 1], dst.t[:, o0:o1], ALU.mult, ALU.add),
                         reads=[zr, mt, dst], writes=[dst])

            twd, ads = bts[2], bts[3]
            shift_load(12, twd)
            S.op("act", lambda e: e.activation(twd[:], twd[:], AF.Tanh), reads=[twd], writes=[twd])
            shift_load(13, ads)
            rs_, ks_, vs_, kk, tq, a_d, dec, ka_d, kd0, kd1, uu = bts[4:15]
            bsS = sbt(st, [128, NT, 8], "bsS")
            vtm = [sbt(st, [128, 4, 128], "vtm") for _ in range(2)]
            for q in range(4):
                shift_load(q, rs_)
                shift_load(4 + q, ks_)
                shift_load(8 + q, vs_)
                S.op("dve", lambda e, q=q: e.tensor_scalar(kk[:], ks_[:], kkp.t[:, q:q + 1], None, ALU.mult), reads=[ks_, kkp], writes=[kk])
                S.op("pool", lambda e: e.tensor_tensor(tq[:], kk[:], kk[:], ALU.mult), reads=[kk], writes=[tq])
                for ci, (t0, tn) in enumerate(TCH):
                    ps = nps()
                    mm(ps, ps.t[:, 0:tn], bones, tq.t[:, t0:t0 + tn], True, True, [tq, cst])
                    S.op("dve", lambda e, ps=ps, t0=t0, tn=tn: e.tensor_scalar_max(a_d.t[:, t0:t0 + tn], ps.t[:, 0:tn], 1e-24), reads=[ps],
                         writes=[a_d] if ci == 0 else (), pwrites=[a_d] if ci > 0 else ())
                S.op("act", lambda e: e.activation(a_d[:], a_d[:], AF.Sqrt), reads=[a_d], writes=[a_d])
                S.op("dve", lambda e: e.reciprocal(a_d[:], a_d[:]), reads=[a_d], writes=[a_d])
                S.op("dve", lambda e: e.tensor_tensor(kk[:], kk[:], a_d[:], ALU.mult), reads=[kk, a_d], writes=[kk])
                S.dma("pool", SC[0, :, 3, b * 4 + q, :], kk[:], reads=[kk], pwrites=[bSC])
                S.dma("pool", SC[1, :, 3, b * 4 + q, :], kk[:], reads=[kk], pwrites=[bSC])
                S.dma("pool", SC[0, :, 4, b * 4 + q, :], rs_[:], reads=[rs_], pwrites=[bSC])
                S.dma("pool", SC[1, :, 4, b * 4 + q, :], rs_[:], reads=[rs_], pwrites=[bSC])
                for d in range(2):
                    kd = kd0 if d == 0 else kd1
                    for ci, (t0, tn) in enumerate(TCH):
                        ps = nps()
                        mm(ps, ps.t[:, 0:tn], wup.t[d * 64:(d + 1) * 64, q * 128:(q + 1) * 128], twd.t[d * 64:(d + 1) * 64, t0:t0 + tn], True, True, [wup, twd])
                        evac(dec.t[:, t0:t0 + tn], ps.t[:, 0:tn], reads=[ps, w0t], writes=[dec] if ci == 0 else (), pwrites=[dec] if ci > 0 else (),
                             func=AF.Sigmoid, bias=w0t.t[:, d * 4 + q:d * 4 + q + 1])
                    S.op("act", lambda e: e.activation(dec[:], dec[:], AF.Exp, scale=-math.exp(-0.5)), reads=[dec], writes=[dec])
                    S.dma("pool", SC[d, :, 0, b * 4 + q, :], dec[:], reads=[dec], pwrites=[bSC])
                    for ci, (t0, tn) in enumerate(TCH):
                        ps = nps()
                        mm(ps, ps.t[:, 0:tn], aup.t[d * 64:(d + 1) * 64, q * 128:(q + 1) * 128], ads.t[d * 64:(d + 1) * 64, t0:t0 + tn], True, True, [aup, ads])
                        evac(a_d.t[:, t0:t0 + tn], ps.t[:, 0:tn], reads=[ps, a0t], writes=[a_d] if ci == 0 else (), pwrites=[a_d] if ci > 0 else (),
                             func=AF.Sigmoid, bias=a0t.t[:, d * 4 + q:d * 4 + q + 1])
                    S.op("pool", lambda e: e.tensor_tensor(ka_d[:], kk[:], a_d[:], ALU.mult), reads=[kk, a_d], writes=[ka_d])
                    S.dma("pool", SC[d, :, 1, b * 4 + q, :], ka_d[:], reads=[ka_d], pwrites=[bSC])
                    S.op("dve", lambda e, kd=kd, q=q: e.tensor_scalar(kd[:], a_d[:], kap.t[:, q:q + 1], kap.t[:, q:q + 1], ALU.mult, ALU.subtract), reads=[a_d, kap], writes=[kd])
                    S.op("dve", lambda e, kd=kd: e.scalar_tensor_tensor(kd[:], kd[:], 1.0, ks_[:], ALU.add, ALU.mult), reads=[kd, ks_], writes=[kd])
                    S.dma("pool", SC[d, :, 2, b * 4 + q, :], kd[:], reads=[kd], pwrites=[bSC])
                S.op("pool", lambda e: e.tensor_tensor(uu[:], kd0[:], kd1[:], ALU.add), reads=[kd0, kd1], writes=[uu])
                S.op("dve", lambda e, q=q: e.scalar_tensor_tensor(uu[:], uu[:], rkp.t[:, q:q + 1], rs_[:], ALU.mult, ALU.mult), reads=[uu, rkp, rs_], writes=[uu])
                psb = nps()
                for i in range(NT):
                    mm(psb, psb.t[:, i * 2:i * 2 + 2], uu.t[:, i * 128:(i + 1) * 128], ind2, True, True, [uu, cst]) if i == 0 else \
                        mmp(psb, psb.t[:, i * 2:i * 2 + 2], uu.t[:, i * 128:(i + 1) * 128], ind2, True, True, [uu, cst])
                S.op("dve", lambda e, q=q, psb=psb: e.tensor_copy(bsS.t[:, :, 2 * q:2 * q + 2], psb.t[:, 0:2 * NT].rearrange("p (i c) -> p i c", c=2)),
                     reads=[psb], writes=[bsS] if q == 0 else (), pwrites=[bsS] if q > 0 else ())
                for g0 in range(0, NT, 4):
                    ng = min(4, NT - g0)
                    pt = nps()
                    for j in range(ng):
                        i = g0 + j
                        S.op("pe", lambda e, pt=pt, j=j, i=i: e.transpose(pt.t[:, j * 128:(j + 1) * 128], vs_.t[:, i * 128:(i + 1) * 128], ident),
                             reads=[vs_, cst], writes=[pt] if j == 0 else (), pwrites=[pt] if j > 0 else ())
                    vt_ = vtm[(g0 // 4) % 2]
                    evac(vt_.t[:, 0:ng, :], pt.t[:, 0:ng * 128].rearrange("p (j f) -> p j f", f=128), reads=[pt], writes=[vt_])
                    for j in range(ng):
                        i = g0 + j
                        dst = VT[i * 128:(i + 1) * 128, :].rearrange("p (c bb qq v) -> p c bb qq v", c=2, bb=2, qq=4)[:, :, b, q, :]
                        S.dma("pool", dst, vt_.t[:, j, :].rearrange("p (c v) -> p c v", c=2), reads=[vt_], pwrites=[bVT])
            S.dma("pool", BS[b, :, :].rearrange("(i p) h -> p i h", p=128), bsS[:], reads=[bsS], writes=[bBS[b]])
            phase_end(st)
            if stop_after == "P2":
                break

            st = ExitStack()
            qln = bc_load(st, W["q_ln"][l, :, :], 256, name="qln")
            kvln = bc_load(st, W["kv_ln"][l, :, :], 128, name="kvln")
            bqkg = bc_load(st, W["bqk_g"][l, :, :], 192, name="bqkg")
            cqn = bc_load(st, W["c_qn"][l, :, :], 64, name="cqn")
            ckn = bc_load(st, W["c_kn"][l, :, :], 64, name="ckn")
            wuq = sbt(st, [128, 2, 768], "wuq")
            wukv = sbt(st, [128, 1024], "wukv")
            S.dma("sp", wuq[:], W["w_uq"][l, :, :].rearrange("(k p) n -> p k n", p=128), reads=[bW], writes=[wuq])
            S.dma("sp", wukv[:], W["w_ukv"][l, :, :], reads=[bW], writes=[wukv])
            NBUF = 2
            zbs = [sbt(st, [128, 416], "zb") for _ in range(NBUF)]
            zcs = [sbt(st, [128, 768], "zc") for _ in range(NBUF)]
            rbs = [sbt(st, [128, 64], "rb") for _ in range(NBUF)]
            rcs = [sbt(st, [128, 128], "rc") for _ in range(NBUF)]
            ss2 = [sbt(st, [128, 2], "ss2") for _ in range(NBUF)]
            junk = sbt(st, [128, 1536], "junk")
            cn = [sbt(st, [128, 384], "cn") for _ in range(NBUF)]
            cT3 = [sbt(st, [128, 3, 128], "cT3") for _ in range(NBUF)]
            qk = [sbt(st, [128, 16, 96], "qk") for _ in range(NBUF)]
            kv = [sbt(st, [128, 8, 128], "kv") for _ in range(NBUF)]
            ssq = [sbt(st, [128, 16], "ssq") for _ in range(NBUF)]
            rt1 = sbt(st, [128, 640], "rt1")
            rt2 = sbt(st, [128, 320], "rt2")
            qkT = [sbt(st, [96, 16, 128], "qkT") for _ in range(NBUF)]
            qkTc = [sbt(st, [64, 10, 128], "qkTc") for _ in range(NBUF)]
            for i in range(NT):
                u = i % NBUF
                zb, zc, rb, rc = zbs[u], zcs[u], rbs[u], rcs[u]
                S.dma("sp", zb[:], ZTM[b, i * 128:(i + 1) * 128, ZO_ZB:ZO_ZB + 416], reads=[bZTM[b]], writes=[zb])
                S.dma("sp", zc[:], ZTM[b, i * 128:(i + 1) * 128, ZO_ZC:ZO_ZC + 768], reads=[bZTM[b]], writes=[zc])
                S.dma("sp", rb[:], ropeB_d[i * 128:(i + 1) * 128, :], reads=[bW], writes=[rb])
                S.dma("sp", rc[:], ropeC_d[i * 128:(i + 1) * 128, :], reads=[bW], writes=[rc])
                s2 = ss2[u]
                S.op("pool", lambda e, s2=s2: e.memset(s2[:], 0.0), writes=[s2])
                S.op("act", lambda e, zb=zb, s2=s2: e.activation(junk.t[:, 0:256], zb.t[:, 0:256], AF.Square, accum_out=s2.t[:, 0:1]), reads=[zb, s2], writes=[junk, s2])
                S.op("act", lambda e, zb=zb, s2=s2: e.activation(junk.t[:, 256:384], zb.t[:, 256:384], AF.Square, accum_out=s2.t[:, 1:2]), reads=[zb, s2], writes=[junk, s2])
                S.op("dve", lambda e, s2=s2: e.tensor_scalar(s2.t[:, 0:1], s2.t[:, 0:1], 1.0 / 256, EPS, ALU.mult, ALU.add), reads=[s2], writes=[s2])
                S.op("dve", lambda e, s2=s2: e.tensor_scalar(s2.t[:, 1:2], s2.t[:, 1:2], 1.0 / 128, EPS, ALU.mult, ALU.add), reads=[s2], writes=[s2])
                S.op("act", lambda e, s2=s2: e.activation(s2[:], s2[:], AF.Sqrt), reads=[s2], writes=[s2])
                S.op("dve", lambda e, s2=s2: e.reciprocal(s2[:], s2[:]), reads=[s2], writes=[s2])
                c_ = cn[u]
                S.op("dve", lambda e, c_=c_, zb=zb, s2=s2: e.scalar_tensor_tensor(c_.t[:, 0:256], zb.t[:, 0:256], s2.t[:, 0:1], qln[:], ALU.mult, ALU.mult), reads=[zb, s2, qln], writes=[c_])
                S.op("dve", lambda e, c_=c_, zb=zb, s2=s2: e.scalar_tensor_tensor(c_.t[:, 256:384], zb.t[:, 256:384], s2.t[:, 1:2], kvln[:], ALU.mult, ALU.mult), reads=[zb, s2, kvln], pwrites=[c_])
                pt = nps()
                for j in range(3):
                    S.op("pe", lambda e, pt=pt, j=j, c_=c_: e.transpose(pt.t[:, j * 128:(j + 1) * 128], c_.t[:, j * 128:(j + 1) * 128], ident),
                         reads=[c_, cst], writes=[pt] if j == 0 else (), pwrites=[pt] if j > 0 else ())
                c3 = cT3[u]
                evac(c3[:], pt.t[:, 0:384].rearrange("p (j f) -> p j f", f=128), reads=[pt], writes=[c3])
                qk_ = qk[u]
                kv_ = kv[u]
                for nh in range(2):
                    ps = nps()
                    for k in range(2):
                        mm(ps, ps.t[:, 0:384], c3.t[:, k, :], wuq.t[:, k, nh * 384:(nh + 1) * 384], k == 0, k == 1, [c3, wuq])
                    evac(qk_.t[:, nh * 4:(nh + 1) * 4, :], ps.t[:, 0:384].rearrange("p (h r) -> p h r", r=96), reads=[ps], writes=[qk_] if nh == 0 else (), pwrites=[qk_] if nh > 0 else ())
                for nh in range(2):
                    ps = nps()
                    mm(ps, ps.t[:, :], c3.t[:, 2, :], wukv.t[:, nh * 512:(nh + 1) * 512], True, True, [c3, wukv])
                    evac(kv_.t[:, nh * 4:(nh + 1) * 4, :], ps.t[:, :].rearrange("p (h r) -> p h r", r=128), reads=[ps], writes=[kv_] if nh == 0 else (), pwrites=[kv_] if nh > 0 else ())
                S.op("dve", lambda e, qk_=qk_, kv_=kv_: e.tensor_copy(qk_.t[:, 8:16, 0:64], kv_.t[:, :, 0:64]), reads=[kv_], pwrites=[qk_])
                S.op("dve", lambda e, qk_=qk_, zb=zb: e.tensor_copy(qk_.t[:, 8:16, 64:96], zb.t[:, 384:416].unsqueeze(1).to_broadcast([128, 8, 32])), reads=[zb], pwrites=[qk_])
                S.dma("pool", VB[b, i * 128:(i + 1) * 128, :].rearrange("p (h v) -> p h v", v=64), kv_.t[:, :, 64:128], reads=[kv_], pwrites=[bVB[b]])
                sq_ = ssq[u]
                S.op("pool", lambda e, qk_=qk_: e.tensor_tensor(junk.t[:, 0:1536], qk_.t[:, :, :].rearrange("p h r -> p (h r)"), qk_.t[:, :, :].rearrange("p h r -> p (h r)"), ALU.mult), reads=[qk_], writes=[junk])
                S.op("dve", lambda e, sq_=sq_: e.tensor_reduce(sq_[:], junk.t[:, 0:1536].rearrange("p (h r) -> p h r", r=96), AX.X, ALU.add), reads=[junk], writes=[sq_])
                rstd_from_ss(None, sq_, 16, 1.0 / 96, EPS)
                S.op("dve", lambda e, qk_=qk_, sq_=sq_: e.tensor_tensor(qk_[:], qk_[:], sq_.t[:, 0:16].unsqueeze(2).to_broadcast([128, 16, 96]), ALU.mult), reads=[qk_, sq_], writes=[qk_])
                S.op("dve", lambda e, qk_=qk_: e.tensor_tensor(qk_.t[:, :, :].rearrange("p (a h) r -> p a h r", a=2), qk_.t[:, :, :].rearrange("p (a h) r -> p a h r", a=2),
                                                              bqkg.t[:, :].rearrange("p (a r) -> p a r", a=2).unsqueeze(2).to_broadcast([128, 2, 8, 96]), ALU.mult), reads=[qk_, bqkg], writes=[qk_])
                x3_buf[0] = qk_
                rope(st, qk_.t[:, :, 64:96], 16, 32, rb, rt1, rt2)
                qT_ = qkT[u]
                for g0 in range(0, 16, 4):
                    pt = nps()
                    for j in range(4):
                        S.op("pe", lambda e, pt=pt, j=j, g0=g0, qk_=qk_: e.transpose(pt.t[0:96, j * 128:(j + 1) * 128], qk_.t[:, g0 + j, :], ident),
                             reads=[qk_, cst], writes=[pt] if j == 0 else (), pwrites=[pt] if j > 0 else ())
                    evac(qT_.t[0:96, g0:g0 + 4, :], pt.t[0:96, :].rearrange("p (j f) -> p j f", f=128), reads=[pt], writes=[qT_] if g0 == 0 else (), pwrites=[qT_] if g0 > 0 else ())
                S.dma("pool", QKT[b, :, :, i * 128:(i + 1) * 128].rearrange("h p t -> p h t"), qT_[:], reads=[qT_], pwrites=[bQKT[b]])
                S.dma("pool", VC[b, i * 128:(i + 1) * 128, :], zc.t[:, 640:768], reads=[zc], pwrites=[bVC[b]])
                S.op("pool", lambda e, zc=zc: e.tensor_tensor(junk.t[:, 0:640], zc.t[:, 0:640], zc.t[:, 0:640], ALU.mult), reads=[zc], writes=[junk])
                S.op("dve", lambda e, sq_=sq_: e.tensor_reduce(sq_.t[:, 0:10], junk.t[:, 0:640].rearrange("p (h r) -> p h r", r=64), AX.X, ALU.add), reads=[junk], writes=[sq_])
                rstd_from_ss(None, sq_, 10, 1.0 / 64, EPS)
                z3 = zc.t[:, 0:640].rearrange("p (h r) -> p h r", r=64)
                S.op("dve", lambda e, z3=z3, sq_=sq_, zc=zc: e.tensor_tensor(z3, z3, sq_.t[:, 0:10].unsqueeze(2).to_broadcast([128, 10, 64]), ALU.mult), reads=[zc, sq_], writes=[zc])
                S.op("dve", lambda e, z3=z3, zc=zc: e.tensor_tensor(z3[:, 0:8, :], z3[:, 0:8, :], cqn.t[:, :].unsqueeze(1).to_broadcast([128, 8, 64]), ALU.mult), reads=[zc, cqn], writes=[zc])
                S.op("dve", lambda e, z3=z3, zc=zc: e.tensor_tensor(z3[:, 8:10, :], z3[:, 8:10, :], ckn.t[:, :].unsqueeze(1).to_broadcast([128, 2, 64]), ALU.mult), reads=[zc, ckn], writes=[zc])
                x3_buf[0] = zc
                rope(st, z3, 10, 64, rc, rt1, rt2)
                qTc_ = qkTc[u]
                for g0 in range(0, 10, 4):
                    ng = min(4, 10 - g0)
                    pt = nps()
                    for j in range(ng):
                        S.op("pe", lambda e, pt=pt, j=j, g0=g0, z3=z3: e.transpose(pt.t[0:64, j * 128:(j + 1) * 128], z3[:, g0 + j, :], ident),
                             reads=[zc, cst], writes=[pt] if j == 0 else (), pwrites=[pt] if j > 0 else ())
                    evac(qTc_.t[0:64, g0:g0 + ng, :], pt.t[0:64, 0:ng * 128].rearrange("p (j f) -> p j f", f=128), reads=[pt], writes=[qTc_] if g0 == 0 else (), pwrites=[qTc_] if g0 > 0 else ())
                S.dma("pool", QKTC[b, :, :, i * 128:(i + 1) * 128].rearrange("h p t -> p h t"), qTc_[:], reads=[qTc_], pwrites=[bQKTC[b]])
            phase_end(st)
        if stop_after in ("P1a", "P1", "P2", "P3"):
            break

        st = ExitStack()
        Sb = [[sbt(st, [128, 8, 64], "S%d_%d" % (d, k)) for k in range(2)] for d in range(2)]
        for d in range(2):
            S.op("dve", lambda e, d=d: e.memset(Sb[d][0][:], 0.0), writes=[Sb[d][0]])
        Sw = [sbt(st, [128, 8, 64], "Sw") for d in range(2)]
        tA = [[sbt(st, [128, 8, 64], "tA") for _ in range(2)] for d in range(2)]
        tB = [sbt(st, [128, 8, 64], "tB") for d in range(2)]
        tC = [[sbt(st, [128, 8, 64], "tC") for _ in range(2)] for d in range(2)]
        t4 = [[sbt(st, [128, 8, 64], "t4") for _ in range(2)] for d in range(2)]
        SCb = [[sbt(st, [128, 5, 8, 64], "SCb") for _ in range(2)] for d in range(2)]
        Vb = [[sbt(st, [64, 1024], "Vb") for _ in range(2)] for d in range(2)]
        ysb = [sbt(st, [128, 512], "ysb") for d in range(2)]
        psV = [PS[0], PS[1]]
        psSA = [PS[2], PS[3]]
        psY = PS[4]
        psS5, psO5, psD5 = PS[5], PS[6], PS[7]
        KTs = [sbt(st, [96, T], "KT") for _ in range(2)]
        QTs = [sbt(st, [96, T], "QT") for _ in range(2)]
        GTs = [sbt(st, [64, T], "GTh") for _ in range(2)]
        Vhs = [sbt(st, [128, NT, 64], "Vh") for _ in range(2)]
        Pbs = [sbt(st, [128, 512], "Pb") for _ in range(3)]
        rdn = [sbt(st, [64, 512], "rdn") for _ in range(2)]
        uob = [sbt(st, [64, T], "uob") for _ in range(2)]
        scale_b = 96 ** -0.5

        def p5_units():
            units = [(b, h) for b in range(NB) for h in range(8)]

            def issue_loads(idx):
                b, h = units[idx]
                u = idx % 2
                S.dma("sp", KTs[u][:], QKT[b, 8 + h, :, :], reads=[bQKT[b]], writes=[KTs[u]])
                S.dma("sp", QTs[u][:], QKT[b, h, :, :], reads=[bQKT[b]], writes=[QTs[u]])
                S.dma("sp", GTs[u][:], GT[b, 0, h, :, :], reads=[bGT[b]], writes=[GTs[u]])
                S.dma("sp", Vhs[u][:], VB[b, :, h * 64:(h + 1) * 64].rearrange("(i p) v -> p i v", p=128), reads=[bVB[b]], writes=[Vhs[u]])
            issue_loads(0)
            yield
            pi = 0
            cc = 0
            for idx, (b, h) in enumerate(units):
                u = idx % 2
                KT, QT, GTh, Vh, uo = KTs[u], QTs[u], GTs[u], Vhs[u], uob[u]
                if idx + 1 < len(units):
                    issue_loads(idx + 1)
                qchunks = [(LC + j * 512, 512, list(range(NT))) for j in range(4)]
                if not last:
                    qchunks.append((0, 256, [0, 1]))
                for ci, (q0, qn, kts) in enumerate(qchunks):
                    nk = len(kts)

                    def emit_pv(ki, kt, Pb, qn=qn, Vh=Vh, nk=nk):
                        mm(psO5, psO5.t[0:64, 0:qn], Vh.t[:, kt, :], Pb.t[:, 0:qn], ki == 0, ki == nk - 1, [Vh, Pb])
                        mm(psD5, psD5.t[0:64, 0:qn], ones64, Pb.t[:, 0:qn], ki == 0, ki == nk - 1, [cst, Pb])
                    prev = None
                    for ki, kt in enumerate(kts):
                        if prev is not None:
                            emit_pv(*prev)
                        mm(psS5, psS5.t[:, 0:qn], KT.t[0:96, kt * 128:(kt + 1) * 128], QT.t[0:96, q0:q0 + qn], True, True, [KT, QT])
                        Pb = Pbs[pi % 3]
                        pi += 1
                        S.op("act", lambda e, Pb=Pb, qn=qn: e.activation(Pb.t[:, 0:qn], psS5.t[:, 0:qn], AF.Exp, scale=scale_b), reads=[psS5], writes=[Pb])
                        prev = (ki, kt, Pb)
                        yield
                    emit_pv(*prev)
                    yield
                    rd = rdn[cc % 2]
                    cc += 1
                    S.op("dve", lambda e, rd=rd, qn=qn: e.reciprocal(rd.t[:, 0:qn], psD5.t[0:64, 0:qn]), reads=[psD5], writes=[rd])
                    S.op("dve", lambda e, rd=rd, qn=qn: e.tensor_tensor(rd.t[:, 0:qn], psO5.t[0:64, 0:qn], rd.t[:, 0:qn], ALU.mult), reads=[psO5, rd], writes=[rd])
                    S.op("pool", lambda e, rd=rd, uo=uo, GTh=GTh, q0=q0, qn=qn: e.tensor_tensor(uo.t[:, q0:q0 + qn], rd.t[:, 0:qn], GTh.t[:, q0:q0 + qn], ALU.mult), reads=[rd, GTh],
                         writes=[uo] if ci == 0 else (), pwrites=[uo] if ci > 0 else ())
                    yield
                if last:
                    S.dma("sp", UT[b, 0, h, :, LC:T], uo.t[:, LC:T], reads=[uo], pwrites=[bUT[b]])
                else:
                    S.dma("sp", UT[b, 0, h, :, :], uo[:], reads=[uo], pwrites=[bUT[b]])
                yield
        gen5 = p5_units() if os.environ.get("KDBG_NO_P5") is None else iter(())
        NBLK = T // 64
        n_scan_blocks = int(os.environ.get("KDBG_SCAN_BLOCKS", NBLK))

        def tok0(d, B):
            if d == 0:
                return 64 * B
            if B < 4:
                return LC - 64 * (B + 1)
            return T - 64 * (B - 3)

        def v3(buf):
            return buf.t[:, :, :]

        def f2(buf):
            return buf.t[:, :, :].rearrange("p a v -> p (a v)")

        gstep = 0
        for B in range(n_scan_blocks):
            u = B % 2
            for d in range(2):
                t0 = tok0(d, B)
                for a in range(5):
                    S.dma("sp", SCb[d][u].t[:, a, :, :], SC[d, :, a, :, t0:t0 + 64], reads=[bSC], writes=[SCb[d][u]] if a == 0 else (), pwrites=[SCb[d][u]] if a else ())
                S.dma("sp", Vb[d][u][:], VT[t0:t0 + 64, :], reads=[bVT], writes=[Vb[d][u]])
            sc = [SCb[d][u] for d in range(2)]
            pend = None

            def emit_t4(pd):
                s_, tls_, Sn_, par_ = pd
                for d in range(2):
                    for j in range(8):
                        S.op("act", lambda e, d=d, j=j, o=t4[d][par_], sn=Sn_[d], r_=sc[d].t[:, 4, j, tls_[d]:tls_[d] + 1]: e.activation(o.t[:, j, :], sn.t[:, j, :], AF.Identity, scale=r_),
                             reads=[Sn_[d], sc[d]], writes=[t4[d][par_]] if j == 0 else (), pwrites=[t4[d][par_]] if j else ())

            def emit_y(pd):
                s_, tls_, Sn_, par_ = pd
                for d in range(2):
                    t32 = tls_[d] % 32
                    first = (s_ % 32 == 0)
                    S.op("pe", lambda e, d=d, i_=t4[d][par_], t32=t32, s_=s_: e.matmul(psY.t[d * 64:(d + 1) * 64, :], Z3[:, 31 - t32:95 - t32], f2(i_), start=(s_ % 32 == 0), stop=(s_ % 32 == 31)),
                         reads=[t4[d][par_], cst], writes=[psY] if (first and d == 0) else (), pwrites=() if (first and d == 0) else [psY])

            def evac_half(sh, B=B):
                yb = ysb[sh]
                S.op("act", lambda e, yb=yb: e.copy(yb[:], psY.t[:, :]), reads=[psY], writes=[yb])
                for d in range(2):
                    hb = sh if d == 0 else 1 - sh
                    tb = tok0(d, B) + 32 * hb
                    for c2 in range(2):
                        S.dma("pool", YD[d, tb:tb + 32, c2 * 512:(c2 + 1) * 512], yb.t[d * 64 + c2 * 32:d * 64 + c2 * 32 + 32, :], reads=[yb], pwrites=[bYD])

            for s in range(64):
                tls = [s, 63 - s]
                par = gstep % 2
                Sc = [Sb[d][par] for d in range(2)]
                Sn = [Sb[d][1 - par] for d in range(2)]
                gstep += 1

                def bcs(d, a):
                    return sc[d].t[:, a, :, tls[d]].unsqueeze(2).to_broadcast([128, 8, 64])
                pv = [psV[d] for d in range(2)]
                ta = [tA[d][s % 2] for d in range(2)]
                tc = [tC[d][s % 2] for d in range(2)]
                for d in range(2):
                    for c2 in range(2):
                        S.op("pe", lambda e, d=d, c2=c2, p=pv[d], tl=tls[d], vb_=Vb[d][u]: e.matmul(p.t[c2 * 64:(c2 + 1) * 64, :], ident[0:64, tl:tl + 1].to_broadcast([64, 64]),
                                                                                              vb_.t[0:64, c2 * 512:(c2 + 1) * 512], start=True, stop=True),
                             reads=[Vb[d][u], cst], writes=[pv[d]] if c2 == 0 else (), pwrites=[pv[d]] if c2 == 1 else ())
                for d in range(2):
                    S.op("dve", lambda e, d=d, o=ta[d], sc_=Sc[d], kb=bcs(d, 3): e.tensor_tensor(v3(o), v3(sc_), kb, ALU.mult), reads=[Sc[d], sc[d]], writes=[ta[d]])
                for d in range(2):
                    S.op("pe", lambda e, d=d, i_=ta[d]: e.matmul(psSA[d].t[:, :], bones, f2(i_), start=True, stop=True), reads=[ta[d], cst], writes=[psSA[d]])
                for d in range(2):
                    S.op("dve", lambda e, d=d, o=tc[d], kb=bcs(d, 2), p=pv[d]: e.tensor_tensor(v3(o), p.t[:, :].rearrange("p (a v) -> p a v", v=64), kb, ALU.mult), reads=[pv[d], sc[d]], writes=[tc[d]])
                for d in range(2):
                    S.op("pool", lambda e, d=d, sc_=Sc[d], wb_=bcs(d, 0): e.tensor_tensor(v3(Sw[d]), v3(sc_), wb_, ALU.mult), reads=[Sc[d], sc[d]], writes=[Sw[d]])
                    S.op("pool", lambda e, d=d, t_=tc[d]: e.tensor_tensor(v3(Sw[d]), v3(Sw[d]), v3(t_), ALU.add), reads=[Sw[d], tc[d]], writes=[Sw[d]])
                if pend is not None:
                    emit_t4(pend)
                    emit_y(pend)
                    if s == 32:
                        evac_half(0)
                next(gen5, None)
                S.op("dve", lambda e, kb=bcs(0, 1): e.tensor_tensor(v3(tB[0]), psSA[0].t[:, :].rearrange("p (a v) -> p a v", v=64), kb, ALU.mult), reads=[psSA[0], sc[0]], writes=[tB[0]])
                S.op("dve", lambda e, sn=Sn[0]: e.tensor_tensor(v3(sn), v3(Sw[0]), v3(tB[0]), ALU.subtract), reads=[Sw[0], tB[0]], writes=[Sn[0]])
                S.op("dve", lambda e, kb=bcs(1, 1): e.tensor_tensor(v3(tB[1]), psSA[1].t[:, :].rearrange("p (a v) -> p a v", v=64), kb, ALU.mult), reads=[psSA[1], sc[1]], writes=[tB[1]])
                S.op("dve", lambda e, sn=Sn[1]: e.tensor_tensor(v3(sn), v3(Sw[1]), v3(tB[1]), ALU.subtract), reads=[Sw[1], tB[1]], writes=[Sn[1]])
                pend = (s, tls, Sn, s % 2)
            emit_t4(pend)
            emit_y(pend)
            evac_half(1)
        for _ in gen5:
            pass
        phase_end(st)
        if stop_after in ("P4", "P5"):
            break

        st = ExitStack()
        esk = bc_load(st, W["c_sink"][l, :, :], 8, parts=64, name="esk")
        S.op("act", lambda e: e.activation(esk[:], esk[:], AF.Exp), reads=[esk], writes=[esk])
        KTg = [sbt(st, [64, T], "KTg") for _ in range(2)]
        Q4 = [[sbt(st, [64, T], "Q4") for _ in range(4)] for _ in range(2)]
        G4 = [[sbt(st, [64, T], "G4") for _ in range(4)] for _ in range(2)]
        Vg = [sbt(st, [128, NT, 64], "Vg") for _ in range(2)]
        Pcs = [sbt(st, [128, 512], "Pc") for _ in range(3)]
        rdc = [sbt(st, [64, 512], "rdc") for _ in range(2)]
        uoc = [sbt(st, [64, 4, 128], "uoc") for _ in range(2)]
        scale_c = 64 ** -0.5
        pi = 0
        bi = 0
        for b in range(NB):
            for g in range(2):
                u = (b * 2 + g) % 2
                S.dma("sp", KTg[u][:], QKTC[b, 8 + g, :, :], reads=[bQKTC[b]], writes=[KTg[u]])
                S.dma("sp", Vg[u][:], VC[b, :, g * 64:(g + 1) * 64].rearrange("(i p) v -> p i v", p=128), reads=[bVC[b]], writes=[Vg[u]])
                for hh in range(4):
                    S.dma("sp", Q4[u][hh][:], QKTC[b, 4 * g + hh, :, :], reads=[bQKTC[b]], writes=[Q4[u][hh]])
                    S.dma("sp", G4[u][hh][:], GT[b, 1, 4 * g + hh, :, :], reads=[bGT[b]], writes=[G4[u][hh]])
                blocks = list(range(2, NT))
                if not last:
                    blocks = [0, 1] + blocks
                for n in blocks:
                    if n < 2:
                        kts = [(0, None), (1, None)]
                    else:
                        kts = [(0, None), (1, None)]
                        if n - 1 >= 2:
                            kts.append((n - 1, mask_lo))
                        kts.append((n, None))
                        if n + 1 < NT:
                            kts.append((n + 1, mask_hi))
                    psO, psD = PS[(bi % 2) * 2], PS[(bi % 2) * 2 + 1]
                    for ki, (kt, msk) in enumerate(kts):
                        psS = PS[4 + pi % 4]
                        for hh in range(4):
                            S.op("pe", lambda e, psS=psS, hh=hh, kt=kt, n=n, u=u: e.matmul(psS.t[:, hh * 128:(hh + 1) * 128], KTg[u].t[0:64, kt * 128:(kt + 1) * 128],
                                                                                       Q4[u][hh].t[0:64, n * 128:(n + 1) * 128], start=True, stop=True),
                                 reads=[KTg[u], Q4[u][hh]], writes=[psS] if hh == 0 else (), pwrites=[psS] if hh > 0 else ())
                        Pc = Pcs[pi % 3]
                        pi += 1
                        S.op("act", lambda e, Pc=Pc, psS=psS: e.activation(Pc[:], psS.t[:, :], AF.Exp, scale=scale_c), reads=[psS], writes=[Pc])
                        if msk is not None:
                            S.op("dve", lambda e, Pc=Pc, msk=msk: e.tensor_tensor(Pc.t[:, :].rearrange("p (h q) -> p h q", h=4), Pc.t[:, :].rearrange("p (h q) -> p h q", h=4),
                                                                                   msk.unsqueeze(1).to_broadcast([128, 4, 128]), ALU.mult), reads=[Pc, cst], writes=[Pc])
                        mm(psO, psO.t[0:64, :], Vg[u].t[:, kt, :], Pc.t[:, :], ki == 0, ki == len(kts) - 1, [Vg[u], Pc])
                        mm(psD, psD.t[0:64, :], ones64, Pc.t[:, :], ki == 0, ki == len(kts) - 1, [cst, Pc])
                    rd = rdc[bi % 2]
                    uo = uoc[bi % 2]
                    bi += 1
                    S.op("dve", lambda e, rd=rd, psD=psD, g=g: e.tensor_tensor(rd.t[:, :].rearrange("p (h q) -> p h q", h=4), psD.t[0:64, :].rearrange("p (h q) -> p h q", h=4),
                                                                                esk.t[:, 4 * g:4 * g + 4].unsqueeze(2).to_broadcast([64, 4, 128]), ALU.add), reads=[psD, esk], writes=[rd])
                    S.op("dve", lambda e, rd=rd: e.reciprocal(rd[:], rd[:]), reads=[rd], writes=[rd])
                    S.op("dve", lambda e, rd=rd, psO=psO: e.tensor_tensor(rd[:], psO.t[0:64, :], rd[:], ALU.mult), reads=[psO, rd], writes=[rd])
                    for hh in range(4):
                        S.op("pool", lambda e, rd=rd, uo=uo, hh=hh, n=n, u=u: e.tensor_tensor(uo.t[:, hh, :], rd.t[:, hh * 128:(hh + 1) * 128], G4[u][hh].t[:, n * 128:(n + 1) * 128], ALU.mult),
                             reads=[rd, G4[u][hh]], writes=[uo] if hh == 0 else (), pwrites=[uo] if hh > 0 else ())
                    S.dma("pool", UT[b, 1, 4 * g:4 * g + 4, :, n * 128:(n + 1) * 128].rearrange("h p t -> p h t"), uo[:], reads=[uo], pwrites=[bUT[b]])
        phase_end(st)
        if stop_after == "P6":
            break

        st = ExitStack()
        gng = bc_load(st, W["gn_g"][l, :, :], 512, name="gng")
        gnb = bc_load(st, W["gn_b"][l, :, :], 512, name="gnb")
        wbo0 = sbt(st, [128, 4, D], "wbo0")
        wbo1 = sbt(st, [128, 4, D], "wbo1")
        wbo2 = sbt(st, [128, 4, D], "wbo2")
        wo = sbt(st, [128, 8, D], "wo")
        S.dma("sp", wbo0[:], W["wbo"][l, 0, :, :].rearrange("(k p) n -> p k n", p=128), reads=[bW], writes=[wbo0])
        S.dma("sp", wbo1[:], W["wbo"][l, 1, :, :].rearrange("(k p) n -> p k n", p=128), reads=[bW], writes=[wbo1])
        S.dma("sp", wbo2[:], W["wbo"][l, 2, :, :].rearrange("(k p) n -> p k n", p=128), reads=[bW], writes=[wbo2])
        S.dma("sp", wo[:], W["w_out"][l, :, :].rearrange("(k p) n -> p k n", p=128), reads=[bW], writes=[wo])
        gateb = [sbt(st, [128, D], "gateb") for _ in range(3)]
        for r in range(3):
            S.dma("sp", gateb[r][:], MODG[r:r + 1, :].partition_broadcast(128), reads=[bMODG], writes=[gateb[r]])
        y0s = [sbt(st, [128, 512], "y0") for _ in range(2)]
        y1s = [sbt(st, [128, 512], "y1") for _ in range(2)]
        vts = [sbt(st, [128, 512], "vt") for _ in range(2)]
        sgas = [sbt(st, [128, 512], "sga") for _ in range(2)]
        bss = [sbt(st, [128, 8], "bs") for _ in range(2)]
        sgs = [sbt(st, [128, 3 * D], "sg") for _ in range(2)]
        xts = [sbt(st, [128, D], "xt") for _ in range(2)]
        utb = [sbt(st, [128, 4, 128], "utb") for _ in range(2)]
        utc = [sbt(st, [128, 4, 128], "utc") for _ in range(2)]
        st8 = sbt(st, [128, 8], "st8")
        yc = sbt(st, [128, 512], "yc")
        ysq = sbt(st, [128, 512], "ysq")
        uAT = sbt(st, [128, 4, 128], "uAT")
        mt = sbt(st, [128, D], "mt")
        tmpm = sbt(st, [128, 512], "tmpm")
        mTt = sbt(st, [128, 8, 128], "mTt")
        xo = [sbt(st, [128, D], "xo") for _ in range(2)]
        it = 0
        for b in range(NB):
            for i in (range(2, NT) if last else range(NT)):
                u = it % 2
                it += 1
                r = b if i >= 2 else 2
                y0, y1, vt, sga, bs_, sg, xt = y0s[u], y1s[u], vts[u], sgas[u], bss[u], sgs[u], xts[u]
                rows = slice(i * 128, (i + 1) * 128)

                def perm_src(a, c):
                    return a.rearrange("p (c bb qq v) -> p c bb qq v", c=2, bb=2, qq=4)[:, c, b, :, :]

                def perm_dst(tile, c):
                    return tile.t[:, :].rearrange("p (qq c v) -> p c qq v", qq=4, c=2)[:, c, :, :]
                for c in range(2):
                    S.dma("sp", perm_dst(y0, c), perm_src(YD[0, rows, :], c), reads=[bYD], writes=[y0] if c == 0 else (), pwrites=[y0] if c else ())
                    S.dma("sp", perm_dst(y1, c), perm_src(YD[1, rows, :], c), reads=[bYD], writes=[y1] if c == 0 else (), pwrites=[y1] if c else ())
                    S.dma("sp", perm_dst(vt, c), perm_src(VT[rows, :], c), reads=[bVT], writes=[vt] if c == 0 else (), pwrites=[vt] if c else ())
                S.dma("sp", sga[:], ZTM[b, rows, ZO_GA:ZO_GA + 512], reads=[bZTM[b]], writes=[sga])
                S.dma("sp", bs_[:], BS[b, rows, :], reads=[bBS[b]], writes=[bs_])
                S.dma("sp", sg[:], ZTM[b, rows, ZO_ZG:ZO_ZG + 3 * D], reads=[bZTM[b]], writes=[sg])
                S.dma("sp", xt[:], Xcur[b, rows, :], reads=[bXcur], writes=[xt])
                S.dma("sp", utb[u][:], UT[b, 0, :, :, :].rearrange("h p t -> (h p) t").rearrange("(k q) t -> q k t", q=128)[:, :, rows], reads=[bUT[b]], writes=[utb[u]])
                S.dma("sp", utc[u][:], UT[b, 1, :, :, :].rearrange("h p t -> (h p) t").rearrange("(k q) t -> q k t", q=128)[:, :, rows], reads=[bUT[b]], writes=[utc[u]])
                S.op("pool", lambda e, y0=y0, y1=y1: e.tensor_tensor(y0[:], y0[:], y1[:], ALU.add), reads=[y0, y1], writes=[y0])
                y3 = y0.t[:, :].rearrange("p (h v) -> p h v", v=64)
                yc3 = yc.t[:, :].rearrange("p (h v) -> p h v", v=64)
                S.op("dve", lambda e, y3=y3: e.tensor_reduce(st8[:], y3, AX.X, ALU.add), reads=[y0], writes=[st8])
                S.op("dve", lambda e: e.tensor_scalar(st8[:], st8[:], 1.0 / 64, None, ALU.mult), reads=[st8], writes=[st8])
                S.op("dve", lambda e, y3=y3, yc3=yc3: e.tensor_tensor(yc3, y3, st8.t[:, :].unsqueeze(2).to_broadcast([128, 8, 64]), ALU.subtract), reads=[y0, st8], writes=[yc])
                S.op("pool", lambda e: e.tensor_tensor(ysq[:], yc[:], yc[:], ALU.mult), reads=[yc], writes=[ysq])
                S.op("dve", lambda e: e.tensor_reduce(st8[:], ysq.t[:, :].rearrange("p (h v) -> p h v", v=64), AX.X, ALU.add), reads=[ysq], writes=[st8])
                rstd_from_ss(None, st8, 8, 1.0 / 64, GN_EPS)
                S.op("dve", lambda e, yc3=yc3: e.tensor_tensor(yc3, yc3, st8.t[:, :].unsqueeze(2).to_broadcast([128, 8, 64]), ALU.mult), reads=[yc, st8], writes=[yc])
                S.op("dve", lambda e: e.tensor_tensor(yc[:], yc[:], gng[:], ALU.mult), reads=[yc, gng], writes=[yc])
                S.op("pool", lambda e: e.tensor_tensor(yc[:], yc[:], gnb[:], ALU.add), reads=[yc, gnb], writes=[yc])
                S.op("dve", lambda e, vt=vt, bs_=bs_: e.tensor_tensor(vt.t[:, :].rearrange("p (h v) -> p h v", v=64), vt.t[:, :].rearrange("p (h v) -> p h v", v=64),
                                                                      bs_.t[:, :].unsqueeze(2).to_broadcast([128, 8, 64]), ALU.mult), reads=[vt, bs_], writes=[vt])
                S.op("pool", lambda e, vt=vt: e.tensor_tensor(yc[:], yc[:], vt[:], ALU.add), reads=[yc, vt], writes=[yc])
                S.op("dve", lambda e, sga=sga: e.tensor_tensor(yc[:], yc[:], sga[:], ALU.mult), reads=[yc, sga], writes=[yc])
                if "uA" in dbg and l == 0 and b == 0 and i == 2:
                    uAd = nc.dram_tensor("uAd", [128, 512], F32, kind="ExternalOutput").ap()
                    S.dma("pool", uAd[:, :], yc[:], reads=[yc])
                pt = nps()
                for j in range(4):
                    S.op("pe", lambda e, pt=pt, j=j: e.transpose(pt.t[:, j * 128:(j + 1) * 128], yc.t[:, j * 128:(j + 1) * 128], ident),
                         reads=[yc, cst], writes=[pt] if j == 0 else (), pwrites=[pt] if j > 0 else ())
                evac(uAT[:], pt.t[:, :].rearrange("p (j f) -> p j f", f=128), reads=[pt], writes=[uAT])
                for cg in range(2):
                    cs_ = slice(cg * 512, (cg + 1) * 512)
                    pA, pB, pC = nps(), nps(), nps()
                    for k in range(4):
                        mm(pA, pA.t[:, :], uAT.t[:, k, :], wbo0.t[:, k, cs_], k == 0, k == 3, [uAT, wbo0])
                    for k in range(4):
                        mm(pB, pB.t[:, :], utb[u].t[:, k, :], wbo1.t[:, k, cs_], k == 0, k == 3, [utb[u], wbo1])
                    for k in range(4):
                        mm(pC, pC.t[:, :], utc[u].t[:, k, :], wbo2.t[:, k, cs_], k == 0, k == 3, [utc[u], wbo2])
                    S.op("dve", lambda e, pA=pA, cs_=cs_, sg=sg, cg=cg: e.tensor_tensor(mt.t[:, cs_], pA.t[:, :], sg.t[:, cg * 512:(cg + 1) * 512], ALU.mult), reads=[pA, sg],
                         writes=[mt] if cg == 0 else (), pwrites=[mt] if cg > 0 else ())
                    S.op("dve", lambda e, pB=pB, sg=sg, cg=cg: e.tensor_tensor(tmpm[:], pB.t[:, :], sg.t[:, D + cg * 512:D + (cg + 1) * 512], ALU.mult), reads=[pB, sg], writes=[tmpm])
                    S.op("pool", lambda e, cs_=cs_: e.tensor_tensor(mt.t[:, cs_], mt.t[:, cs_], tmpm[:], ALU.add), reads=[mt, tmpm], writes=[mt])
                    S.op("dve", lambda e, pC=pC, sg=sg, cg=cg: e.tensor_tensor(tmpm[:], pC.t[:, :], sg.t[:, 2 * D + cg * 512:2 * D + (cg + 1) * 512], ALU.mult), reads=[pC, sg], writes=[tmpm])
                    S.op("pool", lambda e, cs_=cs_: e.tensor_tensor(mt.t[:, cs_], mt.t[:, cs_], tmpm[:], ALU.add), reads=[mt, tmpm], writes=[mt])
                for half in range(2):
                    pt = nps()
                    for j in range(4):
                        k = half * 4 + j
                        S.op("pe", lambda e, pt=pt, j=j, k=k: e.transpose(pt.t[:, j * 128:(j + 1) * 128], mt.t[:, k * 128:(k + 1) * 128], ident),
                             reads=[mt, cst], writes=[pt] if j == 0 else (), pwrites=[pt] if j > 0 else ())
                    evac(mTt.t[:, half * 4:(half + 1) * 4, :], pt.t[:, :].rearrange("p (j f) -> p j f", f=128), reads=[pt], writes=[mTt] if half == 0 else (), pwrites=[mTt] if half > 0 else ())
                xo_ = xo[u]
                for cg in range(2):
                    cs_ = slice(cg * 512, (cg + 1) * 512)
                    pO = nps()
                    for k in range(8):
                        mm(pO, pO.t[:, :], mTt.t[:, k, :], wo.t[:, k, cs_], k == 0, k == 7, [mTt, wo])
                    S.op("dve", lambda e, pO=pO, cs_=cs_, xo_=xo_, r=r: e.tensor_tensor(xo_.t[:, cs_], pO.t[:, :], gateb[r].t[:, cs_], ALU.mult), reads=[pO, gateb[r]],
                         writes=[xo_] if cg == 0 else (), pwrites=[xo_] if cg > 0 else ())
                    S.op("pool", lambda e, cs_=cs_, xo_=xo_, xt=xt: e.tensor_tensor(xo_.t[:, cs_], xo_.t[:, cs_], xt.t[:, cs_], ALU.add), reads=[xo_, xt], writes=[xo_])
                if last:
                    S.dma("pool", yout[b, (i - 2) * 128:(i - 1) * 128, :], xo_[:], reads=[xo_], pwrites=[bX[2]])
                else:
                    S.dma("pool", Xnext[b, rows, :], xo_[:], reads=[xo_], pwrites=[bXnext])
        phase_end(st)
        Xcur, bXcur = Xnext, bXnext

    S.barrier()
    S.emit()
    outer.close()
    return nc, S.total


def _rope_full(rot_dim):
    grid_w = 64
    t = np.arange(TL)
    row = (t // grid_w).astype(np.float32)
    col = (t % grid_w).astype(np.float32)
    axis_dim = rot_dim // 2
    inv = (np.float32(10000.0) ** (-(2.0 * np.arange(axis_dim // 2, dtype=np.float32)) / np.float32(axis_dim))).astype(np.float32)
    ang = np.concatenate([row[:, None] * inv, col[:, None] * inv], axis=-1).astype(np.float32)
    cos, sin = np.cos(ang).astype(np.float32), np.sin(ang).astype(np.float32)
    q = rot_dim // 4
    cr, cc, sr, sc = cos[:, :q], cos[:, q:], sin[:, :q], sin[:, q:]
    cosF = np.concatenate([cr, cr, cc, cc], axis=-1)
    sinF = np.concatenate([-sr, sr, -sc, sc], axis=-1)
    out = np.zeros((T, 2 * rot_dim), np.float32)
    out[:LC, :rot_dim] = 1.0
    out[LC:, :rot_dim] = cosF
    out[LC:, rot_dim:] = sinF
    return out


def _consts():
    c = np.zeros((128, 1024), np.float32)
    c[:, 0:128] = np.eye(128, dtype=np.float32)
    c[0:64, 128:192] = 1.0
    c[64:128, 192:256] = 1.0
    c[0:64, 256 + 127] = 1.0
    c[64:128, 256 + 191] = 1.0
    kj = np.arange(128)[:, None]
    qi = np.arange(128)[None, :]
    c[:, 512:640] = (kj >= qi)
    c[:, 640:768] = (kj <= qi)
    c[0:64, 768] = 1.0
    c[64:128, 769] = 1.0
    c[:, 776:840] = 1.0
    c[0:64, 848 + 31] = 1.0
    c[64:128, 848 + 63] = 1.0
    return c


def _fm(a, nch):
    lead = a.shape[:-1]
    return np.ascontiguousarray(np.swapaxes(a.reshape(lead + (nch, 128)), -1, -2))


def prep_shared(inp):
    f = lambda k: np.asarray(inp[k], np.float32)
    L = DEPTH
    sh = {
        "consts": _consts(), "ropeB": _rope_full(32), "ropeC": _rope_full(64),
        "ada_w": f("ada_w"), "ada_bT": _fm(f("ada_b"), 24), "norm_gT": _fm(f("norm_g"), 8),
        "w_in": f("w_in"), "mupT": _fm(f("a_mu_prev"), 14), "munT": _fm(f("a_mu_next"), 14),
        "w0T": np.ascontiguousarray(np.transpose(f("a_w0").reshape(L, 2, 4, 128), (0, 3, 1, 2)).reshape(L, 128, 8)),
        "a0T": np.ascontiguousarray(np.transpose(f("a_a0").reshape(L, 2, 4, 128), (0, 3, 1, 2)).reshape(L, 128, 8)),
        "kkT": _fm(f("a_k_k"), 4), "kaT": _fm(f("a_k_a"), 4), "rkT": _fm(f("a_r_k").reshape(L, 512), 4),
        "w_up": np.ascontiguousarray(f("a_w_up").reshape(L, 128, 512)), "a_up": np.ascontiguousarray(f("a_a_up").reshape(L, 128, 512)),
        "gn_g": f("a_gn_g").reshape(L, 1, 512), "gn_b": f("a_gn_b").reshape(L, 1, 512),
        "q_ln": f("b_q_ln").reshape(L, 1, 256), "kv_ln": f("b_kv_ln").reshape(L, 1, 128),
        "w_uq": f("b_w_uq"), "w_ukv": f("b_w_ukv"),
        "bqk_g": np.ascontiguousarray(np.concatenate([f("b_qn_g"), f("b_kn_g")], axis=-1).reshape(L, 1, 192)),
        "c_qn": f("c_qn_g").reshape(L, 1, 64), "c_kn": f("c_kn_g").reshape(L, 1, 64), "c_sink": f("c_sink").reshape(L, 1, 8),
        "wbo": f("w_branch_out"), "w_out": f("w_out"),
    }
    return sh


def prep_core(inp, core, sh):
    x = np.asarray(inp["x"], np.float32)
    ctx = np.asarray(inp["ctx"], np.float32)
    c = np.asarray(inp["c"], np.float32)
    c_ctx = np.asarray(inp["c_ctx"], np.float32)
    b0 = core * NB
    xin = np.concatenate([ctx[b0:b0 + NB], x[b0:b0 + NB]], axis=1)
    rows = np.stack([c[b0], c[b0 + 1], c_ctx], axis=0)
    cT = np.ascontiguousarray(np.transpose(rows.reshape(3, 8, 128), (2, 1, 0)))
    m = dict(sh)
    m["xin"] = np.ascontiguousarray(xin)
    m["cT"] = cT
    return m


_CACHE = {}


def kernel(**inputs):
    n_cores = 8
    if "nc" not in _CACHE:
        _CACHE["nc"] = build_program()[0]
    nc = _CACHE["nc"]
    sh = prep_shared(inputs)
    in_maps = [prep_core(inputs, c, sh) for c in range(n_cores)]
    res = run_bass_kernel_spmd(nc, in_maps, core_ids=list(range(n_cores)))
    out = np.concatenate([np.asarray(r["yout"], np.float32) for r in res.results], axis=0)
    return out
```

```python
import math
import os
from contextlib import ExitStack

import numpy as np
import concourse.bass as bass
import concourse.mybir as mybir
from concourse.bass_utils import run_bass_kernel_spmd

F32 = mybir.dt.float32
ALU = mybir.AluOpType
AF = mybir.ActivationFunctionType
AX = mybir.AxisListType

D = 1024
NB = 2
LC = 256
TL = 2048
T = LC + TL
NT = T // 128
DEPTH = 2
NIN = 7584
EPS = 1e-6
GN_EPS = 64e-5
ZW = 4768
ZO_GA, ZO_ZB, ZO_ZC, ZO_ZG = 0, 512, 928, 1696
TCH = [(0, 512), (512, 512), (1024, 512), (1536, 512), (2048, 256)]


class Buf:
    __slots__ = ("t", "w", "r", "name", "psum")

    def __init__(self, t=None, name="", psum=False):
        self.t = t
        self.w = {}
        self.r = {}
        self.name = name
        self.psum = psum

    def __getitem__(self, k):
        return self.t[k]


class Sched:
    ENGS = ("pe", "act", "dve", "pool", "sp")
    QS = ("sp", "pool")

    def __init__(self, nc, stack, ndma=8):
        self.nc = nc
        self.prog = {e: [] for e in self.ENGS}
        self.sem = {e: stack.enter_context(nc.semaphore("s_" + e)) for e in self.ENGS}
        self.cnt = {e: 0 for e in self.ENGS}
        self.waited = {e: {} for e in self.ENGS}
        self.ndma = ndma
        self.dsem = {q: [stack.enter_context(nc.semaphore("d_%s%d" % (q, i))) for i in range(ndma)] for q in self.QS}
        self.dcnt = {q: [0] * ndma for q in self.QS}
        self.dnext = {q: 0 for q in self.QS}
        self.total = 0

    def _deps(self, e, reads, writes, pwrites):
        best = {}

        def add(d):
            for k, sv in d.items():
                if k not in best or best[k][1] < sv[1]:
                    best[k] = sv
        for b in reads:
            add(b.w)
            if b.psum:
                own = id(self.sem[e]) if e in self.sem else None
                add({k: sv for k, sv in b.r.items() if k != own})
        for b in writes:
            add(b.w)
            add(b.r)
        for b in pwrites:
            add(b.r)
        out = []
        wd = self.waited[e]
        for k, (s, v) in best.items():
            if e == "pe" and s is self.sem["pe"]:
                continue
            if wd.get(k, 0) >= v:
                continue
            wd[k] = v
            out.append((s, v))
        return out

    @staticmethod
    def _mark(reads, writes, pwrites, tok):
        k = id(tok[0])
        for b in reads:
            if k not in b.r or b.r[k][1] < tok[1]:
                b.r[k] = tok
        for b in writes:
            b.w = {k: tok}
            b.r = {}
        for b in pwrites:
            if k not in b.w or b.w[k][1] < tok[1]:
                b.w[k] = tok

    def op(self, e, fn, reads=(), writes=(), pwrites=()):
        deps = self._deps(e, reads, writes, pwrites)
        self.cnt[e] += 1
        tok = (self.sem[e], self.cnt[e])
        self.prog[e].append((deps, fn, (self.sem[e], 1)))
        self._mark(reads, writes, pwrites, tok)
        return tok

    def dma(self, q, out, in_, reads=(), writes=(), pwrites=(), **kw):
        i = self.dnext[q]
        self.dnext[q] = (i + 1) % self.ndma
        s = self.dsem[q][i]
        deps = self._deps(q, reads, writes, pwrites)
        prev = self.dcnt[q][i]
        if prev > 0 and self.waited[q].get(id(s), 0) < prev:
            deps.append((s, prev))
            self.waited[q][id(s)] = prev
        self.dcnt[q][i] = prev + 16
        tok = (s, prev + 16)
        self.prog[q].append((deps, (lambda eng: eng.dma_start(out=out, in_=in_, **kw)), (s, 16)))
        self._mark(reads, writes, pwrites, tok)
        return tok

    def barrier(self):
        alld = [(self.sem[x], self.cnt[x]) for x in self.ENGS if self.cnt[x] > 0]
        for q in self.QS:
            for i in range(self.ndma):
                if self.dcnt[q][i] > 0:
                    alld.append((self.dsem[q][i], self.dcnt[q][i]))
        for e in self.ENGS:
            deps = []
            wd = self.waited[e]
            for (s, v) in alld:
                if s is self.sem[e]:
                    continue
                if wd.get(id(s), 0) >= v:
                    continue
                wd[id(s)] = v
                deps.append((s, v))
            self.prog[e].append((deps, None, None))

    def emit(self):
        with self.nc.Block() as block:
            def mk(e):
                def body(eng):
                    for deps, fn, inc in self.prog[e]:
                        for (s, v) in deps:
                            eng.wait_ge(s, v)
                        if fn is not None:
                            fn(eng).then_inc(inc[0], inc[1])
                return body
            block.tensor(mk("pe"))
            block.scalar(mk("act"))
            block.vector(mk("dve"))
            block.gpsimd(mk("pool"))
            block.sync(mk("sp"))
        for e in self.ENGS:
            self.total += len(self.prog[e]) + sum(len(d) for d, _, _ in self.prog[e])
            self.prog[e] = []


def build_program(n_layers=DEPTH, dbg=(), stop_after=None):
    nc = bass.Bass("TRN2", target_bir_lowering=False)
    dbg = set(dbg)

    def din(name, shape):
        return nc.dram_tensor(name, list(shape), F32, kind="ExternalInput").ap()

    def dscr(name, shape):
        kind = "ExternalOutput" if name in dbg else "Internal"
        return nc.dram_tensor(name, list(shape), F32, kind=kind).ap()

    xin = din("xin", [NB, T, D])
    cT_d = din("cT", [128, 8, 3])
    consts_d = din("consts", [128, 1024])
    ropeB_d = din("ropeB", [T, 64])
    ropeC_d = din("ropeC", [T, 128])
    W = {}
    for nm, shp in [("ada_w", [DEPTH, D, 3 * D]), ("ada_bT", [DEPTH, 128, 24]), ("norm_gT", [DEPTH, 128, 8]),
                    ("w_in", [DEPTH, D, NIN]), ("mupT", [DEPTH, 128, 14]), ("munT", [DEPTH, 128, 14]),
                    ("w0T", [DEPTH, 128, 8]), ("a0T", [DEPTH, 128, 8]), ("kkT", [DEPTH, 128, 4]),
                    ("kaT", [DEPTH, 128, 4]), ("rkT", [DEPTH, 128, 4]), ("w_up", [DEPTH, 128, 512]),
                    ("a_up", [DEPTH, 128, 512]), ("gn_g", [DEPTH, 1, 512]), ("gn_b", [DEPTH, 1, 512]),
                    ("q_ln", [DEPTH, 1, 256]), ("kv_ln", [DEPTH, 1, 128]), ("w_uq", [DEPTH, 256, 768]),
                    ("w_ukv", [DEPTH, 128, 1024]), ("bqk_g", [DEPTH, 1, 192]), ("c_qn", [DEPTH, 1, 64]),
                    ("c_kn", [DEPTH, 1, 64]), ("c_sink", [DEPTH, 1, 8]), ("wbo", [DEPTH, 3, 512, D]),
                    ("w_out", [DEPTH, D, D])]:
        W[nm] = din(nm, shp)
    yout = nc.dram_tensor("yout", [NB, TL, D], F32, kind="ExternalOutput").ap()

    X1 = dscr("X1", [NB, T, D])
    MODG = dscr("MODG", [3, D])
    ZTM = dscr("ZTM", [NB, T, ZW])
    ZTA = dscr("ZTA", [NB, 14, 128, T])
    GT = dscr("GT", [NB, 2, 8, 64, T])
    SC = dscr("SC", [2, 128, 5, 8, T])
    VT = dscr("VT", [T, 1024])
    YD = dscr("YD", [2, T, 1024])
    BS = dscr("BS", [NB, T, 8])
    QKT = dscr("QKT", [NB, 16, 96, T])
    VB = dscr("VB", [NB, T, 512])
    QKTC = dscr("QKTC", [NB, 10, 64, T])
    VC = dscr("VC", [NB, T, 128])
    UT = dscr("UT", [NB, 2, 8, 64, T])

    bX = [Buf(None, "xin"), Buf(None, "X1"), Buf(None, "yout")]
    bMODG = Buf(None, "MODG")
    bZTM = [Buf(None, "ZTM%d" % b) for b in range(NB)]
    bZTA = [Buf(None, "ZTA%d" % b) for b in range(NB)]
    bGT = [Buf(None, "GT%d" % b) for b in range(NB)]
    bSC = Buf(None, "SC")
    bVT = Buf(None, "VT")
    bYD = Buf(None, "YD")
    bBS = [Buf(None, "BS%d" % b) for b in range(NB)]
    bQKT = [Buf(None, "QKT%d" % b) for b in range(NB)]
    bVB = [Buf(None, "VB%d" % b) for b in range(NB)]
    bQKTC = [Buf(None, "QKTC%d" % b) for b in range(NB)]
    bVC = [Buf(None, "VC%d" % b) for b in range(NB)]
    bUT = [Buf(None, "UT%d" % b) for b in range(NB)]
    bW = Buf(None, "weights")

    outer = ExitStack()
    S = Sched(nc, outer)
    uid = [0]

    def sbt(stack, shape, name=None):
        uid[0] += 1
        nm = "%s_%d" % (name or "t", uid[0])
        return Buf(stack.enter_context(nc.sbuf_tensor(nm, list(shape), F32)), nm)

    cst = sbt(outer, [128, 1024], "cst")
    S.dma("sp", cst[:], consts_d[:, :], reads=[bW], writes=[cst])
    ident = cst.t[:, 0:128]
    bones = cst.t[:, 128:256]
    Z2 = cst.t[:, 256:511]
    mask_lo = cst.t[:, 512:640]
    mask_hi = cst.t[:, 640:768]
    ind2 = cst.t[:, 768:770]
    ones64 = cst.t[:, 776:840]
    Z3 = cst.t[:, 848:943]
    PS = [Buf(outer.enter_context(nc.psum_tensor("ps%d" % i, [128, 512], F32)), "ps%d" % i, psum=True) for i in range(8)]
    psi = [0]

    def nps():
        p = PS[psi[0] % 8]
        psi[0] += 1
        return p

    modT = sbt(outer, [128, 24, 3], "modT")
    Gt = sbt(outer, [128, 8, 3], "Gt")
    csil = sbt(outer, [128, 8, 3], "csil")
    S.dma("sp", csil[:], cT_d[:, :, :], reads=[bW], writes=[csil])
    S.op("act", lambda e: e.activation(csil[:], csil[:], AF.Silu), reads=[csil], writes=[csil])

    def phase_end(st):
        S.barrier()
        S.emit()
        st.close()

    def mm(ps, out_ap, lhsT, rhs, start, stop, reads):
        S.op("pe", lambda e: e.matmul(out_ap, lhsT, rhs, start=start, stop=stop), reads=reads, pwrites=[ps] if not start else (), writes=[ps] if start else ())

    def mmp(ps, out_ap, lhsT, rhs, start, stop, reads):
        S.op("pe", lambda e: e.matmul(out_ap, lhsT, rhs, start=start, stop=stop), reads=reads, pwrites=[ps])

    def bc_load(st, src_row_ap, n, parts=128, name="bc"):
        t = sbt(st, [parts, n], name)
        S.dma("sp", t[:], src_row_ap.partition_broadcast(parts), reads=[bW], writes=[t])
        return t

    evac_rr = [0]

    def evac(out_ap, in_ap, reads, writes=(), pwrites=(), func=None, bias=None, scale=None):
        if func is None and bias is None and scale is None:
            evac_rr[0] += 1
            if evac_rr[0] % 2 == 0:
                S.op("dve", lambda e: e.tensor_copy(out_ap, in_ap), reads=reads, writes=writes, pwrites=pwrites)
            else:
                S.op("act", lambda e: e.copy(out_ap, in_ap), reads=reads, writes=writes, pwrites=pwrites)
        else:
            kw = {}
            if bias is not None:
                kw["bias"] = bias
            if scale is not None:
                kw["scale"] = scale
            f = func if func is not None else AF.Identity
            S.op("act", lambda e: e.activation(out_ap, in_ap, f, **kw), reads=reads, writes=writes, pwrites=pwrites)

    def rstd_from_ss(st_tiles, ss, n, inv_n, eps):
        S.op("dve", lambda e: e.tensor_scalar(ss[:, 0:n], ss[:, 0:n], inv_n, eps, ALU.mult, ALU.add), reads=[ss], writes=[ss])
        S.op("act", lambda e: e.activation(ss[:, 0:n], ss[:, 0:n], AF.Sqrt), reads=[ss], writes=[ss])
        S.op("dve", lambda e: e.reciprocal(ss[:, 0:n], ss[:, 0:n]), reads=[ss], writes=[ss])

    def rope(st, x3, nh, R, cs, tmp1, tmp2):
        q = R // 4
        xb = x3_buf[0]
        t1 = tmp1.t[:, 0:nh * R].rearrange("p (h r) -> p h r", h=nh)
        cosb = cs.t[:, 0:R].unsqueeze(1).to_broadcast([128, nh, R])
        S.op("dve", lambda e: e.tensor_tensor(t1, x3, cosb, ALU.mult), reads=[xb, cs], writes=[tmp1])
        x5 = x3.rearrange("p h (a f i) -> p h a f i", a=2, f=2)
        t5 = t1.rearrange("p h (a f i) -> p h a f i", a=2, f=2)
        s4 = cs.t[:, R:2 * R].rearrange("p (a f i) -> p a f i", a=2, f=2)
        t2 = tmp2.t[:, 0:nh * R // 2].rearrange("p (h a i) -> p h a i", h=nh, a=2)
        for hf in (0, 1):
            xin_ = x5[:, :, :, 1 - hf, :]
            sb_ = s4[:, :, hf, :].unsqueeze(1).to_broadcast([128, nh, 2, q])
            S.op("dve", lambda e, xin_=xin_, sb_=sb_: e.tensor_tensor(t2, xin_, sb_, ALU.mult), reads=[xb, cs], writes=[tmp2])
            tt = t5[:, :, :, hf, :]
            S.op("dve", lambda e, tt=tt: e.tensor_tensor(tt, tt, t2, ALU.add), reads=[tmp2, tmp1], writes=[tmp1])
        S.op("dve", lambda e: e.tensor_copy(x3, t1), reads=[tmp1], writes=[xb])

    x3_buf = [None]

    Xcur, bXcur = xin, bX[0]
    for l in range(n_layers):
        last = (l == DEPTH - 1)
        Xnext, bXnext = (X1, bX[1])
        st = ExitStack()
        adab = sbt(st, [128, 24], "adab")
        ngt = sbt(st, [128, 8], "ngt")
        S.dma("sp", adab[:], W["ada_bT"][l, :, :], reads=[bW], writes=[adab])
        S.dma("sp", ngt[:], W["norm_gT"][l, :, :], reads=[bW], writes=[ngt])
        wa = [sbt(st, [128, 8, 128], "wa") for _ in range(3)]
        psM = nps()
        for ch in range(24):
            wt = wa[ch % 3]
            for kh in range(2):
                S.dma("sp", wt.t[:, kh * 4:(kh + 1) * 4, :], W["ada_w"][l, kh * 512:(kh + 1) * 512, ch * 128:(ch + 1) * 128].rearrange("(k p) n -> p k n", p=128), reads=[bW],
                      writes=[wt] if kh == 0 else (), pwrites=[wt] if kh else ())
            for k in range(8):
                mmp(psM, psM.t[:, ch * 3:ch * 3 + 3], wt.t[:, k, :], csil.t[:, k, :], k == 0, k == 7, [wt, csil]) if ch > 0 or k > 0 else \
                    mm(psM, psM.t[:, 0:3], wt.t[:, k, :], csil.t[:, k, :], True, False, [wt, csil])
        S.op("dve", lambda e: e.tensor_tensor(modT[:], psM.t[:, 0:72].rearrange("p (c r) -> p c r", r=3),
                                              adab.t[:, :].unsqueeze(2).to_broadcast([128, 24, 3]), ALU.add),
             reads=[psM, adab], writes=[modT])
        S.op("dve", lambda e: e.tensor_scalar(Gt[:], modT.t[:, 8:16, :], 1.0, None, ALU.add), reads=[modT], writes=[Gt])
        S.op("dve", lambda e: e.tensor_tensor(Gt[:], Gt[:], ngt.t[:, :].unsqueeze(2).to_broadcast([128, 8, 3]), ALU.mult),
             reads=[Gt, ngt], writes=[Gt])
        psG = [nps(), nps()]
        for ch in range(8):
            pg = psG[ch // 4]
            (mm if ch % 4 == 0 else mmp)(pg, pg.t[0:3, (ch % 4) * 128:(ch % 4 + 1) * 128], modT.t[:, 16 + ch, :], ident, True, True, [modT, cst])
        gsb = sbt(st, [3, 1024], "gsb")
        for hlf in range(2):
            S.op("dve", lambda e, hlf=hlf: e.tensor_copy(gsb.t[0:3, hlf * 512:(hlf + 1) * 512], psG[hlf].t[0:3, :]), reads=[psG[hlf]], pwrites=[gsb])
        S.dma("pool", MODG[:, :], gsb[:], reads=[gsb], writes=[bMODG])
        phase_end(st)
        if stop_after == "P0":
            break

        for b in range(NB):
            st = ExitStack()
            hT = sbt(st, [128, 8, T], "hT")
            xts = [sbt(st, [128, D], "xt") for _ in range(2)]
            sqs = [sbt(st, [128, D], "sq") for _ in range(2)]
            sss = [sbt(st, [128, 1], "ss") for _ in range(2)]
            for i in range(NT):
                r = b if i >= 2 else 2
                xt, sq, ss = xts[i % 2], sqs[i % 2], sss[i % 2]
                S.dma("sp", xt[:], Xcur[b, i * 128:(i + 1) * 128, :], reads=[bXcur], writes=[xt])
                S.op("pool", lambda e, ss=ss: e.memset(ss[:], 0.0), writes=[ss])
                S.op("act", lambda e, xt=xt, sq=sq, ss=ss: e.activation(sq[:], xt[:], AF.Square, accum_out=ss[:]), reads=[xt, ss], writes=[sq, ss])
                NLV = int(os.environ.get("KDBG_NLV", 9))
                if NLV < 2:
                    continue
                rstd_from_ss(None, ss, 1, 1.0 / D, EPS)
                S.op("dve", lambda e, xt=xt, sq=sq, ss=ss: e.tensor_scalar(sq[:], xt[:], ss.t[:, 0:1], None, ALU.mult), reads=[xt, ss], writes=[sq])
                if NLV < 3:
                    continue
                for half in range(2):
                    pt = nps()
                    for c4 in range(4):
                        ch = half * 4 + c4
                        S.op("pe", lambda e, pt=pt, c4=c4, ch=ch, sq=sq: e.transpose(pt.t[:, c4 * 128:(c4 + 1) * 128], sq.t[:, ch * 128:(ch + 1) * 128], ident),
                             reads=[sq, cst], writes=[pt] if c4 == 0 else (), pwrites=[pt] if c4 > 0 else ())
                    if NLV < 4:
                        continue
                    for c4 in range(4):
                        ch = half * 4 + c4
                        o_ap = hT.t[:, ch, i * 128:(i + 1) * 128]
                        i_ap = pt.t[:, c4 * 128:(c4 + 1) * 128]
                        g_ap = Gt.t[:, ch, r:r + 1]
                        s_ap = modT.t[:, ch, r:r + 1]
                        EV = os.environ.get("KDBG_EV", "")
                        if (c4 % 2 == 0 and EV != "act") or EV == "dve":
                            S.op("dve", lambda e, o_ap=o_ap, i_ap=i_ap, g_ap=g_ap, s_ap=s_ap: e.tensor_scalar(o_ap, i_ap, g_ap, s_ap, ALU.mult, ALU.add),
                                 reads=[pt, Gt, modT], pwrites=[hT])
                        else:
                            S.op("act", lambda e, o_ap=o_ap, i_ap=i_ap, g_ap=g_ap, s_ap=s_ap: e.activation(o_ap, i_ap, AF.Identity, bias=s_ap, scale=g_ap),
                                 reads=[pt, Gt, modT], pwrites=[hT])
            if "hT" in dbg and b == 0 and l == 0:
                hTd = nc.dram_tensor("hTd", [128, 8, T], F32, kind="ExternalOutput").ap()
                for k in range(8):
                    S.dma("pool", hTd[:, k, :], hT.t[:, k, :], reads=[hT])
            if stop_after == "P1a":
                phase_end(st)
                break
            wbufs = [sbt(st, [128, 8, 512], "wb") for _ in range(2)]
            zos = [sbt(st, [128, 512], "zo") for _ in range(3)]
            groups = [(1792, 512, ZO_GA, AF.Silu), (2304, 416, ZO_ZB, None), (3232, 512, ZO_ZC, None), (3744, 256, ZO_ZC + 512, None)]
            for g in range(6):
                groups.append((4512 + g * 512, 512, ZO_ZG + g * 512, AF.Sigmoid))
            zi = 0
            for gi, (c0, n, zoff, fn) in enumerate(groups):
                wb = wbufs[gi % 2]
                for kh in range(2):
                    S.dma("sp", wb.t[:, kh * 4:(kh + 1) * 4, 0:n], W["w_in"][l, kh * 512:(kh + 1) * 512, c0:c0 + n].rearrange("(k p) n -> p k n", p=128), reads=[bW],
                          writes=[wb] if kh == 0 else (), pwrites=[wb] if kh else ())
                for i in range(NT):
                    ps = nps()
                    for k in range(8):
                        mm(ps, ps.t[:, 0:n], hT.t[:, k, i * 128:(i + 1) * 128], wb.t[:, k, 0:n], k == 0, k == 7, [hT, wb])
                    zo = zos[zi % 3]
                    zi += 1
                    evac(zo.t[:, 0:n], ps.t[:, 0:n], reads=[ps], writes=[zo], func=fn)
                    S.dma("pool", ZTM[b, i * 128:(i + 1) * 128, zoff:zoff + n], zo.t[:, 0:n], reads=[zo], pwrites=[bZTM[b]])
            zfs = [sbt(st, [128, T], "zf") for _ in range(2)]
            fm = [(c * 128, 128, ("A", c), None) for c in range(14)]
            fm += [(2720 + h * 64, 64, ("G", 0, h), AF.Silu) for h in range(8)]
            fm += [(4000 + h * 64, 64, ("G", 1, h), AF.Silu) for h in range(8)]
            for fi, (c0, m, dst, fn) in enumerate(fm):
                wb = wbufs[fi % 2]
                zf = zfs[fi % 2]
                for kh in range(2):
                    S.dma("sp", wb.t[:, kh * 4:(kh + 1) * 4, 0:m], W["w_in"][l, kh * 512:(kh + 1) * 512, c0:c0 + m].rearrange("(k p) n -> p k n", p=128), reads=[bW],
                          writes=[wb] if kh == 0 else (), pwrites=[wb] if kh else ())
                for ci, (t0, tn) in enumerate(TCH):
                    ps = nps()
                    for k in range(8):
                        mm(ps, ps.t[0:m, 0:tn], wb.t[:, k, 0:m], hT.t[:, k, t0:t0 + tn], k == 0, k == 7, [hT, wb])
                    evac(zf.t[0:m, t0:t0 + tn], ps.t[0:m, 0:tn], reads=[ps], writes=[zf] if ci == 0 else (), pwrites=[zf] if ci > 0 else (), func=fn)
                if dst[0] == "A":
                    S.dma("pool", ZTA[b, dst[1], :, :], zf.t[:, :], reads=[zf], pwrites=[bZTA[b]])
                else:
                    S.dma("pool", GT[b, dst[1], dst[2], :, :], zf.t[0:64, :], reads=[zf], pwrites=[bGT[b]])
            phase_end(st)
            if stop_after == "P1":
                break

            st = ExitStack()
            mup = sbt(st, [128, 14], "mup")
            mun = sbt(st, [128, 14], "mun")
            c0t = sbt(st, [128, 14], "c0t")
            w0t = sbt(st, [128, 8], "w0t")
            a0t = sbt(st, [128, 8], "a0t")
            kkp = sbt(st, [128, 4], "kkp")
            kap = sbt(st, [128, 4], "kap")
            rkp = sbt(st, [128, 4], "rkp")
            wup = sbt(st, [128, 512], "wup")
            aup = sbt(st, [128, 512], "aup")
            for tl_, nm in [(mup, "mupT"), (mun, "munT"), (w0t, "w0T"), (a0t, "a0T"), (kkp, "kkT"), (kap, "kaT"), (rkp, "rkT"), (wup, "w_up"), (aup, "a_up")]:
                S.dma("sp", tl_[:], W[nm][l, :, :], reads=[bW], writes=[tl_])
            S.op("dve", lambda e: e.tensor_tensor(c0t[:], mup[:], mun[:], ALU.add), reads=[mup, mun], writes=[c0t])
            S.op("dve", lambda e: e.tensor_scalar(c0t[:], c0t[:], -1.0, 1.0, ALU.mult, ALU.add), reads=[c0t], writes=[c0t])
            NBT = 15
            bts = [sbt(st, [128, T], "bt") for _ in range(NBT)]
            zraw = [bts[0], bts[1]]
            zri = [0]

            def shift_load(c, dst):
                zr = zraw[zri[0] % 2]
                zri[0] += 1
                S.dma("sp", zr[:], ZTA[b, c, :, :], reads=[bZTA[b]], writes=[zr])
                S.op("act", lambda e: e.activation(dst[:], zr[:], AF.Identity, scale=c0t.t[:, c:c + 1]), reads=[zr, c0t], writes=[dst])
                for (o0, o1, i0, i1, mt) in [(1, 256, 0, 255, mup), (257, T, 256, T - 1, mup), (0, 255, 1, 256, mun), (256, T - 1, 257, T, mun)]:
                    S.op("dve", lambda e, o0=o0, o1=o1, i0=i0, i1=i1, mt=mt: e.scalar_tensor_tensor(dst.t[:, o0:o1], zr.t[:, i0:i1], mt.t[:, c:c + 1], dst.t[:, o0:o1], ALU.mult, ALU.add),
                         reads=[zr, mt, dst], writes=[dst])

            twd, ads = bts[2], bts[3]
            shift_load(12, twd)
            S.op("act", lambda e: e.activation(twd[:], twd[:], AF.Tanh), reads=[twd], writes=[twd])
            shift_load(13, ads)
            rs_, ks_, vs_, kk, tq, a_d, dec, ka_d, kd0, kd1, uu = bts[4:15]
            bsS = sbt(st, [128, NT, 8], "bsS")
            vtm = [sbt(st, [128, 4, 128], "vtm") for _ in range(2)]
            for q in range(4):
                shift_load(q, rs_)
                shift_load(4 + q, ks_)
                shift_load(8 + q, vs_)
                S.op("dve", lambda e, q=q: e.tensor_scalar(kk[:], ks_[:], kkp.t[:, q:q + 1], None, ALU.mult), reads=[ks_, kkp], writes=[kk])
                S.op("pool", lambda e: e.tensor_tensor(tq[:], kk[:], kk[:], ALU.mult), reads=[kk], writes=[tq])
                for ci, (t0, tn) in enumerate(TCH):
                    ps = nps()
                    mm(ps, ps.t[:, 0:tn], bones, tq.t[:, t0:t0 + tn], True, True, [tq, cst])
                    S.op("dve", lambda e, ps=ps, t0=t0, tn=tn: e.tensor_scalar_max(a_d.t[:, t0:t0 + tn], ps.t[:, 0:tn], 1e-24), reads=[ps],
                         writes=[a_d] if ci == 0 else (), pwrites=[a_d] if ci > 0 else ())
                S.op("act", lambda e: e.activation(a_d[:], a_d[:], AF.Sqrt), reads=[a_d], writes=[a_d])
                S.op("dve", lambda e: e.reciprocal(a_d[:], a_d[:]), reads=[a_d], writes=[a_d])
                S.op("dve", lambda e: e.tensor_tensor(kk[:], kk[:], a_d[:], ALU.mult), reads=[kk, a_d], writes=[kk])
                S.dma("pool", SC[0, :, 3, b * 4 + q, :], kk[:], reads=[kk], pwrites=[bSC])
                S.dma("pool", SC[1, :, 3, b * 4 + q, :], kk[:], reads=[kk], pwrites=[bSC])
                S.dma("pool", SC[0, :, 4, b * 4 + q, :], rs_[:], reads=[rs_], pwrites=[bSC])
                S.dma("pool", SC[1, :, 4, b * 4 + q, :], rs_[:], reads=[rs_], pwrites=[bSC])
                for d in range(2):
                    kd = kd0 if d == 0 else kd1
                    for ci, (t0, tn) in enumerate(TCH):
                        ps = nps()
                        mm(ps, ps.t[:, 0:tn], wup.t[d * 64:(d + 1) * 64, q * 128:(q + 1) * 128], twd.t[d * 64:(d + 1) * 64, t0:t0 + tn], True, True, [wup, twd])
                        evac(dec.t[:, t0:t0 + tn], ps.t[:, 0:tn], reads=[ps, w0t], writes=[dec] if ci == 0 else (), pwrites=[dec] if ci > 0 else (),
                             func=AF.Sigmoid, bias=w0t.t[:, d * 4 + q:d * 4 + q + 1])
                    S.op("act", lambda e: e.activation(dec[:], dec[:], AF.Exp, scale=-math.exp(-0.5)), reads=[dec], writes=[dec])
                    S.dma("pool", SC[d, :, 0, b * 4 + q, :], dec[:], reads=[dec], pwrites=[bSC])
                    for ci, (t0, tn) in enumerate(TCH):
                        ps = nps()
                        mm(ps, ps.t[:, 0:tn], aup.t[d * 64:(d + 1) * 64, q * 128:(q + 1) * 128], ads.t[d * 64:(d + 1) * 64, t0:t0 + tn], True, True, [aup, ads])
                        evac(a_d.t[:, t0:t0 + tn], ps.t[:, 0:tn], reads=[ps, a0t], writes=[a_d] if ci == 0 else (), pwrites=[a_d] if ci > 0 else (),
                             func=AF.Sigmoid, bias=a0t.t[:, d * 4 + q:d * 4 + q + 1])
                    S.op("pool", lambda e: e.tensor_tensor(ka_d[:], kk[:], a_d[:], ALU.mult), reads=[kk, a_d], writes=[ka_d])
                    S.dma("pool", SC[d, :, 1, b * 4 + q, :], ka_d[:], reads=[ka_d], pwrites=[bSC])
                    S.op("dve", lambda e, kd=kd, q=q: e.tensor_scalar(kd[:], a_d[:], kap.t[:, q:q + 1], kap.t[:, q:q + 1], ALU.mult, ALU.subtract), reads=[a_d, kap], writes=[kd])
                    S.op("dve", lambda e, kd=kd: e.scalar_tensor_tensor(kd[:], kd[:], 1.0, ks_[:], ALU.add, ALU.mult), reads=[kd, ks_], writes=[kd])
                    S.dma("pool", SC[d, :, 2, b * 4 + q, :], kd[:], reads=[kd], pwrites=[bSC])
                S.op("pool", lambda e: e.tensor_tensor(uu[:], kd0[:], kd1[:], ALU.add), reads=[kd0, kd1], writes=[uu])
                S.op("dve", lambda e, q=q: e.scalar_tensor_tensor(uu[:], uu[:], rkp.t[:, q:q + 1], rs_[:], ALU.mult, ALU.mult), reads=[uu, rkp, rs_], writes=[uu])
                psb = nps()
                for i in range(NT):
                    mm(psb, psb.t[:, i * 2:i * 2 + 2], uu.t[:, i * 128:(i + 1) * 128], ind2, True, True, [uu, cst]) if i == 0 else \
                        mmp(psb, psb.t[:, i * 2:i * 2 + 2], uu.t[:, i * 128:(i + 1) * 128], ind2, True, True, [uu, cst])
                S.op("dve", lambda e, q=q, psb=psb: e.tensor_copy(bsS.t[:, :, 2 * q:2 * q + 2], psb.t[:, 0:2 * NT].rearrange("p (i c) -> p i c", c=2)),
                     reads=[psb], writes=[bsS] if q == 0 else (), pwrites=[bsS] if q > 0 else ())
                for g0 in range(0, NT, 4):
                    ng = min(4, NT - g0)
                    pt = nps()
                    for j in range(ng):
                        i = g0 + j
                        S.op("pe", lambda e, pt=pt, j=j, i=i: e.transpose(pt.t[:, j * 128:(j + 1) * 128], vs_.t[:, i * 128:(i + 1) * 128], ident),
                             reads=[vs_, cst], writes=[pt] if j == 0 else (), pwrites=[pt] if j > 0 else ())
                    vt_ = vtm[(g0 // 4) % 2]
                    evac(vt_.t[:, 0:ng, :], pt.t[:, 0:ng * 128].rearrange("p (j f) -> p j f", f=128), reads=[pt], writes=[vt_])
                    for j in range(ng):
                        i = g0 + j
                        dst = VT[i * 128:(i + 1) * 128, :].rearrange("p (c bb qq v) -> p c bb qq v", c=2, bb=2, qq=4)[:, :, b, q, :]
                        S.dma("pool", dst, vt_.t[:, j, :].rearrange("p (c v) -> p c v", c=2), reads=[vt_], pwrites=[bVT])
            for i3 in range(3):
                S.dma("pool", BS[b, i3 * 768:(i3 + 1) * 768, :].rearrange("(i p) h -> p i h", p=128), bsS.t[:, i3 * 6:(i3 + 1) * 6, :], reads=[bsS], writes=[bBS[b]] if i3 == 0 else (), pwrites=[bBS[b]] if i3 else ())
            phase_end(st)
            if stop_after == "P2":
                break

            st = ExitStack()
            qln = bc_load(st, W["q_ln"][l, :, :], 256, name="qln")
            kvln = bc_load(st, W["kv_ln"][l, :, :], 128, name="kvln")
            bqkg = bc_load(st, W["bqk_g"][l, :, :], 192, name="bqkg")
            cqn = bc_load(st, W["c_qn"][l, :, :], 64, name="cqn")
            ckn = bc_load(st, W["c_kn"][l, :, :], 64, name="ckn")
            wuq = sbt(st, [128, 2, 768], "wuq")
            wukv = sbt(st, [128, 1024], "wukv")
            S.dma("sp", wuq[:], W["w_uq"][l, :, :].rearrange("(k p) n -> p k n", p=128), reads=[bW], writes=[wuq])
            S.dma("sp", wukv[:], W["w_ukv"][l, :, :], reads=[bW], writes=[wukv])
            NBUF = 2
            zbs = [sbt(st, [128, 416], "zb") for _ in range(NBUF)]
            zcs = [sbt(st, [128, 768], "zc") for _ in range(NBUF)]
            rbs = [sbt(st, [128, 64], "rb") for _ in range(NBUF)]
            rcs = [sbt(st, [128, 128], "rc") for _ in range(NBUF)]
            ss2 = [sbt(st, [128, 2], "ss2") for _ in range(NBUF)]
            junk = sbt(st, [128, 1536], "junk")
            cn = [sbt(st, [128, 384], "cn") for _ in range(NBUF)]
            cT3 = [sbt(st, [128, 3, 128], "cT3") for _ in range(NBUF)]
            qk = [sbt(st, [128, 16, 96], "qk") for _ in range(NBUF)]
            kv = [sbt(st, [128, 8, 128], "kv") for _ in range(NBUF)]
            ssq = [sbt(st, [128, 16], "ssq") for _ in range(NBUF)]
            rt1 = sbt(st, [128, 640], "rt1")
            rt2 = sbt(st, [128, 320], "rt2")
            qkT = [sbt(st, [96, 16, 128], "qkT") for _ in range(NBUF)]
            qkTc = [sbt(st, [64, 10, 128], "qkTc") for _ in range(NBUF)]
            for i in range(NT):
                u = i % NBUF
                zb, zc, rb, rc = zbs[u], zcs[u], rbs[u], rcs[u]
                S.dma("sp", zb[:], ZTM[b, i * 128:(i + 1) * 128, ZO_ZB:ZO_ZB + 416], reads=[bZTM[b]], writes=[zb])
                S.dma("sp", zc[:], ZTM[b, i * 128:(i + 1) * 128, ZO_ZC:ZO_ZC + 768], reads=[bZTM[b]], writes=[zc])
                S.dma("sp", rb[:], ropeB_d[i * 128:(i + 1) * 128, :], reads=[bW], writes=[rb])
                S.dma("sp", rc[:], ropeC_d[i * 128:(i + 1) * 128, :], reads=[bW], writes=[rc])
                s2 = ss2[u]
                S.op("pool", lambda e, s2=s2: e.memset(s2[:], 0.0), writes=[s2])
                S.op("act", lambda e, zb=zb, s2=s2: e.activation(junk.t[:, 0:256], zb.t[:, 0:256], AF.Square, accum_out=s2.t[:, 0:1]), reads=[zb, s2], writes=[junk, s2])
                S.op("act", lambda e, zb=zb, s2=s2: e.activation(junk.t[:, 256:384], zb.t[:, 256:384], AF.Square, accum_out=s2.t[:, 1:2]), reads=[zb, s2], writes=[junk, s2])
                S.op("dve", lambda e, s2=s2: e.tensor_scalar(s2.t[:, 0:1], s2.t[:, 0:1], 1.0 / 256, EPS, ALU.mult, ALU.add), reads=[s2], writes=[s2])
                S.op("dve", lambda e, s2=s2: e.tensor_scalar(s2.t[:, 1:2], s2.t[:, 1:2], 1.0 / 128, EPS, ALU.mult, ALU.add), reads=[s2], writes=[s2])
                S.op("act", lambda e, s2=s2: e.activation(s2[:], s2[:], AF.Sqrt), reads=[s2], writes=[s2])
                S.op("dve", lambda e, s2=s2: e.reciprocal(s2[:], s2[:]), reads=[s2], writes=[s2])
                c_ = cn[u]
                S.op("dve", lambda e, c_=c_, zb=zb, s2=s2: e.scalar_tensor_tensor(c_.t[:, 0:256], zb.t[:, 0:256], s2.t[:, 0:1], qln[:], ALU.mult, ALU.mult), reads=[zb, s2, qln], writes=[c_])
                S.op("dve", lambda e, c_=c_, zb=zb, s2=s2: e.scalar_tensor_tensor(c_.t[:, 256:384], zb.t[:, 256:384], s2.t[:, 1:2], kvln[:], ALU.mult, ALU.mult), reads=[zb, s2, kvln], pwrites=[c_])
                pt = nps()
                for j in range(3):
                    S.op("pe", lambda e, pt=pt, j=j, c_=c_: e.transpose(pt.t[:, j * 128:(j + 1) * 128], c_.t[:, j * 128:(j + 1) * 128], ident),
                         reads=[c_, cst], writes=[pt] if j == 0 else (), pwrites=[pt] if j > 0 else ())
                c3 = cT3[u]
                evac(c3[:], pt.t[:, 0:384].rearrange("p (j f) -> p j f", f=128), reads=[pt], writes=[c3])
                qk_ = qk[u]
                kv_ = kv[u]
                for nh in range(2):
                    ps = nps()
                    for k in range(2):
                        mm(ps, ps.t[:, 0:384], c3.t[:, k, :], wuq.t[:, k, nh * 384:(nh + 1) * 384], k == 0, k == 1, [c3, wuq])
                    evac(qk_.t[:, nh * 4:(nh + 1) * 4, :], ps.t[:, 0:384].rearrange("p (h r) -> p h r", r=96), reads=[ps], writes=[qk_] if nh == 0 else (), pwrites=[qk_] if nh > 0 else ())
                for nh in range(2):
                    ps = nps()
                    mm(ps, ps.t[:, :], c3.t[:, 2, :], wukv.t[:, nh * 512:(nh + 1) * 512], True, True, [c3, wukv])
                    evac(kv_.t[:, nh * 4:(nh + 1) * 4, :], ps.t[:, :].rearrange("p (h r) -> p h r", r=128), reads=[ps], writes=[kv_] if nh == 0 else (), pwrites=[kv_] if nh > 0 else ())
                S.op("dve", lambda e, qk_=qk_, kv_=kv_: e.tensor_copy(qk_.t[:, 8:16, 0:64], kv_.t[:, :, 0:64]), reads=[kv_], pwrites=[qk_])
                S.op("dve", lambda e, qk_=qk_, zb=zb: e.tensor_copy(qk_.t[:, 8:16, 64:96], zb.t[:, 384:416].unsqueeze(1).to_broadcast([128, 8, 32])), reads=[zb], pwrites=[qk_])
                for h2 in range(2):
                    S.dma("pool", VB[b, i * 128:(i + 1) * 128, h2 * 256:(h2 + 1) * 256].rearrange("p (h v) -> p h v", v=64), kv_.t[:, h2 * 4:(h2 + 1) * 4, 64:128], reads=[kv_], pwrites=[bVB[b]])
                sq_ = ssq[u]
                S.op("pool", lambda e, qk_=qk_: e.tensor_tensor(junk.t[:, 0:1536], qk_.t[:, :, :].rearrange("p h r -> p (h r)"), qk_.t[:, :, :].rearrange("p h r -> p (h r)"), ALU.mult), reads=[qk_], writes=[junk])
                S.op("dve", lambda e, sq_=sq_: e.tensor_reduce(sq_[:], junk.t[:, 0:1536].rearrange("p (h r) -> p h r", r=96), AX.X, ALU.add), reads=[junk], writes=[sq_])
                rstd_from_ss(None, sq_, 16, 1.0 / 96, EPS)
                S.op("dve", lambda e, qk_=qk_, sq_=sq_: e.tensor_tensor(qk_[:], qk_[:], sq_.t[:, 0:16].unsqueeze(2).to_broadcast([128, 16, 96]), ALU.mult), reads=[qk_, sq_], writes=[qk_])
                S.op("dve", lambda e, qk_=qk_: e.tensor_tensor(qk_.t[:, :, :].rearrange("p (a h) r -> p a h r", a=2), qk_.t[:, :, :].rearrange("p (a h) r -> p a h r", a=2),
                                                              bqkg.t[:, :].rearrange("p (a r) -> p a r", a=2).unsqueeze(2).to_broadcast([128, 2, 8, 96]), ALU.mult), reads=[qk_, bqkg], writes=[qk_])
                x3_buf[0] = qk_
                rope(st, qk_.t[:, :, 64:96], 16, 32, rb, rt1, rt2)
                qT_ = qkT[u]
                for g0 in range(0, 16, 4):
                    pt = nps()
                    for j in range(4):
                        S.op("pe", lambda e, pt=pt, j=j, g0=g0, qk_=qk_: e.transpose(pt.t[0:96, j * 128:(j + 1) * 128], qk_.t[:, g0 + j, :], ident),
                             reads=[qk_, cst], writes=[pt] if j == 0 else (), pwrites=[pt] if j > 0 else ())
                    evac(qT_.t[0:96, g0:g0 + 4, :], pt.t[0:96, :].rearrange("p (j f) -> p j f", f=128), reads=[pt], writes=[qT_] if g0 == 0 else (), pwrites=[qT_] if g0 > 0 else ())
                for h2 in range(2):
                    S.dma("pool", QKT[b, h2 * 8:(h2 + 1) * 8, :, i * 128:(i + 1) * 128].rearrange("h p t -> p h t"), qT_.t[0:96, h2 * 8:(h2 + 1) * 8, :], reads=[qT_], pwrites=[bQKT[b]])
                S.dma("pool", VC[b, i * 128:(i + 1) * 128, :], zc.t[:, 640:768], reads=[zc], pwrites=[bVC[b]])
                S.op("pool", lambda e, zc=zc: e.tensor_tensor(junk.t[:, 0:640], zc.t[:, 0:640], zc.t[:, 0:640], ALU.mult), reads=[zc], writes=[junk])
                S.op("dve", lambda e, sq_=sq_: e.tensor_reduce(sq_.t[:, 0:10], junk.t[:, 0:640].rearrange("p (h r) -> p h r", r=64), AX.X, ALU.add), reads=[junk], writes=[sq_])
                rstd_from_ss(None, sq_, 10, 1.0 / 64, EPS)
                z3 = zc.t[:, 0:640].rearrange("p (h r) -> p h r", r=64)
                S.op("dve", lambda e, z3=z3, sq_=sq_, zc=zc: e.tensor_tensor(z3, z3, sq_.t[:, 0:10].unsqueeze(2).to_broadcast([128, 10, 64]), ALU.mult), reads=[zc, sq_], writes=[zc])
                S.op("dve", lambda e, z3=z3, zc=zc: e.tensor_tensor(z3[:, 0:8, :], z3[:, 0:8, :], cqn.t[:, :].unsqueeze(1).to_broadcast([128, 8, 64]), ALU.mult), reads=[zc, cqn], writes=[zc])
                S.op("dve", lambda e, z3=z3, zc=zc: e.tensor_tensor(z3[:, 8:10, :], z3[:, 8:10, :], ckn.t[:, :].unsqueeze(1).to_broadcast([128, 2, 64]), ALU.mult), reads=[zc, ckn], writes=[zc])
                x3_buf[0] = zc
                rope(st, z3, 10, 64, rc, rt1, rt2)
                qTc_ = qkTc[u]
                for g0 in range(0, 10, 4):
                    ng = min(4, 10 - g0)
                    pt = nps()
                    for j in range(ng):
                        S.op("pe", lambda e, pt=pt, j=j, g0=g0, z3=z3: e.transpose(pt.t[0:64, j * 128:(j + 1) * 128], z3[:, g0 + j, :], ident),
                             reads=[zc, cst], writes=[pt] if j == 0 else (), pwrites=[pt] if j > 0 else ())
                    evac(qTc_.t[0:64, g0:g0 + ng, :], pt.t[0:64, 0:ng * 128].rearrange("p (j f) -> p j f", f=128), reads=[pt], writes=[qTc_] if g0 == 0 else (), pwrites=[qTc_] if g0 > 0 else ())
                S.dma("pool", QKTC[b, :, :, i * 128:(i + 1) * 128].rearrange("h p t -> p h t"), qTc_[:], reads=[qTc_], pwrites=[bQKTC[b]])
            phase_end(st)
        if stop_after in ("P1a", "P1", "P2", "P3"):
            break

        st = ExitStack()
        Sb = [[sbt(st, [128, 8, 64], "S%d_%d" % (d, k)) for k in range(2)] for d in range(2)]
        for d in range(2):
            S.op("dve", lambda e, d=d: e.memset(Sb[d][0][:], 0.0), writes=[Sb[d][0]])
        Sw = [sbt(st, [128, 8, 64], "Sw") for d in range(2)]
        tA = [[sbt(st, [128, 8, 64], "tA") for _ in range(2)] for d in range(2)]
        tB = [sbt(st, [128, 8, 64], "tB") for d in range(2)]
        tC = [[sbt(st, [128, 8, 64], "tC")] * 2 for d in range(2)]
        t4 = [[sbt(st, [128, 8, 64], "t4") for _ in range(2)] for d in range(2)]
        SCb = [[sbt(st, [128, 5, 8, 64], "SCb") for _ in range(2)] for d in range(2)]
        Vb = [[sbt(st, [64, 1024], "Vb") for _ in range(2)] for d in range(2)]
        ysb = [sbt(st, [128, 512], "ysb") for d in range(2)]
        psV = [PS[0], PS[1]]
        psSA = [PS[2], PS[3]]
        psY = PS[4]
        psS5, psO5, psD5 = PS[5], PS[6], PS[7]
        KTs = [sbt(st, [96, T], "KT") for _ in range(2)]
        QTs = [sbt(st, [96, T], "QT") for _ in range(2)]
        GTs = [sbt(st, [64, T], "GTh") for _ in range(2)]
        Vhs = [sbt(st, [128, NT, 64], "Vh") for _ in range(2)]
        Pbs = [sbt(st, [128, 512], "Pb") for _ in range(3)]
        rdn = [sbt(st, [64, 512], "rdn") for _ in range(2)]
        uob = [sbt(st, [64, T], "uob") for _ in range(2)]
        scale_b = 96 ** -0.5

        def p5_units():
            units = [(b, h) for b in range(NB) for h in range(8)]

            def issue_loads(idx):
                b, h = units[idx]
                u = idx % 2
                S.dma("sp", KTs[u][:], QKT[b, 8 + h, :, :], reads=[bQKT[b]], writes=[KTs[u]])
                S.dma("sp", QTs[u][:], QKT[b, h, :, :], reads=[bQKT[b]], writes=[QTs[u]])
                S.dma("sp", GTs[u][:], GT[b, 0, h, :, :], reads=[bGT[b]], writes=[GTs[u]])
                for i3 in range(3):
                    S.dma("sp", Vhs[u].t[:, i3 * 6:(i3 + 1) * 6, :], VB[b, i3 * 768:(i3 + 1) * 768, h * 64:(h + 1) * 64].rearrange("(i p) v -> p i v", p=128), reads=[bVB[b]],
                          writes=[Vhs[u]] if i3 == 0 else (), pwrites=[Vhs[u]] if i3 else ())
            issue_loads(0)
            yield
            pi = 0
            cc = 0
            for idx, (b, h) in enumerate(units):
                u = idx % 2
                KT, QT, GTh, Vh, uo = KTs[u], QTs[u], GTs[u], Vhs[u], uob[u]
                if idx + 1 < len(units):
                    issue_loads(idx + 1)
                qchunks = [(LC + j * 512, 512, list(range(NT))) for j in range(4)]
                if not last:
                    qchunks.append((0, 256, [0, 1]))
                for ci, (q0, qn, kts) in enumerate(qchunks):
                    nk = len(kts)

                    def emit_pv(ki, kt, Pb, qn=qn, Vh=Vh, nk=nk):
                        mm(psO5, psO5.t[0:64, 0:qn], Vh.t[:, kt, :], Pb.t[:, 0:qn], ki == 0, ki == nk - 1, [Vh, Pb])
                        mm(psD5, psD5.t[0:64, 0:qn], ones64, Pb.t[:, 0:qn], ki == 0, ki == nk - 1, [cst, Pb])
                    prev = None
                    for ki, kt in enumerate(kts):
                        if prev is not None:
                            emit_pv(*prev)
                        mm(psS5, psS5.t[:, 0:qn], KT.t[0:96, kt * 128:(kt + 1) * 128], QT.t[0:96, q0:q0 + qn], True, True, [KT, QT])
                        Pb = Pbs[pi % 3]
                        pi += 1
                        S.op("act", lambda e, Pb=Pb, qn=qn: e.activation(Pb.t[:, 0:qn], psS5.t[:, 0:qn], AF.Exp, scale=scale_b), reads=[psS5], writes=[Pb])
                        prev = (ki, kt, Pb)
                        yield
                    emit_pv(*prev)
                    yield
                    rd = rdn[cc % 2]
                    cc += 1
                    S.op("dve", lambda e, rd=rd, qn=qn: e.reciprocal(rd.t[:, 0:qn], psD5.t[0:64, 0:qn]), reads=[psD5], writes=[rd])
                    S.op("dve", lambda e, rd=rd, qn=qn: e.tensor_tensor(rd.t[:, 0:qn], psO5.t[0:64, 0:qn], rd.t[:, 0:qn], ALU.mult), reads=[psO5, rd], writes=[rd])
                    S.op("pool", lambda e, rd=rd, uo=uo, GTh=GTh, q0=q0, qn=qn: e.tensor_tensor(uo.t[:, q0:q0 + qn], rd.t[:, 0:qn], GTh.t[:, q0:q0 + qn], ALU.mult), reads=[rd, GTh],
                         writes=[uo] if ci == 0 else (), pwrites=[uo] if ci > 0 else ())
                    yield
                if last:
                    S.dma("sp", UT[b, 0, h, :, LC:T], uo.t[:, LC:T], reads=[uo], pwrites=[bUT[b]])
                else:
                    S.dma("sp", UT[b, 0, h, :, :], uo[:], reads=[uo], pwrites=[bUT[b]])
                yield
        x9 = sbt(st, [64, T], "x9")
        esk = bc_load(st, W["c_sink"][l, :, :], 8, parts=64, name="esk")
        S.op("act", lambda e: e.activation(esk[:], esk[:], AF.Exp), reads=[esk], writes=[esk])
        uoc = [sbt(st, [64, 4, 128], "uoc") for _ in range(2)]
        scale_c = 64 ** -0.5

        def p6_units():
            KTg = KTs[0]
            Q4 = [QTs[0], QTs[1], KTs[1], x9]
            G4 = [GTs[0], GTs[1], uob[0], uob[1]]
            Vg = Vhs[0]
            pi = 0
            bi = 0
            for b in range(NB):
                for g in range(2):
                    S.dma("sp", KTg.t[0:64, :], QKTC[b, 8 + g, :, :], reads=[bQKTC[b]], writes=[KTg])
                    for i3 in range(3):
                        S.dma("sp", Vg.t[:, i3 * 6:(i3 + 1) * 6, :], VC[b, i3 * 768:(i3 + 1) * 768, g * 64:(g + 1) * 64].rearrange("(i p) v -> p i v", p=128), reads=[bVC[b]],
                              writes=[Vg] if i3 == 0 else (), pwrites=[Vg] if i3 else ())
                    for hh in range(4):
                        S.dma("sp", Q4[hh].t[0:64, :], QKTC[b, 4 * g + hh, :, :], reads=[bQKTC[b]], writes=[Q4[hh]])
                        S.dma("sp", G4[hh].t[0:64, :], GT[b, 1, 4 * g + hh, :, :], reads=[bGT[b]], writes=[G4[hh]])
                    for _ in range(14):
                        yield
                    blocks = list(range(2, NT))
                    if not last:
                        blocks = [0, 1] + blocks
                    for n in blocks:
                        kts = [(0, None), (1, None)]
                        if n >= 2:
                            if n - 1 >= 2:
                                kts.append((n - 1, mask_lo))
                            kts.append((n, None))
                            if n + 1 < NT:
                                kts.append((n + 1, mask_hi))
                        nk = len(kts)

                        def emit_pv(ki, kt, Pc, msk, nk=nk):
                            if msk is not None:
                                S.op("dve", lambda e, Pc=Pc, msk=msk: e.tensor_tensor(Pc.t[:, :].rearrange("p (h q) -> p h q", h=4), Pc.t[:, :].rearrange("p (h q) -> p h q", h=4),
                                                                                       msk.unsqueeze(1).to_broadcast([128, 4, 128]), ALU.mult), reads=[Pc, cst], writes=[Pc])
                            mm(psO5, psO5.t[0:64, :], Vg.t[:, kt, :], Pc.t[:, :], ki == 0, ki == nk - 1, [Vg, Pc])
                            mm(psD5, psD5.t[0:64, :], ones64, Pc.t[:, :], ki == 0, ki == nk - 1, [cst, Pc])
                        prev = None
                        for ki, (kt, msk) in enumerate(kts):
                            if prev is not None:
                                emit_pv(*prev)
                            for hh in range(4):
                                S.op("pe", lambda e, hh=hh, kt=kt, n=n, q_=Q4[hh]: e.matmul(psS5.t[:, hh * 128:(hh + 1) * 128], KTg.t[0:64, kt * 128:(kt + 1) * 128],
                                                                                      q_.t[0:64, n * 128:(n + 1) * 128], start=True, stop=True),
                                     reads=[KTg, Q4[hh]], writes=[psS5] if hh == 0 else (), pwrites=[psS5] if hh > 0 else ())
                            Pc = Pbs[pi % 3]
                            pi += 1
                            S.op("act", lambda e, Pc=Pc: e.activation(Pc[:], psS5.t[:, :], AF.Exp, scale=scale_c), reads=[psS5], writes=[Pc])
                            prev = (ki, kt, Pc, msk)
                            yield
                        emit_pv(*prev)
                        yield
                        rd = rdn[bi % 2]
                        uo = uoc[bi % 2]
                        bi += 1
                        S.op("dve", lambda e, rd=rd, g=g: e.tensor_tensor(rd.t[:, :].rearrange("p (h q) -> p h q", h=4), psD5.t[0:64, :].rearrange("p (h q) -> p h q", h=4),
                                                                          esk.t[:, 4 * g:4 * g + 4].unsqueeze(2).to_broadcast([64, 4, 128]), ALU.add), reads=[psD5, esk], writes=[rd])
                        S.op("dve", lambda e, rd=rd: e.reciprocal(rd[:], rd[:]), reads=[rd], writes=[rd])
                        S.op("dve", lambda e, rd=rd: e.tensor_tensor(rd[:], psO5.t[0:64, :], rd[:], ALU.mult), reads=[psO5, rd], writes=[rd])
                        yield
                        for hh in range(4):
                            S.op("pool", lambda e, rd=rd, uo=uo, hh=hh, n=n, g_=G4[hh]: e.tensor_tensor(uo.t[:, hh, :], rd.t[:, hh * 128:(hh + 1) * 128], g_.t[0:64, n * 128:(n + 1) * 128], ALU.mult),
                                 reads=[rd, G4[hh]], writes=[uo] if hh == 0 else (), pwrites=[uo] if hh > 0 else ())
                        S.dma("sp", UT[b, 1, 4 * g:4 * g + 4, :, n * 128:(n + 1) * 128].rearrange("h p t -> p h t"), uo[:], reads=[uo], pwrites=[bUT[b]])

        def chain_gens():
            if os.environ.get("KDBG_NO_P5") is None:
                yield from p5_units()
                yield from p6_units()
        gen5 = chain_gens()
        NBLK = T // 64
        n_scan_blocks = int(os.environ.get("KDBG_SCAN_BLOCKS", NBLK))

        def tok0(d, B):
            if d == 0:
                return 64 * B
            if B < 4:
                return LC - 64 * (B + 1)
            return T - 64 * (B - 3)

        def v3(buf):
            return buf.t[:, :, :]

        def f2(buf):
            return buf.t[:, :, :].rearrange("p a v -> p (a v)")

        gstep = 0
        for B in range(n_scan_blocks):
            u = B % 2
            for d in range(2):
                t0 = tok0(d, B)
                for a in range(5):
                    for jh in range(2):
                        S.dma("sp", SCb[d][u].t[:, a, jh * 4:(jh + 1) * 4, :], SC[d, :, a, jh * 4:(jh + 1) * 4, t0:t0 + 64], reads=[bSC],
                              writes=[SCb[d][u]] if (a == 0 and jh == 0) else (), pwrites=() if (a == 0 and jh == 0) else [SCb[d][u]])
                S.dma("sp", Vb[d][u][:], VT[t0:t0 + 64, :], reads=[bVT], writes=[Vb[d][u]])
            sc = [SCb[d][u] for d in range(2)]
            pend = None

            def emit_t4(pd):
                s_, tls_, Sn_, par_ = pd
                for d in range(2):
                    for j in range(8):
                        S.op("act", lambda e, d=d, j=j, o=t4[d][par_], sn=Sn_[d], r_=sc[d].t[:, 4, j, tls_[d]:tls_[d] + 1]: e.activation(o.t[:, j, :], sn.t[:, j, :], AF.Identity, scale=r_),
                             reads=[Sn_[d], sc[d]], writes=[t4[d][par_]] if j == 0 else (), pwrites=[t4[d][par_]] if j else ())

            def emit_y(pd):
                s_, tls_, Sn_, par_ = pd
                for d in range(2):
                    t32 = tls_[d] % 32
                    first = (s_ % 32 == 0)
                    S.op("pe", lambda e, d=d, i_=t4[d][par_], t32=t32, s_=s_: e.matmul(psY.t[d * 64:(d + 1) * 64, :], Z3[:, 31 - t32:95 - t32], f2(i_), start=(s_ % 32 == 0), stop=(s_ % 32 == 31)),
                         reads=[t4[d][par_], cst], writes=[psY] if (first and d == 0) else (), pwrites=() if (first and d == 0) else [psY])

            def evac_half(sh, B=B):
                yb = ysb[sh]
                S.op("act", lambda e, yb=yb: e.copy(yb[:], psY.t[:, :]), reads=[psY], writes=[yb])
                for d in range(2):
                    hb = sh if d == 0 else 1 - sh
                    tb = tok0(d, B) + 32 * hb
                    for c2 in range(2):
                        S.dma("pool", YD[d, tb:tb + 32, c2 * 512:(c2 + 1) * 512], yb.t[d * 64 + c2 * 32:d * 64 + c2 * 32 + 32, :], reads=[yb], pwrites=[bYD])

            for s in range(64):
                tls = [s, 63 - s]
                par = gstep % 2
                Sc = [Sb[d][par] for d in range(2)]
                Sn = [Sb[d][1 - par] for d in range(2)]
                gstep += 1

                def bcs(d, a):
                    return sc[d].t[:, a, :, tls[d]].unsqueeze(2).to_broadcast([128, 8, 64])
                pv = [psV[d] for d in range(2)]
                ta = [tA[d][s % 2] for d in range(2)]
                tc = [tC[d][s % 2] for d in range(2)]
                for d in range(2):
                    for c2 in range(2):
                        S.op("pe", lambda e, d=d, c2=c2, p=pv[d], tl=tls[d], vb_=Vb[d][u]: e.matmul(p.t[c2 * 64:(c2 + 1) * 64, :], ident[0:64, tl:tl + 1].to_broadcast([64, 64]),
                                                                                              vb_.t[0:64, c2 * 512:(c2 + 1) * 512], start=True, stop=True),
                             reads=[Vb[d][u], cst], writes=[pv[d]] if c2 == 0 else (), pwrites=[pv[d]] if c2 == 1 else ())
                for d in range(2):
                    S.op("dve", lambda e, d=d, o=ta[d], sc_=Sc[d], kb=bcs(d, 3): e.tensor_tensor(v3(o), v3(sc_), kb, ALU.mult), reads=[Sc[d], sc[d]], writes=[ta[d]])
                for d in range(2):
                    S.op("pe", lambda e, d=d, i_=ta[d]: e.matmul(psSA[d].t[:, :], bones, f2(i_), start=True, stop=True), reads=[ta[d], cst], writes=[psSA[d]])
                for d in range(2):
                    S.op("dve", lambda e, d=d, o=tc[d], kb=bcs(d, 2), p=pv[d]: e.tensor_tensor(v3(o), p.t[:, :].rearrange("p (a v) -> p a v", v=64), kb, ALU.mult), reads=[pv[d], sc[d]], writes=[tc[d]])
                for d in range(2):
                    S.op("pool", lambda e, d=d, sc_=Sc[d], wb_=bcs(d, 0): e.tensor_tensor(v3(Sw[d]), v3(sc_), wb_, ALU.mult), reads=[Sc[d], sc[d]], writes=[Sw[d]])
                    S.op("pool", lambda e, d=d, t_=tc[d]: e.tensor_tensor(v3(Sw[d]), v3(Sw[d]), v3(t_), ALU.add), reads=[Sw[d], tc[d]], writes=[Sw[d]])
                if pend is not None:
                    emit_t4(pend)
                    emit_y(pend)
                    if s == 32:
                        evac_half(0)
                next(gen5, None)
                S.op("dve", lambda e, kb=bcs(0, 1): e.tensor_tensor(v3(tB[0]), psSA[0].t[:, :].rearrange("p (a v) -> p a v", v=64), kb, ALU.mult), reads=[psSA[0], sc[0]], writes=[tB[0]])
                S.op("dve", lambda e, sn=Sn[0]: e.tensor_tensor(v3(sn), v3(Sw[0]), v3(tB[0]), ALU.subtract), reads=[Sw[0], tB[0]], writes=[Sn[0]])
                S.op("dve", lambda e, kb=bcs(1, 1): e.tensor_tensor(v3(tB[1]), psSA[1].t[:, :].rearrange("p (a v) -> p a v", v=64), kb, ALU.mult), reads=[psSA[1], sc[1]], writes=[tB[1]])
                S.op("dve", lambda e, sn=Sn[1]: e.tensor_tensor(v3(sn), v3(Sw[1]), v3(tB[1]), ALU.subtract), reads=[Sw[1], tB[1]], writes=[Sn[1]])
                pend = (s, tls, Sn, s % 2)
            emit_t4(pend)
            emit_y(pend)
            evac_half(1)
        for _ in gen5:
            pass
        phase_end(st)
        if stop_after in ("P4", "P5", "P6"):
            break

        st = ExitStack()
        gng = bc_load(st, W["gn_g"][l, :, :], 512, name="gng")
        gnb = bc_load(st, W["gn_b"][l, :, :], 512, name="gnb")
        wbo0 = sbt(st, [128, 4, D], "wbo0")
        wbo1 = sbt(st, [128, 4, D], "wbo1")
        wbo2 = sbt(st, [128, 4, D], "wbo2")
        wo = sbt(st, [128, 8, D], "wo")
        S.dma("sp", wbo0[:], W["wbo"][l, 0, :, :].rearrange("(k p) n -> p k n", p=128), reads=[bW], writes=[wbo0])
        S.dma("sp", wbo1[:], W["wbo"][l, 1, :, :].rearrange("(k p) n -> p k n", p=128), reads=[bW], writes=[wbo1])
        S.dma("sp", wbo2[:], W["wbo"][l, 2, :, :].rearrange("(k p) n -> p k n", p=128), reads=[bW], writes=[wbo2])
        for kh in range(2):
            S.dma("sp", wo.t[:, kh * 4:(kh + 1) * 4, :], W["w_out"][l, kh * 512:(kh + 1) * 512, :].rearrange("(k p) n -> p k n", p=128), reads=[bW], writes=[wo] if kh == 0 else (), pwrites=[wo] if kh else ())
        gateb = [sbt(st, [128, D], "gateb") for _ in range(3)]
        for r in range(3):
            S.dma("sp", gateb[r][:], MODG[r:r + 1, :].partition_broadcast(128), reads=[bMODG], writes=[gateb[r]])
        y0s = [sbt(st, [128, 512], "y0") for _ in range(2)]
        y1s = [sbt(st, [128, 512], "y1") for _ in range(2)]
        vts = [sbt(st, [128, 512], "vt") for _ in range(2)]
        sgas = [sbt(st, [128, 512], "sga") for _ in range(2)]
        bss = [sbt(st, [128, 8], "bs") for _ in range(2)]
        sgs = [sbt(st, [128, 3 * D], "sg") for _ in range(2)]
        xts = [sbt(st, [128, D], "xt") for _ in range(2)]
        utb = [sbt(st, [128, 4, 128], "utb") for _ in range(2)]
        utc = [sbt(st, [128, 4, 128], "utc") for _ in range(2)]
        st8 = sbt(st, [128, 8], "st8")
        yc = sbt(st, [128, 512], "yc")
        ysq = sbt(st, [128, 512], "ysq")
        uAT = sbt(st, [128, 4, 128], "uAT")
        mt = sbt(st, [128, D], "mt")
        tmpm = sbt(st, [128, 512], "tmpm")
        mTt = sbt(st, [128, 8, 128], "mTt")
        xo = [sbt(st, [128, D], "xo") for _ in range(2)]
        it = 0
        for b in range(NB):
            for i in (range(2, NT) if last else range(NT)):
                u = it % 2
                it += 1
                r = b if i >= 2 else 2
                y0, y1, vt, sga, bs_, sg, xt = y0s[u], y1s[u], vts[u], sgas[u], bss[u], sgs[u], xts[u]
                rows = slice(i * 128, (i + 1) * 128)

                def perm_src(a, c):
                    return a.rearrange("p (c bb qq v) -> p c bb qq v", c=2, bb=2, qq=4)[:, c, b, :, :]

                def perm_dst(tile, c):
                    return tile.t[:, :].rearrange("p (qq c v) -> p c qq v", qq=4, c=2)[:, c, :, :]
                for c in range(2):
                    S.dma("sp", perm_dst(y0, c), perm_src(YD[0, rows, :], c), reads=[bYD], writes=[y0] if c == 0 else (), pwrites=[y0] if c else ())
                    S.dma("sp", perm_dst(y1, c), perm_src(YD[1, rows, :], c), reads=[bYD], writes=[y1] if c == 0 else (), pwrites=[y1] if c else ())
                    S.dma("sp", perm_dst(vt, c), perm_src(VT[rows, :], c), reads=[bVT], writes=[vt] if c == 0 else (), pwrites=[vt] if c else ())
                S.dma("sp", sga[:], ZTM[b, rows, ZO_GA:ZO_GA + 512], reads=[bZTM[b]], writes=[sga])
                S.dma("sp", bs_[:], BS[b, rows, :], reads=[bBS[b]], writes=[bs_])
                S.dma("sp", sg[:], ZTM[b, rows, ZO_ZG:ZO_ZG + 3 * D], reads=[bZTM[b]], writes=[sg])
                S.dma("sp", xt[:], Xcur[b, rows, :], reads=[bXcur], writes=[xt])
                S.dma("sp", utb[u][:], UT[b, 0, :, :, :].rearrange("h p t -> (h p) t").rearrange("(k q) t -> q k t", q=128)[:, :, rows], reads=[bUT[b]], writes=[utb[u]])
                S.dma("sp", utc[u][:], UT[b, 1, :, :, :].rearrange("h p t -> (h p) t").rearrange("(k q) t -> q k t", q=128)[:, :, rows], reads=[bUT[b]], writes=[utc[u]])
                S.op("pool", lambda e, y0=y0, y1=y1: e.tensor_tensor(y0[:], y0[:], y1[:], ALU.add), reads=[y0, y1], writes=[y0])
                y3 = y0.t[:, :].rearrange("p (h v) -> p h v", v=64)
                yc3 = yc.t[:, :].rearrange("p (h v) -> p h v", v=64)
                S.op("dve", lambda e, y3=y3: e.tensor_reduce(st8[:], y3, AX.X, ALU.add), reads=[y0], writes=[st8])
                S.op("dve", lambda e: e.tensor_scalar(st8[:], st8[:], 1.0 / 64, None, ALU.mult), reads=[st8], writes=[st8])
                S.op("dve", lambda e, y3=y3, yc3=yc3: e.tensor_tensor(yc3, y3, st8.t[:, :].unsqueeze(2).to_broadcast([128, 8, 64]), ALU.subtract), reads=[y0, st8], writes=[yc])
                S.op("pool", lambda e: e.tensor_tensor(ysq[:], yc[:], yc[:], ALU.mult), reads=[yc], writes=[ysq])
                S.op("dve", lambda e: e.tensor_reduce(st8[:], ysq.t[:, :].rearrange("p (h v) -> p h v", v=64), AX.X, ALU.add), reads=[ysq], writes=[st8])
                rstd_from_ss(None, st8, 8, 1.0 / 64, GN_EPS)
                S.op("dve", lambda e, yc3=yc3: e.tensor_tensor(yc3, yc3, st8.t[:, :].unsqueeze(2).to_broadcast([128, 8, 64]), ALU.mult), reads=[yc, st8], writes=[yc])
                S.op("dve", lambda e: e.tensor_tensor(yc[:], yc[:], gng[:], ALU.mult), reads=[yc, gng], writes=[yc])
                S.op("pool", lambda e: e.tensor_tensor(yc[:], yc[:], gnb[:], ALU.add), reads=[yc, gnb], writes=[yc])
                S.op("dve", lambda e, vt=vt, bs_=bs_: e.tensor_tensor(vt.t[:, :].rearrange("p (h v) -> p h v", v=64), vt.t[:, :].rearrange("p (h v) -> p h v", v=64),
                                                                      bs_.t[:, :].unsqueeze(2).to_broadcast([128, 8, 64]), ALU.mult), reads=[vt, bs_], writes=[vt])
                S.op("pool", lambda e, vt=vt: e.tensor_tensor(yc[:], yc[:], vt[:], ALU.add), reads=[yc, vt], writes=[yc])
                S.op("dve", lambda e, sga=sga: e.tensor_tensor(yc[:], yc[:], sga[:], ALU.mult), reads=[yc, sga], writes=[yc])
                if "uA" in dbg and l == 0 and b == 0 and i == 2:
                    uAd = nc.dram_tensor("uAd", [128, 512], F32, kind="ExternalOutput").ap()
                    S.dma("pool", uAd[:, :], yc[:], reads=[yc])
                pt = nps()
                for j in range(4):
                    S.op("pe", lambda e, pt=pt, j=j: e.transpose(pt.t[:, j * 128:(j + 1) * 128], yc.t[:, j * 128:(j + 1) * 128], ident),
                         reads=[yc, cst], writes=[pt] if j == 0 else (), pwrites=[pt] if j > 0 else ())
                evac(uAT[:], pt.t[:, :].rearrange("p (j f) -> p j f", f=128), reads=[pt], writes=[uAT])
                for cg in range(2):
                    cs_ = slice(cg * 512, (cg + 1) * 512)
                    pA, pB, pC = nps(), nps(), nps()
                    for k in range(4):
                        mm(pA, pA.t[:, :], uAT.t[:, k, :], wbo0.t[:, k, cs_], k == 0, k == 3, [uAT, wbo0])
                    for k in range(4):
                        mm(pB, pB.t[:, :], utb[u].t[:, k, :], wbo1.t[:, k, cs_], k == 0, k == 3, [utb[u], wbo1])
                    for k in range(4):
                        mm(pC, pC.t[:, :], utc[u].t[:, k, :], wbo2.t[:, k, cs_], k == 0, k == 3, [utc[u], wbo2])
                    S.op("dve", lambda e, pA=pA, cs_=cs_, sg=sg, cg=cg: e.tensor_tensor(mt.t[:, cs_], pA.t[:, :], sg.t[:, cg * 512:(cg + 1) * 512], ALU.mult), reads=[pA, sg],
                         writes=[mt] if cg == 0 else (), pwrites=[mt] if cg > 0 else ())
                    S.op("dve", lambda e, pB=pB, sg=sg, cg=cg: e.tensor_tensor(tmpm[:], pB.t[:, :], sg.t[:, D + cg * 512:D + (cg + 1) * 512], ALU.mult), reads=[pB, sg], writes=[tmpm])
                    S.op("pool", lambda e, cs_=cs_: e.tensor_tensor(mt.t[:, cs_], mt.t[:, cs_], tmpm[:], ALU.add), reads=[mt, tmpm], writes=[mt])
                    S.op("dve", lambda e, pC=pC, sg=sg, cg=cg: e.tensor_tensor(tmpm[:], pC.t[:, :], sg.t[:, 2 * D + cg * 512:2 * D + (cg + 1) * 512], ALU.mult), reads=[pC, sg], writes=[tmpm])
                    S.op("pool", lambda e, cs_=cs_: e.tensor_tensor(mt.t[:, cs_], mt.t[:, cs_], tmpm[:], ALU.add), reads=[mt, tmpm], writes=[mt])
                for half in range(2):
                    pt = nps()
                    for j in range(4):
                        k = half * 4 + j
                        S.op("pe", lambda e, pt=pt, j=j, k=k: e.transpose(pt.t[:, j * 128:(j + 1) * 128], mt.t[:, k * 128:(k + 1) * 128], ident),
                             reads=[mt, cst], writes=[pt] if j == 0 else (), pwrites=[pt] if j > 0 else ())
                    evac(mTt.t[:, half * 4:(half + 1) * 4, :], pt.t[:, :].rearrange("p (j f) -> p j f", f=128), reads=[pt], writes=[mTt] if half == 0 else (), pwrites=[mTt] if half > 0 else ())
                xo_ = xo[u]
                for cg in range(2):
                    cs_ = slice(cg * 512, (cg + 1) * 512)
                    pO = nps()
                    for k in range(8):
                        mm(pO, pO.t[:, :], mTt.t[:, k, :], wo.t[:, k, cs_], k == 0, k == 7, [mTt, wo])
                    S.op("dve", lambda e, pO=pO, cs_=cs_, xo_=xo_, r=r: e.tensor_tensor(xo_.t[:, cs_], pO.t[:, :], gateb[r].t[:, cs_], ALU.mult), reads=[pO, gateb[r]],
                         writes=[xo_] if cg == 0 else (), pwrites=[xo_] if cg > 0 else ())
                    S.op("pool", lambda e, cs_=cs_, xo_=xo_, xt=xt: e.tensor_tensor(xo_.t[:, cs_], xo_.t[:, cs_], xt.t[:, cs_], ALU.add), reads=[xo_, xt], writes=[xo_])
                if last:
                    S.dma("pool", yout[b, (i - 2) * 128:(i - 1) * 128, :], xo_[:], reads=[xo_], pwrites=[bX[2]])
                else:
                    S.dma("pool", Xnext[b, rows, :], xo_[:], reads=[xo_], pwrites=[bXnext])
        phase_end(st)
        Xcur, bXcur = Xnext, bXnext

    S.barrier()
    S.emit()
    outer.close()
    return nc, S.total


def _rope_full(rot_dim):
    grid_w = 64
    t = np.arange(TL)
    row = (t // grid_w).astype(np.float32)
    col = (t % grid_w).astype(np.float32)
    axis_dim = rot_dim // 2
    inv = (np.float32(10000.0) ** (-(2.0 * np.arange(axis_dim // 2, dtype=np.float32)) / np.float32(axis_dim))).astype(np.float32)
    ang = np.concatenate([row[:, None] * inv, col[:, None] * inv], axis=-1).astype(np.float32)
    cos, sin = np.cos(ang).astype(np.float32), np.sin(ang).astype(np.float32)
    q = rot_dim // 4
    cr, cc, sr, sc = cos[:, :q], cos[:, q:], sin[:, :q], sin[:, q:]
    cosF = np.concatenate([cr, cr, cc, cc], axis=-1)
    sinF = np.concatenate([-sr, sr, -sc, sc], axis=-1)
    out = np.zeros((T, 2 * rot_dim), np.float32)
    out[:LC, :rot_dim] = 1.0
    out[LC:, :rot_dim] = cosF
    out[LC:, rot_dim:] = sinF
    return out


def _consts():
    c = np.zeros((128, 1024), np.float32)
    c[:, 0:128] = np.eye(128, dtype=np.float32)
    c[0:64, 128:192] = 1.0
    c[64:128, 192:256] = 1.0
    c[0:64, 256 + 127] = 1.0
    c[64:128, 256 + 191] = 1.0
    kj = np.arange(128)[:, None]
    qi = np.arange(128)[None, :]
    c[:, 512:640] = (kj >= qi)
    c[:, 640:768] = (kj <= qi)
    c[0:64, 768] = 1.0
    c[64:128, 769] = 1.0
    c[:, 776:840] = 1.0
    c[0:64, 848 + 31] = 1.0
    c[64:128, 848 + 63] = 1.0
    return c


def _fm(a, nch):
    lead = a.shape[:-1]
    return np.ascontiguousarray(np.swapaxes(a.reshape(lead + (nch, 128)), -1, -2))


def prep_shared(inp):
    f = lambda k: np.asarray(inp[k], np.float32)
    L = DEPTH
    sh = {
        "consts": _consts(), "ropeB": _rope_full(32), "ropeC": _rope_full(64),
        "ada_w": f("ada_w"), "ada_bT": _fm(f("ada_b"), 24), "norm_gT": _fm(f("norm_g"), 8),
        "w_in": f("w_in"), "mupT": _fm(f("a_mu_prev"), 14), "munT": _fm(f("a_mu_next"), 14),
        "w0T": np.ascontiguousarray(np.transpose(f("a_w0").reshape(L, 2, 4, 128), (0, 3, 1, 2)).reshape(L, 128, 8)),
        "a0T": np.ascontiguousarray(np.transpose(f("a_a0").reshape(L, 2, 4, 128), (0, 3, 1, 2)).reshape(L, 128, 8)),
        "kkT": _fm(f("a_k_k"), 4), "kaT": _fm(f("a_k_a"), 4), "rkT": _fm(f("a_r_k").reshape(L, 512), 4),
        "w_up": np.ascontiguousarray(f("a_w_up").reshape(L, 128, 512)), "a_up": np.ascontiguousarray(f("a_a_up").reshape(L, 128, 512)),
        "gn_g": f("a_gn_g").reshape(L, 1, 512), "gn_b": f("a_gn_b").reshape(L, 1, 512),
        "q_ln": f("b_q_ln").reshape(L, 1, 256), "kv_ln": f("b_kv_ln").reshape(L, 1, 128),
        "w_uq": f("b_w_uq"), "w_ukv": f("b_w_ukv"),
        "bqk_g": np.ascontiguousarray(np.concatenate([f("b_qn_g"), f("b_kn_g")], axis=-1).reshape(L, 1, 192)),
        "c_qn": f("c_qn_g").reshape(L, 1, 64), "c_kn": f("c_kn_g").reshape(L, 1, 64), "c_sink": f("c_sink").reshape(L, 1, 8),
        "wbo": f("w_branch_out"), "w_out": f("w_out"),
    }
    return sh


def prep_core(inp, core, sh):
    x = np.asarray(inp["x"], np.float32)
    ctx = np.asarray(inp["ctx"], np.float32)
    c = np.asarray(inp["c"], np.float32)
    c_ctx = np.asarray(inp["c_ctx"], np.float32)
    b0 = core * NB
    xin = np.concatenate([ctx[b0:b0 + NB], x[b0:b0 + NB]], axis=1)
    rows = np.stack([c[b0], c[b0 + 1], c_ctx], axis=0)
    cT = np.ascontiguousarray(np.transpose(rows.reshape(3, 8, 128), (2, 1, 0)))
    m = dict(sh)
    m["xin"] = np.ascontiguousarray(xin)
    m["cT"] = cT
    return m


_CACHE = {}


def kernel(**inputs):
    n_cores = 8
    if "nc" not in _CACHE:
        _CACHE["nc"] = build_program()[0]
    nc = _CACHE["nc"]
    sh = prep_shared(inputs)
    in_maps = [prep_core(inputs, c, sh) for c in range(n_cores)]
    res = run_bass_kernel_spmd(nc, in_maps, core_ids=list(range(n_cores)))
    out = np.concatenate([np.asarray(r["yout"], np.float32) for r in res.results], axis=0)
    return out
```
